# Optimizing a Trainium2 kernel written in Bass

```python
import jax, jax.numpy as jnp
from jax import lax
import numpy as np

D_MODEL = 1024
BATCH = 4
SEQ = 4096
DEPTH = 2
DEC_BATCH = 128
DEC_SEQ = 1
PAST_LEN = 16384
PAGE_SIZE = 128

N_A_LAYERS = DEPTH // 2
N_B_LAYERS = DEPTH - N_A_LAYERS
HEAD_DIM = 64
N_HEADS = D_MODEL // HEAD_DIM
N_KV_HEADS = 4
GROUP = N_HEADS // N_KV_HEADS
ROT_DIM = HEAD_DIM // 4
ROPE_THETA = 500000.0
WINDOW = 128
BLOCK = 128
CONV_WIDTH = 3
FFN_HIDDEN = -(-8 * D_MODEL // (3 * 256)) * 256
PLE_DIM = 256
RMS_EPS = 1e-6
NEG_INF = -1e30

kernel_name = "yoco_shortconv_swa_sink_decoder_step"


def rmsnorm(x, g):
    xf = x.astype(jnp.float32)
    y = xf * lax.rsqrt(jnp.mean(xf * xf, axis=-1, keepdims=True) + RMS_EPS)
    return (y * g.astype(jnp.float32)).astype(x.dtype)


def rope_partial(x, pos):
    half = ROT_DIM // 2
    inv_freq = jnp.power(jnp.float32(ROPE_THETA), -jnp.arange(half, dtype=jnp.float32) / half)
    ang = pos.astype(jnp.float32)[:, None] * inv_freq[None, :]
    cos = jnp.cos(ang)[:, None, :]
    sin = jnp.sin(ang)[:, None, :]
    xf = x.astype(jnp.float32)
    x1 = xf[..., :half]
    x2 = xf[..., half:ROT_DIM]
    out = jnp.concatenate([x1 * cos - x2 * sin, x2 * cos + x1 * sin, xf[..., ROT_DIM:]], axis=-1)
    return out.astype(x.dtype)


def short_conv_mixer(h, conv_state, w_in, w_conv, w_out):
    bcx = h @ w_in
    b_gate, c_gate, xin = jnp.split(bcx, 3, axis=-1)
    u = c_gate * xin
    ext = jnp.concatenate([conv_state.astype(u.dtype), u], axis=1)
    T = u.shape[1]
    conv = w_conv[0] * ext[:, 0:T]
    for j in range(1, CONV_WIDTH):
        conv = conv + w_conv[j] * ext[:, j:j + T]
    y = (b_gate * conv) @ w_out
    return y, ext[:, -(CONV_WIDTH - 1):]


def sink_softmax(s, sink_hg):
    sink = jnp.broadcast_to(sink_hg.astype(jnp.float32)[:, :, None, None], s.shape[:-1] + (1,))
    return jax.nn.softmax(jnp.concatenate([s, sink], axis=-1), axis=-1)[..., :-1]


def swa_prompt(q, k, v, sinks_h):
    B, S = q.shape[:2]
    nb = S // BLOCK
    scale = HEAD_DIM ** -0.5
    qb = q.reshape(B, nb, BLOCK, N_KV_HEADS, GROUP, HEAD_DIM).astype(jnp.float32)
    kb = k.reshape(B, nb, BLOCK, N_KV_HEADS, HEAD_DIM)
    vb = v.reshape(B, nb, BLOCK, N_KV_HEADS, HEAD_DIM)
    kk = jnp.concatenate([jnp.concatenate([jnp.zeros_like(kb[:, :1]), kb[:, :-1]], 1), kb], 2).astype(jnp.float32)
    vv = jnp.concatenate([jnp.concatenate([jnp.zeros_like(vb[:, :1]), vb[:, :-1]], 1), vb], 2).astype(jnp.float32)
    s = jnp.einsum('bnqhgd,bnkhd->bnhgqk', qb, kk) * scale
    blk = jnp.arange(nb, dtype=jnp.int32)[:, None] * BLOCK
    qpos = blk + jnp.arange(BLOCK, dtype=jnp.int32)[None, :]
    kpos = blk - BLOCK + jnp.arange(2 * BLOCK, dtype=jnp.int32)[None, :]
    diff = qpos[:, :, None] - kpos[:, None, :]
    valid = (diff >= 0) & (diff <= WINDOW) & (kpos[:, None, :] >= 0)
    s = jnp.where(valid[None, :, None, None, :, :], s, NEG_INF)
    p = sink_softmax(s, sinks_h.reshape(N_KV_HEADS, GROUP))
    o = jnp.einsum('bnhgqk,bnkhd->bnqhgd', p, vv)
    return o.reshape(B, S, N_HEADS * HEAD_DIM).astype(q.dtype)


def swa_sample(q, k_all, v_all, qpos, kpos, sinks_h):
    Bd, S = q.shape[:2]
    scale = HEAD_DIM ** -0.5
    qg = q.reshape(Bd, S, N_KV_HEADS, GROUP, HEAD_DIM).astype(jnp.float32)
    s = jnp.einsum('bqhgd,bkhd->bhgqk', qg, k_all.astype(jnp.float32)) * scale
    diff = qpos[:, None] - kpos[None, :]
    valid = (diff >= 0) & (diff <= WINDOW)
    s = jnp.where(valid, s, NEG_INF)
    p = sink_softmax(s, sinks_h.reshape(N_KV_HEADS, GROUP))
    o = jnp.einsum('bhgqk,bkhd->bqhgd', p, v_all.astype(jnp.float32))
    return o.reshape(Bd, S, N_HEADS * HEAD_DIM).astype(q.dtype)


def run_trunk(x, p, pos, conv_states, k_buf, v_buf, kpos, w_buf,
              norm_mix_g, norm_ffn_g, norm_ple_g, kv_norm_g, final_norm_g,
              conv_w_in, conv_w, conv_w_out, w_k, w_v, w_q, sinks, w_o,
              ffn_w_gate, ffn_w_up, ffn_w_down, ple_w_proj, ple_w_gate):
    B, T, _ = x.shape
    h = x
    new_conv = []
    k_state = v_state = None
    k_all = v_all = k = v = None
    for i in range(DEPTH):
        if i == N_A_LAYERS:
            hk = rmsnorm(h, kv_norm_g)
            k = rope_partial((hk @ w_k).reshape(B, T, N_KV_HEADS, HEAD_DIM), pos)
            v = (hk @ w_v).reshape(B, T, N_KV_HEADS, HEAD_DIM)
            if k_buf is None:
                k_state, v_state = k[:, -w_buf:], v[:, -w_buf:]
            else:
                k_all = jnp.concatenate([k_buf.astype(k.dtype), k], axis=1)
                v_all = jnp.concatenate([v_buf.astype(v.dtype), v], axis=1)
                k_state, v_state = k_all[:, -w_buf:], v_all[:, -w_buf:]
        hn = rmsnorm(h, norm_mix_g[i])
        if i < N_A_LAYERS:
            y, st = short_conv_mixer(hn, conv_states[i], conv_w_in[i], conv_w[i], conv_w_out[i])
            new_conv.append(st)
        else:
            j = i - N_A_LAYERS
            q = rope_partial((hn @ w_q[j]).reshape(B, T, N_HEADS, HEAD_DIM), pos)
            if k_buf is None:
                o = swa_prompt(q, k, v, sinks[j])
            else:
                o = swa_sample(q, k_all, v_all, pos, kpos, sinks[j])
            y = o @ w_o[j]
        h = h + y
        hf = rmsnorm(h, norm_ffn_g[i])
        h = h + (jax.nn.silu(hf @ ffn_w_gate[i]) * (hf @ ffn_w_up[i])) @ ffn_w_down[i]
        gate = jax.nn.sigmoid(rmsnorm(h, norm_ple_g[i]) @ ple_w_gate[i])
        h = h + gate * (p[i].astype(h.dtype) @ ple_w_proj[i])
    return rmsnorm(h, final_norm_g), jnp.stack(new_conv, axis=0), k_state, v_state


def setup_inputs(seed: int = 0) -> dict:
    key = jax.random.key(seed)
    ks = iter(jax.random.split(key, 40))
    f32 = jnp.float32

    def nrm(shape, scale):
        return jax.random.normal(next(ks), shape, f32) * scale

    w_buf = min(WINDOW, PAST_LEN)
    qw = N_HEADS * HEAD_DIM
    kvw = N_KV_HEADS * HEAD_DIM
    return {
        "x_prompt": nrm((BATCH, SEQ, D_MODEL), 1.0),
        "x_sample": nrm((DEC_BATCH, DEC_SEQ, D_MODEL), 1.0),
        "state_conv": nrm((N_A_LAYERS, DEC_BATCH, CONV_WIDTH - 1, D_MODEL), 1.0),
        "cache_k_win": nrm((DEC_BATCH, w_buf, N_KV_HEADS, HEAD_DIM), 1.0),
        "cache_v_win": nrm((DEC_BATCH, w_buf, N_KV_HEADS, HEAD_DIM), 1.0),
        "p_prompt": nrm((DEPTH, BATCH, SEQ, PLE_DIM), 1.0),
        "p_sample": nrm((DEPTH, DEC_BATCH, DEC_SEQ, PLE_DIM), 1.0),
        "norm_mix_g": 1.0 + nrm((DEPTH, D_MODEL), 0.02),
        "norm_ffn_g": 1.0 + nrm((DEPTH, D_MODEL), 0.02),
        "norm_ple_g": 1.0 + nrm((DEPTH, D_MODEL), 0.02),
        "kv_norm_g": 1.0 + nrm((D_MODEL,), 0.02),
        "final_norm_g": 1.0 + nrm((D_MODEL,), 0.02),
        "conv_w_in": nrm((N_A_LAYERS, D_MODEL, 3 * D_MODEL), D_MODEL ** -0.5),
        "conv_w": nrm((N_A_LAYERS, CONV_WIDTH, D_MODEL), CONV_WIDTH ** -0.5),
        "conv_w_out": nrm((N_A_LAYERS, D_MODEL, D_MODEL), D_MODEL ** -0.5),
        "w_k": nrm((D_MODEL, kvw), D_MODEL ** -0.5),
        "w_v": nrm((D_MODEL, kvw), D_MODEL ** -0.5),
        "w_q": nrm((N_B_LAYERS, D_MODEL, qw), D_MODEL ** -0.5),
        "sinks": nrm((N_B_LAYERS, N_HEADS), 0.5),
        "w_o": nrm((N_B_LAYERS, qw, D_MODEL), qw ** -0.5),
        "ffn_w_gate": nrm((DEPTH, D_MODEL, FFN_HIDDEN), D_MODEL ** -0.5),
        "ffn_w_up": nrm((DEPTH, D_MODEL, FFN_HIDDEN), D_MODEL ** -0.5),
        "ffn_w_down": nrm((DEPTH, FFN_HIDDEN, D_MODEL), FFN_HIDDEN ** -0.5),
        "ple_w_proj": nrm((DEPTH, PLE_DIM, D_MODEL), PLE_DIM ** -0.5),
        "ple_w_gate": nrm((DEPTH, D_MODEL, D_MODEL), D_MODEL ** -0.5),
    }


def reference(x_prompt, x_sample, state_conv, cache_k_win, cache_v_win, p_prompt, p_sample,
              norm_mix_g, norm_ffn_g, norm_ple_g, kv_norm_g, final_norm_g,
              conv_w_in, conv_w, conv_w_out, w_k, w_v, w_q, sinks, w_o,
              ffn_w_gate, ffn_w_up, ffn_w_down, ple_w_proj, ple_w_gate):
    weights = (norm_mix_g, norm_ffn_g, norm_ple_g, kv_norm_g, final_norm_g,
               conv_w_in, conv_w, conv_w_out, w_k, w_v, w_q, sinks, w_o,
               ffn_w_gate, ffn_w_up, ffn_w_down, ple_w_proj, ple_w_gate)
    w_buf = cache_k_win.shape[1]
    s_len = x_prompt.shape[1]
    d_len = x_sample.shape[1]
    pos_p = jnp.arange(s_len, dtype=jnp.int32)
    conv0 = jnp.zeros((N_A_LAYERS, x_prompt.shape[0], CONV_WIDTH - 1, D_MODEL), x_prompt.dtype)
    y_prompt, conv_state_prompt, k_win_prompt, v_win_prompt = run_trunk(
        x_prompt, p_prompt, pos_p, conv0, None, None, None, w_buf, *weights)
    pos_s = PAST_LEN + jnp.arange(d_len, dtype=jnp.int32)
    kpos_s = PAST_LEN - w_buf + jnp.arange(w_buf + d_len, dtype=jnp.int32)
    y_sample, conv_state_sample, k_win_sample, v_win_sample = run_trunk(
        x_sample, p_sample, pos_s, state_conv, cache_k_win, cache_v_win, kpos_s, w_buf, *weights)
    return (y_prompt, y_sample, conv_state_prompt, conv_state_sample,
            k_win_prompt, v_win_prompt, k_win_sample, v_win_sample)
```

```python
import numpy as np
from contextlib import ExitStack
import concourse.bass as bass
import concourse.mybir as mybir
from concourse.bass_utils import run_bass_kernel_spmd

F32 = mybir.dt.float32
BF16 = mybir.dt.bfloat16
ALU = mybir.AluOpType
AF = mybir.ActivationFunctionType
AX = mybir.AxisListType

NCORES = 8
D = 1024
KC = 8
FH = 2816
HC = 22
PLE = 256
HALO = 130
OWN = 2048
NS = 16
XR = HALO + OWN
RMS_EPS = 1e-6
SCALE = 0.125

ENGS = ("pe", "act", "dve", "pool", "sp")
MERGE = True


class Prog:
    def __init__(self, nc, es, n_dma_sp=24, n_dma_pool=8):
        self.nc = nc
        self.sem = {e: es.enter_context(nc.semaphore("s_" + e)) for e in ENGS[:4]}
        self.dsem = {}
        self.dpool = {"sp": [], "pool": []}
        for i in range(n_dma_sp):
            k = "dsp%d" % i
            self.dsem[k] = es.enter_context(nc.semaphore(k))
            self.dpool["sp"].append(k)
        for i in range(n_dma_pool):
            k = "dpl%d" % i
            self.dsem[k] = es.enter_context(nc.semaphore(k))
            self.dpool["pool"].append(k)
        self.dnext = {"sp": 0, "pool": 0}
        self.dcum = {k: 0 for k in self.dsem}
        self.ops = {e: [] for e in ENGS}
        self.tick = {e: 0 for e in ENGS}
        self.seen = {e: {} for e in ENGS}
        self.res = {}
        self.out_events = []
        self.cap = None
        self.debug_names = None

    def capture(self, fn):
        assert self.cap is None
        self.cap = []
        fn()
        ops, self.cap = self.cap, None
        return ops

    def replay(self, ops):
        for eng, fn, reads, writes, inc, dma, is_out in ops:
            self.add(eng, fn, reads=reads, writes=writes, inc=inc, dma=dma, is_out=is_out)

    @staticmethod
    def segments(ops):
        segs, cur = [], []
        for op in ops:
            cur.append(op)
            if op[0] == "pe" and op[4]:
                segs.append(cur)
                cur = []
        if cur:
            if segs:
                segs[-1].extend(cur)
            else:
                segs.append(cur)
        return segs

    def merge_replay(self, opsA, opsB):
        if not MERGE:
            self.replay(opsA)
            self.replay(opsB)
            return
        sa, sb = self.segments(opsA), self.segments(opsB)
        na, nb = len(sa), len(sb)
        out = []
        j = 0
        for i, seg in enumerate(sa):
            out.extend(seg)
            tgt = ((i + 1) * nb) // na
            while j < tgt:
                out.extend(sb[j])
                j += 1
        while j < nb:
            out.extend(sb[j])
            j += 1
        self.replay(out)

    def _handle(self, k):
        return self.sem[k] if k in self.sem else self.dsem[k]

    def add(self, eng, fn, reads=(), writes=(), inc=True, dma=False, is_out=False):
        if self.cap is not None:
            self.cap.append((eng, fn, tuple(reads), tuple(writes), inc, dma, is_out))
            return None
        waits = {}

        def need(ev):
            if ev is None:
                return
            k, v = ev
            if k == eng and eng == "pe":
                return
            if v > waits.get(k, 0):
                waits[k] = v

        for r in reads:
            s = self.res.get(r)
            if s is not None:
                need(s[0])
                if r.startswith("ps"):
                    for k, v in s[1].items():
                        if k != eng:
                            need((k, v))
        for w in writes:
            s = self.res.get(w)
            if s is not None:
                need(s[0])
                for k, v in s[1].items():
                    need((k, v))
        if dma:
            pool = self.dpool[eng]
            sk = pool[self.dnext[eng] % len(pool)]
            self.dnext[eng] += 1
            if self.dcum[sk] > 0:
                need((sk, self.dcum[sk]))
            self.dcum[sk] += 16
            ev = (sk, self.dcum[sk])
            incspec = (sk, 16)
        else:
            if inc:
                self.tick[eng] += 1
                ev = (eng, self.tick[eng])
                incspec = (eng, 1)
            else:
                ev = (eng, self.tick[eng] + 1)
                incspec = None
        wl = []
        for k, v in waits.items():
            if self.seen[eng].get(k, 0) < v:
                self.seen[eng][k] = v
                wl.append((k, v))
        self.ops[eng].append((fn, wl, incspec))
        for r in reads:
            s = self.res.get(r)
            if s is None:
                s = self.res[r] = [None, {}]
            if s[1].get(ev[0], 0) < ev[1]:
                s[1][ev[0]] = ev[1]
        for w in writes:
            self.res[w] = [ev, {}]
        if is_out:
            self.out_events.append(ev)
        return ev

    def finish(self):
        fin = {}
        for k, v in self.out_events:
            fin[k] = max(fin.get(k, 0), v)
        wl = [(k, v) for k, v in fin.items()]
        self.ops["sp"].append((None, wl, None))

    def emit(self):
        nc = self.nc
        with nc.Block() as block:
            def run(engname):
                def body(e):
                    for fn, wl, incspec in self.ops[engname]:
                        for k, v in wl:
                            e.wait_ge(self._handle(k), v)
                        if fn is None:
                            continue
                        op, args, kw = fn
                        ins = getattr(e, op)(*args, **kw)
                        if self.debug_names is not None:
                            try:
                                self.debug_names[ins.ins.name] = (engname, op, str(kw.get("func", "")), [str(a)[:80] for a in args] + [k + "=" + str(v)[:90] for k, v in kw.items() if k in ("out", "in_", "in0", "lhsT")])
                            except Exception:
                                pass
                        if incspec is not None:
                            ins.then_inc(self._handle(incspec[0]), incspec[1])
                return body
            block.tensor(run("pe"))
            block.scalar(run("act"))
            block.vector(run("dve"))
            block.gpsimd(run("pool"))
            block.sync(run("sp"))


class Ring:
    def __init__(self, items):
        self.items = items
        self.i = 0

    def next(self):
        it = self.items[self.i % len(self.items)]
        self.i += 1
        return it


class Pipe:
    def __init__(self):
        self.pending = None

    def item(self, main, post=None):
        main()
        if self.pending is not None:
            self.pending()
        self.pending = post

    def flush(self):
        if self.pending is not None:
            self.pending()
            self.pending = None


STS = [
    dict(ncols=1170, tiles=[(0, 386), (386, 384), (770, 400)], tiles1=[(130, 256), (386, 384), (770, 400)], npr=[386, 384, 384], samp=[False, False, True],
         xrow0=0, tabcol0=0, kcol0=2, kb0=0, ownc0=130, qb0=0, nqb=8, yrow0=0),
    dict(ncols=1024, tiles=[(0, 384), (384, 384), (768, 256)], tiles1=[(0, 384), (384, 384), (768, 256)], npr=[384, 384, 256], samp=[False, False, False],
         xrow0=1154, tabcol0=1170, kcol0=0, kb0=9, ownc0=0, qb0=8, nqb=8, yrow0=1024),
]
NCOLMAX = 1170
FFN_GROUPS = [(0, 4), (4, 4), (8, 4), (12, 4), (16, 3), (19, 3)]
G_MIX0, G_FFN0, G_PLE0, G_KV, G_MIX1, G_FFN1, G_PLE1 = range(7)


def build_program():
    nc = bass.Bass("TRN2", target_bir_lowering=False)

    def din(name, shape):
        return nc.dram_tensor(name, list(shape), F32, kind="ExternalInput")

    def dout(name, shape):
        return nc.dram_tensor(name, list(shape), F32, kind="ExternalOutput")

    xin = din("xin", [XR, D]); xsm = din("xsm", [NS, D]); stc = din("stc", [2 * NS, D])
    ck = din("ck", [NS, 128, 256]); cv = din("cv", [NS, 128, 256])
    pin = din("pin", [2, XR, PLE]); psm = din("psm", [2, NS, PLE])
    w_in = din("w_in", [D, 3 * D]); w_out = din("w_out", [D, D])
    wk = din("wk", [D, 256]); wv = din("wv", [D, 256]); wq = din("wq", [D, D]); wo = din("wo", [D, D])
    wg = din("wg", [2, D, FH]); wu = din("wu", [2, D, FH]); wd = din("wd", [2, FH, D])
    pproj = din("pproj", [2, PLE, D]); pgate = din("pgate", [2, D, D])
    gvec_d = din("gvec", [128, 56]); gfin_d = din("gfin", [128, D]); convw_d = din("convw", [128, 24])
    sink_d = din("sinkb", [128, 16]); idn_d = din("idn", [128, 128]); rm_d = din("rm", [128, 128])
    masks_d = din("masks", [3, 128, 128]); tab_d = din("tab", [2, 128, 2194]); esel_d = din("esel", [128, 256])

    y_o = dout("y", [OWN, D]); ys_o = dout("ys", [NS, D]); csp_o = dout("csp", [2, D]); css_o = dout("css", [NS, 2, D])
    kwp_o = dout("kwp", [128, 256]); vwp_o = dout("vwp", [128, 256])
    kws_o = dout("kws", [NS, 128, 256]); vws_o = dout("vws", [NS, 128, 256])

    def dap(t, off, dims):
        return bass.AP(t, off, [list(d) for d in dims])

    with ExitStack() as es:
        def T(name, shape, dt):
            return es.enter_context(nc.sbuf_tensor("sb_" + name, list(shape), dt))

        hT = T("hT", [128, KC, NCOLMAX], F32)
        A = T("A", [128, KC, NCOLMAX], BF16)
        B = T("B", [128, KC, NCOLMAX], BF16)
        KT = T("KT", [128, 2, 17 * 128], BF16)
        Vt = T("Vt", [128, 17, 4, 66], BF16)
        wbuf = [T("wbuf0", [128, 12288], BF16), T("wbuf1", [128, 12288], BF16)]
        pT = T("pT", [128, 2, NCOLMAX], BF16)
        tabs = T("tabs", [128, 2, 512], F32)
        xs = [T("xs0", [128, D], F32), T("xs1", [128, D], F32)]
        fr_t = [T("fr%d" % i, [128, 516], F32) for i in range(6)]
        hid_t = [T("hid%d" % i, [128, 4, 512], BF16) for i in range(2)]
        br_t = [T("br%d" % i, [128, 512], BF16) for i in range(4)]
        PT_t = [T("PT%d" % i, [128, 2, 512], BF16) for i in range(2)]
        o_sb = T("o_sb", [128, 1024], BF16)
        identF = T("identF", [128, 128], F32); identB = T("identB", [128, 128], BF16)
        onesM = T("onesM", [128, 128], BF16); RmB = T("RmB", [128, 128], BF16)
        masks = T("masks", [128, 3, 128], BF16)
        gvec = T("gvec", [128, 56], F32); gfin = T("gfin", [128, D], F32); convw = T("convw", [128, 24], F32)
        esink = T("esink", [128, 16], F32); epsT = T("epsT", [128, 1], F32)
        uhist = T("uhist", [128, KC, 2], F32); usamp = T("usamp", [128, KC, NS], F32); stT = T("stT", [128, KC, 2 * NS], F32)
        qs32 = T("qs32", [128, KC, NS], F32)
        small = T("small", [128, 64], F32)
        small2 = T("small2", [128, 32], F32)
        sm_pitch = [64, 32]
        Ks_t = [T("Ks%d" % i, [128, 256], F32) for i in range(2)]
        Vs_t = [T("Vs%d" % i, [128, 256], F32) for i in range(2)]
        Pz_t = [T("Pz%d" % i, [128, 16, 16], BF16) for i in range(2)]
        Vsb_t = [T("Vsb%d" % i, [128, 256], BF16) for i in range(2)]
        Esel = T("Esel", [128, 16, 16], BF16)
        prod = T("prod", [128, 1024], F32)
        kvout = T("kvout", [128, 512], F32)
        q_tm = xs[0][0:NS, :]
        knew = kvout[0:NS, 0:256]
        vnew = kvout[0:NS, 256:512]
        on_bf = o_sb[0:NS, :]
        ps = es.enter_context(nc.psum_tensor("ps", [128, 8, 512], F32))
        print("sbuf bytes remaining:", nc.sbuf_bytes_remaining)

        p = Prog(nc, es)
        import os as _os
        if _os.environ.get("MK_DEBUG"):
            p.debug_names = {}
            _NC_CACHE["dbg"] = p.debug_names

        def I(eng, opname, /, *args, reads=(), writes=(), inc=True, dma=False, is_out=False, **kw):
            return p.add(eng, (opname, args, kw), reads=reads, writes=writes, inc=inc, dma=dma, is_out=is_out)

        fr = Ring([("fr%d" % i, t) for i, t in enumerate(fr_t)])
        hidr = Ring([("hid%d" % i, t) for i, t in enumerate(hid_t)])
        brr = Ring([("br%d" % i, t) for i, t in enumerate(br_t)])
        PTr = Ring([("PT%d" % i, t) for i, t in enumerate(PT_t)])
        xsr = Ring([("xs%d" % i, t) for i, t in enumerate(xs)])
        class Sw:
            def __init__(self, ring):
                self.base = ring
                self.cur = ring

            def next(self):
                return self.cur.next()

        mmr = Sw(Ring([0, 1, 2, 3, 4]))
        auxr = Sw(Ring([5, 6, 7]))
        ringsA = (Ring([0, 1, 2]), Ring([5, 6]))
        ringsB = (Ring([3, 4]), Ring([7]))

        def psn(b):
            return "ps%d" % b

        cur = dict(st=0, l1=False)

        def trange(st, t):
            return st["tiles1"][t] if cur["l1"] else st["tiles"][t]
        touched = set()
        OVER = {0: [0], 1: [0, 1], 2: [1, 2]}

        def tname(buf, t):
            return "%s@%d_%d" % (buf, cur["st"], t)

        def W(buf, t):
            names = [tname(buf, t)]
            if cur["st"] == 1 and (buf, t) not in touched:
                touched.add((buf, t))
                names += ["%s@0_%d" % (buf, o) for o in OVER[t]]
            return names

        def ld(eng, out, in_, wr):
            I(eng, "dma_start", out=out, in_=in_, writes=wr, dma=True)

        ld("sp", identF[:], idn_d.ap(), ["identF"])
        ld("pool", RmB[:], rm_d.ap(), ["RmB"])
        ld("sp", gvec[:], gvec_d.ap(), ["gvec"])
        ld("sp", gfin[:], gfin_d.ap(), ["gfin"])
        ld("sp", convw[:], convw_d.ap(), ["convw"])
        ld("sp", esink[:], sink_d.ap(), ["esink"])
        ld("pool", Esel[:].rearrange("p a b -> p (a b)"), esel_d.ap(), ["Esel"])
        ld("pool", identB[:], idn_d.ap(), ["identB"])
        ld("pool", masks[:], dap(masks_d, 0, [[128, 128], [128 * 128, 3], [1, 128]]), ["masks"])
        I("pool", "memset", onesM[:], 1.0 / 1024.0, writes=["onesM"])
        I("pool", "memset", epsT[:], RMS_EPS, writes=["epsT"])
        I("pool", "memset", uhist[:], 0.0, writes=["uhist%d" % j for j in range(KC)])
        I("pool", "memset", Vt[:], 1.0, writes=["Vt%d" % i for i in range(17)])
        for i in range(2):
            I("pool", "memset", Pz_t[i][:], 0.0, writes=["Pz%d" % i])
        I("act", "activation", out=esink[:], in_=esink[:], func=AF.Exp, reads=["esink"], writes=["esink"])
        def passthrough():
            I("sp", "dma_start", out=dap(css_o, 0, [[2 * D, NS], [1, D]]), in_=dap(stc, D, [[2 * D, NS], [1, D]]), dma=True, is_out=True)
            I("sp", "dma_start", out=dap(kws_o, 0, [[128 * 256, NS], [1, 127 * 256]]), in_=dap(ck, 256, [[128 * 256, NS], [1, 127 * 256]]), dma=True, is_out=True)
            I("sp", "dma_start", out=dap(vws_o, 0, [[128 * 256, NS], [1, 127 * 256]]), in_=dap(cv, 256, [[128 * 256, NS], [1, 127 * 256]]), dma=True, is_out=True)

        def wload(dst_ap, src_t, off, ld_, ncols, nk, wname):
            src = dap(src_t, off, [[ld_, 128], [128 * ld_, nk], [1, ncols]])
            I("pool", "dma_start", out=dst_ap, in_=src, writes=[wname], dma=True)

        def v3(buf, lo, hi, a):
            return buf[:, lo:hi].rearrange("p (a b) -> p a b", a=a)

        def norm(st, t, outs):
            c0, n = trange(st, t)
            b = auxr.next()
            for kc in range(KC):
                nm, sq = brr.next()
                I("act", "activation", out=sq[:, 0:n], in_=hT[:, kc, c0:c0 + n], func=AF.Square, reads=[tname("hT", t)], writes=[nm])
                I("pe", "matmul", ps[:, b, 0:n], lhsT=onesM[:], rhs=sq[:, 0:n], start=(kc == 0), stop=(kc == KC - 1),
                  reads=[nm, "onesM"], writes=[psn(b)])
            n1, lnv = fr.next()
            I("act", "activation", out=lnv[:, 0:n], in_=ps[:, b, 0:n], func=AF.Ln, bias=epsT[:, 0:1], scale=1.0, reads=[psn(b), "epsT"], writes=[n1])
            n2, rstd = fr.next()
            I("act", "activation", out=rstd[:, 0:n], in_=lnv[:, 0:n], func=AF.Exp, scale=-0.5, reads=[n1], writes=[n2])
            for gi, dst, dname in outs:
                for kc in range(KC):
                    I("dve", "scalar_tensor_tensor", out=dst[:, kc, c0:c0 + n], in0=hT[:, kc, c0:c0 + n],
                      scalar=gvec[:, gi * 8 + kc:gi * 8 + kc + 1], in1=rstd[:, 0:n], op0=ALU.mult, op1=ALU.mult,
                      reads=[tname("hT", t), n2, "gvec"], writes=W(dname, t))

        def mm_acc(out_ap, b, lhs_fn, rhs_fn, nk, reads):
            for k in range(nk):
                I("pe", "matmul", out_ap, lhsT=lhs_fn(k), rhs=rhs_fn(k), start=(k == 0), stop=(k == nk - 1),
                  reads=reads, writes=[psn(b)], inc=(k == nk - 1))

        def resid_add(t, jo, b, c0, n):
            I("dve", "tensor_tensor", out=hT[:, jo, c0:c0 + n], in0=ps[:, b, 0:n], in1=hT[:, jo, c0:c0 + n], op=ALU.add,
              reads=[psn(b), tname("hT", t)], writes=[tname("hT", t)])

        def load_blocks(st, t):
            c0, n = st["tiles"][t]
            npr = st["npr"][t]
            blks = []
            r = 0
            while r < npr:
                nb = min(128, npr - r)
                blks.append(("p", st["xrow0"] + c0 + r, nb, c0 + r))
                r += nb
            if st["samp"][t]:
                blks.append(("s", 0, NS, c0 + npr))
            return blks

        def stage0_main(st, t):
            for kind, r0, nb, col in load_blocks(st, t):
                xn, xt = xsr.next()
                src = dap(xin, r0 * D, [[D, nb], [1, D]]) if kind == "p" else dap(xsm, 0, [[D, nb], [1, D]])
                I("pool" if cur["st"] == 1 else "sp", "dma_start", out=xt[0:nb, :], in_=src, writes=[xn], dma=True)
                for half in range(2):
                    b = mmr.next()
                    for j in range(4):
                        kc = half * 4 + j
                        I("pe", "transpose", ps[:, b, j * 128:j * 128 + nb], xt[0:nb, kc * 128:(kc + 1) * 128], identF[0:nb, 0:nb],
                          reads=[xn, "identF"], writes=[psn(b)], inc=(j == 3))
                    I("act", "activation", out=hT[:, half * 4:half * 4 + 4, col:col + nb],
                      in_=ps[:, b, :].rearrange("p (a b) -> p a b", a=4)[:, :, 0:nb], func=AF.Copy,
                      reads=[psn(b)], writes=W("hT", t))

        def load_state():
            xn, xt = xsr.next()
            I("sp", "dma_start", out=xt[0:32, :], in_=stc.ap(), writes=[xn], dma=True)
            b = mmr.next()
            for kc in range(KC):
                I("pe", "transpose", ps[:, b, kc * 32:(kc + 1) * 32], xt[0:32, kc * 128:(kc + 1) * 128], identF[0:32, 0:32],
                  reads=[xn, "identF"], writes=[psn(b)], inc=(kc == KC - 1))
            I("act", "activation", out=stT[:], in_=ps[:, b, 0:256].rearrange("p (a b) -> p a b", a=KC), func=AF.Copy,
              reads=[psn(b)], writes=["stT"])

        def load_p(st, t, layer):
            for kind, r0, nb, col in load_blocks(st, t):
                xn, xt = xsr.next()
                src = dap(pin, (layer * XR + r0) * PLE, [[PLE, nb], [1, PLE]]) if kind == "p" else dap(psm, layer * NS * PLE, [[PLE, nb], [1, PLE]])
                I("pool" if (cur["st"] == 1 and layer == 0) else "sp", "dma_start", out=xt[0:nb, 0:PLE], in_=src, writes=[xn], dma=True)
                b = auxr.next()
                for c in range(2):
                    I("pe", "transpose", ps[:, b, c * 128:c * 128 + nb], xt[0:nb, c * 128:(c + 1) * 128], identF[0:nb, 0:nb],
                      reads=[xn, "identF"], writes=[psn(b)], inc=(c == 1))
                I("act", "activation", out=pT[:, 0:2, col:col + nb], in_=ps[:, b, 0:256].rearrange("p (a b) -> p a b", a=2)[:, :, 0:nb],
                  func=AF.Copy, reads=[psn(b)], writes=W("pT", t))

        def s1_main(st, t, grp, wv_, wname):
            c0, n = st["tiles"][t]
            npr = st["npr"][t]
            has_s = st["samp"][t]
            w3 = v3(wv_, 0, 12288, KC)
            for jj in range(4):
                j = grp * 4 + jj
                bc, bx, bb = mmr.next(), mmr.next(), mmr.next()
                for sel, b in ((1, bc), (2, bx), (0, bb)):
                    mm_acc(ps[:, b, 0:n], b, lambda k: w3[:, k, sel * 512 + jj * 128: sel * 512 + (jj + 1) * 128],
                           lambda k: A[:, k, c0:c0 + n], KC, [wname, tname("A", t)])
                ncs, c_sb = fr.next()
                I("act", "activation", out=c_sb[:, 0:n], in_=ps[:, bc, 0:n], func=AF.Copy, reads=[psn(bc)], writes=[ncs])
                nub, ub = fr.next()
                I("act", "activation", out=ub[:, 0:2], in_=uhist[:, j, :], func=AF.Copy, reads=["uhist%d" % j], writes=[nub])
                I("dve", "tensor_tensor", out=ub[:, 2:2 + n], in0=c_sb[:, 0:n], in1=ps[:, bx, 0:n], op=ALU.mult, reads=[ncs, psn(bx), nub], writes=[nub])
                ntm, tmp = fr.next()
                nac, acc = fr.next()
                w0 = convw[:, j * 3 + 0:j * 3 + 1]; w1 = convw[:, j * 3 + 1:j * 3 + 2]; w2 = convw[:, j * 3 + 2:j * 3 + 3]
                I("act", "activation", out=tmp[:, 0:npr], in_=ub[:, 0:npr], func=AF.Copy, scale=w0, reads=[nub, "convw"], writes=[ntm])
                I("dve", "scalar_tensor_tensor", out=acc[:, 0:npr], in0=ub[:, 1:1 + npr], scalar=w1, in1=tmp[:, 0:npr], op0=ALU.mult, op1=ALU.add,
                  reads=[nub, ntm, "convw"], writes=[nac])
                I("dve", "scalar_tensor_tensor", out=acc[:, 0:npr], in0=ub[:, 2:2 + npr], scalar=w2, in1=acc[:, 0:npr], op0=ALU.mult, op1=ALU.add,
                  reads=[nub, nac, "convw"], writes=[nac])
                I("act", "activation", out=uhist[:, j, :], in_=ub[:, npr:npr + 2], func=AF.Copy, reads=[nub], writes=["uhist%d" % j])
                if has_s:
                    us = ub[:, 2 + npr:2 + npr + NS]
                    st0 = stT[:, j, 0:2 * NS:2]
                    st1 = stT[:, j, 1:2 * NS:2]
                    I("dve", "tensor_scalar", out=tmp[:, npr:npr + NS], in0=st0, scalar1=w0, scalar2=None, op0=ALU.mult, reads=["stT", "convw", ntm], writes=[ntm])
                    I("dve", "scalar_tensor_tensor", out=tmp[:, npr:npr + NS], in0=st1, scalar=w1, in1=tmp[:, npr:npr + NS], op0=ALU.mult, op1=ALU.add,
                      reads=["stT", ntm, "convw"], writes=[ntm])
                    I("dve", "scalar_tensor_tensor", out=acc[:, npr:npr + NS], in0=us, scalar=w2, in1=tmp[:, npr:npr + NS], op0=ALU.mult, op1=ALU.add,
                      reads=[nub, ntm, "convw", nac], writes=[nac])
                    I("act", "activation", out=usamp[:, j, :], in_=us, func=AF.Copy, reads=[nub], writes=["usamp"])
                I("dve", "tensor_tensor", out=B[:, j, c0:c0 + n], in0=ps[:, bb, 0:n], in1=acc[:, 0:n], op=ALU.mult,
                  reads=[psn(bb), nac], writes=W("B", t))

        def proj_main(st, t, wv_, wname, src, sname):
            c0, n = trange(st, t)
            w3 = v3(wv_, 0, 8192, KC)
            for jo in range(KC):
                b = mmr.next()
                mm_acc(ps[:, b, 0:n], b, lambda k: w3[:, k, jo * 128:(jo + 1) * 128], lambda k: src[:, k, c0:c0 + n], KC, [wname, tname(sname, t)])
                resid_add(t, jo, b, c0, n)

        def ffn_main(st, t, nm_, wv_, wname):
            c0, n = trange(st, t)
            Wg = v3(wv_, 0, 4096, KC); Wu = v3(wv_, 4096, 8192, KC); Wd = v3(wv_, 8192, 12288, 4)
            hn, hid = hidr.next()
            for mi in range(nm_):
                bg, bu = mmr.next(), mmr.next()
                mm_acc(ps[:, bg, 0:n], bg, lambda k: Wg[:, k, mi * 128:(mi + 1) * 128], lambda k: A[:, k, c0:c0 + n], KC, [wname, tname("A", t)])
                mm_acc(ps[:, bu, 0:n], bu, lambda k: Wu[:, k, mi * 128:(mi + 1) * 128], lambda k: A[:, k, c0:c0 + n], KC, [wname, tname("A", t)])
                nsg, sg = fr.next()
                I("act", "activation", out=sg[:, 0:n], in_=ps[:, bg, 0:n], func=AF.Silu, reads=[psn(bg)], writes=[nsg])
                I("dve", "tensor_tensor", out=hid[:, mi, 0:n], in0=sg[:, 0:n], in1=ps[:, bu, 0:n], op=ALU.mult, reads=[nsg, psn(bu), hn], writes=[hn])
            for jo in range(KC):
                b = mmr.next()
                mm_acc(ps[:, b, 0:n], b, lambda k: Wd[:, k, jo * 128:(jo + 1) * 128], lambda k: hid[:, k, 0:n], nm_, [wname, hn])
                resid_add(t, jo, b, c0, n)

        def ple_main(st, t, wv_, wname):
            c0, n = trange(st, t)
            Wgt = v3(wv_, 0, 8192, KC); Wpr = v3(wv_, 8192, 10240, 2)
            for jo in range(KC):
                bg, bp = mmr.next(), mmr.next()
                mm_acc(ps[:, bg, 0:n], bg, lambda k: Wgt[:, k, jo * 128:(jo + 1) * 128], lambda k: A[:, k, c0:c0 + n], KC, [wname, tname("A", t)])
                mm_acc(ps[:, bp, 0:n], bp, lambda k: Wpr[:, k, jo * 128:(jo + 1) * 128], lambda k: pT[:, k, c0:c0 + n], 2, [wname, tname("pT", t)])
                nsg, sg = fr.next()
                I("act", "activation", out=sg[:, 0:n], in_=ps[:, bg, 0:n], func=AF.Sigmoid, reads=[psn(bg)], writes=[nsg])
                I("dve", "tensor_tensor", out=sg[:, 0:n], in0=sg[:, 0:n], in1=ps[:, bp, 0:n], op=ALU.mult, reads=[nsg, psn(bp)], writes=[nsg])
                I("dve", "tensor_tensor", out=hT[:, jo, c0:c0 + n], in0=sg[:, 0:n], in1=hT[:, jo, c0:c0 + n], op=ALU.add,
                  reads=[nsg, tname("hT", t)], writes=[tname("hT", t)])

        def rope_a(pb, n):
            nq, qf = brr.next()
            I("act", "activation", out=qf[:, 0:n], in_=ps[:, pb, 0:n], func=AF.Copy, reads=[psn(pb)], writes=[nq])
            return (nq, pb), qf

        def rope_b(nqpb, qf, n, toff=0):
            nq, pb = nqpb
            rb = auxr.next()
            I("pe", "matmul", ps[:, rb, 0:n], lhsT=RmB[:], rhs=qf[:, 0:n], start=True, stop=True, reads=[nq, "RmB"], writes=[psn(rb)])
            n1, t1 = fr.next()
            I("dve", "tensor_tensor", out=t1[:, 0:n], in0=ps[:, pb, 0:n], in1=tabs[:, 0, toff:toff + n], op=ALU.mult, reads=[psn(pb), "tabs", nq], writes=[n1])
            n2, t2 = fr.next()
            I("dve", "tensor_tensor", out=t2[:, 0:n], in0=ps[:, rb, 0:n], in1=tabs[:, 1, toff:toff + n], op=ALU.mult, reads=[psn(rb), "tabs"], writes=[n2])
            return n1, t1, n2, t2

        def kvq_main(st, t, wv_, wname):
            c0, n = st["tiles"][t]
            npr = st["npr"][t]
            has_s = st["samp"][t]
            Wk = v3(wv_, 0, 2048, KC); Wv = v3(wv_, 2048, 4096, KC); Wq = v3(wv_, 4096, 12288, KC)
            I("sp", "dma_start", out=tabs[:, :, 0:n], in_=dap(tab_d, st["tabcol0"] + c0, [[2194, 128], [128 * 2194, 2], [1, n]]), writes=["tabs"], dma=True)
            ka = max(0, st["kcol0"] - c0)
            nk = npr - ka
            kp0 = st["kb0"] * 128 + (c0 + ka - st["kcol0"])
            kbs = [(kp0 // 128 + i, ka + 128 * i) for i in range(nk // 128)]
            last_rel = None
            for kb, rel in kbs:
                if kb == 16:
                    last_rel = rel
            pendq = []

            def flush(keep=0):
                while len(pendq) > keep:
                    pendq.pop(0)()

            def k_fin(ch, nq, qf):
                n1, t1, n2, t2 = rope_b(nq, qf, n)
                I("dve", "tensor_tensor", out=t1[:, 0:n], in0=t1[:, 0:n], in1=t2[:, 0:n], op=ALU.add, reads=[n1, n2], writes=[n1])
                I("act", "activation", out=KT[:, ch, kp0:kp0 + nk], in_=t1[:, ka:ka + nk], func=AF.Copy, reads=[n1], writes=["KT%d" % kb for kb, _ in kbs])
                if last_rel is not None:
                    ob = auxr.next()
                    I("pe", "transpose", ps[:, ob, 0:128], t1[:, last_rel:last_rel + 128], identF[:], reads=[n1, "identF"], writes=[psn(ob)])
                    I("act", "activation", out=kvout[:, ch * 128:(ch + 1) * 128], in_=ps[:, ob, 0:128], func=AF.Copy, reads=[psn(ob)], writes=["kvoutK"])
                    if ch == 1:
                        I("sp", "dma_start", out=kwp_o.ap(), in_=kvout[:, 0:256], reads=["kvoutK"], dma=True, is_out=True)
                if has_s:
                    ob = auxr.next()
                    I("pe", "transpose", ps[0:NS, ob, 0:128], t1[:, npr:npr + NS], identF[:], reads=[n1, "identF"], writes=[psn(ob)])
                    I("act", "activation", out=knew[:, ch * 128:(ch + 1) * 128], in_=ps[0:NS, ob, 0:128], func=AF.Copy, reads=[psn(ob)], writes=["kvoutK"])
                    if ch == 1:
                        I("sp", "dma_start", out=dap(kws_o, 127 * 256, [[128 * 256, NS], [1, 256]]), in_=knew[:, :], reads=["kvoutK"], dma=True, is_out=True)

            q0, nqc = st["tiles1"][t]
            qoff = q0 - c0

            def q_fin(cq, nq, qf):
                n1, t1, n2, t2 = rope_b(nq, qf, nqc, qoff)
                I("dve", "tensor_tensor", out=A[:, cq, q0:q0 + nqc], in0=t1[:, 0:nqc], in1=t2[:, 0:nqc], op=ALU.add, reads=[n1, n2], writes=W("A", t))
                if has_s:
                    I("dve", "tensor_tensor", out=qs32[:, cq, :], in0=t1[:, npr - qoff:npr - qoff + NS], in1=t2[:, npr - qoff:npr - qoff + NS], op=ALU.add, reads=[n1, n2], writes=["qs32"])

            for ch in range(2):
                pb = mmr.next()
                mm_acc(ps[:, pb, 0:n], pb, lambda k: Wk[:, k, ch * 128:(ch + 1) * 128], lambda k: A[:, k, c0:c0 + n], KC, [wname, tname("A", t)])
                nq, qf = rope_a(pb, n)
                flush(0)
                pendq.append(lambda ch=ch, nq=nq, qf=qf: k_fin(ch, nq, qf))
            for vi, (kb, rel) in enumerate(kbs):
                pb = mmr.next()
                mm_acc(ps[:, pb, 0:256], pb, lambda k: A[:, k, c0 + rel:c0 + rel + 128], lambda k: Wv[:, k, :], KC, [wname, tname("A", t)])
                if vi == 0:
                    flush()
                I("act", "activation", out=Vt[:, kb, :, 0:64], in_=ps[:, pb, 0:256].rearrange("p (a b) -> p a b", a=4), func=AF.Copy,
                  reads=[psn(pb)], writes=["Vt%d" % kb])
                if kb == 16:
                    I("act", "activation", out=kvout[:, 256:512], in_=ps[:, pb, 0:256], func=AF.Copy, reads=[psn(pb)], writes=["kvoutV"])
                    I("sp", "dma_start", out=vwp_o.ap(), in_=kvout[:, 256:512], reads=["kvoutV"], dma=True, is_out=True)
            if has_s:
                pb = mmr.next()
                mm_acc(ps[0:NS, pb, 0:256], pb, lambda k: A[:, k, c0 + npr:c0 + npr + NS], lambda k: Wv[:, k, :], KC, [wname, tname("A", t)])
                I("act", "activation", out=vnew[:, :], in_=ps[0:NS, pb, 0:256], func=AF.Copy, reads=[psn(pb)], writes=["kvoutV"])
                I("sp", "dma_start", out=dap(vws_o, 127 * 256, [[128 * 256, NS], [1, 256]]), in_=vnew[:, :], reads=["kvoutV"], dma=True, is_out=True)
            for cq in range(KC):
                pb = mmr.next()
                mm_acc(ps[:, pb, 0:nqc], pb, lambda k: Wq[:, k, cq * 128:(cq + 1) * 128], lambda k: B[:, k, q0:q0 + nqc], KC, [wname, tname("B", t)])
                nq, qf = rope_a(pb, nqc)
                flush(0)
                pendq.append(lambda cq=cq, nq=nq, qf=qf: q_fin(cq, nq, qf))
            flush()

        def tile_of(st, col):
            for i, (c0, n) in enumerate(st["tiles"]):
                if c0 <= col < c0 + n:
                    return i
            raise ValueError(col)

        prod_bf = prod[:, :].bitcast(BF16)
        tabs_bf = tabs[:, :, :].rearrange("p a b -> p (a b)").bitcast(BF16)
        PT_slot = [
            Ring([("PT0", PT_t[0]), ("PT1", PT_t[1])]),
            Ring([("prod", prod_bf[:, 0:1024].rearrange("p (a b) -> p a b", a=2)), ("prodB", prod_bf[:, 1024:2048].rearrange("p (a b) -> p a b", a=2))]),
        ]
        osb_slot = [("o_sb", o_sb[:, :]), ("xs1", xs[1][:, :].bitcast(BF16)[:, 0:1024])]
        small_slot = [small, small2]

        def att_phases(st, qi, slot):
            qb = st["qb0"] + qi
            qc0 = st["ownc0"] + 128 * qi
            tq = tile_of(st, qc0)
            PTs = {}
            osn, osb = osb_slot[slot]
            sm = small_slot[slot]

            def S_phase(g):
                pair, hh = g // 2, g % 2
                PTn, PTt = PT_slot[slot].next()
                PTs[g] = (PTn, PTt)
                for kbi, kb in enumerate((qb, qb + 1)):
                    sb = mmr.next()
                    I("pe", "matmul", ps[:, sb, :], lhsT=KT[hh * 64:(hh + 1) * 64, pair, kb * 128:(kb + 1) * 128],
                      rhs=A[hh * 64:(hh + 1) * 64, pair * 4:pair * 4 + 4, qc0:qc0 + 128], start=True, stop=True,
                      reads=["KT%d" % kb, tname("A", tq)], writes=[psn(sb)])
                    I("act", "activation", out=PTt[:, kbi, :], in_=ps[:, sb, :], func=AF.Exp, scale=SCALE, reads=[psn(sb)], writes=[PTn])
                    mi = 0 if kbi == 1 else (2 if qb == 0 else 1)
                    mk = bass.AP(masks, mi * 128, [[384, 128], [0, 4], [1, 128]])
                    pv = PTt[:, kbi, :].rearrange("p (a b) -> p a b", a=4)
                    I("dve", "tensor_tensor", out=pv, in0=pv, in1=mk, op=ALU.mult, reads=[PTn, "masks"], writes=[PTn])

            def PV_phase(g):
                PTn, PTt = PTs[g]
                ob = auxr.next()
                for r in range(4):
                    for kbi, kb in enumerate((qb, qb + 1)):
                        I("pe", "matmul", ps[:, ob, r * 65:(r + 1) * 65], lhsT=PTt[:, kbi, r * 128:(r + 1) * 128], rhs=Vt[:, kb, g, 0:65],
                          start=(kbi == 0), stop=(kbi == 1), reads=[PTn, "Vt%d" % kb], writes=[psn(ob)], inc=(r == 3 and kbi == 1))
                sn = "small%d_%d" % (slot, g)
                o3 = ps[:, ob, 0:260].rearrange("p (a b) -> p a b", a=4)
                I("dve", "tensor_tensor", out=sm[:, g * 8:g * 8 + 4], in0=o3[:, :, 64], in1=esink[:, 4 * g:4 * g + 4], op=ALU.add,
                  reads=[psn(ob), "esink"], writes=[sn])
                I("dve", "reciprocal", out=sm[:, g * 8 + 4:g * 8 + 8], in_=sm[:, g * 8:g * 8 + 4], reads=[sn], writes=[sn])
                I("dve", "tensor_tensor", out=osb[:, g * 256:(g + 1) * 256].rearrange("p (a b) -> p a b", a=4), in0=o3[:, :, 0:64],
                  in1=bass.AP(sm, g * 8 + 4, [[sm_pitch[slot], 128], [1, 4], [0, 64]]), op=ALU.mult, reads=[psn(ob), sn], writes=[osn])

            def T_phase():
                tb = auxr.next()
                psb = ps[:, tb, :].bitcast(BF16)
                for c in range(KC):
                    I("pe", "transpose", psb[:, c * 128:(c + 1) * 128], osb[:, c * 128:(c + 1) * 128], identB[:], reads=[osn, "identB"], writes=[psn(tb)], inc=(c == KC - 1))
                I("act", "activation", out=B[:, 0:KC, qc0:qc0 + 128], in_=psb.rearrange("p (a b) -> p a b", a=KC), func=AF.Copy, reads=[psn(tb)], writes=W("B", tq))

            return [lambda: S_phase(0), lambda: S_phase(1), lambda: PV_phase(0), lambda: S_phase(2), lambda: PV_phase(1),
                    lambda: S_phase(3), lambda: PV_phase(2), lambda: PV_phase(3), T_phase]

        def att_group(st, qis):
            phs = [att_phases(st, qi, j) for j, qi in enumerate(qis)]
            for k in range(len(phs[0])):
                for ph in phs:
                    ph[k]()

        def att_main(st, qi):
            att_group(st, [qi])

        def samp_attn(st):
            sc0 = 1154
            tq = 2
            for cq in range(KC):
                I("pe", "transpose", ps[0:NS, cq // 4, (cq % 4) * 128:(cq % 4 + 1) * 128], qs32[:, cq, :], identF[:],
                  reads=["qs32", "identF"], writes=[psn(0), psn(1)], inc=(cq == KC - 1))
            for pair in range(2):
                I("act", "activation", out=q_tm[:, pair * 512:(pair + 1) * 512].rearrange("p (h c d) -> p h c d", h=2, c=4),
                  in_=ps[0:NS, pair, :].rearrange("p (c h d) -> p h c d", c=4, h=2), func=AF.Copy, reads=[psn(pair)], writes=["xs0"])
            for s in range(NS):
                kn, Kst = ("Ks%d" % (s % 2), Ks_t[s % 2])
                vn, Vst = ("Vs%d" % (s % 2), Vs_t[s % 2])
                pzn, Pzt = ("Pz%d" % (s % 2), Pz_t[s % 2])
                I("sp", "dma_start", out=Kst[:, :], in_=dap(ck, s * 128 * 256, [[256, 128], [1, 256]]), writes=[kn], dma=True)
                I("sp", "dma_start", out=Vst[:, :], in_=dap(cv, s * 128 * 256, [[256, 128], [1, 256]]), writes=[vn], dma=True)
                vbn, Vsb = ("Vsb%d" % (s % 2), Vsb_t[s % 2])
                I("act", "activation", out=Vsb[:, :], in_=Vst[:, :], func=AF.Copy, reads=[vn], writes=[vbn])
                b0 = 2 * (s % 2)
                sel = bass.AP(identF, s, [[128, NS], [0, 128]])
                for half in range(2):
                    I("pe", "matmul", ps[:, b0 + half, :], lhsT=sel, rhs=q_tm[:, half * 512:(half + 1) * 512], start=True, stop=True,
                      reads=["xs0", "identF"], writes=[psn(b0 + half)])
                    I("dve", "tensor_tensor", out=prod[:, half * 512:(half + 1) * 512].rearrange("p (g r d) -> p g r d", g=2, r=4),
                      in0=ps[:, b0 + half, :].rearrange("p (g r d) -> p g r d", g=2, r=4),
                      in1=bass.AP(Kst, half * 128, [[256, 128], [64, 2], [0, 4], [1, 64]]), op=ALU.mult,
                      reads=[psn(b0 + half), kn], writes=["prod", "prodB"])
                I("dve", "tensor_reduce", out=small[:, 32:48], in_=prod[:, :].rearrange("p (h d) -> p h d", d=64), axis=AX.X, op=ALU.add,
                  reads=["prod", "prodB"], writes=["smallS"])
                I("act", "activation", out=Pzt[:, :, s], in_=small[:, 32:48], func=AF.Exp, scale=SCALE, reads=["smallS"], writes=[pzn])
                for h in range(16):
                    I("pe", "matmul", ps[0:NS, 5 + h // 8, (h % 8) * 64:(h % 8 + 1) * 64], lhsT=Pzt[:, h, :], rhs=Vsb[:, (h // 4) * 64:(h // 4 + 1) * 64],
                      start=(s == 0 and h % 8 == 0), stop=(s == NS - 1 and h % 8 == 7), reads=[pzn, vbn], writes=[psn(5), psn(6)], inc=False)
                I("pe", "matmul", ps[0:NS, 7, 0:16], lhsT=Esel[:, s, :], rhs=Pzt[:, :, s], start=(s == 0), stop=(s == NS - 1),
                  reads=[pzn, "Esel"], writes=[psn(7)])
                I("dve", "memset", Pzt[:, :, s], 0.0, writes=[pzn])
            I("dve", "tensor_tensor", out=prod[0:NS, :].rearrange("p (g r d) -> p g r d", g=4, r=4), in0=q_tm[:, :].rearrange("p (g r d) -> p g r d", g=4, r=4),
              in1=bass.AP(kvout, 0, [[512, NS], [64, 4], [0, 4], [1, 64]]), op=ALU.mult, reads=["xs0", "kvoutK"], writes=["prod", "prodB"])
            I("dve", "tensor_reduce", out=small[0:NS, 48:64], in_=prod[0:NS, :].rearrange("p (h d) -> p h d", d=64), axis=AX.X, op=ALU.add,
              reads=["prod", "prodB"], writes=["smallN"])
            I("act", "activation", out=small[0:NS, 48:64], in_=small[0:NS, 48:64], func=AF.Exp, scale=SCALE, reads=["smallN"], writes=["smallN"])
            I("dve", "tensor_tensor", out=small[0:NS, 32:48], in0=ps[0:NS, 7, 0:16], in1=small[0:NS, 48:64], op=ALU.add, reads=[psn(7), "smallN"], writes=["smallS"])
            I("dve", "tensor_tensor", out=small[0:NS, 32:48], in0=small[0:NS, 32:48], in1=esink[0:NS, :], op=ALU.add, reads=["smallS", "esink"], writes=["smallS"])
            I("dve", "reciprocal", out=small[0:NS, 32:48], in_=small[0:NS, 32:48], reads=["smallS"], writes=["smallS"])
            I("dve", "tensor_tensor", out=prod[0:NS, :].rearrange("p (g r d) -> p g r d", g=4, r=4),
              in0=bass.AP(kvout, 256, [[512, NS], [64, 4], [0, 4], [1, 64]]), in1=bass.AP(small, 48, [[64, NS], [4, 4], [1, 4], [0, 64]]), op=ALU.mult,
              reads=["kvoutV", "smallN", "prod", "prodB"], writes=["prod", "prodB"])
            I("dve", "tensor_tensor", out=prod[0:NS, :].rearrange("p (a b) -> p a b", a=2), in0=prod[0:NS, :].rearrange("p (a b) -> p a b", a=2),
              in1=ps[0:NS, 5:7, :], op=ALU.add, reads=["prod", "prodB", psn(5), psn(6)], writes=["prod", "prodB"])
            I("dve", "tensor_tensor", out=on_bf[:, :].rearrange("p (h d) -> p h d", d=64), in0=prod[0:NS, :].rearrange("p (h d) -> p h d", d=64),
              in1=bass.AP(small, 32, [[64, NS], [1, 16], [0, 64]]), op=ALU.mult, reads=["prod", "prodB", "smallS"], writes=["o_sb"])
            psb = ps[:, 4, :].bitcast(BF16)
            for c in range(KC):
                I("pe", "transpose", psb[:, c * NS:(c + 1) * NS], on_bf[:, c * 128:(c + 1) * 128], identB[0:NS, 0:NS], reads=["o_sb", "identB"], writes=[psn(4)], inc=(c == KC - 1))
            I("act", "activation", out=B[:, 0:KC, sc0:sc0 + NS], in_=psb[:, 0:KC * NS].rearrange("p (a b) -> p a b", a=KC), func=AF.Copy, reads=[psn(4)], writes=W("B", tq))

        pairr = Ring([0, 2])
        ysr = Ring([(["prod", "prodB"], prod), (["tabs"], tabs[:, :, :].rearrange("p a b -> p (a b)"))])

        def final_block(st, t, col, nb, dst_ap):
            b0 = pairr.next()
            for kc in range(KC):
                I("pe", "transpose", ps[0:nb, b0 + kc // 4, (kc % 4) * 128:(kc % 4 + 1) * 128], hT[:, kc, col:col + nb], identF[:],
                  reads=[tname("hT", t), "identF"], writes=[psn(b0), psn(b0 + 1)], inc=(kc == KC - 1))
            nj, junk = fr.next()
            for half in range(2):
                I("act", "activation", out=junk[0:nb, 0:512], in_=ps[0:nb, b0 + half, :], func=AF.Square, accum_out=small[0:nb, 16 + half:17 + half],
                  reads=[psn(b0 + half)], writes=[nj, "smallF"])
            I("dve", "tensor_tensor", out=small[0:nb, 18:19], in0=small[0:nb, 16:17], in1=small[0:nb, 17:18], op=ALU.add, reads=["smallF"], writes=["smallF"])
            I("act", "activation", out=small[0:nb, 19:20], in_=small[0:nb, 18:19], func=AF.Ln, bias=epsT[0:nb, 0:1], scale=1.0 / 1024.0, reads=["smallF", "epsT"], writes=["smallF"])
            I("act", "activation", out=small[0:nb, 20:21], in_=small[0:nb, 19:20], func=AF.Exp, scale=-0.5, reads=["smallF"], writes=["smallF"])
            yn, yt = ysr.next()
            for half in range(2):
                I("dve", "scalar_tensor_tensor", out=yt[0:nb, half * 512:(half + 1) * 512], in0=ps[0:nb, b0 + half, :], scalar=small[0:nb, 20:21],
                  in1=gfin[0:nb, half * 512:(half + 1) * 512], op0=ALU.mult, op1=ALU.mult, reads=[psn(b0 + half), "smallF", "gfin"], writes=yn)
            I("sp", "dma_start", out=dst_ap, in_=yt[0:nb, :], reads=yn, dma=True, is_out=True)

        def final_post(st, t):
            c0, n = st["tiles"][t]
            npr = st["npr"][t]
            col = max(c0, st["ownc0"])
            while col < c0 + npr:
                row = st["yrow0"] + (col - st["ownc0"])
                final_block(st, t, col, 128, dap(y_o, row * D, [[D, 128], [1, D]]))
                col += 128
            if st["samp"][t]:
                final_block(st, t, c0 + npr, NS, ys_o.ap())

        def tm_out(src3, width, dst_ap, wait_names):
            b0 = pairr.next()
            for kc in range(KC):
                I("pe", "transpose", ps[0:width, b0 + kc // 4, (kc % 4) * 128:(kc % 4 + 1) * 128], src3[:, kc, :], identF[:],
                  reads=wait_names + ["identF"], writes=[psn(b0), psn(b0 + 1)], inc=(kc == KC - 1))
            yn, yt = ysr.next()
            I("act", "activation", out=yt[0:width, :].rearrange("p (a b) -> p a b", a=2), in_=ps[0:width, b0:b0 + 2, :], func=AF.Copy,
              reads=[psn(b0), psn(b0 + 1)], writes=yn)
            I("sp", "dma_start", out=dst_ap, in_=yt[0:width, :], reads=yn, dma=True, is_out=True)

        pipe = Pipe()

        def with_st(si, fn, l1=False):
            def g():
                old = (cur["st"], cur["l1"])
                cur["st"], cur["l1"] = si, l1
                fn()
                cur["st"], cur["l1"] = old
            return g

        early = {}

        def groups_for(si):
            st = STS[si]
            nt = len(st["tiles"])
            G = []

            def item(main, post=None, l1=False, post_l1=None):
                pl1 = l1 if post_l1 is None else post_l1
                pipe.item(with_st(si, main, l1), with_st(si, post, pl1) if post is not None else None)

            def load_s1(grp):
                def f(wv_, wname):
                    w3 = v3(wv_, 0, 12288, KC)
                    for sel in range(3):
                        wload(w3[:, :, sel * 512:(sel + 1) * 512], w_in, sel * 1024 + grp * 512, 3 * D, 512, KC, wname)
                return f

            def s0_main(t):
                stage0_main(st, t)
                load_p(st, t, 0)

            def s0_post(t):
                norm(st, t, [(G_MIX0, A, "A")])

            if si == 1:
                early["s0_main0"] = with_st(1, lambda: s0_main(0))
                early["s0_post0"] = with_st(1, lambda: s0_post(0))

            def run_s1a(wv_, wname):
                if si == 0:
                    with_st(si, load_state)()
                def s0(t):
                    item(lambda t=t: s0_main(t), lambda t=t: s0_post(t))

                def s1(t):
                    item(lambda t=t: s1_main(st, t, 0, wv_, wname))

                if si == 1 and early.get("done"):
                    pipe.item(lambda: (early["s0_post0"](), with_st(1, lambda: s0_main(1))()), with_st(1, lambda: s0_post(1)))
                else:
                    s0(0)
                    s0(1)
                s1(0)
                for t in range(2, nt):
                    s0(t)
                    s1(t - 1)
                if si == 0:
                    passthrough()
                s1(nt - 1)

            def run_s1b(wv_, wname):
                for t in range(nt):
                    item(lambda t=t: s1_main(st, t, 1, wv_, wname))
                if si == 0:
                    item(lambda: tm_out(usamp, NS, dap(css_o, D, [[2 * D, NS], [1, D]]), ["usamp"]))
                else:
                    item(lambda: tm_out(uhist, 2, csp_o.ap(), ["uhist%d" % j for j in range(KC)]))

            G.append((load_s1(0), run_s1a))
            G.append((load_s1(1), run_s1b))

            def load_proj(src_t):
                def f(wv_, wname):
                    wload(v3(wv_, 0, 8192, KC), src_t, 0, D, 1024, KC, wname)
                return f

            def run_s2(wv_, wname):
                for t in range(nt):
                    item(lambda t=t: proj_main(st, t, wv_, wname, B, "B"), lambda t=t: norm(st, t, [(G_FFN0, A, "A")]))

            G.append((load_proj(w_out), run_s2))

            def ffn_groups(layer, gnext):
                for gi, (m0, nm_) in enumerate(FFN_GROUPS):
                    def lf(wv_, wname, m0=m0, nm_=nm_):
                        wload(v3(wv_, 0, 4096, KC)[:, :, 0:nm_ * 128], wg, layer * D * FH + m0 * 128, FH, nm_ * 128, KC, wname)
                        wload(v3(wv_, 4096, 8192, KC)[:, :, 0:nm_ * 128], wu, layer * D * FH + m0 * 128, FH, nm_ * 128, KC, wname)
                        wload(v3(wv_, 8192, 12288, 4)[:, 0:nm_, :], wd, layer * FH * D + m0 * 128 * D, D, 1024, nm_, wname)

                    def rf(wv_, wname, nm_=nm_, last=(gi == len(FFN_GROUPS) - 1)):
                        for t in range(nt):
                            post = (lambda t=t: norm(st, t, [(gnext, A, "A")])) if last else None
                            item(lambda t=t: ffn_main(st, t, nm_, wv_, wname), post, l1=(layer == 1))
                    G.append((lf, rf))

            ffn_groups(0, G_PLE0)

            def load_ple(layer):
                def f(wv_, wname):
                    wload(v3(wv_, 0, 8192, KC), pgate, layer * D * D, D, 1024, KC, wname)
                    wload(v3(wv_, 8192, 10240, 2), pproj, layer * PLE * D, D, 1024, 2, wname)
                return f

            def run_ple0(wv_, wname):
                for t in range(nt):
                    item(lambda t=t: ple_main(st, t, wv_, wname), lambda t=t: (norm(st, t, [(G_KV, A, "A"), (G_MIX1, B, "B")]), load_p(st, t, 1)))

            G.append((load_ple(0), run_ple0))

            def load_kvq(wv_, wname):
                wload(v3(wv_, 0, 2048, KC), wk, 0, 256, 256, KC, wname)
                wload(v3(wv_, 2048, 4096, KC), wv, 0, 256, 256, KC, wname)
                Wq3 = v3(wv_, 4096, 12288, KC)
                for pair in range(2):
                    for hh in range(2):
                        for c_ in range(4):
                            col = (pair * 4 + c_) * 128 + hh * 64
                            dst = Wq3[:, :, col:col + 64]
                            src = dap(wq, (pair * 8 + hh * 4 + c_) * 64, [[D, 128], [128 * D, KC], [1, 64]])
                            I("pool", "dma_start", out=dst, in_=src, writes=[wname], dma=True)

            qb_of_tile = [[] for _ in range(nt)]
            for qi in range(st["nqb"]):
                qb_of_tile[tile_of(st, st["ownc0"] + 128 * qi)].append(qi)

            def cap(fn, rings):
                mmr.cur, auxr.cur = rings
                try:
                    return p.capture(with_st(si, fn, cur["l1"]))
                finally:
                    mmr.cur, auxr.cur = mmr.base, auxr.base

            def att_ops(t):
                def f():
                    qs = qb_of_tile[t]
                    for i in range(0, len(qs), 2):
                        att_group(st, qs[i:i + 2])
                return cap(f, ringsA)

            def run_kvq(wv_, wname):
                item(lambda: kvq_main(st, 0, wv_, wname))
                for t in range(1, nt):
                    def main(t=t):
                        a = att_ops(t - 1)
                        b = cap(lambda: kvq_main(st, t, wv_, wname), ringsB)
                        p.merge_replay(a, b)
                    pipe.item(main)

            G.append((load_kvq, run_kvq))

            def run_wo(wv_, wname):
                def main0():
                    a = att_ops(nt - 1)
                    b = cap(lambda: proj_main(st, 0, wv_, wname, B, "B"), ringsB)
                    p.merge_replay(a, b)
                pipe.item(with_st(si, main0, True), with_st(si, lambda: norm(st, 0, [(G_FFN1, A, "A")]), True))
                if si == 0:
                    item(lambda: samp_attn(st))
                for t in range(1, nt):
                    item(lambda t=t: proj_main(st, t, wv_, wname, B, "B"), lambda t=t: norm(st, t, [(G_FFN1, A, "A")]), l1=True)

            G.append((load_proj(wo), run_wo))
            ffn_groups(1, G_PLE1)

            def run_ple1(wv_, wname):
                hoist = (si == 0 and "s0_main0" in early)
                for t in range(nt - 1 if hoist else nt):
                    item(lambda t=t: ple_main(st, t, wv_, wname), lambda t=t: final_post(st, t), l1=True)
                if hoist:
                    tl = nt - 1
                    pipe.item(lambda: (early["s0_main0"](), with_st(0, lambda: ple_main(st, tl, wv_, wname), True)()),
                              with_st(0, lambda: final_post(st, tl), True))
                    early["done"] = True

            G.append((load_ple(1), run_ple1))
            return G

        allg = groups_for(0) + groups_for(1)

        def do_load(i):
            if i < len(allg):
                allg[i][0](wbuf[i % 2], "wbuf%d" % (i % 2))

        do_load(0)
        do_load(1)
        for i, (lf, rf) in enumerate(allg):
            rf(wbuf[i % 2], "wbuf%d" % (i % 2))
            do_load(i + 2)
        pipe.flush()
        p.finish()
        p.emit()
        print("ops per engine:", {k: len(v) for k, v in p.ops.items()})
    return nc


_NC_CACHE = {}


def _rope_tables(pos):
    half = 8
    inv_freq = np.power(np.float32(500000.0), -np.arange(half, dtype=np.float32) / np.float32(half)).astype(np.float32)
    ang = (pos.astype(np.float32)[:, None] * inv_freq[None, :]).astype(np.float32)
    cos = np.cos(ang).astype(np.float32).T
    sin = np.sin(ang).astype(np.float32).T
    n = pos.shape[0]
    C = np.ones((128, n), np.float32)
    S = np.zeros((128, n), np.float32)
    for hb in range(2):
        base = hb * 64
        C[base:base + 8] = cos
        C[base + 8:base + 16] = cos
        S[base:base + 8] = -sin
        S[base + 8:base + 16] = sin
    return C, S


def prepare(x_prompt, x_sample, state_conv, cache_k_win, cache_v_win, p_prompt, p_sample,
           norm_mix_g, norm_ffn_g, norm_ple_g, kv_norm_g, final_norm_g,
           conv_w_in, conv_w, conv_w_out, w_k, w_v, w_q, sinks, w_o,
           ffn_w_gate, ffn_w_up, ffn_w_down, ple_w_proj, ple_w_gate):
    f32 = np.float32
    A_ = lambda a: np.ascontiguousarray(np.asarray(a, dtype=f32))
    x_prompt = A_(x_prompt); x_sample = A_(x_sample); state_conv = A_(state_conv)
    cache_k_win = A_(cache_k_win); cache_v_win = A_(cache_v_win); p_prompt = A_(p_prompt); p_sample = A_(p_sample)

    def colvec(g):
        return np.asarray(g, f32).reshape(KC, 128).T

    gains = [norm_mix_g[0], norm_ffn_g[0], norm_ple_g[0], kv_norm_g, norm_mix_g[1], norm_ffn_g[1], norm_ple_g[1]]
    gvec = np.ascontiguousarray(np.concatenate([colvec(g) for g in gains], axis=1))
    gfin = np.ascontiguousarray(np.broadcast_to(np.asarray(final_norm_g, f32)[None, :], (128, D)))
    cw = np.asarray(conv_w, f32)[0]
    convw = np.ascontiguousarray(np.stack([colvec(cw[j]) for j in range(3)], axis=2).reshape(128, 24))
    sinkb = np.ascontiguousarray(np.broadcast_to(np.asarray(sinks, f32)[0][None, :], (128, 16)))
    idn = np.eye(128, dtype=f32)
    rm = np.zeros((128, 128), f32)
    for m in range(128):
        d = m % 64
        if d < 8:
            rm[m + 8, m] = 1.0
        elif d < 16:
            rm[m - 8, m] = 1.0
    jj = np.arange(128)[:, None]; ii = np.arange(128)[None, :]
    mcur = (jj <= ii).astype(f32); mprev = (jj >= ii).astype(f32)
    esel = np.zeros((128, 16, 16), f32)
    for s in range(16):
        esel[:, s, s] = 1.0
    esel = esel.reshape(128, 256)

    shared = dict(
        w_in=A_(conv_w_in)[0], w_out=A_(conv_w_out)[0], wk=A_(w_k), wv=A_(w_v), wq=A_(w_q)[0], wo=A_(w_o)[0],
        wg=A_(ffn_w_gate), wu=A_(ffn_w_up), wd=A_(ffn_w_down), pproj=A_(ple_w_proj), pgate=A_(ple_w_gate),
        gvec=gvec, gfin=gfin, convw=convw, sinkb=sinkb, idn=idn, rm=rm, esel=esel)

    in_maps = []
    for c in range(NCORES):
        b, half = c // 2, c % 2
        t0 = half * OWN
        xin = np.zeros((XR, D), f32)
        pin = np.zeros((2, XR, PLE), f32)
        if half == 1:
            xin[:] = x_prompt[b, t0 - HALO:t0 + OWN]
            pin[:] = p_prompt[:, b, t0 - HALO:t0 + OWN]
        else:
            xin[HALO:] = x_prompt[b, 0:OWN]
            pin[:, HALO:] = p_prompt[:, b, 0:OWN]
        s0 = c * NS
        pos1 = np.concatenate([np.maximum(t0 - HALO + np.arange(1154), 0), np.full(NS, 16384)]).astype(f32)
        pos2 = (t0 + 1024 + np.arange(1024)).astype(f32)
        C1, S1 = _rope_tables(pos1)
        C2, S2 = _rope_tables(pos2)
        tab = np.ascontiguousarray(np.stack([np.concatenate([C1, C2], 1), np.concatenate([S1, S2], 1)], 0))
        masks = np.ascontiguousarray(np.stack([mcur, mprev, mprev if half == 1 else np.zeros_like(mprev)], 0))
        m = dict(shared)
        m.update(xin=xin, xsm=np.ascontiguousarray(x_sample[s0:s0 + NS, 0]), stc=np.ascontiguousarray(state_conv[0, s0:s0 + NS].reshape(2 * NS, D)),
                 ck=np.ascontiguousarray(cache_k_win[s0:s0 + NS].reshape(NS, 128, 256)), cv=np.ascontiguousarray(cache_v_win[s0:s0 + NS].reshape(NS, 128, 256)),
                 pin=pin, psm=np.ascontiguousarray(p_sample[:, s0:s0 + NS, 0]), masks=masks, tab=tab)
        in_maps.append(m)

    return in_maps


def assemble(R):
    f32 = np.float32
    y_prompt = np.zeros((4, 4096, D), f32); y_sample = np.zeros((128, 1, D), f32)
    csp = np.zeros((1, 4, 2, D), f32); css = np.zeros((1, 128, 2, D), f32)
    kwp = np.zeros((4, 128, 4, 64), f32); vwp = np.zeros((4, 128, 4, 64), f32)
    kws = np.zeros((128, 128, 4, 64), f32); vws = np.zeros((128, 128, 4, 64), f32)
    for c in range(NCORES):
        b, half = c // 2, c % 2
        r = R[c]
        y_prompt[b, half * OWN:(half + 1) * OWN] = r["y"]
        s0 = c * NS
        y_sample[s0:s0 + NS, 0] = r["ys"]
        css[0, s0:s0 + NS] = r["css"]
        kws[s0:s0 + NS] = r["kws"].reshape(NS, 128, 4, 64)
        vws[s0:s0 + NS] = r["vws"].reshape(NS, 128, 4, 64)
        if half == 1:
            csp[0, b] = r["csp"]
            kwp[b] = r["kwp"].reshape(128, 4, 64)
            vwp[b] = r["vwp"].reshape(128, 4, 64)
    return (y_prompt, y_sample, csp, css, kwp, vwp, kws, vws)


def kernel(**inputs):
    in_maps = prepare(**inputs)
    if "nc" not in _NC_CACHE:
        _NC_CACHE["nc"] = build_program()
    nc = _NC_CACHE["nc"]
    res = run_bass_kernel_spmd(nc, in_maps, core_ids=list(range(NCORES)))
    return assemble(res.results)
```

```python
import numpy as np
from contextlib import ExitStack
import concourse.bass as bass
import concourse.mybir as mybir
from concourse.bass_utils import run_bass_kernel_spmd

F32 = mybir.dt.float32
BF16 = mybir.dt.bfloat16
ALU = mybir.AluOpType
AF = mybir.ActivationFunctionType
AX = mybir.AxisListType

NCORES = 8
D = 1024
KC = 8
FH = 2816
HC = 22
PLE = 256
HALO = 130
OWN = 2048
NS = 16
XR = HALO + OWN
RMS_EPS = 1e-6
SCALE = 0.125

ENGS = ("pe", "act", "dve", "pool", "sp")
MERGE = True


class Prog:
    def __init__(self, nc, es, n_dma_sp=24, n_dma_pool=8):
        self.nc = nc
        self.sem = {e: es.enter_context(nc.semaphore("s_" + e)) for e in ENGS[:4]}
        self.dsem = {}
        self.dpool = {"sp": [], "pool": []}
        for i in range(n_dma_sp):
            k = "dsp%d" % i
            self.dsem[k] = es.enter_context(nc.semaphore(k))
            self.dpool["sp"].append(k)
        for i in range(n_dma_pool):
            k = "dpl%d" % i
            self.dsem[k] = es.enter_context(nc.semaphore(k))
            self.dpool["pool"].append(k)
        self.dpool["act"] = []
        for i in range(6):
            k = "dac%d" % i
            self.dsem[k] = es.enter_context(nc.semaphore(k))
            self.dpool["act"].append(k)
        self.dnext = {"sp": 0, "pool": 0, "act": 0}
        self.dcum = {k: 0 for k in self.dsem}
        self.ops = {e: [] for e in ENGS}
        self.tick = {e: 0 for e in ENGS}
        self.seen = {e: {} for e in ENGS}
        self.res = {}
        self.out_events = []
        self.cap = None
        self.debug_names = None

    def capture(self, fn):
        assert self.cap is None
        self.cap = []
        fn()
        ops, self.cap = self.cap, None
        return ops

    def replay(self, ops):
        for eng, fn, reads, writes, inc, dma, is_out in ops:
            self.add(eng, fn, reads=reads, writes=writes, inc=inc, dma=dma, is_out=is_out)

    @staticmethod
    def segments(ops):
        segs, cur = [], []
        for op in ops:
            cur.append(op)
            if op[0] == "pe" and op[4]:
                segs.append(cur)
                cur = []
        if cur:
            if segs:
                segs[-1].extend(cur)
            else:
                segs.append(cur)
        return segs

    def merge_replay(self, opsA, opsB):
        if not MERGE:
            self.replay(opsA)
            self.replay(opsB)
            return
        sa, sb = self.segments(opsA), self.segments(opsB)
        na, nb = len(sa), len(sb)
        out = []
        j = 0
        for i, seg in enumerate(sa):
            out.extend(seg)
            tgt = ((i + 1) * nb) // na
            while j < tgt:
                out.extend(sb[j])
                j += 1
        while j < nb:
            out.extend(sb[j])
            j += 1
        self.replay(out)

    def _handle(self, k):
        return self.sem[k] if k in self.sem else self.dsem[k]

    def add(self, eng, fn, reads=(), writes=(), inc=True, dma=False, is_out=False):
        if self.cap is not None:
            self.cap.append((eng, fn, tuple(reads), tuple(writes), inc, dma, is_out))
            return None
        waits = {}

        def need(ev):
            if ev is None:
                return
            k, v = ev
            if k == eng and eng == "pe":
                return
            if v > waits.get(k, 0):
                waits[k] = v

        for r in reads:
            s = self.res.get(r)
            if s is not None:
                need(s[0])
                if r.startswith("ps"):
                    for k, v in s[1].items():
                        if k != eng:
                            need((k, v))
        for w in writes:
            s = self.res.get(w)
            if s is not None:
                need(s[0])
                for k, v in s[1].items():
                    need((k, v))
        if dma:
            pool = self.dpool[eng]
            sk = pool[self.dnext[eng] % len(pool)]
            self.dnext[eng] += 1
            if self.dcum[sk] > 0:
                need((sk, self.dcum[sk]))
            self.dcum[sk] += 16
            ev = (sk, self.dcum[sk])
            incspec = (sk, 16)
        else:
            if inc:
                self.tick[eng] += 1
                ev = (eng, self.tick[eng])
                incspec = (eng, 1)
            else:
                ev = (eng, self.tick[eng] + 1)
                incspec = None
        wl = []
        for k, v in waits.items():
            if self.seen[eng].get(k, 0) < v:
                self.seen[eng][k] = v
                wl.append((k, v))
        self.ops[eng].append((fn, wl, incspec))
        for r in reads:
            s = self.res.get(r)
            if s is None:
                s = self.res[r] = [None, {}]
            if s[1].get(ev[0], 0) < ev[1]:
                s[1][ev[0]] = ev[1]
        for w in writes:
            self.res[w] = [ev, {}]
        if is_out:
            self.out_events.append(ev)
        return ev

    def finish(self):
        fin = {}
        for k, v in self.out_events:
            fin[k] = max(fin.get(k, 0), v)
        wl = [(k, v) for k, v in fin.items()]
        self.ops["sp"].append((None, wl, None))

    def emit(self):
        nc = self.nc
        with nc.Block() as block:
            def run(engname):
                def body(e):
                    for fn, wl, incspec in self.ops[engname]:
                        for k, v in wl:
                            e.wait_ge(self._handle(k), v)
                        if fn is None:
                            continue
                        op, args, kw = fn
                        ins = getattr(e, op)(*args, **kw)
                        if self.debug_names is not None:
                            _NC_CACHE.setdefault("dbg_waits", {})[ins.ins.name] = list(wl)
                            try:
                                self.debug_names[ins.ins.name] = (engname, op, str(kw.get("func", "")), [str(a)[:80] for a in args] + [k + "=" + str(v)[:90] for k, v in kw.items() if k in ("out", "in_", "in0", "lhsT")])
                            except Exception:
                                pass
                        if incspec is not None:
                            ins.then_inc(self._handle(incspec[0]), incspec[1])
                return body
            block.tensor(run("pe"))
            block.scalar(run("act"))
            block.vector(run("dve"))
            block.gpsimd(run("pool"))
            block.sync(run("sp"))


class Ring:
    def __init__(self, items):
        self.items = items
        self.i = 0

    def next(self):
        it = self.items[self.i % len(self.items)]
        self.i += 1
        return it


class Pipe:
    def __init__(self):
        self.pending = None

    def item(self, main, post=None):
        main()
        if self.pending is not None:
            self.pending()
        self.pending = post

    def flush(self):
        if self.pending is not None:
            self.pending()
            self.pending = None


STS = [
    dict(ncols=1170, tiles=[(0, 386), (386, 384), (770, 400)], tiles1=[(130, 256), (386, 384), (770, 400)], npr=[386, 384, 384], samp=[False, False, True],
         xrow0=0, tabcol0=0, kcol0=2, kb0=0, ownc0=130, qb0=0, nqb=8, yrow0=0),
    dict(ncols=1024, tiles=[(0, 384), (384, 384), (768, 256)], tiles1=[(0, 384), (384, 384), (768, 256)], npr=[384, 384, 256], samp=[False, False, False],
         xrow0=1154, tabcol0=1170, kcol0=0, kb0=9, ownc0=0, qb0=8, nqb=8, yrow0=1024),
]
NCOLMAX = 1170
FFN_GROUPS = [(0, 4), (4, 4), (8, 4), (12, 4), (16, 3), (19, 3)]
G_MIX0, G_FFN0, G_PLE0, G_KV, G_MIX1, G_FFN1, G_PLE1 = range(7)


def build_program():
    nc = bass.Bass("TRN2", target_bir_lowering=False)

    def din(name, shape):
        return nc.dram_tensor(name, list(shape), F32, kind="ExternalInput")

    def dout(name, shape):
        return nc.dram_tensor(name, list(shape), F32, kind="ExternalOutput")

    xin = din("xin", [XR, D]); xsm = din("xsm", [NS, D]); stc = din("stc", [2 * NS, D])
    ck = din("ck", [NS, 128, 256]); cv = din("cv", [NS, 128, 256])
    pin = din("pin", [2, XR, PLE]); psm = din("psm", [2, NS, PLE])
    w_in = din("w_in", [D, 3 * D]); w_out = din("w_out", [D, D])
    wk = din("wk", [D, 256]); wv = din("wv", [D, 256]); wq = din("wq", [D, D]); wo = din("wo", [D, D])
    wg = din("wg", [2, D, FH]); wu = din("wu", [2, D, FH]); wd = din("wd", [2, FH, D])
    pproj = din("pproj", [2, PLE, D]); pgate = din("pgate", [2, D, D])
    gvec_d = din("gvec", [128, 56]); gfin_d = din("gfin", [128, D]); convw_d = din("convw", [128, 24])
    sink_d = din("sinkb", [128, 16]); idn_d = din("idn", [128, 128]); rm_d = din("rm", [128, 128])
    masks_d = din("masks", [3, 128, 128]); tab_d = din("tab", [2, 128, 2194]); esel_d = din("esel", [128, 256])

    y_o = dout("y", [OWN, D]); ys_o = dout("ys", [NS, D]); csp_o = dout("csp", [2, D]); css_o = dout("css", [NS, 2, D])
    kwp_o = dout("kwp", [128, 256]); vwp_o = dout("vwp", [128, 256])
    kws_o = dout("kws", [NS, 128, 256]); vws_o = dout("vws", [NS, 128, 256])

    def dap(t, off, dims):
        return bass.AP(t, off, [list(d) for d in dims])

    with ExitStack() as es:
        def T(name, shape, dt):
            return es.enter_context(nc.sbuf_tensor("sb_" + name, list(shape), dt))

        hT = T("hT", [128, KC, NCOLMAX], F32)
        A = T("A", [128, KC, NCOLMAX], BF16)
        B = T("B", [128, KC, NCOLMAX], BF16)
        KT = T("KT", [128, 2, 17 * 128], BF16)
        Vt = T("Vt", [128, 17, 4, 66], BF16)
        wbuf = [T("wbuf0", [128, 12288], BF16), T("wbuf1", [128, 12288], BF16)]
        pT = T("pT", [128, 2, NCOLMAX], BF16)
        tabs = T("tabs", [128, 2, 512], F32)
        xs = [T("xs0", [128, D], F32), T("xs1", [128, D], F32)]
        fr_t = [T("fr%d" % i, [128, 516], F32) for i in range(6)]
        hid_t = [T("hid%d" % i, [128, 4, 512], BF16) for i in range(2)]
        br_t = [T("br%d" % i, [128, 512], BF16) for i in range(4)]
        PT_t = [T("PT%d" % i, [128, 2, 512], BF16) for i in range(2)]
        o_sb = T("o_sb", [128, 1024], BF16)
        identF = T("identF", [128, 128], F32); identB = T("identB", [128, 128], BF16)
        onesM = T("onesM", [128, 128], BF16); RmB = T("RmB", [128, 128], BF16)
        masks = T("masks", [128, 3, 128], BF16)
        gvec = T("gvec", [128, 56], F32); gfin = T("gfin", [128, D], F32); convw = T("convw", [128, 24], F32)
        esink = T("esink", [128, 16], F32); epsT = T("epsT", [128, 1], F32)
        uhist = T("uhist", [128, KC, 2], F32); usamp = T("usamp", [128, KC, NS], F32); stT = T("stT", [128, KC, 2 * NS], F32)
        qs32 = T("qs32", [128, KC, NS], F32)
        small = T("small", [128, 64], F32)
        small2 = T("small2", [128, 32], F32)
        sm_pitch = [64, 32]
        Ks_t = [T("Ks%d" % i, [128, 256], F32) for i in range(2)]
        Vs_t = [T("Vs%d" % i, [128, 256], F32) for i in range(2)]
        Pz_t = [T("Pz%d" % i, [128, 16, 16], BF16) for i in range(2)]
        Vsb_t = [T("Vsb%d" % i, [128, 256], BF16) for i in range(2)]
        Esel = T("Esel", [128, 16, 16], BF16)
        prod = T("prod", [128, 1024], F32)
        kvout = T("kvout", [128, 512], F32)
        q_tm = xs[0][0:NS, :]
        knew = kvout[0:NS, 0:256]
        vnew = kvout[0:NS, 256:512]
        on_bf = o_sb[0:NS, :]
        ps = es.enter_context(nc.psum_tensor("ps", [128, 8, 512], F32))
        print("sbuf bytes remaining:", nc.sbuf_bytes_remaining)

        p = Prog(nc, es)
        import os as _os
        if _os.environ.get("MK_DEBUG"):
            p.debug_names = {}
            _NC_CACHE["dbg"] = p.debug_names

        def I(eng, opname, /, *args, reads=(), writes=(), inc=True, dma=False, is_out=False, **kw):
            return p.add(eng, (opname, args, kw), reads=reads, writes=writes, inc=inc, dma=dma, is_out=is_out)

        fr = Ring([("fr%d" % i, t) for i, t in enumerate(fr_t)])
        hidr = Ring([("hid%d" % i, t) for i, t in enumerate(hid_t)])
        brr = Ring([("br%d" % i, t) for i, t in enumerate(br_t)])
        PTr = Ring([("PT%d" % i, t) for i, t in enumerate(PT_t)])
        xsr = Ring([("xs%d" % i, t) for i, t in enumerate(xs)])
        class Sw:
            def __init__(self, ring):
                self.base = ring
                self.cur = ring

            def next(self):
                return self.cur.next()

        mmr = Sw(Ring([0, 1, 2, 3, 4]))
        auxr = Sw(Ring([5, 6, 7]))
        ringsA = (Ring([0, 1, 2]), Ring([5, 6]))
        ringsB = (Ring([3, 4]), Ring([7]))

        def psn(b):
            return "ps%d" % b

        cur = dict(st=0, l1=False)

        def trange(st, t):
            return st["tiles1"][t] if cur["l1"] else st["tiles"][t]
        touched = set()
        OVER = {0: [0], 1: [0, 1], 2: [1, 2]}

        def tname(buf, t):
            return "%s@%d_%d" % (buf, cur["st"], t)

        def W(buf, t):
            names = [tname(buf, t)]
            if cur["st"] == 1 and (buf, t) not in touched:
                touched.add((buf, t))
                names += ["%s@0_%d" % (buf, o) for o in OVER[t]]
            return names

        def ld(eng, out, in_, wr):
            I(eng, "dma_start", out=out, in_=in_, writes=wr, dma=True)

        ld("sp", identF[:], idn_d.ap(), ["identF"])
        ld("pool", RmB[:], rm_d.ap(), ["RmB"])
        ld("sp", gvec[:], gvec_d.ap(), ["gvec"])
        ld("sp", gfin[:], gfin_d.ap(), ["gfin"])
        ld("sp", convw[:], convw_d.ap(), ["convw"])
        ld("sp", esink[:], sink_d.ap(), ["esink"])
        ld("pool", Esel[:].rearrange("p a b -> p (a b)"), esel_d.ap(), ["Esel"])
        ld("pool", identB[:], idn_d.ap(), ["identB"])
        ld("pool", masks[:], dap(masks_d, 0, [[128, 128], [128 * 128, 3], [1, 128]]), ["masks"])
        I("pool", "memset", onesM[:], 1.0 / 1024.0, writes=["onesM"])
        I("pool", "memset", epsT[:], RMS_EPS, writes=["epsT"])
        I("pool", "memset", uhist[:], 0.0, writes=["uhist%d" % j for j in range(KC)])
        I("pool", "memset", Vt[:], 1.0, writes=["Vt%d" % i for i in range(17)])
        for i in range(2):
            I("pool", "memset", Pz_t[i][:], 0.0, writes=["Pz%d" % i])
        I("act", "activation", out=esink[:], in_=esink[:], func=AF.Exp, reads=["esink"], writes=["esink"])
        def passthrough():
            I("sp", "dma_start", out=dap(css_o, 0, [[2 * D, NS], [1, D]]), in_=dap(stc, D, [[2 * D, NS], [1, D]]), dma=True, is_out=True)
            I("sp", "dma_start", out=dap(kws_o, 0, [[128 * 256, NS], [1, 127 * 256]]), in_=dap(ck, 256, [[128 * 256, NS], [1, 127 * 256]]), dma=True, is_out=True)
            I("sp", "dma_start", out=dap(vws_o, 0, [[128 * 256, NS], [1, 127 * 256]]), in_=dap(cv, 256, [[128 * 256, NS], [1, 127 * 256]]), dma=True, is_out=True)

        def wload(dst_ap, src_t, off, ld_, ncols, nk, wname):
            src = dap(src_t, off, [[ld_, 128], [128 * ld_, nk], [1, ncols]])
            I("pool", "dma_start", out=dst_ap, in_=src, writes=[wname], dma=True)

        def v3(buf, lo, hi, a):
            return buf[:, lo:hi].rearrange("p (a b) -> p a b", a=a)

        def norm(st, t, outs):
            c0, n = trange(st, t)
            b = auxr.next()
            for kc in range(KC):
                nm, sq = brr.next()
                I("act", "activation", out=sq[:, 0:n], in_=hT[:, kc, c0:c0 + n], func=AF.Square, reads=[tname("hT", t)], writes=[nm])
                I("pe", "matmul", ps[:, b, 0:n], lhsT=onesM[:], rhs=sq[:, 0:n], start=(kc == 0), stop=(kc == KC - 1),
                  reads=[nm, "onesM"], writes=[psn(b)])
            n1, lnv = fr.next()
            I("act", "activation", out=lnv[:, 0:n], in_=ps[:, b, 0:n], func=AF.Ln, bias=epsT[:, 0:1], scale=1.0, reads=[psn(b), "epsT"], writes=[n1])
            n2, rstd = fr.next()
            I("act", "activation", out=rstd[:, 0:n], in_=lnv[:, 0:n], func=AF.Exp, scale=-0.5, reads=[n1], writes=[n2])
            for gi, dst, dname in outs:
                for kc in range(KC):
                    I("dve", "scalar_tensor_tensor", out=dst[:, kc, c0:c0 + n], in0=hT[:, kc, c0:c0 + n],
                      scalar=gvec[:, gi * 8 + kc:gi * 8 + kc + 1], in1=rstd[:, 0:n], op0=ALU.mult, op1=ALU.mult,
                      reads=[tname("hT", t), n2, "gvec"], writes=W(dname, t))

        def mm_acc(out_ap, b, lhs_fn, rhs_fn, nk, reads):
            for k in range(nk):
                I("pe", "matmul", out_ap, lhsT=lhs_fn(k), rhs=rhs_fn(k), start=(k == 0), stop=(k == nk - 1),
                  reads=reads, writes=[psn(b)], inc=(k == nk - 1))

        def resid_add(t, jo, b, c0, n):
            I("dve", "tensor_tensor", out=hT[:, jo, c0:c0 + n], in0=ps[:, b, 0:n], in1=hT[:, jo, c0:c0 + n], op=ALU.add,
              reads=[psn(b), tname("hT", t)], writes=[tname("hT", t)])

        def load_blocks(st, t):
            c0, n = st["tiles"][t]
            npr = st["npr"][t]
            blks = []
            r = 0
            while r < npr:
                nb = min(128, npr - r)
                blks.append(("p", st["xrow0"] + c0 + r, nb, c0 + r))
                r += nb
            if st["samp"][t]:
                blks.append(("s", 0, NS, c0 + npr))
            return blks

        def stage0_main(st, t):
            for kind, r0, nb, col in load_blocks(st, t):
                xn, xt = xsr.next()
                src = dap(xin, r0 * D, [[D, nb], [1, D]]) if kind == "p" else dap(xsm, 0, [[D, nb], [1, D]])
                I("act" if cur["st"] == 1 else "sp", "dma_start", out=xt[0:nb, :], in_=src, writes=[xn], dma=True)
                for half in range(2):
                    b = mmr.next()
                    for j in range(4):
                        kc = half * 4 + j
                        I("pe", "transpose", ps[:, b, j * 128:j * 128 + nb], xt[0:nb, kc * 128:(kc + 1) * 128], identF[0:nb, 0:nb],
                          reads=[xn, "identF"], writes=[psn(b)], inc=(j == 3))
                    I("act", "activation", out=hT[:, half * 4:half * 4 + 4, col:col + nb],
                      in_=ps[:, b, :].rearrange("p (a b) -> p a b", a=4)[:, :, 0:nb], func=AF.Copy,
                      reads=[psn(b)], writes=W("hT", t))

        def load_state():
            xn, xt = xsr.next()
            I("sp", "dma_start", out=xt[0:32, :], in_=stc.ap(), writes=[xn], dma=True)
            b = mmr.next()
            for kc in range(KC):
                I("pe", "transpose", ps[:, b, kc * 32:(kc + 1) * 32], xt[0:32, kc * 128:(kc + 1) * 128], identF[0:32, 0:32],
                  reads=[xn, "identF"], writes=[psn(b)], inc=(kc == KC - 1))
            I("act", "activation", out=stT[:], in_=ps[:, b, 0:256].rearrange("p (a b) -> p a b", a=KC), func=AF.Copy,
              reads=[psn(b)], writes=["stT"])

        def load_p(st, t, layer):
            for kind, r0, nb, col in load_blocks(st, t):
                xn, xt = xsr.next()
                src = dap(pin, (layer * XR + r0) * PLE, [[PLE, nb], [1, PLE]]) if kind == "p" else dap(psm, layer * NS * PLE, [[PLE, nb], [1, PLE]])
                I("act" if (cur["st"] == 1 and layer == 0) else "sp", "dma_start", out=xt[0:nb, 0:PLE], in_=src, writes=[xn], dma=True)
                b = auxr.next()
                for c in range(2):
                    I("pe", "transpose", ps[:, b, c * 128:c * 128 + nb], xt[0:nb, c * 128:(c + 1) * 128], identF[0:nb, 0:nb],
                      reads=[xn, "identF"], writes=[psn(b)], inc=(c == 1))
                I("act", "activation", out=pT[:, 0:2, col:col + nb], in_=ps[:, b, 0:256].rearrange("p (a b) -> p a b", a=2)[:, :, 0:nb],
                  func=AF.Copy, reads=[psn(b)], writes=W("pT", t))

        def s1_main(st, t, grp, wv_, wname):
            c0, n = st["tiles"][t]
            npr = st["npr"][t]
            has_s = st["samp"][t]
            w3 = v3(wv_, 0, 12288, KC)
            for jj in range(4):
                j = grp * 4 + jj
                bc, bx, bb = mmr.next(), mmr.next(), mmr.next()
                for sel, b in ((1, bc), (2, bx), (0, bb)):
                    mm_acc(ps[:, b, 0:n], b, lambda k: w3[:, k, sel * 512 + jj * 128: sel * 512 + (jj + 1) * 128],
                           lambda k: A[:, k, c0:c0 + n], KC, [wname, tname("A", t)])
                ncs, c_sb = fr.next()
                I("act", "activation", out=c_sb[:, 0:n], in_=ps[:, bc, 0:n], func=AF.Copy, reads=[psn(bc)], writes=[ncs])
                nub, ub = fr.next()
                I("act", "activation", out=ub[:, 0:2], in_=uhist[:, j, :], func=AF.Copy, reads=["uhist%d" % j], writes=[nub])
                I("dve", "tensor_tensor", out=ub[:, 2:2 + n], in0=c_sb[:, 0:n], in1=ps[:, bx, 0:n], op=ALU.mult, reads=[ncs, psn(bx), nub], writes=[nub])
                ntm, tmp = fr.next()
                nac, acc = fr.next()
                w0 = convw[:, j * 3 + 0:j * 3 + 1]; w1 = convw[:, j * 3 + 1:j * 3 + 2]; w2 = convw[:, j * 3 + 2:j * 3 + 3]
                I("act", "activation", out=tmp[:, 0:npr], in_=ub[:, 0:npr], func=AF.Copy, scale=w0, reads=[nub, "convw"], writes=[ntm])
                I("dve", "scalar_tensor_tensor", out=acc[:, 0:npr], in0=ub[:, 1:1 + npr], scalar=w1, in1=tmp[:, 0:npr], op0=ALU.mult, op1=ALU.add,
                  reads=[nub, ntm, "convw"], writes=[nac])
                I("dve", "scalar_tensor_tensor", out=acc[:, 0:npr], in0=ub[:, 2:2 + npr], scalar=w2, in1=acc[:, 0:npr], op0=ALU.mult, op1=ALU.add,
                  reads=[nub, nac, "convw"], writes=[nac])
                I("act", "activation", out=uhist[:, j, :], in_=ub[:, npr:npr + 2], func=AF.Copy, reads=[nub], writes=["uhist%d" % j])
                if has_s:
                    us = ub[:, 2 + npr:2 + npr + NS]
                    st0 = stT[:, j, 0:2 * NS:2]
                    st1 = stT[:, j, 1:2 * NS:2]
                    I("dve", "tensor_scalar", out=tmp[:, npr:npr + NS], in0=st0, scalar1=w0, scalar2=None, op0=ALU.mult, reads=["stT", "convw", ntm], writes=[ntm])
                    I("dve", "scalar_tensor_tensor", out=tmp[:, npr:npr + NS], in0=st1, scalar=w1, in1=tmp[:, npr:npr + NS], op0=ALU.mult, op1=ALU.add,
                      reads=["stT", ntm, "convw"], writes=[ntm])
                    I("dve", "scalar_tensor_tensor", out=acc[:, npr:npr + NS], in0=us, scalar=w2, in1=tmp[:, npr:npr + NS], op0=ALU.mult, op1=ALU.add,
                      reads=[nub, ntm, "convw", nac], writes=[nac])
                    I("act", "activation", out=usamp[:, j, :], in_=us, func=AF.Copy, reads=[nub], writes=["usamp"])
                I("dve", "tensor_tensor", out=B[:, j, c0:c0 + n], in0=ps[:, bb, 0:n], in1=acc[:, 0:n], op=ALU.mult,
                  reads=[psn(bb), nac], writes=W("B", t))

        def proj_main(st, t, wv_, wname, src, sname):
            c0, n = trange(st, t)
            w3 = v3(wv_, 0, 8192, KC)
            for jo in range(KC):
                b = mmr.next()
                mm_acc(ps[:, b, 0:n], b, lambda k: w3[:, k, jo * 128:(jo + 1) * 128], lambda k: src[:, k, c0:c0 + n], KC, [wname, tname(sname, t)])
                resid_add(t, jo, b, c0, n)

        def ffn_main(st, t, nm_, wv_, wname):
            c0, n = trange(st, t)
            Wg = v3(wv_, 0, 4096, KC); Wu = v3(wv_, 4096, 8192, KC); Wd = v3(wv_, 8192, 12288, 4)
            hn, hid = hidr.next()
            for mi in range(nm_):
                bg, bu = mmr.next(), mmr.next()
                mm_acc(ps[:, bg, 0:n], bg, lambda k: Wg[:, k, mi * 128:(mi + 1) * 128], lambda k: A[:, k, c0:c0 + n], KC, [wname, tname("A", t)])
                mm_acc(ps[:, bu, 0:n], bu, lambda k: Wu[:, k, mi * 128:(mi + 1) * 128], lambda k: A[:, k, c0:c0 + n], KC, [wname, tname("A", t)])
                nsg, sg = fr.next()
                I("act", "activation", out=sg[:, 0:n], in_=ps[:, bg, 0:n], func=AF.Silu, reads=[psn(bg)], writes=[nsg])
                I("dve", "tensor_tensor", out=hid[:, mi, 0:n], in0=sg[:, 0:n], in1=ps[:, bu, 0:n], op=ALU.mult, reads=[nsg, psn(bu), hn], writes=[hn])
            for jo in range(KC):
                b = mmr.next()
                mm_acc(ps[:, b, 0:n], b, lambda k: Wd[:, k, jo * 128:(jo + 1) * 128], lambda k: hid[:, k, 0:n], nm_, [wname, hn])
                resid_add(t, jo, b, c0, n)

        def ple_main(st, t, wv_, wname):
            c0, n = trange(st, t)
            Wgt = v3(wv_, 0, 8192, KC); Wpr = v3(wv_, 8192, 10240, 2)
            for jo in range(KC):
                bg, bp = mmr.next(), mmr.next()
                mm_acc(ps[:, bg, 0:n], bg, lambda k: Wgt[:, k, jo * 128:(jo + 1) * 128], lambda k: A[:, k, c0:c0 + n], KC, [wname, tname("A", t)])
                mm_acc(ps[:, bp, 0:n], bp, lambda k: Wpr[:, k, jo * 128:(jo + 1) * 128], lambda k: pT[:, k, c0:c0 + n], 2, [wname, tname("pT", t)])
                nsg, sg = fr.next()
                I("act", "activation", out=sg[:, 0:n], in_=ps[:, bg, 0:n], func=AF.Sigmoid, reads=[psn(bg)], writes=[nsg])
                I("dve", "tensor_tensor", out=sg[:, 0:n], in0=sg[:, 0:n], in1=ps[:, bp, 0:n], op=ALU.mult, reads=[nsg, psn(bp)], writes=[nsg])
                I("dve", "tensor_tensor", out=hT[:, jo, c0:c0 + n], in0=sg[:, 0:n], in1=hT[:, jo, c0:c0 + n], op=ALU.add,
                  reads=[nsg, tname("hT", t)], writes=[tname("hT", t)])

        def rope_a(pb, n):
            nq, qf = brr.next()
            I("act", "activation", out=qf[:, 0:n], in_=ps[:, pb, 0:n], func=AF.Copy, reads=[psn(pb)], writes=[nq])
            return (nq, pb), qf

        def rope_b(nqpb, qf, n, toff=0):
            nq, pb = nqpb
            rb = auxr.next()
            I("pe", "matmul", ps[:, rb, 0:n], lhsT=RmB[:], rhs=qf[:, 0:n], start=True, stop=True, reads=[nq, "RmB"], writes=[psn(rb)])
            n1, t1 = fr.next()
            I("dve", "tensor_tensor", out=t1[:, 0:n], in0=ps[:, pb, 0:n], in1=tabs[:, 0, toff:toff + n], op=ALU.mult, reads=[psn(pb), "tabs", nq], writes=[n1])
            n2, t2 = fr.next()
            I("dve", "tensor_tensor", out=t2[:, 0:n], in0=ps[:, rb, 0:n], in1=tabs[:, 1, toff:toff + n], op=ALU.mult, reads=[psn(rb), "tabs"], writes=[n2])
            return n1, t1, n2, t2

        def kvq_main(st, t, wv_, wname):
            c0, n = st["tiles"][t]
            npr = st["npr"][t]
            has_s = st["samp"][t]
            Wk = v3(wv_, 0, 2048, KC); Wv = v3(wv_, 2048, 4096, KC); Wq = v3(wv_, 4096, 12288, KC)
            I("sp", "dma_start", out=tabs[:, :, 0:n], in_=dap(tab_d, st["tabcol0"] + c0, [[2194, 128], [128 * 2194, 2], [1, n]]), writes=["tabs"], dma=True)
            ka = max(0, st["kcol0"] - c0)
            nk = npr - ka
            kp0 = st["kb0"] * 128 + (c0 + ka - st["kcol0"])
            kbs = [(kp0 // 128 + i, ka + 128 * i) for i in range(nk // 128)]
            last_rel = None
            for kb, rel in kbs:
                if kb == 16:
                    last_rel = rel
            pendq = []

            def flush(keep=0):
                while len(pendq) > keep:
                    pendq.pop(0)()

            def k_fin(ch, nq, qf):
                n1, t1, n2, t2 = rope_b(nq, qf, n)
                I("dve", "tensor_tensor", out=t1[:, 0:n], in0=t1[:, 0:n], in1=t2[:, 0:n], op=ALU.add, reads=[n1, n2], writes=[n1])
                I("act", "activation", out=KT[:, ch, kp0:kp0 + nk], in_=t1[:, ka:ka + nk], func=AF.Copy, reads=[n1], writes=["KT%d" % kb for kb, _ in kbs])
                if last_rel is not None:
                    ob = auxr.next()
                    I("pe", "transpose", ps[:, ob, 0:128], t1[:, last_rel:last_rel + 128], identF[:], reads=[n1, "identF"], writes=[psn(ob)])
                    I("act", "activation", out=kvout[:, ch * 128:(ch + 1) * 128], in_=ps[:, ob, 0:128], func=AF.Copy, reads=[psn(ob)], writes=["kvoutK"])
                    if ch == 1:
                        I("sp", "dma_start", out=kwp_o.ap(), in_=kvout[:, 0:256], reads=["kvoutK"], dma=True, is_out=True)
                if has_s:
                    ob = auxr.next()
                    I("pe", "transpose", ps[0:NS, ob, 0:128], t1[:, npr:npr + NS], identF[:], reads=[n1, "identF"], writes=[psn(ob)])
                    I("act", "activation", out=knew[:, ch * 128:(ch + 1) * 128], in_=ps[0:NS, ob, 0:128], func=AF.Copy, reads=[psn(ob)], writes=["kvoutK"])
                    if ch == 1:
                        I("sp", "dma_start", out=dap(kws_o, 127 * 256, [[128 * 256, NS], [1, 256]]), in_=knew[:, :], reads=["kvoutK"], dma=True, is_out=True)

            q0, nqc = st["tiles1"][t]
            qoff = q0 - c0

            def q_fin(cq, nq, qf):
                n1, t1, n2, t2 = rope_b(nq, qf, nqc, qoff)
                I("dve", "tensor_tensor", out=A[:, cq, q0:q0 + nqc], in0=t1[:, 0:nqc], in1=t2[:, 0:nqc], op=ALU.add, reads=[n1, n2], writes=W("A", t))
                if has_s:
                    I("dve", "tensor_tensor", out=qs32[:, cq, :], in0=t1[:, npr - qoff:npr - qoff + NS], in1=t2[:, npr - qoff:npr - qoff + NS], op=ALU.add, reads=[n1, n2], writes=["qs32"])

            for ch in range(2):
                pb = mmr.next()
                mm_acc(ps[:, pb, 0:n], pb, lambda k: Wk[:, k, ch * 128:(ch + 1) * 128], lambda k: A[:, k, c0:c0 + n], KC, [wname, tname("A", t)])
                nq, qf = rope_a(pb, n)
                flush(0)
                pendq.append(lambda ch=ch, nq=nq, qf=qf: k_fin(ch, nq, qf))
            for vi, (kb, rel) in enumerate(kbs):
                pb = mmr.next()
                mm_acc(ps[:, pb, 0:256], pb, lambda k: A[:, k, c0 + rel:c0 + rel + 128], lambda k: Wv[:, k, :], KC, [wname, tname("A", t)])
                if vi == 0:
                    flush()
                I("act", "activation", out=Vt[:, kb, :, 0:64], in_=ps[:, pb, 0:256].rearrange("p (a b) -> p a b", a=4), func=AF.Copy,
                  reads=[psn(pb)], writes=["Vt%d" % kb])
                if kb == 16:
                    I("act", "activation", out=kvout[:, 256:512], in_=ps[:, pb, 0:256], func=AF.Copy, reads=[psn(pb)], writes=["kvoutV"])
                    I("sp", "dma_start", out=vwp_o.ap(), in_=kvout[:, 256:512], reads=["kvoutV"], dma=True, is_out=True)
            if has_s:
                pb = mmr.next()
                mm_acc(ps[0:NS, pb, 0:256], pb, lambda k: A[:, k, c0 + npr:c0 + npr + NS], lambda k: Wv[:, k, :], KC, [wname, tname("A", t)])
                I("act", "activation", out=vnew[:, :], in_=ps[0:NS, pb, 0:256], func=AF.Copy, reads=[psn(pb)], writes=["kvoutV"])
                I("sp", "dma_start", out=dap(vws_o, 127 * 256, [[128 * 256, NS], [1, 256]]), in_=vnew[:, :], reads=["kvoutV"], dma=True, is_out=True)
            for cq in range(KC):
                pb = mmr.next()
                mm_acc(ps[:, pb, 0:nqc], pb, lambda k: Wq[:, k, cq * 128:(cq + 1) * 128], lambda k: B[:, k, q0:q0 + nqc], KC, [wname, tname("B", t)])
                nq, qf = rope_a(pb, nqc)
                flush(0)
                pendq.append(lambda cq=cq, nq=nq, qf=qf: q_fin(cq, nq, qf))
            flush()

        def tile_of(st, col):
            for i, (c0, n) in enumerate(st["tiles"]):
                if c0 <= col < c0 + n:
                    return i
            raise ValueError(col)

        prod_bf = prod[:, :].bitcast(BF16)
        tabs_bf = tabs[:, :, :].rearrange("p a b -> p (a b)").bitcast(BF16)
        PT_slot = [
            Ring([("PT0", PT_t[0]), ("PT1", PT_t[1])]),
            Ring([("prod", prod_bf[:, 0:1024].rearrange("p (a b) -> p a b", a=2)), ("prodB", prod_bf[:, 1024:2048].rearrange("p (a b) -> p a b", a=2))]),
        ]
        osb_slot = [("o_sb", o_sb[:, :]), ("xs1", xs[1][:, :].bitcast(BF16)[:, 0:1024])]
        small_slot = [small, small2]

        def att_phases(st, qi, slot):
            qb = st["qb0"] + qi
            qc0 = st["ownc0"] + 128 * qi
            tq = tile_of(st, qc0)
            PTs = {}
            osn, osb = osb_slot[slot]
            sm = small_slot[slot]

            def S_phase(g):
                pair, hh = g // 2, g % 2
                PTn, PTt = PT_slot[slot].next()
                PTs[g] = (PTn, PTt)
                for kbi, kb in enumerate((qb, qb + 1)):
                    sb = mmr.next()
                    I("pe", "matmul", ps[:, sb, :], lhsT=KT[hh * 64:(hh + 1) * 64, pair, kb * 128:(kb + 1) * 128],
                      rhs=A[hh * 64:(hh + 1) * 64, pair * 4:pair * 4 + 4, qc0:qc0 + 128], start=True, stop=True,
                      reads=["KT%d" % kb, tname("A", tq)], writes=[psn(sb)])
                    I("act", "activation", out=PTt[:, kbi, :], in_=ps[:, sb, :], func=AF.Exp, scale=SCALE, reads=[psn(sb)], writes=[PTn])
                    mi = 0 if kbi == 1 else (2 if qb == 0 else 1)
                    mk = bass.AP(masks, mi * 128, [[384, 128], [0, 4], [1, 128]])
                    pv = PTt[:, kbi, :].rearrange("p (a b) -> p a b", a=4)
                    I("dve", "tensor_tensor", out=pv, in0=pv, in1=mk, op=ALU.mult, reads=[PTn, "masks"], writes=[PTn])

            def PV_phase(g):
                PTn, PTt = PTs[g]
                ob = auxr.next()
                for r in range(4):
                    for kbi, kb in enumerate((qb, qb + 1)):
                        I("pe", "matmul", ps[:, ob, r * 65:(r + 1) * 65], lhsT=PTt[:, kbi, r * 128:(r + 1) * 128], rhs=Vt[:, kb, g, 0:65],
                          start=(kbi == 0), stop=(kbi == 1), reads=[PTn, "Vt%d" % kb], writes=[psn(ob)], inc=(r == 3 and kbi == 1))
                sn = "small%d_%d" % (slot, g)
                o3 = ps[:, ob, 0:260].rearrange("p (a b) -> p a b", a=4)
                I("dve", "tensor_tensor", out=sm[:, g * 8:g * 8 + 4], in0=o3[:, :, 64], in1=esink[:, 4 * g:4 * g + 4], op=ALU.add,
                  reads=[psn(ob), "esink"], writes=[sn])
                I("dve", "reciprocal", out=sm[:, g * 8 + 4:g * 8 + 8], in_=sm[:, g * 8:g * 8 + 4], reads=[sn], writes=[sn])
                I("dve", "tensor_tensor", out=osb[:, g * 256:(g + 1) * 256].rearrange("p (a b) -> p a b", a=4), in0=o3[:, :, 0:64],
                  in1=bass.AP(sm, g * 8 + 4, [[sm_pitch[slot], 128], [1, 4], [0, 64]]), op=ALU.mult, reads=[psn(ob), sn], writes=[osn])

            def T_phase():
                tb = auxr.next()
                psb = ps[:, tb, :].bitcast(BF16)
                for c in range(KC):
                    I("pe", "transpose", psb[:, c * 128:(c + 1) * 128], osb[:, c * 128:(c + 1) * 128], identB[:], reads=[osn, "identB"], writes=[psn(tb)], inc=(c == KC - 1))
                I("act", "activation", out=B[:, 0:KC, qc0:qc0 + 128], in_=psb.rearrange("p (a b) -> p a b", a=KC), func=AF.Copy, reads=[psn(tb)], writes=W("B", tq))

            return [lambda: S_phase(0), lambda: S_phase(1), lambda: PV_phase(0), lambda: S_phase(2), lambda: PV_phase(1),
                    lambda: S_phase(3), lambda: PV_phase(2), lambda: PV_phase(3), T_phase]

        def att_group(st, qis):
            phs = [att_phases(st, qi, j) for j, qi in enumerate(qis)]
            for k in range(len(phs[0])):
                for ph in phs:
                    ph[k]()

        def att_main(st, qi):
            att_group(st, [qi])

        def samp_attn(st):
            sc0 = 1154
            tq = 2
            for cq in range(KC):
                I("pe", "transpose", ps[0:NS, cq // 4, (cq % 4) * 128:(cq % 4 + 1) * 128], qs32[:, cq, :], identF[:],
                  reads=["qs32", "identF"], writes=[psn(0), psn(1)], inc=(cq == KC - 1))
            for pair in range(2):
                I("act", "activation", out=q_tm[:, pair * 512:(pair + 1) * 512].rearrange("p (h c d) -> p h c d", h=2, c=4),
                  in_=ps[0:NS, pair, :].rearrange("p (c h d) -> p h c d", c=4, h=2), func=AF.Copy, reads=[psn(pair)], writes=["xs0"])
            for s in range(NS):
                kn, Kst = ("Ks%d" % (s % 2), Ks_t[s % 2])
                vn, Vst = ("Vs%d" % (s % 2), Vs_t[s % 2])
                pzn, Pzt = ("Pz%d" % (s % 2), Pz_t[s % 2])
                I("sp", "dma_start", out=Kst[:, :], in_=dap(ck, s * 128 * 256, [[256, 128], [1, 256]]), writes=[kn], dma=True)
                I("sp", "dma_start", out=Vst[:, :], in_=dap(cv, s * 128 * 256, [[256, 128], [1, 256]]), writes=[vn], dma=True)
                vbn, Vsb = ("Vsb%d" % (s % 2), Vsb_t[s % 2])
                I("act", "activation", out=Vsb[:, :], in_=Vst[:, :], func=AF.Copy, reads=[vn], writes=[vbn])
                b0 = 2 * (s % 2)
                sel = bass.AP(identF, s, [[128, NS], [0, 128]])
                for half in range(2):
                    I("pe", "matmul", ps[:, b0 + half, :], lhsT=sel, rhs=q_tm[:, half * 512:(half + 1) * 512], start=True, stop=True,
                      reads=["xs0", "identF"], writes=[psn(b0 + half)])
                    I("dve", "tensor_tensor", out=prod[:, half * 512:(half + 1) * 512].rearrange("p (g r d) -> p g r d", g=2, r=4),
                      in0=ps[:, b0 + half, :].rearrange("p (g r d) -> p g r d", g=2, r=4),
                      in1=bass.AP(Kst, half * 128, [[256, 128], [64, 2], [0, 4], [1, 64]]), op=ALU.mult,
                      reads=[psn(b0 + half), kn], writes=["prod", "prodB"])
                I("dve", "tensor_reduce", out=small[:, 32:48], in_=prod[:, :].rearrange("p (h d) -> p h d", d=64), axis=AX.X, op=ALU.add,
                  reads=["prod", "prodB"], writes=["smallS"])
                I("act", "activation", out=Pzt[:, :, s], in_=small[:, 32:48], func=AF.Exp, scale=SCALE, reads=["smallS"], writes=[pzn])
                for h in range(16):
                    I("pe", "matmul", ps[0:NS, 5 + h // 8, (h % 8) * 64:(h % 8 + 1) * 64], lhsT=Pzt[:, h, :], rhs=Vsb[:, (h // 4) * 64:(h // 4 + 1) * 64],
                      start=(s == 0 and h % 8 == 0), stop=(s == NS - 1 and h % 8 == 7), reads=[pzn, vbn], writes=[psn(5), psn(6)], inc=False)
                I("pe", "matmul", ps[0:NS, 7, 0:16], lhsT=Esel[:, s, :], rhs=Pzt[:, :, s], start=(s == 0), stop=(s == NS - 1),
                  reads=[pzn, "Esel"], writes=[psn(7)])
                I("dve", "memset", Pzt[:, :, s], 0.0, writes=[pzn])
            I("dve", "tensor_tensor", out=prod[0:NS, :].rearrange("p (g r d) -> p g r d", g=4, r=4), in0=q_tm[:, :].rearrange("p (g r d) -> p g r d", g=4, r=4),
              in1=bass.AP(kvout, 0, [[512, NS], [64, 4], [0, 4], [1, 64]]), op=ALU.mult, reads=["xs0", "kvoutK"], writes=["prod", "prodB"])
            I("dve", "tensor_reduce", out=small[0:NS, 48:64], in_=prod[0:NS, :].rearrange("p (h d) -> p h d", d=64), axis=AX.X, op=ALU.add,
              reads=["prod", "prodB"], writes=["smallN"])
            I("act", "activation", out=small[0:NS, 48:64], in_=small[0:NS, 48:64], func=AF.Exp, scale=SCALE, reads=["smallN"], writes=["smallN"])
            I("dve", "tensor_tensor", out=small[0:NS, 32:48], in0=ps[0:NS, 7, 0:16], in1=small[0:NS, 48:64], op=ALU.add, reads=[psn(7), "smallN"], writes=["smallS"])
            I("dve", "tensor_tensor", out=small[0:NS, 32:48], in0=small[0:NS, 32:48], in1=esink[0:NS, :], op=ALU.add, reads=["smallS", "esink"], writes=["smallS"])
            I("dve", "reciprocal", out=small[0:NS, 32:48], in_=small[0:NS, 32:48], reads=["smallS"], writes=["smallS"])
            I("dve", "tensor_tensor", out=prod[0:NS, :].rearrange("p (g r d) -> p g r d", g=4, r=4),
              in0=bass.AP(kvout, 256, [[512, NS], [64, 4], [0, 4], [1, 64]]), in1=bass.AP(small, 48, [[64, NS], [4, 4], [1, 4], [0, 64]]), op=ALU.mult,
              reads=["kvoutV", "smallN", "prod", "prodB"], writes=["prod", "prodB"])
            I("dve", "tensor_tensor", out=prod[0:NS, :].rearrange("p (a b) -> p a b", a=2), in0=prod[0:NS, :].rearrange("p (a b) -> p a b", a=2),
              in1=ps[0:NS, 5:7, :], op=ALU.add, reads=["prod", "prodB", psn(5), psn(6)], writes=["prod", "prodB"])
            I("dve", "tensor_tensor", out=on_bf[:, :].rearrange("p (h d) -> p h d", d=64), in0=prod[0:NS, :].rearrange("p (h d) -> p h d", d=64),
              in1=bass.AP(small, 32, [[64, NS], [1, 16], [0, 64]]), op=ALU.mult, reads=["prod", "prodB", "smallS"], writes=["o_sb"])
            psb = ps[:, 4, :].bitcast(BF16)
            for c in range(KC):
                I("pe", "transpose", psb[:, c * NS:(c + 1) * NS], on_bf[:, c * 128:(c + 1) * 128], identB[0:NS, 0:NS], reads=["o_sb", "identB"], writes=[psn(4)], inc=(c == KC - 1))
            I("act", "activation", out=B[:, 0:KC, sc0:sc0 + NS], in_=psb[:, 0:KC * NS].rearrange("p (a b) -> p a b", a=KC), func=AF.Copy, reads=[psn(4)], writes=W("B", tq))

        pairr = Ring([0, 2])
        ysr = Ring([(["prod", "prodB"], prod), (["tabs"], tabs[:, :, :].rearrange("p a b -> p (a b)"))])

        def final_block(st, t, col, nb, dst_ap):
            b0 = pairr.next()
            for kc in range(KC):
                I("pe", "transpose", ps[0:nb, b0 + kc // 4, (kc % 4) * 128:(kc % 4 + 1) * 128], hT[:, kc, col:col + nb], identF[:],
                  reads=[tname("hT", t), "identF"], writes=[psn(b0), psn(b0 + 1)], inc=(kc == KC - 1))
            nj, junk = fr.next()
            for half in range(2):
                I("act", "activation", out=junk[0:nb, 0:512], in_=ps[0:nb, b0 + half, :], func=AF.Square, accum_out=small[0:nb, 16 + half:17 + half],
                  reads=[psn(b0 + half)], writes=[nj, "smallF"])
            I("dve", "tensor_tensor", out=small[0:nb, 18:19], in0=small[0:nb, 16:17], in1=small[0:nb, 17:18], op=ALU.add, reads=["smallF"], writes=["smallF"])
            I("act", "activation", out=small[0:nb, 19:20], in_=small[0:nb, 18:19], func=AF.Ln, bias=epsT[0:nb, 0:1], scale=1.0 / 1024.0, reads=["smallF", "epsT"], writes=["smallF"])
            I("act", "activation", out=small[0:nb, 20:21], in_=small[0:nb, 19:20], func=AF.Exp, scale=-0.5, reads=["smallF"], writes=["smallF"])
            yn, yt = ysr.next()
            for half in range(2):
                I("dve", "scalar_tensor_tensor", out=yt[0:nb, half * 512:(half + 1) * 512], in0=ps[0:nb, b0 + half, :], scalar=small[0:nb, 20:21],
                  in1=gfin[0:nb, half * 512:(half + 1) * 512], op0=ALU.mult, op1=ALU.mult, reads=[psn(b0 + half), "smallF", "gfin"], writes=yn)
            I("sp", "dma_start", out=dst_ap, in_=yt[0:nb, :], reads=yn, dma=True, is_out=True)

        def final_post(st, t):
            c0, n = st["tiles"][t]
            npr = st["npr"][t]
            col = max(c0, st["ownc0"])
            while col < c0 + npr:
                row = st["yrow0"] + (col - st["ownc0"])
                final_block(st, t, col, 128, dap(y_o, row * D, [[D, 128], [1, D]]))
                col += 128
            if st["samp"][t]:
                final_block(st, t, c0 + npr, NS, ys_o.ap())

        def tm_out(src3, width, dst_ap, wait_names):
            b0 = pairr.next()
            for kc in range(KC):
                I("pe", "transpose", ps[0:width, b0 + kc // 4, (kc % 4) * 128:(kc % 4 + 1) * 128], src3[:, kc, :], identF[:],
                  reads=wait_names + ["identF"], writes=[psn(b0), psn(b0 + 1)], inc=(kc == KC - 1))
            yn, yt = ysr.next()
            I("act", "activation", out=yt[0:width, :].rearrange("p (a b) -> p a b", a=2), in_=ps[0:width, b0:b0 + 2, :], func=AF.Copy,
              reads=[psn(b0), psn(b0 + 1)], writes=yn)
            I("sp", "dma_start", out=dst_ap, in_=yt[0:width, :], reads=yn, dma=True, is_out=True)

        pipe = Pipe()

        def with_st(si, fn, l1=False):
            def g():
                old = (cur["st"], cur["l1"])
                cur["st"], cur["l1"] = si, l1
                fn()
                cur["st"], cur["l1"] = old
            return g

        early = {}

        def groups_for(si):
            st = STS[si]
            nt = len(st["tiles"])
            G = []

            def item(main, post=None, l1=False, post_l1=None):
                pl1 = l1 if post_l1 is None else post_l1
                pipe.item(with_st(si, main, l1), with_st(si, post, pl1) if post is not None else None)

            def load_s1(grp):
                def f(wv_, wname):
                    w3 = v3(wv_, 0, 12288, KC)
                    for sel in range(3):
                        wload(w3[:, :, sel * 512:(sel + 1) * 512], w_in, sel * 1024 + grp * 512, 3 * D, 512, KC, wname)
                return f

            def s0_main(t):
                stage0_main(st, t)
                load_p(st, t, 0)

            def s0_post(t):
                norm(st, t, [(G_MIX0, A, "A")])

            if si == 1:
                early["s0_main0"] = with_st(1, lambda: s0_main(0))
                early["s0_post0"] = with_st(1, lambda: s0_post(0))

            def run_s1a(wv_, wname):
                if si == 0:
                    with_st(si, load_state)()
                def s0(t):
                    item(lambda t=t: s0_main(t), lambda t=t: s0_post(t))

                def s1(t):
                    item(lambda t=t: s1_main(st, t, 0, wv_, wname))

                if si == 1 and early.get("done"):
                    pipe.item(lambda: (early["s0_post0"](), with_st(1, lambda: s0_main(1))()), with_st(1, lambda: s0_post(1)))
                else:
                    s0(0)
                    s0(1)
                s1(0)
                for t in range(2, nt):
                    s0(t)
                    s1(t - 1)
                if si == 0:
                    passthrough()
                s1(nt - 1)

            def run_s1b(wv_, wname):
                for t in range(nt):
                    item(lambda t=t: s1_main(st, t, 1, wv_, wname))
                if si == 0:
                    item(lambda: tm_out(usamp, NS, dap(css_o, D, [[2 * D, NS], [1, D]]), ["usamp"]))
                else:
                    item(lambda: tm_out(uhist, 2, csp_o.ap(), ["uhist%d" % j for j in range(KC)]))

            G.append((load_s1(0), run_s1a))
            G.append((load_s1(1), run_s1b))

            def load_proj(src_t):
                def f(wv_, wname):
                    wload(v3(wv_, 0, 8192, KC), src_t, 0, D, 1024, KC, wname)
                return f

            def run_s2(wv_, wname):
                for t in range(nt):
                    item(lambda t=t: proj_main(st, t, wv_, wname, B, "B"), lambda t=t: norm(st, t, [(G_FFN0, A, "A")]))

            G.append((load_proj(w_out), run_s2))

            def ffn_groups(layer, gnext):
                for gi, (m0, nm_) in enumerate(FFN_GROUPS):
                    def lf(wv_, wname, m0=m0, nm_=nm_):
                        wload(v3(wv_, 0, 4096, KC)[:, :, 0:nm_ * 128], wg, layer * D * FH + m0 * 128, FH, nm_ * 128, KC, wname)
                        wload(v3(wv_, 4096, 8192, KC)[:, :, 0:nm_ * 128], wu, layer * D * FH + m0 * 128, FH, nm_ * 128, KC, wname)
                        wload(v3(wv_, 8192, 12288, 4)[:, 0:nm_, :], wd, layer * FH * D + m0 * 128 * D, D, 1024, nm_, wname)

                    def rf(wv_, wname, nm_=nm_, last=(gi == len(FFN_GROUPS) - 1)):
                        for t in range(nt):
                            post = (lambda t=t: norm(st, t, [(gnext, A, "A")])) if last else None
                            item(lambda t=t: ffn_main(st, t, nm_, wv_, wname), post, l1=(layer == 1))
                    G.append((lf, rf))

            ffn_groups(0, G_PLE0)

            def load_ple(layer):
                def f(wv_, wname):
                    wload(v3(wv_, 0, 8192, KC), pgate, layer * D * D, D, 1024, KC, wname)
                    wload(v3(wv_, 8192, 10240, 2), pproj, layer * PLE * D, D, 1024, 2, wname)
                return f

            def run_ple0(wv_, wname):
                for t in range(nt):
                    item(lambda t=t: ple_main(st, t, wv_, wname), lambda t=t: (norm(st, t, [(G_KV, A, "A"), (G_MIX1, B, "B")]), load_p(st, t, 1)))

            G.append((load_ple(0), run_ple0))

            def load_kvq(wv_, wname):
                wload(v3(wv_, 0, 2048, KC), wk, 0, 256, 256, KC, wname)
                wload(v3(wv_, 2048, 4096, KC), wv, 0, 256, 256, KC, wname)
                Wq3 = v3(wv_, 4096, 12288, KC)
                for pair in range(2):
                    for hh in range(2):
                        for c_ in range(4):
                            col = (pair * 4 + c_) * 128 + hh * 64
                            dst = Wq3[:, :, col:col + 64]
                            src = dap(wq, (pair * 8 + hh * 4 + c_) * 64, [[D, 128], [128 * D, KC], [1, 64]])
                            I("pool", "dma_start", out=dst, in_=src, writes=[wname], dma=True)

            qb_of_tile = [[] for _ in range(nt)]
            for qi in range(st["nqb"]):
                qb_of_tile[tile_of(st, st["ownc0"] + 128 * qi)].append(qi)

            def cap(fn, rings):
                mmr.cur, auxr.cur = rings
                try:
                    return p.capture(with_st(si, fn, cur["l1"]))
                finally:
                    mmr.cur, auxr.cur = mmr.base, auxr.base

            def att_ops(t):
                def f():
                    qs = qb_of_tile[t]
                    for i in range(0, len(qs), 2):
                        att_group(st, qs[i:i + 2])
                return cap(f, ringsA)

            def run_kvq(wv_, wname):
                item(lambda: kvq_main(st, 0, wv_, wname))
                for t in range(1, nt):
                    def main(t=t):
                        a = att_ops(t - 1)
                        b = cap(lambda: kvq_main(st, t, wv_, wname), ringsB)
                        p.merge_replay(a, b)
                    pipe.item(main)

            G.append((load_kvq, run_kvq))

            def run_wo(wv_, wname):
                def main0():
                    a = att_ops(nt - 1)
                    b = cap(lambda: proj_main(st, 0, wv_, wname, B, "B"), ringsB)
                    p.merge_replay(a, b)
                pipe.item(with_st(si, main0, True), with_st(si, lambda: norm(st, 0, [(G_FFN1, A, "A")]), True))
                if si == 0:
                    item(lambda: samp_attn(st))
                for t in range(1, nt):
                    item(lambda t=t: proj_main(st, t, wv_, wname, B, "B"), lambda t=t: norm(st, t, [(G_FFN1, A, "A")]), l1=True)

            G.append((load_proj(wo), run_wo))
            ffn_groups(1, G_PLE1)

            def run_ple1(wv_, wname):
                hoist = (si == 0 and "s0_main0" in early)
                for t in range(nt - 1 if hoist else nt):
                    item(lambda t=t: ple_main(st, t, wv_, wname), lambda t=t: final_post(st, t), l1=True)
                if hoist:
                    tl = nt - 1
                    pipe.item(lambda: (early["s0_main0"](), with_st(0, lambda: ple_main(st, tl, wv_, wname), True)()),
                              with_st(0, lambda: final_post(st, tl), True))
                    early["done"] = True

            G.append((load_ple(1), run_ple1))
            return G

        allg = groups_for(0) + groups_for(1)

        def do_load(i):
            if i < len(allg):
                allg[i][0](wbuf[i % 2], "wbuf%d" % (i % 2))

        do_load(0)
        do_load(1)
        for i, (lf, rf) in enumerate(allg):
            rf(wbuf[i % 2], "wbuf%d" % (i % 2))
            do_load(i + 2)
        pipe.flush()
        p.finish()
        p.emit()
        print("ops per engine:", {k: len(v) for k, v in p.ops.items()})
    return nc


_NC_CACHE = {}


def _rope_tables(pos):
    half = 8
    inv_freq = np.power(np.float32(500000.0), -np.arange(half, dtype=np.float32) / np.float32(half)).astype(np.float32)
    ang = (pos.astype(np.float32)[:, None] * inv_freq[None, :]).astype(np.float32)
    cos = np.cos(ang).astype(np.float32).T
    sin = np.sin(ang).astype(np.float32).T
    n = pos.shape[0]
    C = np.ones((128, n), np.float32)
    S = np.zeros((128, n), np.float32)
    for hb in range(2):
        base = hb * 64
        C[base:base + 8] = cos
        C[base + 8:base + 16] = cos
        S[base:base + 8] = -sin
        S[base + 8:base + 16] = sin
    return C, S


def prepare(x_prompt, x_sample, state_conv, cache_k_win, cache_v_win, p_prompt, p_sample,
           norm_mix_g, norm_ffn_g, norm_ple_g, kv_norm_g, final_norm_g,
           conv_w_in, conv_w, conv_w_out, w_k, w_v, w_q, sinks, w_o,
           ffn_w_gate, ffn_w_up, ffn_w_down, ple_w_proj, ple_w_gate):
    f32 = np.float32
    A_ = lambda a: np.ascontiguousarray(np.asarray(a, dtype=f32))
    x_prompt = A_(x_prompt); x_sample = A_(x_sample); state_conv = A_(state_conv)
    cache_k_win = A_(cache_k_win); cache_v_win = A_(cache_v_win); p_prompt = A_(p_prompt); p_sample = A_(p_sample)

    def colvec(g):
        return np.asarray(g, f32).reshape(KC, 128).T

    gains = [norm_mix_g[0], norm_ffn_g[0], norm_ple_g[0], kv_norm_g, norm_mix_g[1], norm_ffn_g[1], norm_ple_g[1]]
    gvec = np.ascontiguousarray(np.concatenate([colvec(g) for g in gains], axis=1))
    gfin = np.ascontiguousarray(np.broadcast_to(np.asarray(final_norm_g, f32)[None, :], (128, D)))
    cw = np.asarray(conv_w, f32)[0]
    convw = np.ascontiguousarray(np.stack([colvec(cw[j]) for j in range(3)], axis=2).reshape(128, 24))
    sinkb = np.ascontiguousarray(np.broadcast_to(np.asarray(sinks, f32)[0][None, :], (128, 16)))
    idn = np.eye(128, dtype=f32)
    rm = np.zeros((128, 128), f32)
    for m in range(128):
        d = m % 64
        if d < 8:
            rm[m + 8, m] = 1.0
        elif d < 16:
            rm[m - 8, m] = 1.0
    jj = np.arange(128)[:, None]; ii = np.arange(128)[None, :]
    mcur = (jj <= ii).astype(f32); mprev = (jj >= ii).astype(f32)
    esel = np.zeros((128, 16, 16), f32)
    for s in range(16):
        esel[:, s, s] = 1.0
    esel = esel.reshape(128, 256)

    shared = dict(
        w_in=A_(conv_w_in)[0], w_out=A_(conv_w_out)[0], wk=A_(w_k), wv=A_(w_v), wq=A_(w_q)[0], wo=A_(w_o)[0],
        wg=A_(ffn_w_gate), wu=A_(ffn_w_up), wd=A_(ffn_w_down), pproj=A_(ple_w_proj), pgate=A_(ple_w_gate),
        gvec=gvec, gfin=gfin, convw=convw, sinkb=sinkb, idn=idn, rm=rm, esel=esel)

    in_maps = []
    for c in range(NCORES):
        b, half = c // 2, c % 2
        t0 = half * OWN
        xin = np.zeros((XR, D), f32)
        pin = np.zeros((2, XR, PLE), f32)
        if half == 1:
            xin[:] = x_prompt[b, t0 - HALO:t0 + OWN]
            pin[:] = p_prompt[:, b, t0 - HALO:t0 + OWN]
        else:
            xin[HALO:] = x_prompt[b, 0:OWN]
            pin[:, HALO:] = p_prompt[:, b, 0:OWN]
        s0 = c * NS
        pos1 = np.concatenate([np.maximum(t0 - HALO + np.arange(1154), 0), np.full(NS, 16384)]).astype(f32)
        pos2 = (t0 + 1024 + np.arange(1024)).astype(f32)
        C1, S1 = _rope_tables(pos1)
        C2, S2 = _rope_tables(pos2)
        tab = np.ascontiguousarray(np.stack([np.concatenate([C1, C2], 1), np.concatenate([S1, S2], 1)], 0))
        masks = np.ascontiguousarray(np.stack([mcur, mprev, mprev if half == 1 else np.zeros_like(mprev)], 0))
        m = dict(shared)
        m.update(xin=xin, xsm=np.ascontiguousarray(x_sample[s0:s0 + NS, 0]), stc=np.ascontiguousarray(state_conv[0, s0:s0 + NS].reshape(2 * NS, D)),
                 ck=np.ascontiguousarray(cache_k_win[s0:s0 + NS].reshape(NS, 128, 256)), cv=np.ascontiguousarray(cache_v_win[s0:s0 + NS].reshape(NS, 128, 256)),
                 pin=pin, psm=np.ascontiguousarray(p_sample[:, s0:s0 + NS, 0]), masks=masks, tab=tab)
        in_maps.append(m)

    return in_maps


def assemble(R):
    f32 = np.float32
    y_prompt = np.zeros((4, 4096, D), f32); y_sample = np.zeros((128, 1, D), f32)
    csp = np.zeros((1, 4, 2, D), f32); css = np.zeros((1, 128, 2, D), f32)
    kwp = np.zeros((4, 128, 4, 64), f32); vwp = np.zeros((4, 128, 4, 64), f32)
    kws = np.zeros((128, 128, 4, 64), f32); vws = np.zeros((128, 128, 4, 64), f32)
    for c in range(NCORES):
        b, half = c // 2, c % 2
        r = R[c]
        y_prompt[b, half * OWN:(half + 1) * OWN] = r["y"]
        s0 = c * NS
        y_sample[s0:s0 + NS, 0] = r["ys"]
        css[0, s0:s0 + NS] = r["css"]
        kws[s0:s0 + NS] = r["kws"].reshape(NS, 128, 4, 64)
        vws[s0:s0 + NS] = r["vws"].reshape(NS, 128, 4, 64)
        if half == 1:
            csp[0, b] = r["csp"]
            kwp[b] = r["kwp"].reshape(128, 4, 64)
            vwp[b] = r["vwp"].reshape(128, 4, 64)
    return (y_prompt, y_sample, csp, css, kwp, vwp, kws, vws)


def kernel(**inputs):
    in_maps = prepare(**inputs)
    if "nc" not in _NC_CACHE:
        _NC_CACHE["nc"] = build_program()
    nc = _NC_CACHE["nc"]
    res = run_bass_kernel_spmd(nc, in_maps, core_ids=list(range(NCORES)))
    return assemble(res.results)
```

```python
import numpy as np
from contextlib import ExitStack
import concourse.bass as bass
import concourse.mybir as mybir
from concourse.bass_utils import run_bass_kernel_spmd

F32 = mybir.dt.float32
BF16 = mybir.dt.bfloat16
ALU = mybir.AluOpType
AF = mybir.ActivationFunctionType
AX = mybir.AxisListType

NCORES = 8
D = 1024
KC = 8
FH = 2816
HC = 22
PLE = 256
HALO = 130
OWN = 2048
NS = 16
XR = HALO + OWN
RMS_EPS = 1e-6
SCALE = 0.125

ENGS = ("pe", "act", "dve", "pool", "sp")
MERGE = True


class Prog:
    def __init__(self, nc, es, n_dma_sp=24, n_dma_pool=8):
        self.nc = nc
        self.sem = {e: es.enter_context(nc.semaphore("s_" + e)) for e in ENGS[:4]}
        self.dsem = {}
        self.dpool = {"sp": [], "pool": []}
        for i in range(n_dma_sp):
            k = "dsp%d" % i
            self.dsem[k] = es.enter_context(nc.semaphore(k))
            self.dpool["sp"].append(k)
        for i in range(n_dma_pool):
            k = "dpl%d" % i
            self.dsem[k] = es.enter_context(nc.semaphore(k))
            self.dpool["pool"].append(k)
        self.dpool["act"] = []
        for i in range(6):
            k = "dac%d" % i
            self.dsem[k] = es.enter_context(nc.semaphore(k))
            self.dpool["act"].append(k)
        self.dnext = {"sp": 0, "pool": 0, "act": 0}
        self.dcum = {k: 0 for k in self.dsem}
        self.ops = {e: [] for e in ENGS}
        self.tick = {e: 0 for e in ENGS}
        self.seen = {e: {} for e in ENGS}
        self.res = {}
        self.out_events = []
        self.cap = None
        self.debug_names = None

    def capture(self, fn):
        assert self.cap is None
        self.cap = []
        fn()
        ops, self.cap = self.cap, None
        return ops

    def replay(self, ops):
        for eng, fn, reads, writes, inc, dma, is_out in ops:
            self.add(eng, fn, reads=reads, writes=writes, inc=inc, dma=dma, is_out=is_out)

    @staticmethod
    def segments(ops):
        segs, cur = [], []
        for op in ops:
            cur.append(op)
            if op[0] == "pe" and op[4]:
                segs.append(cur)
                cur = []
        if cur:
            if segs:
                segs[-1].extend(cur)
            else:
                segs.append(cur)
        return segs

    def merge_replay(self, opsA, opsB):
        if not MERGE:
            self.replay(opsA)
            self.replay(opsB)
            return
        sa, sb = self.segments(opsA), self.segments(opsB)
        na, nb = len(sa), len(sb)
        out = []
        j = 0
        for i, seg in enumerate(sa):
            out.extend(seg)
            tgt = ((i + 1) * nb) // na
            while j < tgt:
                out.extend(sb[j])
                j += 1
        while j < nb:
            out.extend(sb[j])
            j += 1
        self.replay(out)

    def _handle(self, k):
        return self.sem[k] if k in self.sem else self.dsem[k]

    def add(self, eng, fn, reads=(), writes=(), inc=True, dma=False, is_out=False):
        if self.cap is not None:
            self.cap.append((eng, fn, tuple(reads), tuple(writes), inc, dma, is_out))
            return None
        waits = {}

        def need(ev):
            if ev is None:
                return
            k, v = ev
            if k == eng and eng == "pe":
                return
            if v > waits.get(k, 0):
                waits[k] = v

        for r in reads:
            s = self.res.get(r)
            if s is not None:
                need(s[0])
                if r.startswith("ps"):
                    for k, v in s[1].items():
                        if k != eng:
                            need((k, v))
        for w in writes:
            s = self.res.get(w)
            if s is not None:
                need(s[0])
                for k, v in s[1].items():
                    need((k, v))
        if dma:
            pool = self.dpool[eng]
            sk = pool[self.dnext[eng] % len(pool)]
            self.dnext[eng] += 1
            if self.dcum[sk] > 0:
                need((sk, self.dcum[sk]))
            self.dcum[sk] += 16
            ev = (sk, self.dcum[sk])
            incspec = (sk, 16)
        else:
            if inc:
                self.tick[eng] += 1
                ev = (eng, self.tick[eng])
                incspec = (eng, 1)
            else:
                ev = (eng, self.tick[eng] + 1)
                incspec = None
        wl = []
        for k, v in waits.items():
            if self.seen[eng].get(k, 0) < v:
                self.seen[eng][k] = v
                wl.append((k, v))
        self.ops[eng].append((fn, wl, incspec))
        for r in reads:
            s = self.res.get(r)
            if s is None:
                s = self.res[r] = [None, {}]
            if s[1].get(ev[0], 0) < ev[1]:
                s[1][ev[0]] = ev[1]
        for w in writes:
            self.res[w] = [ev, {}]
        if is_out:
            self.out_events.append(ev)
        return ev

    def finish(self):
        fin = {}
        for k, v in self.out_events:
            fin[k] = max(fin.get(k, 0), v)
        wl = [(k, v) for k, v in fin.items()]
        self.ops["sp"].append((None, wl, None))

    def emit(self):
        nc = self.nc
        with nc.Block() as block:
            def run(engname):
                def body(e):
                    for fn, wl, incspec in self.ops[engname]:
                        for k, v in wl:
                            e.wait_ge(self._handle(k), v)
                        if fn is None:
                            continue
                        op, args, kw = fn
                        ins = getattr(e, op)(*args, **kw)
                        if self.debug_names is not None:
                            _NC_CACHE.setdefault("dbg_waits", {})[ins.ins.name] = list(wl)
                            try:
                                self.debug_names[ins.ins.name] = (engname, op, str(kw.get("func", "")), [str(a)[:80] for a in args] + [k + "=" + str(v)[:90] for k, v in kw.items() if k in ("out", "in_", "in0", "lhsT")])
                            except Exception:
                                pass
                        if incspec is not None:
                            ins.then_inc(self._handle(incspec[0]), incspec[1])
                return body
            block.tensor(run("pe"))
            block.scalar(run("act"))
            block.vector(run("dve"))
            block.gpsimd(run("pool"))
            block.sync(run("sp"))


class Ring:
    def __init__(self, items):
        self.items = items
        self.i = 0

    def next(self):
        it = self.items[self.i % len(self.items)]
        self.i += 1
        return it


class Pipe:
    def __init__(self):
        self.pending = None

    def item(self, main, post=None):
        main()
        if self.pending is not None:
            self.pending()
        self.pending = post

    def flush(self):
        if self.pending is not None:
            self.pending()
            self.pending = None


STS = [
    dict(ncols=1170, tiles=[(0, 386), (386, 384), (770, 400)], tiles1=[(130, 256), (386, 384), (770, 400)], npr=[386, 384, 384], samp=[False, False, True],
         xrow0=0, tabcol0=0, kcol0=2, kb0=0, ownc0=130, qb0=0, nqb=8, yrow0=0),
    dict(ncols=1024, tiles=[(0, 384), (384, 384), (768, 256)], tiles1=[(0, 384), (384, 384), (768, 256)], npr=[384, 384, 256], samp=[False, False, False],
         xrow0=1154, tabcol0=1170, kcol0=0, kb0=9, ownc0=0, qb0=8, nqb=8, yrow0=1024),
]
NCOLMAX = 1170
FFN_GROUPS = [(0, 4), (4, 4), (8, 4), (12, 4), (16, 3), (19, 3)]
G_MIX0, G_FFN0, G_PLE0, G_KV, G_MIX1, G_FFN1, G_PLE1 = range(7)


def build_program():
    nc = bass.Bass("TRN2", target_bir_lowering=False)

    def din(name, shape):
        return nc.dram_tensor(name, list(shape), F32, kind="ExternalInput")

    def dout(name, shape):
        return nc.dram_tensor(name, list(shape), F32, kind="ExternalOutput")

    xin = din("xin", [XR, D]); xsm = din("xsm", [NS, D]); stc = din("stc", [2 * NS, D])
    ck = din("ck", [NS, 128, 256]); cv = din("cv", [NS, 128, 256])
    pin = din("pin", [2, XR, PLE]); psm = din("psm", [2, NS, PLE])
    w_in = din("w_in", [D, 3 * D]); w_out = din("w_out", [D, D])
    wk = din("wk", [D, 256]); wv = din("wv", [D, 256]); wq = din("wq", [D, D]); wo = din("wo", [D, D])
    wg = din("wg", [2, D, FH]); wu = din("wu", [2, D, FH]); wd = din("wd", [2, FH, D])
    pproj = din("pproj", [2, PLE, D]); pgate = din("pgate", [2, D, D])
    gvec_d = din("gvec", [128, 56]); gfin_d = din("gfin", [128, D]); convw_d = din("convw", [128, 24])
    sink_d = din("sinkb", [128, 16]); idn_d = din("idn", [128, 128]); rm_d = din("rm", [128, 128])
    masks_d = din("masks", [3, 128, 128]); tab_d = din("tab", [2, 128, 2194]); esel_d = din("esel", [128, 256])

    y_o = dout("y", [OWN, D]); ys_o = dout("ys", [NS, D]); csp_o = dout("csp", [2, D]); css_o = dout("css", [NS, 2, D])
    kwp_o = dout("kwp", [128, 256]); vwp_o = dout("vwp", [128, 256])
    kws_o = dout("kws", [NS, 128, 256]); vws_o = dout("vws", [NS, 128, 256])

    def dap(t, off, dims):
        return bass.AP(t, off, [list(d) for d in dims])

    with ExitStack() as es:
        def T(name, shape, dt):
            return es.enter_context(nc.sbuf_tensor("sb_" + name, list(shape), dt))

        hT = T("hT", [128, KC, NCOLMAX], F32)
        A = T("A", [128, KC, NCOLMAX], BF16)
        B = T("B", [128, KC, NCOLMAX], BF16)
        KT = T("KT", [128, 2, 17 * 128], BF16)
        Vt = T("Vt", [128, 17, 4, 66], BF16)
        wbuf = [T("wbuf0", [128, 12288], BF16), T("wbuf1", [128, 12288], BF16)]
        pT = T("pT", [128, 2, NCOLMAX], BF16)
        tabs = T("tabs", [128, 2, 512], F32)
        xs = [T("xs0", [128, D], F32), T("xs1", [128, D], F32)]
        fr_t = [T("fr%d" % i, [128, 516], F32) for i in range(6)]
        hid_t = [T("hid%d" % i, [128, 4, 512], BF16) for i in range(2)]
        br_t = [T("br%d" % i, [128, 512], BF16) for i in range(4)]
        PT_t = [T("PT%d" % i, [128, 2, 512], BF16) for i in range(2)]
        o_sb = T("o_sb", [128, 1024], BF16)
        identF = T("identF", [128, 128], F32); identB = T("identB", [128, 128], BF16)
        onesM = T("onesM", [128, 128], BF16); RmB = T("RmB", [128, 128], BF16)
        masks = T("masks", [128, 3, 128], BF16)
        gvec = T("gvec", [128, 56], F32); gfin = T("gfin", [128, D], F32); convw = T("convw", [128, 24], F32)
        esink = T("esink", [128, 16], F32); epsT = T("epsT", [128, 1], F32)
        uhist = T("uhist", [128, KC, 2], F32); usamp = T("usamp", [128, KC, NS], F32); stT = T("stT", [128, KC, 2 * NS], F32)
        qs32 = T("qs32", [128, KC, NS], F32)
        small = T("small", [128, 64], F32)
        small2 = T("small2", [128, 32], F32)
        sm_pitch = [64, 32]
        Ks_t = [T("Ks%d" % i, [128, 256], F32) for i in range(2)]
        Vs_t = [T("Vs%d" % i, [128, 256], F32) for i in range(2)]
        Pz_t = [T("Pz%d" % i, [128, 16, 16], BF16) for i in range(2)]
        Vsb_t = [T("Vsb%d" % i, [128, 256], BF16) for i in range(2)]
        Esel = T("Esel", [128, 16, 16], BF16)
        prod = T("prod", [128, 1024], F32)
        kvout = T("kvout", [128, 512], F32)
        q_tm = xs[0][0:NS, :]
        knew = kvout[0:NS, 0:256]
        vnew = kvout[0:NS, 256:512]
        on_bf = o_sb[0:NS, :]
        ps = es.enter_context(nc.psum_tensor("ps", [128, 8, 512], F32))
        print("sbuf bytes remaining:", nc.sbuf_bytes_remaining)

        p = Prog(nc, es)
        import os as _os
        if _os.environ.get("MK_DEBUG"):
            p.debug_names = {}
            _NC_CACHE["dbg"] = p.debug_names

        def I(eng, opname, /, *args, reads=(), writes=(), inc=True, dma=False, is_out=False, **kw):
            return p.add(eng, (opname, args, kw), reads=reads, writes=writes, inc=inc, dma=dma, is_out=is_out)

        fr = Ring([("fr%d" % i, t) for i, t in enumerate(fr_t)])
        hidr = Ring([("hid%d" % i, t) for i, t in enumerate(hid_t)])
        brr = Ring([("br%d" % i, t) for i, t in enumerate(br_t)])
        PTr = Ring([("PT%d" % i, t) for i, t in enumerate(PT_t)])
        xsr = Ring([(["xs0"], xs[0]), (["prod", "prodB"], prod), (["xs1"], xs[1]), (["tabs"], tabs[:, :, :].rearrange("p a b -> p (a b)"))])
        class Sw:
            def __init__(self, ring):
                self.base = ring
                self.cur = ring

            def next(self):
                return self.cur.next()

        mmr = Sw(Ring([0, 1, 2, 3, 4]))
        auxr = Sw(Ring([5, 6, 7]))
        ringsA = (Ring([0, 1, 2]), Ring([5, 6]))
        ringsB = (Ring([3, 4]), Ring([7]))

        def psn(b):
            return "ps%d" % b

        cur = dict(st=0, l1=False)

        def trange(st, t):
            return st["tiles1"][t] if cur["l1"] else st["tiles"][t]
        touched = set()
        OVER = {0: [0], 1: [0, 1], 2: [1, 2]}

        def tname(buf, t):
            return "%s@%d_%d" % (buf, cur["st"], t)

        def W(buf, t):
            names = [tname(buf, t)]
            if cur["st"] == 1 and (buf, t) not in touched:
                touched.add((buf, t))
                names += ["%s@0_%d" % (buf, o) for o in OVER[t]]
            return names

        def ld(eng, out, in_, wr):
            I(eng, "dma_start", out=out, in_=in_, writes=wr, dma=True)

        ld("sp", identF[:], idn_d.ap(), ["identF"])
        ld("pool", RmB[:], rm_d.ap(), ["RmB"])
        ld("sp", gvec[:], gvec_d.ap(), ["gvec"])
        ld("sp", gfin[:], gfin_d.ap(), ["gfin"])
        ld("sp", convw[:], convw_d.ap(), ["convw"])
        ld("sp", esink[:], sink_d.ap(), ["esink"])
        ld("pool", Esel[:].rearrange("p a b -> p (a b)"), esel_d.ap(), ["Esel"])
        ld("pool", identB[:], idn_d.ap(), ["identB"])
        ld("pool", masks[:], dap(masks_d, 0, [[128, 128], [128 * 128, 3], [1, 128]]), ["masks"])
        I("pool", "memset", onesM[:], 1.0 / 1024.0, writes=["onesM"])
        I("pool", "memset", epsT[:], RMS_EPS, writes=["epsT"])
        I("pool", "memset", uhist[:], 0.0, writes=["uhist%d" % j for j in range(KC)])
        I("pool", "memset", Vt[:], 1.0, writes=["Vt%d" % i for i in range(17)])
        for i in range(2):
            I("pool", "memset", Pz_t[i][:], 0.0, writes=["Pz%d" % i])
        I("act", "activation", out=esink[:], in_=esink[:], func=AF.Exp, reads=["esink"], writes=["esink"])
        def passthrough():
            I("sp", "dma_start", out=dap(css_o, 0, [[2 * D, NS], [1, D]]), in_=dap(stc, D, [[2 * D, NS], [1, D]]), dma=True, is_out=True)
            I("sp", "dma_start", out=dap(kws_o, 0, [[128 * 256, NS], [1, 127 * 256]]), in_=dap(ck, 256, [[128 * 256, NS], [1, 127 * 256]]), dma=True, is_out=True)
            I("sp", "dma_start", out=dap(vws_o, 0, [[128 * 256, NS], [1, 127 * 256]]), in_=dap(cv, 256, [[128 * 256, NS], [1, 127 * 256]]), dma=True, is_out=True)

        def wload(dst_ap, src_t, off, ld_, ncols, nk, wname):
            src = dap(src_t, off, [[ld_, 128], [128 * ld_, nk], [1, ncols]])
            I("pool", "dma_start", out=dst_ap, in_=src, writes=[wname], dma=True)

        def v3(buf, lo, hi, a):
            return buf[:, lo:hi].rearrange("p (a b) -> p a b", a=a)

        def norm(st, t, outs):
            c0, n = trange(st, t)
            b = auxr.next()
            for kc in range(KC):
                nm, sq = brr.next()
                I("act", "activation", out=sq[:, 0:n], in_=hT[:, kc, c0:c0 + n], func=AF.Square, reads=[tname("hT", t)], writes=[nm])
                I("pe", "matmul", ps[:, b, 0:n], lhsT=onesM[:], rhs=sq[:, 0:n], start=(kc == 0), stop=(kc == KC - 1),
                  reads=[nm, "onesM"], writes=[psn(b)])
            n1, lnv = fr.next()
            I("act", "activation", out=lnv[:, 0:n], in_=ps[:, b, 0:n], func=AF.Ln, bias=epsT[:, 0:1], scale=1.0, reads=[psn(b), "epsT"], writes=[n1])
            n2, rstd = fr.next()
            I("act", "activation", out=rstd[:, 0:n], in_=lnv[:, 0:n], func=AF.Exp, scale=-0.5, reads=[n1], writes=[n2])
            for gi, dst, dname in outs:
                for kc in range(KC):
                    I("dve", "scalar_tensor_tensor", out=dst[:, kc, c0:c0 + n], in0=hT[:, kc, c0:c0 + n],
                      scalar=gvec[:, gi * 8 + kc:gi * 8 + kc + 1], in1=rstd[:, 0:n], op0=ALU.mult, op1=ALU.mult,
                      reads=[tname("hT", t), n2, "gvec"], writes=W(dname, t))

        def mm_acc(out_ap, b, lhs_fn, rhs_fn, nk, reads):
            for k in range(nk):
                I("pe", "matmul", out_ap, lhsT=lhs_fn(k), rhs=rhs_fn(k), start=(k == 0), stop=(k == nk - 1),
                  reads=reads, writes=[psn(b)], inc=(k == nk - 1))

        def resid_add(t, jo, b, c0, n):
            I("dve", "tensor_tensor", out=hT[:, jo, c0:c0 + n], in0=ps[:, b, 0:n], in1=hT[:, jo, c0:c0 + n], op=ALU.add,
              reads=[psn(b), tname("hT", t)], writes=[tname("hT", t)])

        def load_blocks(st, t):
            c0, n = st["tiles"][t]
            npr = st["npr"][t]
            blks = []
            r = 0
            while r < npr:
                nb = min(128, npr - r)
                blks.append(("p", st["xrow0"] + c0 + r, nb, c0 + r))
                r += nb
            if st["samp"][t]:
                blks.append(("s", 0, NS, c0 + npr))
            return blks

        def stage0_main(st, t):
            for kind, r0, nb, col in load_blocks(st, t):
                xn, xt = xsr.next()
                src = dap(xin, r0 * D, [[D, nb], [1, D]]) if kind == "p" else dap(xsm, 0, [[D, nb], [1, D]])
                I("act" if cur["st"] == 1 else "sp", "dma_start", out=xt[0:nb, :], in_=src, writes=xn, dma=True)
                for half in range(2):
                    b = mmr.next()
                    for j in range(4):
                        kc = half * 4 + j
                        I("pe", "transpose", ps[:, b, j * 128:j * 128 + nb], xt[0:nb, kc * 128:(kc + 1) * 128], identF[0:nb, 0:nb],
                          reads=xn + ["identF"], writes=[psn(b)], inc=(j == 3))
                    I("act", "activation", out=hT[:, half * 4:half * 4 + 4, col:col + nb],
                      in_=ps[:, b, :].rearrange("p (a b) -> p a b", a=4)[:, :, 0:nb], func=AF.Copy,
                      reads=[psn(b)], writes=W("hT", t))

        def load_state():
            xn, xt = xsr.next()
            I("sp", "dma_start", out=xt[0:32, :], in_=stc.ap(), writes=xn, dma=True)
            b = mmr.next()
            for kc in range(KC):
                I("pe", "transpose", ps[:, b, kc * 32:(kc + 1) * 32], xt[0:32, kc * 128:(kc + 1) * 128], identF[0:32, 0:32],
                  reads=xn + ["identF"], writes=[psn(b)], inc=(kc == KC - 1))
            I("act", "activation", out=stT[:], in_=ps[:, b, 0:256].rearrange("p (a b) -> p a b", a=KC), func=AF.Copy,
              reads=[psn(b)], writes=["stT"])

        def load_p(st, t, layer):
            for kind, r0, nb, col in load_blocks(st, t):
                xn, xt = xsr.next()
                src = dap(pin, (layer * XR + r0) * PLE, [[PLE, nb], [1, PLE]]) if kind == "p" else dap(psm, layer * NS * PLE, [[PLE, nb], [1, PLE]])
                I("act" if (cur["st"] == 1 and layer == 0) else "sp", "dma_start", out=xt[0:nb, 0:PLE], in_=src, writes=xn, dma=True)
                b = auxr.next()
                for c in range(2):
                    I("pe", "transpose", ps[:, b, c * 128:c * 128 + nb], xt[0:nb, c * 128:(c + 1) * 128], identF[0:nb, 0:nb],
                      reads=xn + ["identF"], writes=[psn(b)], inc=(c == 1))
                I("act", "activation", out=pT[:, 0:2, col:col + nb], in_=ps[:, b, 0:256].rearrange("p (a b) -> p a b", a=2)[:, :, 0:nb],
                  func=AF.Copy, reads=[psn(b)], writes=W("pT", t))

        def s1_main(st, t, grp, wv_, wname):
            c0, n = st["tiles"][t]
            npr = st["npr"][t]
            has_s = st["samp"][t]
            w3 = v3(wv_, 0, 12288, KC)
            for jj in range(4):
                j = grp * 4 + jj
                bc, bx, bb = mmr.next(), mmr.next(), mmr.next()
                for sel, b in ((1, bc), (2, bx), (0, bb)):
                    mm_acc(ps[:, b, 0:n], b, lambda k: w3[:, k, sel * 512 + jj * 128: sel * 512 + (jj + 1) * 128],
                           lambda k: A[:, k, c0:c0 + n], KC, [wname, tname("A", t)])
                ncs, c_sb = fr.next()
                I("act", "activation", out=c_sb[:, 0:n], in_=ps[:, bc, 0:n], func=AF.Copy, reads=[psn(bc)], writes=[ncs])
                nub, ub = fr.next()
                I("act", "activation", out=ub[:, 0:2], in_=uhist[:, j, :], func=AF.Copy, reads=["uhist%d" % j], writes=[nub])
                I("dve", "tensor_tensor", out=ub[:, 2:2 + n], in0=c_sb[:, 0:n], in1=ps[:, bx, 0:n], op=ALU.mult, reads=[ncs, psn(bx), nub], writes=[nub])
                ntm, tmp = fr.next()
                nac, acc = fr.next()
                w0 = convw[:, j * 3 + 0:j * 3 + 1]; w1 = convw[:, j * 3 + 1:j * 3 + 2]; w2 = convw[:, j * 3 + 2:j * 3 + 3]
                I("act", "activation", out=tmp[:, 0:npr], in_=ub[:, 0:npr], func=AF.Copy, scale=w0, reads=[nub, "convw"], writes=[ntm])
                I("dve", "scalar_tensor_tensor", out=acc[:, 0:npr], in0=ub[:, 1:1 + npr], scalar=w1, in1=tmp[:, 0:npr], op0=ALU.mult, op1=ALU.add,
                  reads=[nub, ntm, "convw"], writes=[nac])
                I("dve", "scalar_tensor_tensor", out=acc[:, 0:npr], in0=ub[:, 2:2 + npr], scalar=w2, in1=acc[:, 0:npr], op0=ALU.mult, op1=ALU.add,
                  reads=[nub, nac, "convw"], writes=[nac])
                I("act", "activation", out=uhist[:, j, :], in_=ub[:, npr:npr + 2], func=AF.Copy, reads=[nub], writes=["uhist%d" % j])
                if has_s:
                    us = ub[:, 2 + npr:2 + npr + NS]
                    st0 = stT[:, j, 0:2 * NS:2]
                    st1 = stT[:, j, 1:2 * NS:2]
                    I("dve", "tensor_scalar", out=tmp[:, npr:npr + NS], in0=st0, scalar1=w0, scalar2=None, op0=ALU.mult, reads=["stT", "convw", ntm], writes=[ntm])
                    I("dve", "scalar_tensor_tensor", out=tmp[:, npr:npr + NS], in0=st1, scalar=w1, in1=tmp[:, npr:npr + NS], op0=ALU.mult, op1=ALU.add,
                      reads=["stT", ntm, "convw"], writes=[ntm])
                    I("dve", "scalar_tensor_tensor", out=acc[:, npr:npr + NS], in0=us, scalar=w2, in1=tmp[:, npr:npr + NS], op0=ALU.mult, op1=ALU.add,
                      reads=[nub, ntm, "convw", nac], writes=[nac])
                    I("act", "activation", out=usamp[:, j, :], in_=us, func=AF.Copy, reads=[nub], writes=["usamp"])
                I("dve", "tensor_tensor", out=B[:, j, c0:c0 + n], in0=ps[:, bb, 0:n], in1=acc[:, 0:n], op=ALU.mult,
                  reads=[psn(bb), nac], writes=W("B", t))

        def proj_main(st, t, wv_, wname, src, sname):
            c0, n = trange(st, t)
            w3 = v3(wv_, 0, 8192, KC)
            for jo in range(KC):
                b = mmr.next()
                mm_acc(ps[:, b, 0:n], b, lambda k: w3[:, k, jo * 128:(jo + 1) * 128], lambda k: src[:, k, c0:c0 + n], KC, [wname, tname(sname, t)])
                resid_add(t, jo, b, c0, n)

        def ffn_main(st, t, nm_, wv_, wname):
            c0, n = trange(st, t)
            Wg = v3(wv_, 0, 4096, KC); Wu = v3(wv_, 4096, 8192, KC); Wd = v3(wv_, 8192, 12288, 4)
            hn, hid = hidr.next()
            for mi in range(nm_):
                bg, bu = mmr.next(), mmr.next()
                mm_acc(ps[:, bg, 0:n], bg, lambda k: Wg[:, k, mi * 128:(mi + 1) * 128], lambda k: A[:, k, c0:c0 + n], KC, [wname, tname("A", t)])
                mm_acc(ps[:, bu, 0:n], bu, lambda k: Wu[:, k, mi * 128:(mi + 1) * 128], lambda k: A[:, k, c0:c0 + n], KC, [wname, tname("A", t)])
                nsg, sg = fr.next()
                I("act", "activation", out=sg[:, 0:n], in_=ps[:, bg, 0:n], func=AF.Silu, reads=[psn(bg)], writes=[nsg])
                I("dve", "tensor_tensor", out=hid[:, mi, 0:n], in0=sg[:, 0:n], in1=ps[:, bu, 0:n], op=ALU.mult, reads=[nsg, psn(bu), hn], writes=[hn])
            for jo in range(KC):
                b = mmr.next()
                mm_acc(ps[:, b, 0:n], b, lambda k: Wd[:, k, jo * 128:(jo + 1) * 128], lambda k: hid[:, k, 0:n], nm_, [wname, hn])
                resid_add(t, jo, b, c0, n)

        def ple_main(st, t, wv_, wname):
            c0, n = trange(st, t)
            Wgt = v3(wv_, 0, 8192, KC); Wpr = v3(wv_, 8192, 10240, 2)
            for jo in range(KC):
                bg, bp = mmr.next(), mmr.next()
                mm_acc(ps[:, bg, 0:n], bg, lambda k: Wgt[:, k, jo * 128:(jo + 1) * 128], lambda k: A[:, k, c0:c0 + n], KC, [wname, tname("A", t)])
                mm_acc(ps[:, bp, 0:n], bp, lambda k: Wpr[:, k, jo * 128:(jo + 1) * 128], lambda k: pT[:, k, c0:c0 + n], 2, [wname, tname("pT", t)])
                nsg, sg = fr.next()
                I("act", "activation", out=sg[:, 0:n], in_=ps[:, bg, 0:n], func=AF.Sigmoid, reads=[psn(bg)], writes=[nsg])
                I("dve", "tensor_tensor", out=sg[:, 0:n], in0=sg[:, 0:n], in1=ps[:, bp, 0:n], op=ALU.mult, reads=[nsg, psn(bp)], writes=[nsg])
                I("dve", "tensor_tensor", out=hT[:, jo, c0:c0 + n], in0=sg[:, 0:n], in1=hT[:, jo, c0:c0 + n], op=ALU.add,
                  reads=[nsg, tname("hT", t)], writes=[tname("hT", t)])

        def rope_a(pb, n):
            nq, qf = brr.next()
            I("act", "activation", out=qf[:, 0:n], in_=ps[:, pb, 0:n], func=AF.Copy, reads=[psn(pb)], writes=[nq])
            return (nq, pb), qf

        def rope_b(nqpb, qf, n, toff=0):
            nq, pb = nqpb
            rb = auxr.next()
            I("pe", "matmul", ps[:, rb, 0:n], lhsT=RmB[:], rhs=qf[:, 0:n], start=True, stop=True, reads=[nq, "RmB"], writes=[psn(rb)])
            n1, t1 = fr.next()
            I("dve", "tensor_tensor", out=t1[:, 0:n], in0=ps[:, pb, 0:n], in1=tabs[:, 0, toff:toff + n], op=ALU.mult, reads=[psn(pb), "tabs", nq], writes=[n1])
            n2, t2 = fr.next()
            I("dve", "tensor_tensor", out=t2[:, 0:n], in0=ps[:, rb, 0:n], in1=tabs[:, 1, toff:toff + n], op=ALU.mult, reads=[psn(rb), "tabs"], writes=[n2])
            return n1, t1, n2, t2

        def kvq_main(st, t, wv_, wname):
            c0, n = st["tiles"][t]
            npr = st["npr"][t]
            has_s = st["samp"][t]
            Wk = v3(wv_, 0, 2048, KC); Wv = v3(wv_, 2048, 4096, KC); Wq = v3(wv_, 4096, 12288, KC)
            I("sp", "dma_start", out=tabs[:, :, 0:n], in_=dap(tab_d, st["tabcol0"] + c0, [[2194, 128], [128 * 2194, 2], [1, n]]), writes=["tabs"], dma=True)
            ka = max(0, st["kcol0"] - c0)
            nk = npr - ka
            kp0 = st["kb0"] * 128 + (c0 + ka - st["kcol0"])
            kbs = [(kp0 // 128 + i, ka + 128 * i) for i in range(nk // 128)]
            last_rel = None
            for kb, rel in kbs:
                if kb == 16:
                    last_rel = rel
            pendq = []

            def flush(keep=0):
                while len(pendq) > keep:
                    pendq.pop(0)()

            def k_fin(ch, nq, qf):
                n1, t1, n2, t2 = rope_b(nq, qf, n)
                I("dve", "tensor_tensor", out=t1[:, 0:n], in0=t1[:, 0:n], in1=t2[:, 0:n], op=ALU.add, reads=[n1, n2], writes=[n1])
                I("act", "activation", out=KT[:, ch, kp0:kp0 + nk], in_=t1[:, ka:ka + nk], func=AF.Copy, reads=[n1], writes=["KT%d" % kb for kb, _ in kbs])
                if last_rel is not None:
                    ob = auxr.next()
                    I("pe", "transpose", ps[:, ob, 0:128], t1[:, last_rel:last_rel + 128], identF[:], reads=[n1, "identF"], writes=[psn(ob)])
                    I("act", "activation", out=kvout[:, ch * 128:(ch + 1) * 128], in_=ps[:, ob, 0:128], func=AF.Copy, reads=[psn(ob)], writes=["kvoutK"])
                    if ch == 1:
                        I("sp", "dma_start", out=kwp_o.ap(), in_=kvout[:, 0:256], reads=["kvoutK"], dma=True, is_out=True)
                if has_s:
                    ob = auxr.next()
                    I("pe", "transpose", ps[0:NS, ob, 0:128], t1[:, npr:npr + NS], identF[:], reads=[n1, "identF"], writes=[psn(ob)])
                    I("act", "activation", out=knew[:, ch * 128:(ch + 1) * 128], in_=ps[0:NS, ob, 0:128], func=AF.Copy, reads=[psn(ob)], writes=["kvoutK"])
                    if ch == 1:
                        I("sp", "dma_start", out=dap(kws_o, 127 * 256, [[128 * 256, NS], [1, 256]]), in_=knew[:, :], reads=["kvoutK"], dma=True, is_out=True)

            q0, nqc = st["tiles1"][t]
            qoff = q0 - c0

            def q_fin(cq, nq, qf):
                n1, t1, n2, t2 = rope_b(nq, qf, nqc, qoff)
                I("dve", "tensor_tensor", out=A[:, cq, q0:q0 + nqc], in0=t1[:, 0:nqc], in1=t2[:, 0:nqc], op=ALU.add, reads=[n1, n2], writes=W("A", t))
                if has_s:
                    I("dve", "tensor_tensor", out=qs32[:, cq, :], in0=t1[:, npr - qoff:npr - qoff + NS], in1=t2[:, npr - qoff:npr - qoff + NS], op=ALU.add, reads=[n1, n2], writes=["qs32"])

            for ch in range(2):
                pb = mmr.next()
                mm_acc(ps[:, pb, 0:n], pb, lambda k: Wk[:, k, ch * 128:(ch + 1) * 128], lambda k: A[:, k, c0:c0 + n], KC, [wname, tname("A", t)])
                nq, qf = rope_a(pb, n)
                flush(0)
                pendq.append(lambda ch=ch, nq=nq, qf=qf: k_fin(ch, nq, qf))
            for vi, (kb, rel) in enumerate(kbs):
                pb = mmr.next()
                mm_acc(ps[:, pb, 0:256], pb, lambda k: A[:, k, c0 + rel:c0 + rel + 128], lambda k: Wv[:, k, :], KC, [wname, tname("A", t)])
                if vi == 0:
                    flush()
                I("act", "activation", out=Vt[:, kb, :, 0:64], in_=ps[:, pb, 0:256].rearrange("p (a b) -> p a b", a=4), func=AF.Copy,
                  reads=[psn(pb)], writes=["Vt%d" % kb])
                if kb == 16:
                    I("act", "activation", out=kvout[:, 256:512], in_=ps[:, pb, 0:256], func=AF.Copy, reads=[psn(pb)], writes=["kvoutV"])
                    I("sp", "dma_start", out=vwp_o.ap(), in_=kvout[:, 256:512], reads=["kvoutV"], dma=True, is_out=True)
            if has_s:
                pb = mmr.next()
                mm_acc(ps[0:NS, pb, 0:256], pb, lambda k: A[:, k, c0 + npr:c0 + npr + NS], lambda k: Wv[:, k, :], KC, [wname, tname("A", t)])
                I("act", "activation", out=vnew[:, :], in_=ps[0:NS, pb, 0:256], func=AF.Copy, reads=[psn(pb)], writes=["kvoutV"])
                I("sp", "dma_start", out=dap(vws_o, 127 * 256, [[128 * 256, NS], [1, 256]]), in_=vnew[:, :], reads=["kvoutV"], dma=True, is_out=True)
            for cq in range(KC):
                pb = mmr.next()
                mm_acc(ps[:, pb, 0:nqc], pb, lambda k: Wq[:, k, cq * 128:(cq + 1) * 128], lambda k: B[:, k, q0:q0 + nqc], KC, [wname, tname("B", t)])
                nq, qf = rope_a(pb, nqc)
                flush(0)
                pendq.append(lambda cq=cq, nq=nq, qf=qf: q_fin(cq, nq, qf))
            flush()

        def tile_of(st, col):
            for i, (c0, n) in enumerate(st["tiles"]):
                if c0 <= col < c0 + n:
                    return i
            raise ValueError(col)

        prod_bf = prod[:, :].bitcast(BF16)
        tabs_bf = tabs[:, :, :].rearrange("p a b -> p (a b)").bitcast(BF16)
        PT_slot = [
            Ring([("PT0", PT_t[0]), ("PT1", PT_t[1])]),
            Ring([("prod", prod_bf[:, 0:1024].rearrange("p (a b) -> p a b", a=2)), ("prodB", prod_bf[:, 1024:2048].rearrange("p (a b) -> p a b", a=2))]),
        ]
        osb_slot = [("o_sb", o_sb[:, :]), ("xs1", xs[1][:, :].bitcast(BF16)[:, 0:1024])]
        small_slot = [small, small2]

        def att_phases(st, qi, slot):
            qb = st["qb0"] + qi
            qc0 = st["ownc0"] + 128 * qi
            tq = tile_of(st, qc0)
            PTs = {}
            osn, osb = osb_slot[slot]
            sm = small_slot[slot]

            def S_phase(g):
                pair, hh = g // 2, g % 2
                PTn, PTt = PT_slot[slot].next()
                PTs[g] = (PTn, PTt)
                for kbi, kb in enumerate((qb, qb + 1)):
                    sb = mmr.next()
                    I("pe", "matmul", ps[:, sb, :], lhsT=KT[hh * 64:(hh + 1) * 64, pair, kb * 128:(kb + 1) * 128],
                      rhs=A[hh * 64:(hh + 1) * 64, pair * 4:pair * 4 + 4, qc0:qc0 + 128], start=True, stop=True,
                      reads=["KT%d" % kb, tname("A", tq)], writes=[psn(sb)])
                    I("act", "activation", out=PTt[:, kbi, :], in_=ps[:, sb, :], func=AF.Exp, scale=SCALE, reads=[psn(sb)], writes=[PTn])
                    mi = 0 if kbi == 1 else (2 if qb == 0 else 1)
                    mk = bass.AP(masks, mi * 128, [[384, 128], [0, 4], [1, 128]])
                    pv = PTt[:, kbi, :].rearrange("p (a b) -> p a b", a=4)
                    I("dve", "tensor_tensor", out=pv, in0=pv, in1=mk, op=ALU.mult, reads=[PTn, "masks"], writes=[PTn])

            def PV_phase(g):
                PTn, PTt = PTs[g]
                ob = auxr.next()
                for r in range(4):
                    for kbi, kb in enumerate((qb, qb + 1)):
                        I("pe", "matmul", ps[:, ob, r * 65:(r + 1) * 65], lhsT=PTt[:, kbi, r * 128:(r + 1) * 128], rhs=Vt[:, kb, g, 0:65],
                          start=(kbi == 0), stop=(kbi == 1), reads=[PTn, "Vt%d" % kb], writes=[psn(ob)], inc=(r == 3 and kbi == 1))
                sn = "small%d_%d" % (slot, g)
                o3 = ps[:, ob, 0:260].rearrange("p (a b) -> p a b", a=4)
                I("dve", "tensor_tensor", out=sm[:, g * 8:g * 8 + 4], in0=o3[:, :, 64], in1=esink[:, 4 * g:4 * g + 4], op=ALU.add,
                  reads=[psn(ob), "esink"], writes=[sn])
                I("dve", "reciprocal", out=sm[:, g * 8 + 4:g * 8 + 8], in_=sm[:, g * 8:g * 8 + 4], reads=[sn], writes=[sn])
                I("dve", "tensor_tensor", out=osb[:, g * 256:(g + 1) * 256].rearrange("p (a b) -> p a b", a=4), in0=o3[:, :, 0:64],
                  in1=bass.AP(sm, g * 8 + 4, [[sm_pitch[slot], 128], [1, 4], [0, 64]]), op=ALU.mult, reads=[psn(ob), sn], writes=[osn])

            def T_phase():
                tb = auxr.next()
                psb = ps[:, tb, :].bitcast(BF16)
                for c in range(KC):
                    I("pe", "transpose", psb[:, c * 128:(c + 1) * 128], osb[:, c * 128:(c + 1) * 128], identB[:], reads=[osn, "identB"], writes=[psn(tb)], inc=(c == KC - 1))
                I("act", "activation", out=B[:, 0:KC, qc0:qc0 + 128], in_=psb.rearrange("p (a b) -> p a b", a=KC), func=AF.Copy, reads=[psn(tb)], writes=W("B", tq))

            return [lambda: S_phase(0), lambda: S_phase(1), lambda: PV_phase(0), lambda: S_phase(2), lambda: PV_phase(1),
                    lambda: S_phase(3), lambda: PV_phase(2), lambda: PV_phase(3), T_phase]

        def att_group(st, qis):
            phs = [att_phases(st, qi, j) for j, qi in enumerate(qis)]
            for k in range(len(phs[0])):
                for ph in phs:
                    ph[k]()

        def att_main(st, qi):
            att_group(st, [qi])

        def samp_attn(st):
            sc0 = 1154
            tq = 2
            for cq in range(KC):
                I("pe", "transpose", ps[0:NS, cq // 4, (cq % 4) * 128:(cq % 4 + 1) * 128], qs32[:, cq, :], identF[:],
                  reads=["qs32", "identF"], writes=[psn(0), psn(1)], inc=(cq == KC - 1))
            for pair in range(2):
                I("act", "activation", out=q_tm[:, pair * 512:(pair + 1) * 512].rearrange("p (h c d) -> p h c d", h=2, c=4),
                  in_=ps[0:NS, pair, :].rearrange("p (c h d) -> p h c d", c=4, h=2), func=AF.Copy, reads=[psn(pair)], writes=["xs0"])
            for s in range(NS):
                kn, Kst = ("Ks%d" % (s % 2), Ks_t[s % 2])
                vn, Vst = ("Vs%d" % (s % 2), Vs_t[s % 2])
                pzn, Pzt = ("Pz%d" % (s % 2), Pz_t[s % 2])
                I("sp", "dma_start", out=Kst[:, :], in_=dap(ck, s * 128 * 256, [[256, 128], [1, 256]]), writes=[kn], dma=True)
                I("sp", "dma_start", out=Vst[:, :], in_=dap(cv, s * 128 * 256, [[256, 128], [1, 256]]), writes=[vn], dma=True)
                vbn, Vsb = ("Vsb%d" % (s % 2), Vsb_t[s % 2])
                I("act", "activation", out=Vsb[:, :], in_=Vst[:, :], func=AF.Copy, reads=[vn], writes=[vbn])
                b0 = 2 * (s % 2)
                sel = bass.AP(identF, s, [[128, NS], [0, 128]])
                for half in range(2):
                    I("pe", "matmul", ps[:, b0 + half, :], lhsT=sel, rhs=q_tm[:, half * 512:(half + 1) * 512], start=True, stop=True,
                      reads=["xs0", "identF"], writes=[psn(b0 + half)])
                    I("dve", "tensor_tensor", out=prod[:, half * 512:(half + 1) * 512].rearrange("p (g r d) -> p g r d", g=2, r=4),
                      in0=ps[:, b0 + half, :].rearrange("p (g r d) -> p g r d", g=2, r=4),
                      in1=bass.AP(Kst, half * 128, [[256, 128], [64, 2], [0, 4], [1, 64]]), op=ALU.mult,
                      reads=[psn(b0 + half), kn], writes=["prod", "prodB"])
                I("dve", "tensor_reduce", out=small[:, 32:48], in_=prod[:, :].rearrange("p (h d) -> p h d", d=64), axis=AX.X, op=ALU.add,
                  reads=["prod", "prodB"], writes=["smallS"])
                I("act", "activation", out=Pzt[:, :, s], in_=small[:, 32:48], func=AF.Exp, scale=SCALE, reads=["smallS"], writes=[pzn])
                for h in range(16):
                    I("pe", "matmul", ps[0:NS, 5 + h // 8, (h % 8) * 64:(h % 8 + 1) * 64], lhsT=Pzt[:, h, :], rhs=Vsb[:, (h // 4) * 64:(h // 4 + 1) * 64],
                      start=(s == 0 and h % 8 == 0), stop=(s == NS - 1 and h % 8 == 7), reads=[pzn, vbn], writes=[psn(5), psn(6)], inc=False)
                I("pe", "matmul", ps[0:NS, 7, 0:16], lhsT=Esel[:, s, :], rhs=Pzt[:, :, s], start=(s == 0), stop=(s == NS - 1),
                  reads=[pzn, "Esel"], writes=[psn(7)])
                I("dve", "memset", Pzt[:, :, s], 0.0, writes=[pzn])
            I("dve", "tensor_tensor", out=prod[0:NS, :].rearrange("p (g r d) -> p g r d", g=4, r=4), in0=q_tm[:, :].rearrange("p (g r d) -> p g r d", g=4, r=4),
              in1=bass.AP(kvout, 0, [[512, NS], [64, 4], [0, 4], [1, 64]]), op=ALU.mult, reads=["xs0", "kvoutK"], writes=["prod", "prodB"])
            I("dve", "tensor_reduce", out=small[0:NS, 48:64], in_=prod[0:NS, :].rearrange("p (h d) -> p h d", d=64), axis=AX.X, op=ALU.add,
              reads=["prod", "prodB"], writes=["smallN"])
            I("act", "activation", out=small[0:NS, 48:64], in_=small[0:NS, 48:64], func=AF.Exp, scale=SCALE, reads=["smallN"], writes=["smallN"])
            I("dve", "tensor_tensor", out=small[0:NS, 32:48], in0=ps[0:NS, 7, 0:16], in1=small[0:NS, 48:64], op=ALU.add, reads=[psn(7), "smallN"], writes=["smallS"])
            I("dve", "tensor_tensor", out=small[0:NS, 32:48], in0=small[0:NS, 32:48], in1=esink[0:NS, :], op=ALU.add, reads=["smallS", "esink"], writes=["smallS"])
            I("dve", "reciprocal", out=small[0:NS, 32:48], in_=small[0:NS, 32:48], reads=["smallS"], writes=["smallS"])
            I("dve", "tensor_tensor", out=prod[0:NS, :].rearrange("p (g r d) -> p g r d", g=4, r=4),
              in0=bass.AP(kvout, 256, [[512, NS], [64, 4], [0, 4], [1, 64]]), in1=bass.AP(small, 48, [[64, NS], [4, 4], [1, 4], [0, 64]]), op=ALU.mult,
              reads=["kvoutV", "smallN", "prod", "prodB"], writes=["prod", "prodB"])
            I("dve", "tensor_tensor", out=prod[0:NS, :].rearrange("p (a b) -> p a b", a=2), in0=prod[0:NS, :].rearrange("p (a b) -> p a b", a=2),
              in1=ps[0:NS, 5:7, :], op=ALU.add, reads=["prod", "prodB", psn(5), psn(6)], writes=["prod", "prodB"])
            I("dve", "tensor_tensor", out=on_bf[:, :].rearrange("p (h d) -> p h d", d=64), in0=prod[0:NS, :].rearrange("p (h d) -> p h d", d=64),
              in1=bass.AP(small, 32, [[64, NS], [1, 16], [0, 64]]), op=ALU.mult, reads=["prod", "prodB", "smallS"], writes=["o_sb"])
            psb = ps[:, 4, :].bitcast(BF16)
            for c in range(KC):
                I("pe", "transpose", psb[:, c * NS:(c + 1) * NS], on_bf[:, c * 128:(c + 1) * 128], identB[0:NS, 0:NS], reads=["o_sb", "identB"], writes=[psn(4)], inc=(c == KC - 1))
            I("act", "activation", out=B[:, 0:KC, sc0:sc0 + NS], in_=psb[:, 0:KC * NS].rearrange("p (a b) -> p a b", a=KC), func=AF.Copy, reads=[psn(4)], writes=W("B", tq))

        pairr = Ring([0, 2])
        ysr = Ring([(["prod", "prodB"], prod), (["tabs"], tabs[:, :, :].rearrange("p a b -> p (a b)"))])

        def final_block(st, t, col, nb, dst_ap):
            b0 = pairr.next()
            for kc in range(KC):
                I("pe", "transpose", ps[0:nb, b0 + kc // 4, (kc % 4) * 128:(kc % 4 + 1) * 128], hT[:, kc, col:col + nb], identF[:],
                  reads=[tname("hT", t), "identF"], writes=[psn(b0), psn(b0 + 1)], inc=(kc == KC - 1))
            nj, junk = fr.next()
            for half in range(2):
                I("act", "activation", out=junk[0:nb, 0:512], in_=ps[0:nb, b0 + half, :], func=AF.Square, accum_out=small[0:nb, 16 + half:17 + half],
                  reads=[psn(b0 + half)], writes=[nj, "smallF"])
            I("dve", "tensor_tensor", out=small[0:nb, 18:19], in0=small[0:nb, 16:17], in1=small[0:nb, 17:18], op=ALU.add, reads=["smallF"], writes=["smallF"])
            I("act", "activation", out=small[0:nb, 19:20], in_=small[0:nb, 18:19], func=AF.Ln, bias=epsT[0:nb, 0:1], scale=1.0 / 1024.0, reads=["smallF", "epsT"], writes=["smallF"])
            I("act", "activation", out=small[0:nb, 20:21], in_=small[0:nb, 19:20], func=AF.Exp, scale=-0.5, reads=["smallF"], writes=["smallF"])
            yn, yt = ysr.next()
            for half in range(2):
                I("dve", "scalar_tensor_tensor", out=yt[0:nb, half * 512:(half + 1) * 512], in0=ps[0:nb, b0 + half, :], scalar=small[0:nb, 20:21],
                  in1=gfin[0:nb, half * 512:(half + 1) * 512], op0=ALU.mult, op1=ALU.mult, reads=[psn(b0 + half), "smallF", "gfin"], writes=yn)
            I("sp", "dma_start", out=dst_ap, in_=yt[0:nb, :], reads=yn, dma=True, is_out=True)

        def final_post(st, t):
            c0, n = st["tiles"][t]
            npr = st["npr"][t]
            col = max(c0, st["ownc0"])
            while col < c0 + npr:
                row = st["yrow0"] + (col - st["ownc0"])
                final_block(st, t, col, 128, dap(y_o, row * D, [[D, 128], [1, D]]))
                col += 128
            if st["samp"][t]:
                final_block(st, t, c0 + npr, NS, ys_o.ap())

        def tm_out(src3, width, dst_ap, wait_names):
            b0 = pairr.next()
            for kc in range(KC):
                I("pe", "transpose", ps[0:width, b0 + kc // 4, (kc % 4) * 128:(kc % 4 + 1) * 128], src3[:, kc, :], identF[:],
                  reads=wait_names + ["identF"], writes=[psn(b0), psn(b0 + 1)], inc=(kc == KC - 1))
            yn, yt = ysr.next()
            I("act", "activation", out=yt[0:width, :].rearrange("p (a b) -> p a b", a=2), in_=ps[0:width, b0:b0 + 2, :], func=AF.Copy,
              reads=[psn(b0), psn(b0 + 1)], writes=yn)
            I("sp", "dma_start", out=dst_ap, in_=yt[0:width, :], reads=yn, dma=True, is_out=True)

        pipe = Pipe()

        def with_st(si, fn, l1=False):
            def g():
                old = (cur["st"], cur["l1"])
                cur["st"], cur["l1"] = si, l1
                fn()
                cur["st"], cur["l1"] = old
            return g

        early = {}

        def groups_for(si):
            st = STS[si]
            nt = len(st["tiles"])
            G = []

            def item(main, post=None, l1=False, post_l1=None):
                pl1 = l1 if post_l1 is None else post_l1
                pipe.item(with_st(si, main, l1), with_st(si, post, pl1) if post is not None else None)

            def load_s1(grp):
                def f(wv_, wname):
                    w3 = v3(wv_, 0, 12288, KC)
                    for sel in range(3):
                        wload(w3[:, :, sel * 512:(sel + 1) * 512], w_in, sel * 1024 + grp * 512, 3 * D, 512, KC, wname)
                return f

            def s0_main(t):
                stage0_main(st, t)
                load_p(st, t, 0)

            def s0_post(t):
                norm(st, t, [(G_MIX0, A, "A")])

            if si == 1:
                early["s0_main0"] = with_st(1, lambda: s0_main(0))
                early["s0_post0"] = with_st(1, lambda: s0_post(0))

            def run_s1a(wv_, wname):
                if si == 0:
                    with_st(si, load_state)()
                def s0(t):
                    item(lambda t=t: s0_main(t), lambda t=t: s0_post(t))

                def s1(t):
                    item(lambda t=t: s1_main(st, t, 0, wv_, wname))

                if si == 1 and early.get("done"):
                    pipe.item(lambda: (early["s0_post0"](), with_st(1, lambda: s0_main(1))()), with_st(1, lambda: s0_post(1)))
                else:
                    s0(0)
                    s0(1)
                s1(0)
                for t in range(2, nt):
                    s0(t)
                    s1(t - 1)
                if si == 0:
                    passthrough()
                s1(nt - 1)

            def run_s1b(wv_, wname):
                for t in range(nt):
                    item(lambda t=t: s1_main(st, t, 1, wv_, wname))
                if si == 0:
                    item(lambda: tm_out(usamp, NS, dap(css_o, D, [[2 * D, NS], [1, D]]), ["usamp"]))
                else:
                    item(lambda: tm_out(uhist, 2, csp_o.ap(), ["uhist%d" % j for j in range(KC)]))

            G.append((load_s1(0), run_s1a))
            G.append((load_s1(1), run_s1b))

            def load_proj(src_t):
                def f(wv_, wname):
                    wload(v3(wv_, 0, 8192, KC), src_t, 0, D, 1024, KC, wname)
                return f

            def run_s2(wv_, wname):
                for t in range(nt):
                    item(lambda t=t: proj_main(st, t, wv_, wname, B, "B"), lambda t=t: norm(st, t, [(G_FFN0, A, "A")]))

            G.append((load_proj(w_out), run_s2))

            def ffn_groups(layer, gnext):
                for gi, (m0, nm_) in enumerate(FFN_GROUPS):
                    def lf(wv_, wname, m0=m0, nm_=nm_):
                        wload(v3(wv_, 0, 4096, KC)[:, :, 0:nm_ * 128], wg, layer * D * FH + m0 * 128, FH, nm_ * 128, KC, wname)
                        wload(v3(wv_, 4096, 8192, KC)[:, :, 0:nm_ * 128], wu, layer * D * FH + m0 * 128, FH, nm_ * 128, KC, wname)
                        wload(v3(wv_, 8192, 12288, 4)[:, 0:nm_, :], wd, layer * FH * D + m0 * 128 * D, D, 1024, nm_, wname)

                    def rf(wv_, wname, nm_=nm_, last=(gi == len(FFN_GROUPS) - 1)):
                        for t in range(nt):
                            post = (lambda t=t: norm(st, t, [(gnext, A, "A")])) if last else None
                            item(lambda t=t: ffn_main(st, t, nm_, wv_, wname), post, l1=(layer == 1))
                    G.append((lf, rf))

            ffn_groups(0, G_PLE0)

            def load_ple(layer):
                def f(wv_, wname):
                    wload(v3(wv_, 0, 8192, KC), pgate, layer * D * D, D, 1024, KC, wname)
                    wload(v3(wv_, 8192, 10240, 2), pproj, layer * PLE * D, D, 1024, 2, wname)
                return f

            def run_ple0(wv_, wname):
                for t in range(nt):
                    item(lambda t=t: ple_main(st, t, wv_, wname), lambda t=t: (norm(st, t, [(G_KV, A, "A"), (G_MIX1, B, "B")]), load_p(st, t, 1)))

            G.append((load_ple(0), run_ple0))

            def load_kvq(wv_, wname):
                wload(v3(wv_, 0, 2048, KC), wk, 0, 256, 256, KC, wname)
                wload(v3(wv_, 2048, 4096, KC), wv, 0, 256, 256, KC, wname)
                Wq3 = v3(wv_, 4096, 12288, KC)
                for pair in range(2):
                    for hh in range(2):
                        for c_ in range(4):
                            col = (pair * 4 + c_) * 128 + hh * 64
                            dst = Wq3[:, :, col:col + 64]
                            src = dap(wq, (pair * 8 + hh * 4 + c_) * 64, [[D, 128], [128 * D, KC], [1, 64]])
                            I("pool", "dma_start", out=dst, in_=src, writes=[wname], dma=True)

            qb_of_tile = [[] for _ in range(nt)]
            for qi in range(st["nqb"]):
                qb_of_tile[tile_of(st, st["ownc0"] + 128 * qi)].append(qi)

            def cap(fn, rings):
                mmr.cur, auxr.cur = rings
                try:
                    return p.capture(with_st(si, fn, cur["l1"]))
                finally:
                    mmr.cur, auxr.cur = mmr.base, auxr.base

            def att_ops(t):
                def f():
                    qs = qb_of_tile[t]
                    for i in range(0, len(qs), 2):
                        att_group(st, qs[i:i + 2])
                return cap(f, ringsA)

            def run_kvq(wv_, wname):
                item(lambda: kvq_main(st, 0, wv_, wname))
                for t in range(1, nt):
                    def main(t=t):
                        a = att_ops(t - 1)
                        b = cap(lambda: kvq_main(st, t, wv_, wname), ringsB)
                        p.merge_replay(a, b)
                    pipe.item(main)

            G.append((load_kvq, run_kvq))

            def run_wo(wv_, wname):
                def main0():
                    a = att_ops(nt - 1)
                    b = cap(lambda: proj_main(st, 0, wv_, wname, B, "B"), ringsB)
                    p.merge_replay(a, b)
                pipe.item(with_st(si, main0, True), with_st(si, lambda: norm(st, 0, [(G_FFN1, A, "A")]), True))
                if si == 0:
                    item(lambda: samp_attn(st))
                for t in range(1, nt):
                    item(lambda t=t: proj_main(st, t, wv_, wname, B, "B"), lambda t=t: norm(st, t, [(G_FFN1, A, "A")]), l1=True)

            G.append((load_proj(wo), run_wo))
            ffn_groups(1, G_PLE1)

            def run_ple1(wv_, wname):
                hoist = (si == 0 and "s0_main0" in early)
                for t in range(nt - 1 if hoist else nt):
                    item(lambda t=t: ple_main(st, t, wv_, wname), lambda t=t: final_post(st, t), l1=True)
                if hoist:
                    tl = nt - 1
                    pipe.item(lambda: (early["s0_main0"](), with_st(0, lambda: ple_main(st, tl, wv_, wname), True)()),
                              with_st(0, lambda: final_post(st, tl), True))
                    early["done"] = True

            G.append((load_ple(1), run_ple1))
            return G

        allg = groups_for(0) + groups_for(1)

        def do_load(i):
            if i < len(allg):
                allg[i][0](wbuf[i % 2], "wbuf%d" % (i % 2))

        do_load(0)
        do_load(1)
        for i, (lf, rf) in enumerate(allg):
            rf(wbuf[i % 2], "wbuf%d" % (i % 2))
            do_load(i + 2)
        pipe.flush()
        p.finish()
        p.emit()
        print("ops per engine:", {k: len(v) for k, v in p.ops.items()})
    return nc


_NC_CACHE = {}


def _rope_tables(pos):
    half = 8
    inv_freq = np.power(np.float32(500000.0), -np.arange(half, dtype=np.float32) / np.float32(half)).astype(np.float32)
    ang = (pos.astype(np.float32)[:, None] * inv_freq[None, :]).astype(np.float32)
    cos = np.cos(ang).astype(np.float32).T
    sin = np.sin(ang).astype(np.float32).T
    n = pos.shape[0]
    C = np.ones((128, n), np.float32)
    S = np.zeros((128, n), np.float32)
    for hb in range(2):
        base = hb * 64
        C[base:base + 8] = cos
        C[base + 8:base + 16] = cos
        S[base:base + 8] = -sin
        S[base + 8:base + 16] = sin
    return C, S


def prepare(x_prompt, x_sample, state_conv, cache_k_win, cache_v_win, p_prompt, p_sample,
           norm_mix_g, norm_ffn_g, norm_ple_g, kv_norm_g, final_norm_g,
           conv_w_in, conv_w, conv_w_out, w_k, w_v, w_q, sinks, w_o,
           ffn_w_gate, ffn_w_up, ffn_w_down, ple_w_proj, ple_w_gate):
    f32 = np.float32
    A_ = lambda a: np.ascontiguousarray(np.asarray(a, dtype=f32))
    x_prompt = A_(x_prompt); x_sample = A_(x_sample); state_conv = A_(state_conv)
    cache_k_win = A_(cache_k_win); cache_v_win = A_(cache_v_win); p_prompt = A_(p_prompt); p_sample = A_(p_sample)

    def colvec(g):
        return np.asarray(g, f32).reshape(KC, 128).T

    gains = [norm_mix_g[0], norm_ffn_g[0], norm_ple_g[0], kv_norm_g, norm_mix_g[1], norm_ffn_g[1], norm_ple_g[1]]
    gvec = np.ascontiguousarray(np.concatenate([colvec(g) for g in gains], axis=1))
    gfin = np.ascontiguousarray(np.broadcast_to(np.asarray(final_norm_g, f32)[None, :], (128, D)))
    cw = np.asarray(conv_w, f32)[0]
    convw = np.ascontiguousarray(np.stack([colvec(cw[j]) for j in range(3)], axis=2).reshape(128, 24))
    sinkb = np.ascontiguousarray(np.broadcast_to(np.asarray(sinks, f32)[0][None, :], (128, 16)))
    idn = np.eye(128, dtype=f32)
    rm = np.zeros((128, 128), f32)
    for m in range(128):
        d = m % 64
        if d < 8:
            rm[m + 8, m] = 1.0
        elif d < 16:
            rm[m - 8, m] = 1.0
    jj = np.arange(128)[:, None]; ii = np.arange(128)[None, :]
    mcur = (jj <= ii).astype(f32); mprev = (jj >= ii).astype(f32)
    esel = np.zeros((128, 16, 16), f32)
    for s in range(16):
        esel[:, s, s] = 1.0
    esel = esel.reshape(128, 256)

    shared = dict(
        w_in=A_(conv_w_in)[0], w_out=A_(conv_w_out)[0], wk=A_(w_k), wv=A_(w_v), wq=A_(w_q)[0], wo=A_(w_o)[0],
        wg=A_(ffn_w_gate), wu=A_(ffn_w_up), wd=A_(ffn_w_down), pproj=A_(ple_w_proj), pgate=A_(ple_w_gate),
        gvec=gvec, gfin=gfin, convw=convw, sinkb=sinkb, idn=idn, rm=rm, esel=esel)

    in_maps = []
    for c in range(NCORES):
        b, half = c // 2, c % 2
        t0 = half * OWN
        xin = np.zeros((XR, D), f32)
        pin = np.zeros((2, XR, PLE), f32)
        if half == 1:
            xin[:] = x_prompt[b, t0 - HALO:t0 + OWN]
            pin[:] = p_prompt[:, b, t0 - HALO:t0 + OWN]
        else:
            xin[HALO:] = x_prompt[b, 0:OWN]
            pin[:, HALO:] = p_prompt[:, b, 0:OWN]
        s0 = c * NS
        pos1 = np.concatenate([np.maximum(t0 - HALO + np.arange(1154), 0), np.full(NS, 16384)]).astype(f32)
        pos2 = (t0 + 1024 + np.arange(1024)).astype(f32)
        C1, S1 = _rope_tables(pos1)
        C2, S2 = _rope_tables(pos2)
        tab = np.ascontiguousarray(np.stack([np.concatenate([C1, C2], 1), np.concatenate([S1, S2], 1)], 0))
        masks = np.ascontiguousarray(np.stack([mcur, mprev, mprev if half == 1 else np.zeros_like(mprev)], 0))
        m = dict(shared)
        m.update(xin=xin, xsm=np.ascontiguousarray(x_sample[s0:s0 + NS, 0]), stc=np.ascontiguousarray(state_conv[0, s0:s0 + NS].reshape(2 * NS, D)),
                 ck=np.ascontiguousarray(cache_k_win[s0:s0 + NS].reshape(NS, 128, 256)), cv=np.ascontiguousarray(cache_v_win[s0:s0 + NS].reshape(NS, 128, 256)),
                 pin=pin, psm=np.ascontiguousarray(p_sample[:, s0:s0 + NS, 0]), masks=masks, tab=tab)
        in_maps.append(m)

    return in_maps


def assemble(R):
    f32 = np.float32
    y_prompt = np.zeros((4, 4096, D), f32); y_sample = np.zeros((128, 1, D), f32)
    csp = np.zeros((1, 4, 2, D), f32); css = np.zeros((1, 128, 2, D), f32)
    kwp = np.zeros((4, 128, 4, 64), f32); vwp = np.zeros((4, 128, 4, 64), f32)
    kws = np.zeros((128, 128, 4, 64), f32); vws = np.zeros((128, 128, 4, 64), f32)
    for c in range(NCORES):
        b, half = c // 2, c % 2
        r = R[c]
        y_prompt[b, half * OWN:(half + 1) * OWN] = r["y"]
        s0 = c * NS
        y_sample[s0:s0 + NS, 0] = r["ys"]
        css[0, s0:s0 + NS] = r["css"]
        kws[s0:s0 + NS] = r["kws"].reshape(NS, 128, 4, 64)
        vws[s0:s0 + NS] = r["vws"].reshape(NS, 128, 4, 64)
        if half == 1:
            csp[0, b] = r["csp"]
            kwp[b] = r["kwp"].reshape(128, 4, 64)
            vwp[b] = r["vwp"].reshape(128, 4, 64)
    return (y_prompt, y_sample, csp, css, kwp, vwp, kws, vws)


def kernel(**inputs):
    in_maps = prepare(**inputs)
    if "nc" not in _NC_CACHE:
        _NC_CACHE["nc"] = build_program()
    nc = _NC_CACHE["nc"]
    res = run_bass_kernel_spmd(nc, in_maps, core_ids=list(range(NCORES)))
    return assemble(res.results)
```

```python
import numpy as np
from contextlib import ExitStack
import concourse.bass as bass
import concourse.mybir as mybir
from concourse.bass_utils import run_bass_kernel_spmd

F32 = mybir.dt.float32
BF16 = mybir.dt.bfloat16
ALU = mybir.AluOpType
AF = mybir.ActivationFunctionType
AX = mybir.AxisListType

NCORES = 8
D = 1024
KC = 8
FH = 2816
HC = 22
PLE = 256
HALO = 130
OWN = 2048
NS = 16
XR = HALO + OWN
RMS_EPS = 1e-6
SCALE = 0.125

ENGS = ("pe", "act", "dve", "pool", "sp")
MERGE = True


class Prog:
    def __init__(self, nc, es, n_dma_sp=24, n_dma_pool=8):
        self.nc = nc
        self.sem = {e: es.enter_context(nc.semaphore("s_" + e)) for e in ENGS[:4]}
        self.dsem = {}
        self.dpool = {"sp": [], "pool": []}
        for i in range(n_dma_sp):
            k = "dsp%d" % i
            self.dsem[k] = es.enter_context(nc.semaphore(k))
            self.dpool["sp"].append(k)
        for i in range(n_dma_pool):
            k = "dpl%d" % i
            self.dsem[k] = es.enter_context(nc.semaphore(k))
            self.dpool["pool"].append(k)
        self.dpool["act"] = []
        for i in range(6):
            k = "dac%d" % i
            self.dsem[k] = es.enter_context(nc.semaphore(k))
            self.dpool["act"].append(k)
        self.dnext = {"sp": 0, "pool": 0, "act": 0}
        self.dcum = {k: 0 for k in self.dsem}
        self.ops = {e: [] for e in ENGS}
        self.tick = {e: 0 for e in ENGS}
        self.seen = {e: {} for e in ENGS}
        self.res = {}
        self.out_events = []
        self.cap = None
        self.debug_names = None

    def capture(self, fn):
        assert self.cap is None
        self.cap = []
        fn()
        ops, self.cap = self.cap, None
        return ops

    def replay(self, ops):
        for eng, fn, reads, writes, inc, dma, is_out in ops:
            self.add(eng, fn, reads=reads, writes=writes, inc=inc, dma=dma, is_out=is_out)

    @staticmethod
    def segments(ops):
        segs, cur = [], []
        for op in ops:
            cur.append(op)
            if op[0] == "pe" and op[4]:
                segs.append(cur)
                cur = []
        if cur:
            if segs:
                segs[-1].extend(cur)
            else:
                segs.append(cur)
        return segs

    def merge_replay(self, opsA, opsB):
        if not MERGE:
            self.replay(opsA)
            self.replay(opsB)
            return
        sa, sb = self.segments(opsA), self.segments(opsB)
        na, nb = len(sa), len(sb)
        out = []
        j = 0
        for i, seg in enumerate(sa):
            out.extend(seg)
            tgt = ((i + 1) * nb) // na
            while j < tgt:
                out.extend(sb[j])
                j += 1
        while j < nb:
            out.extend(sb[j])
            j += 1
        self.replay(out)

    def _handle(self, k):
        return self.sem[k] if k in self.sem else self.dsem[k]

    def add(self, eng, fn, reads=(), writes=(), inc=True, dma=False, is_out=False):
        if self.cap is not None:
            self.cap.append((eng, fn, tuple(reads), tuple(writes), inc, dma, is_out))
            return None
        waits = {}

        def need(ev):
            if ev is None:
                return
            k, v = ev
            if k == eng and eng == "pe":
                return
            if v > waits.get(k, 0):
                waits[k] = v

        for r in reads:
            s = self.res.get(r)
            if s is not None:
                need(s[0])
                if r.startswith("ps"):
                    for k, v in s[1].items():
                        if k != eng:
                            need((k, v))
        for w in writes:
            s = self.res.get(w)
            if s is not None:
                need(s[0])
                for k, v in s[1].items():
                    need((k, v))
        if dma:
            pool = self.dpool[eng]
            sk = pool[self.dnext[eng] % len(pool)]
            self.dnext[eng] += 1
            if self.dcum[sk] > 0:
                need((sk, self.dcum[sk]))
            self.dcum[sk] += 16
            ev = (sk, self.dcum[sk])
            incspec = (sk, 16)
        else:
            if inc:
                self.tick[eng] += 1
                ev = (eng, self.tick[eng])
                incspec = (eng, 1)
            else:
                ev = (eng, self.tick[eng] + 1)
                incspec = None
        wl = []
        for k, v in waits.items():
            if self.seen[eng].get(k, 0) < v:
                self.seen[eng][k] = v
                wl.append((k, v))
        self.ops[eng].append((fn, wl, incspec))
        for r in reads:
            s = self.res.get(r)
            if s is None:
                s = self.res[r] = [None, {}]
            if s[1].get(ev[0], 0) < ev[1]:
                s[1][ev[0]] = ev[1]
        for w in writes:
            self.res[w] = [ev, {}]
        if is_out:
            self.out_events.append(ev)
        return ev

    def finish(self):
        fin = {}
        for k, v in self.out_events:
            fin[k] = max(fin.get(k, 0), v)
        wl = [(k, v) for k, v in fin.items()]
        self.ops["sp"].append((None, wl, None))

    def emit(self):
        nc = self.nc
        with nc.Block() as block:
            def run(engname):
                def body(e):
                    for fn, wl, incspec in self.ops[engname]:
                        for k, v in wl:
                            e.wait_ge(self._handle(k), v)
                        if fn is None:
                            continue
                        op, args, kw = fn
                        ins = getattr(e, op)(*args, **kw)
                        if self.debug_names is not None:
                            _NC_CACHE.setdefault("dbg_waits", {})[ins.ins.name] = list(wl)
                            try:
                                self.debug_names[ins.ins.name] = (engname, op, str(kw.get("func", "")), [str(a)[:80] for a in args] + [k + "=" + str(v)[:90] for k, v in kw.items() if k in ("out", "in_", "in0", "lhsT")])
                            except Exception:
                                pass
                        if incspec is not None:
                            ins.then_inc(self._handle(incspec[0]), incspec[1])
                return body
            block.tensor(run("pe"))
            block.scalar(run("act"))
            block.vector(run("dve"))
            block.gpsimd(run("pool"))
            block.sync(run("sp"))


class Ring:
    def __init__(self, items):
        self.items = items
        self.i = 0

    def next(self):
        it = self.items[self.i % len(self.items)]
        self.i += 1
        return it


class Pipe:
    def __init__(self):
        self.pending = None

    def item(self, main, post=None):
        main()
        if self.pending is not None:
            self.pending()
        self.pending = post

    def flush(self):
        if self.pending is not None:
            self.pending()
            self.pending = None


STS = [
    dict(ncols=1170, tiles=[(0, 386), (386, 384), (770, 400)], tiles1=[(130, 256), (386, 384), (770, 400)], npr=[386, 384, 384], samp=[False, False, True],
         xrow0=0, tabcol0=0, kcol0=2, kb0=0, ownc0=130, qb0=0, nqb=8, yrow0=0),
    dict(ncols=1024, tiles=[(0, 384), (384, 384), (768, 256)], tiles1=[(0, 384), (384, 384), (768, 256)], npr=[384, 384, 256], samp=[False, False, False],
         xrow0=1154, tabcol0=1170, kcol0=0, kb0=9, ownc0=0, qb0=8, nqb=8, yrow0=1024),
]
NCOLMAX = 1170
FFN_GROUPS = [(0, 4), (4, 4), (8, 4), (12, 4), (16, 3), (19, 3)]
G_MIX0, G_FFN0, G_PLE0, G_KV, G_MIX1, G_FFN1, G_PLE1 = range(7)


def build_program():
    nc = bass.Bass("TRN2", target_bir_lowering=False)

    def din(name, shape):
        return nc.dram_tensor(name, list(shape), F32, kind="ExternalInput")

    def dout(name, shape):
        return nc.dram_tensor(name, list(shape), F32, kind="ExternalOutput")

    xin = din("xin", [XR, D]); xsm = din("xsm", [NS, D]); stc = din("stc", [2 * NS, D])
    ck = din("ck", [NS, 128, 256]); cv = din("cv", [NS, 128, 256])
    pin = din("pin", [2, XR, PLE]); psm = din("psm", [2, NS, PLE])
    w_in = din("w_in", [D, 3 * D]); w_out = din("w_out", [D, D])
    wk = din("wk", [D, 256]); wv = din("wv", [D, 256]); wq = din("wq", [D, D]); wo = din("wo", [D, D])
    wg = din("wg", [2, D, FH]); wu = din("wu", [2, D, FH]); wd = din("wd", [2, FH, D])
    pproj = din("pproj", [2, PLE, D]); pgate = din("pgate", [2, D, D])
    gvec_d = din("gvec", [128, 56]); gfin_d = din("gfin", [128, D]); convw_d = din("convw", [128, 24])
    sink_d = din("sinkb", [128, 16]); idn_d = din("idn", [128, 128]); rm_d = din("rm", [128, 128])
    masks_d = din("masks", [3, 128, 128]); tab_d = din("tab", [2, 128, 2194]); esel_d = din("esel", [128, 256])

    y_o = dout("y", [OWN, D]); ys_o = dout("ys", [NS, D]); csp_o = dout("csp", [2, D]); css_o = dout("css", [NS, 2, D])
    kwp_o = dout("kwp", [128, 256]); vwp_o = dout("vwp", [128, 256])
    kws_o = dout("kws", [NS, 128, 256]); vws_o = dout("vws", [NS, 128, 256])

    def dap(t, off, dims):
        return bass.AP(t, off, [list(d) for d in dims])

    with ExitStack() as es:
        def T(name, shape, dt):
            return es.enter_context(nc.sbuf_tensor("sb_" + name, list(shape), dt))

        hT = T("hT", [128, KC, NCOLMAX], F32)
        A = T("A", [128, KC, NCOLMAX], BF16)
        B = T("B", [128, KC, NCOLMAX], BF16)
        KT = T("KT", [128, 2, 17 * 128], BF16)
        Vt = T("Vt", [128, 17, 4, 66], BF16)
        wbuf = [T("wbuf0", [128, 12288], BF16), T("wbuf1", [128, 12288], BF16)]
        pT = T("pT", [128, 2, NCOLMAX], BF16)
        tabs = T("tabs", [128, 2, 512], F32)
        xs = [T("xs0", [128, D], F32), T("xs1", [128, D], F32)]
        fr_t = [T("fr%d" % i, [128, 516], F32) for i in range(6)]
        hid_t = [T("hid%d" % i, [128, 4, 512], BF16) for i in range(2)]
        br_t = [T("br%d" % i, [128, 512], BF16) for i in range(4)]
        PT_t = [T("PT%d" % i, [128, 2, 512], BF16) for i in range(2)]
        o_sb = T("o_sb", [128, 1024], BF16)
        identF = T("identF", [128, 128], F32); identB = T("identB", [128, 128], BF16)
        onesM = T("onesM", [128, 128], BF16); RmB = T("RmB", [128, 128], BF16)
        masks = T("masks", [128, 3, 128], BF16)
        gvec = T("gvec", [128, 56], F32); gfin = T("gfin", [128, D], F32); convw = T("convw", [128, 24], F32)
        esink = T("esink", [128, 16], F32); epsT = T("epsT", [128, 1], F32)
        uhist = T("uhist", [128, KC, 2], F32); usamp = T("usamp", [128, KC, NS], F32); stT = T("stT", [128, KC, 2 * NS], F32)
        qs32 = T("qs32", [128, KC, NS], F32)
        small = T("small", [128, 64], F32)
        small2 = T("small2", [128, 32], F32)
        sm_pitch = [64, 32]
        Ks_t = [T("Ks%d" % i, [128, 256], F32) for i in range(2)]
        Vs_t = [T("Vs%d" % i, [128, 256], F32) for i in range(2)]
        Pz_t = [T("Pz%d" % i, [128, 16, 16], BF16) for i in range(2)]
        Vsb_t = [T("Vsb%d" % i, [128, 256], BF16) for i in range(2)]
        Esel = T("Esel", [128, 16, 16], BF16)
        prod = T("prod", [128, 1024], F32)
        kvout = T("kvout", [128, 512], F32)
        q_tm = xs[0][0:NS, :]
        knew = kvout[0:NS, 0:256]
        vnew = kvout[0:NS, 256:512]
        on_bf = o_sb[0:NS, :]
        ps = es.enter_context(nc.psum_tensor("ps", [128, 8, 512], F32))
        print("sbuf bytes remaining:", nc.sbuf_bytes_remaining)

        p = Prog(nc, es)
        import os as _os
        if _os.environ.get("MK_DEBUG"):
            p.debug_names = {}
            _NC_CACHE["dbg"] = p.debug_names

        def I(eng, opname, /, *args, reads=(), writes=(), inc=True, dma=False, is_out=False, **kw):
            return p.add(eng, (opname, args, kw), reads=reads, writes=writes, inc=inc, dma=dma, is_out=is_out)

        fr = Ring([("fr%d" % i, t) for i, t in enumerate(fr_t)])
        hidr = Ring([("hid%d" % i, t) for i, t in enumerate(hid_t)])
        brr = Ring([("br%d" % i, t) for i, t in enumerate(br_t)])
        PTr = Ring([("PT%d" % i, t) for i, t in enumerate(PT_t)])
        xsr = Ring([(["xs0"], xs[0]), (["prod", "prodB"], prod), (["xs1"], xs[1]), (["tabs"], tabs[:, :, :].rearrange("p a b -> p (a b)"))])
        class Sw:
            def __init__(self, ring):
                self.base = ring
                self.cur = ring

            def next(self):
                return self.cur.next()

        mmr = Sw(Ring([0, 1, 2, 3, 4]))
        auxr = Sw(Ring([5, 6, 7]))
        ringsA = (Ring([0, 1, 2]), Ring([5, 6]))
        ringsB = (Ring([3, 4]), Ring([7]))

        def psn(b):
            return "ps%d" % b

        cur = dict(st=0, l1=False)

        def trange(st, t):
            return st["tiles1"][t] if cur["l1"] else st["tiles"][t]
        touched = set()
        OVER = {0: [0], 1: [0, 1], 2: [1, 2]}

        def tname(buf, t):
            return "%s@%d_%d" % (buf, cur["st"], t)

        def W(buf, t):
            names = [tname(buf, t)]
            if cur["st"] == 1 and (buf, t) not in touched:
                touched.add((buf, t))
                names += ["%s@0_%d" % (buf, o) for o in OVER[t]]
            return names

        def ld(eng, out, in_, wr):
            I(eng, "dma_start", out=out, in_=in_, writes=wr, dma=True)

        ld("sp", identF[:], idn_d.ap(), ["identF"])
        ld("pool", RmB[:], rm_d.ap(), ["RmB"])
        ld("sp", gvec[:], gvec_d.ap(), ["gvec"])
        ld("sp", gfin[:], gfin_d.ap(), ["gfin"])
        ld("sp", convw[:], convw_d.ap(), ["convw"])
        ld("sp", esink[:], sink_d.ap(), ["esink"])
        ld("pool", Esel[:].rearrange("p a b -> p (a b)"), esel_d.ap(), ["Esel"])
        ld("pool", identB[:], idn_d.ap(), ["identB"])
        ld("pool", masks[:], dap(masks_d, 0, [[128, 128], [128 * 128, 3], [1, 128]]), ["masks"])
        I("pool", "memset", onesM[:], 1.0 / 1024.0, writes=["onesM"])
        I("pool", "memset", epsT[:], RMS_EPS, writes=["epsT"])
        I("pool", "memset", uhist[:], 0.0, writes=["uhist%d" % j for j in range(KC)])
        I("pool", "memset", Vt[:], 1.0, writes=["Vt%d" % i for i in range(17)])
        for i in range(2):
            I("pool", "memset", Pz_t[i][:], 0.0, writes=["Pz%d" % i])
        I("act", "activation", out=esink[:], in_=esink[:], func=AF.Exp, reads=["esink"], writes=["esink"])
        def passthrough():
            I("sp", "dma_start", out=dap(css_o, 0, [[2 * D, NS], [1, D]]), in_=dap(stc, D, [[2 * D, NS], [1, D]]), dma=True, is_out=True)
            I("sp", "dma_start", out=dap(kws_o, 0, [[128 * 256, NS], [1, 127 * 256]]), in_=dap(ck, 256, [[128 * 256, NS], [1, 127 * 256]]), dma=True, is_out=True)
            I("sp", "dma_start", out=dap(vws_o, 0, [[128 * 256, NS], [1, 127 * 256]]), in_=dap(cv, 256, [[128 * 256, NS], [1, 127 * 256]]), dma=True, is_out=True)

        def wload(dst_ap, src_t, off, ld_, ncols, nk, wname):
            src = dap(src_t, off, [[ld_, 128], [128 * ld_, nk], [1, ncols]])
            I("pool", "dma_start", out=dst_ap, in_=src, writes=[wname], dma=True)

        def v3(buf, lo, hi, a):
            return buf[:, lo:hi].rearrange("p (a b) -> p a b", a=a)

        def norm(st, t, outs):
            c0, n = trange(st, t)
            b = auxr.next()
            for kc in range(KC):
                nm, sq = brr.next()
                I("act", "activation", out=sq[:, 0:n], in_=hT[:, kc, c0:c0 + n], func=AF.Square, reads=[tname("hT", t)], writes=[nm])
                I("pe", "matmul", ps[:, b, 0:n], lhsT=onesM[:], rhs=sq[:, 0:n], start=(kc == 0), stop=(kc == KC - 1),
                  reads=[nm, "onesM"], writes=[psn(b)])
            n1, lnv = fr.next()
            I("act", "activation", out=lnv[:, 0:n], in_=ps[:, b, 0:n], func=AF.Ln, bias=epsT[:, 0:1], scale=1.0, reads=[psn(b), "epsT"], writes=[n1])
            n2, rstd = fr.next()
            I("act", "activation", out=rstd[:, 0:n], in_=lnv[:, 0:n], func=AF.Exp, scale=-0.5, reads=[n1], writes=[n2])
            for gi, dst, dname in outs:
                for kc in range(KC):
                    I("dve", "scalar_tensor_tensor", out=dst[:, kc, c0:c0 + n], in0=hT[:, kc, c0:c0 + n],
                      scalar=gvec[:, gi * 8 + kc:gi * 8 + kc + 1], in1=rstd[:, 0:n], op0=ALU.mult, op1=ALU.mult,
                      reads=[tname("hT", t), n2, "gvec"], writes=W(dname, t))

        def mm_acc(out_ap, b, lhs_fn, rhs_fn, nk, reads):
            for k in range(nk):
                I("pe", "matmul", out_ap, lhsT=lhs_fn(k), rhs=rhs_fn(k), start=(k == 0), stop=(k == nk - 1),
                  reads=reads, writes=[psn(b)], inc=(k == nk - 1))

        def resid_add(t, jo, b, c0, n):
            I("dve", "tensor_tensor", out=hT[:, jo, c0:c0 + n], in0=ps[:, b, 0:n], in1=hT[:, jo, c0:c0 + n], op=ALU.add,
              reads=[psn(b), tname("hT", t)], writes=[tname("hT", t)])

        def load_blocks(st, t):
            c0, n = st["tiles"][t]
            npr = st["npr"][t]
            blks = []
            r = 0
            while r < npr:
                nb = min(128, npr - r)
                blks.append(("p", st["xrow0"] + c0 + r, nb, c0 + r))
                r += nb
            if st["samp"][t]:
                blks.append(("s", 0, NS, c0 + npr))
            return blks

        def stage0_main(st, t):
            for kind, r0, nb, col in load_blocks(st, t):
                xn, xt = xsr.next()
                src = dap(xin, r0 * D, [[D, nb], [1, D]]) if kind == "p" else dap(xsm, 0, [[D, nb], [1, D]])
                I("act" if cur["st"] == 1 else "sp", "dma_start", out=xt[0:nb, :], in_=src, writes=xn, dma=True)
                for half in range(2):
                    b = mmr.next()
                    for j in range(4):
                        kc = half * 4 + j
                        I("pe", "transpose", ps[:, b, j * 128:j * 128 + nb], xt[0:nb, kc * 128:(kc + 1) * 128], identF[0:nb, 0:nb],
                          reads=xn + ["identF"], writes=[psn(b)], inc=(j == 3))
                    I("act", "activation", out=hT[:, half * 4:half * 4 + 4, col:col + nb],
                      in_=ps[:, b, :].rearrange("p (a b) -> p a b", a=4)[:, :, 0:nb], func=AF.Copy,
                      reads=[psn(b)], writes=W("hT", t))

        def load_state():
            xn, xt = xsr.next()
            I("sp", "dma_start", out=xt[0:32, :], in_=stc.ap(), writes=xn, dma=True)
            b = mmr.next()
            for kc in range(KC):
                I("pe", "transpose", ps[:, b, kc * 32:(kc + 1) * 32], xt[0:32, kc * 128:(kc + 1) * 128], identF[0:32, 0:32],
                  reads=xn + ["identF"], writes=[psn(b)], inc=(kc == KC - 1))
            I("act", "activation", out=stT[:], in_=ps[:, b, 0:256].rearrange("p (a b) -> p a b", a=KC), func=AF.Copy,
              reads=[psn(b)], writes=["stT"])

        def load_p(st, t, layer):
            for kind, r0, nb, col in load_blocks(st, t):
                xn, xt = xsr.next()
                src = dap(pin, (layer * XR + r0) * PLE, [[PLE, nb], [1, PLE]]) if kind == "p" else dap(psm, layer * NS * PLE, [[PLE, nb], [1, PLE]])
                I("act" if (cur["st"] == 1 and layer == 0) else "sp", "dma_start", out=xt[0:nb, 0:PLE], in_=src, writes=xn, dma=True)
                b = auxr.next()
                for c in range(2):
                    I("pe", "transpose", ps[:, b, c * 128:c * 128 + nb], xt[0:nb, c * 128:(c + 1) * 128], identF[0:nb, 0:nb],
                      reads=xn + ["identF"], writes=[psn(b)], inc=(c == 1))
                I("act", "activation", out=pT[:, 0:2, col:col + nb], in_=ps[:, b, 0:256].rearrange("p (a b) -> p a b", a=2)[:, :, 0:nb],
                  func=AF.Copy, reads=[psn(b)], writes=W("pT", t))

        def s1_main(st, t, grp, wv_, wname):
            c0, n = st["tiles"][t]
            npr = st["npr"][t]
            has_s = st["samp"][t]
            w3 = v3(wv_, 0, 12288, KC)
            for jj in range(4):
                j = grp * 4 + jj
                bc, bx, bb = mmr.next(), mmr.next(), mmr.next()
                for sel, b in ((1, bc), (2, bx), (0, bb)):
                    mm_acc(ps[:, b, 0:n], b, lambda k: w3[:, k, sel * 512 + jj * 128: sel * 512 + (jj + 1) * 128],
                           lambda k: A[:, k, c0:c0 + n], KC, [wname, tname("A", t)])
                ncs, c_sb = fr.next()
                I("act", "activation", out=c_sb[:, 0:n], in_=ps[:, bc, 0:n], func=AF.Copy, reads=[psn(bc)], writes=[ncs])
                nub, ub = fr.next()
                I("act", "activation", out=ub[:, 0:2], in_=uhist[:, j, :], func=AF.Copy, reads=["uhist%d" % j], writes=[nub])
                I("dve", "tensor_tensor", out=ub[:, 2:2 + n], in0=c_sb[:, 0:n], in1=ps[:, bx, 0:n], op=ALU.mult, reads=[ncs, psn(bx), nub], writes=[nub])
                ntm, tmp = fr.next()
                nac, acc = fr.next()
                w0 = convw[:, j * 3 + 0:j * 3 + 1]; w1 = convw[:, j * 3 + 1:j * 3 + 2]; w2 = convw[:, j * 3 + 2:j * 3 + 3]
                I("act", "activation", out=tmp[:, 0:npr], in_=ub[:, 0:npr], func=AF.Copy, scale=w0, reads=[nub, "convw"], writes=[ntm])
                I("dve", "scalar_tensor_tensor", out=acc[:, 0:npr], in0=ub[:, 1:1 + npr], scalar=w1, in1=tmp[:, 0:npr], op0=ALU.mult, op1=ALU.add,
                  reads=[nub, ntm, "convw"], writes=[nac])
                I("dve", "scalar_tensor_tensor", out=acc[:, 0:npr], in0=ub[:, 2:2 + npr], scalar=w2, in1=acc[:, 0:npr], op0=ALU.mult, op1=ALU.add,
                  reads=[nub, nac, "convw"], writes=[nac])
                I("act", "activation", out=uhist[:, j, :], in_=ub[:, npr:npr + 2], func=AF.Copy, reads=[nub], writes=["uhist%d" % j])
                if has_s:
                    us = ub[:, 2 + npr:2 + npr + NS]
                    st0 = stT[:, j, 0:2 * NS:2]
                    st1 = stT[:, j, 1:2 * NS:2]
                    I("dve", "tensor_scalar", out=tmp[:, npr:npr + NS], in0=st0, scalar1=w0, scalar2=None, op0=ALU.mult, reads=["stT", "convw", ntm], writes=[ntm])
                    I("dve", "scalar_tensor_tensor", out=tmp[:, npr:npr + NS], in0=st1, scalar=w1, in1=tmp[:, npr:npr + NS], op0=ALU.mult, op1=ALU.add,
                      reads=["stT", ntm, "convw"], writes=[ntm])
                    I("dve", "scalar_tensor_tensor", out=acc[:, npr:npr + NS], in0=us, scalar=w2, in1=tmp[:, npr:npr + NS], op0=ALU.mult, op1=ALU.add,
                      reads=[nub, ntm, "convw", nac], writes=[nac])
                    I("act", "activation", out=usamp[:, j, :], in_=us, func=AF.Copy, reads=[nub], writes=["usamp"])
                I("dve", "tensor_tensor", out=B[:, j, c0:c0 + n], in0=ps[:, bb, 0:n], in1=acc[:, 0:n], op=ALU.mult,
                  reads=[psn(bb), nac], writes=W("B", t))

        def proj_main(st, t, wv_, wname, src, sname):
            c0, n = trange(st, t)
            w3 = v3(wv_, 0, 8192, KC)
            for jo in range(KC):
                b = mmr.next()
                mm_acc(ps[:, b, 0:n], b, lambda k: w3[:, k, jo * 128:(jo + 1) * 128], lambda k: src[:, k, c0:c0 + n], KC, [wname, tname(sname, t)])
                resid_add(t, jo, b, c0, n)

        def ffn_main(st, t, nm_, wv_, wname):
            c0, n = trange(st, t)
            Wg = v3(wv_, 0, 4096, KC); Wu = v3(wv_, 4096, 8192, KC); Wd = v3(wv_, 8192, 12288, 4)
            hn, hid = hidr.next()
            for mi in range(nm_):
                bg, bu = mmr.next(), mmr.next()
                mm_acc(ps[:, bg, 0:n], bg, lambda k: Wg[:, k, mi * 128:(mi + 1) * 128], lambda k: A[:, k, c0:c0 + n], KC, [wname, tname("A", t)])
                mm_acc(ps[:, bu, 0:n], bu, lambda k: Wu[:, k, mi * 128:(mi + 1) * 128], lambda k: A[:, k, c0:c0 + n], KC, [wname, tname("A", t)])
                nsg, sg = fr.next()
                I("act", "activation", out=sg[:, 0:n], in_=ps[:, bg, 0:n], func=AF.Silu, reads=[psn(bg)], writes=[nsg])
                I("dve", "tensor_tensor", out=hid[:, mi, 0:n], in0=sg[:, 0:n], in1=ps[:, bu, 0:n], op=ALU.mult, reads=[nsg, psn(bu), hn], writes=[hn])
            for jo in range(KC):
                b = mmr.next()
                mm_acc(ps[:, b, 0:n], b, lambda k: Wd[:, k, jo * 128:(jo + 1) * 128], lambda k: hid[:, k, 0:n], nm_, [wname, hn])
                resid_add(t, jo, b, c0, n)

        def ple_main(st, t, wv_, wname):
            c0, n = trange(st, t)
            Wgt = v3(wv_, 0, 8192, KC); Wpr = v3(wv_, 8192, 10240, 2)
            for jo in range(KC):
                bg, bp = mmr.next(), mmr.next()
                mm_acc(ps[:, bg, 0:n], bg, lambda k: Wgt[:, k, jo * 128:(jo + 1) * 128], lambda k: A[:, k, c0:c0 + n], KC, [wname, tname("A", t)])
                mm_acc(ps[:, bp, 0:n], bp, lambda k: Wpr[:, k, jo * 128:(jo + 1) * 128], lambda k: pT[:, k, c0:c0 + n], 2, [wname, tname("pT", t)])
                nsg, sg = fr.next()
                I("act", "activation", out=sg[:, 0:n], in_=ps[:, bg, 0:n], func=AF.Sigmoid, reads=[psn(bg)], writes=[nsg])
                I("dve", "tensor_tensor", out=sg[:, 0:n], in0=sg[:, 0:n], in1=ps[:, bp, 0:n], op=ALU.mult, reads=[nsg, psn(bp)], writes=[nsg])
                I("dve", "tensor_tensor", out=hT[:, jo, c0:c0 + n], in0=sg[:, 0:n], in1=hT[:, jo, c0:c0 + n], op=ALU.add,
                  reads=[nsg, tname("hT", t)], writes=[tname("hT", t)])

        def rope_a(pb, n):
            nq, qf = brr.next()
            I("act", "activation", out=qf[:, 0:n], in_=ps[:, pb, 0:n], func=AF.Copy, reads=[psn(pb)], writes=[nq])
            return (nq, pb), qf

        def rope_b(nqpb, qf, n, toff=0):
            nq, pb = nqpb
            rb = auxr.next()
            I("pe", "matmul", ps[:, rb, 0:n], lhsT=RmB[:], rhs=qf[:, 0:n], start=True, stop=True, reads=[nq, "RmB"], writes=[psn(rb)])
            n1, t1 = fr.next()
            I("dve", "tensor_tensor", out=t1[:, 0:n], in0=ps[:, pb, 0:n], in1=tabs[:, 0, toff:toff + n], op=ALU.mult, reads=[psn(pb), "tabs", nq], writes=[n1])
            n2, t2 = fr.next()
            I("dve", "tensor_tensor", out=t2[:, 0:n], in0=ps[:, rb, 0:n], in1=tabs[:, 1, toff:toff + n], op=ALU.mult, reads=[psn(rb), "tabs"], writes=[n2])
            return n1, t1, n2, t2

        def kvq_main(st, t, wv_, wname):
            c0, n = st["tiles"][t]
            npr = st["npr"][t]
            has_s = st["samp"][t]
            Wk = v3(wv_, 0, 2048, KC); Wv = v3(wv_, 2048, 4096, KC); Wq = v3(wv_, 4096, 12288, KC)
            I("sp", "dma_start", out=tabs[:, :, 0:n], in_=dap(tab_d, st["tabcol0"] + c0, [[2194, 128], [128 * 2194, 2], [1, n]]), writes=["tabs"], dma=True)
            ka = max(0, st["kcol0"] - c0)
            nk = npr - ka
            kp0 = st["kb0"] * 128 + (c0 + ka - st["kcol0"])
            kbs = [(kp0 // 128 + i, ka + 128 * i) for i in range(nk // 128)]
            last_rel = None
            for kb, rel in kbs:
                if kb == 16:
                    last_rel = rel
            pendq = []

            def flush(keep=0):
                while len(pendq) > keep:
                    pendq.pop(0)()

            def k_fin(ch, nq, qf):
                n1, t1, n2, t2 = rope_b(nq, qf, n)
                I("dve", "tensor_tensor", out=t1[:, 0:n], in0=t1[:, 0:n], in1=t2[:, 0:n], op=ALU.add, reads=[n1, n2], writes=[n1])
                I("act", "activation", out=KT[:, ch, kp0:kp0 + nk], in_=t1[:, ka:ka + nk], func=AF.Copy, reads=[n1], writes=["KT%d" % kb for kb, _ in kbs])
                if last_rel is not None:
                    ob = auxr.next()
                    I("pe", "transpose", ps[:, ob, 0:128], t1[:, last_rel:last_rel + 128], identF[:], reads=[n1, "identF"], writes=[psn(ob)])
                    I("act", "activation", out=kvout[:, ch * 128:(ch + 1) * 128], in_=ps[:, ob, 0:128], func=AF.Copy, reads=[psn(ob)], writes=["kvoutK"])
                    if ch == 1:
                        I("sp", "dma_start", out=kwp_o.ap(), in_=kvout[:, 0:256], reads=["kvoutK"], dma=True, is_out=True)
                if has_s:
                    ob = auxr.next()
                    I("pe", "transpose", ps[0:NS, ob, 0:128], t1[:, npr:npr + NS], identF[:], reads=[n1, "identF"], writes=[psn(ob)])
                    I("act", "activation", out=knew[:, ch * 128:(ch + 1) * 128], in_=ps[0:NS, ob, 0:128], func=AF.Copy, reads=[psn(ob)], writes=["kvoutK"])
                    if ch == 1:
                        I("sp", "dma_start", out=dap(kws_o, 127 * 256, [[128 * 256, NS], [1, 256]]), in_=knew[:, :], reads=["kvoutK"], dma=True, is_out=True)

            q0, nqc = st["tiles1"][t]
            qoff = q0 - c0

            def q_fin(cq, nq, qf):
                n1, t1, n2, t2 = rope_b(nq, qf, nqc, qoff)
                I("dve", "tensor_tensor", out=A[:, cq, q0:q0 + nqc], in0=t1[:, 0:nqc], in1=t2[:, 0:nqc], op=ALU.add, reads=[n1, n2], writes=W("A", t))
                if has_s:
                    I("dve", "tensor_tensor", out=qs32[:, cq, :], in0=t1[:, npr - qoff:npr - qoff + NS], in1=t2[:, npr - qoff:npr - qoff + NS], op=ALU.add, reads=[n1, n2], writes=["qs32"])

            for ch in range(2):
                pb = mmr.next()
                mm_acc(ps[:, pb, 0:n], pb, lambda k: Wk[:, k, ch * 128:(ch + 1) * 128], lambda k: A[:, k, c0:c0 + n], KC, [wname, tname("A", t)])
                nq, qf = rope_a(pb, n)
                flush(0)
                pendq.append(lambda ch=ch, nq=nq, qf=qf: k_fin(ch, nq, qf))
            for vi, (kb, rel) in enumerate(kbs):
                pb = mmr.next()
                mm_acc(ps[:, pb, 0:256], pb, lambda k: A[:, k, c0 + rel:c0 + rel + 128], lambda k: Wv[:, k, :], KC, [wname, tname("A", t)])
                if vi == 0:
                    flush()
                I("act", "activation", out=Vt[:, kb, :, 0:64], in_=ps[:, pb, 0:256].rearrange("p (a b) -> p a b", a=4), func=AF.Copy,
                  reads=[psn(pb)], writes=["Vt%d" % kb])
                if kb == 16:
                    I("act", "activation", out=kvout[:, 256:512], in_=ps[:, pb, 0:256], func=AF.Copy, reads=[psn(pb)], writes=["kvoutV"])
                    I("sp", "dma_start", out=vwp_o.ap(), in_=kvout[:, 256:512], reads=["kvoutV"], dma=True, is_out=True)
            if has_s:
                pb = mmr.next()
                mm_acc(ps[0:NS, pb, 0:256], pb, lambda k: A[:, k, c0 + npr:c0 + npr + NS], lambda k: Wv[:, k, :], KC, [wname, tname("A", t)])
                I("act", "activation", out=vnew[:, :], in_=ps[0:NS, pb, 0:256], func=AF.Copy, reads=[psn(pb)], writes=["kvoutV"])
                I("sp", "dma_start", out=dap(vws_o, 127 * 256, [[128 * 256, NS], [1, 256]]), in_=vnew[:, :], reads=["kvoutV"], dma=True, is_out=True)
            for cq in range(KC):
                pb = mmr.next()
                mm_acc(ps[:, pb, 0:nqc], pb, lambda k: Wq[:, k, cq * 128:(cq + 1) * 128], lambda k: B[:, k, q0:q0 + nqc], KC, [wname, tname("B", t)])
                nq, qf = rope_a(pb, nqc)
                flush(0)
                pendq.append(lambda cq=cq, nq=nq, qf=qf: q_fin(cq, nq, qf))
            flush()

        def tile_of(st, col):
            for i, (c0, n) in enumerate(st["tiles"]):
                if c0 <= col < c0 + n:
                    return i
            raise ValueError(col)

        prod_bf = prod[:, :].bitcast(BF16)
        tabs_bf = tabs[:, :, :].rearrange("p a b -> p (a b)").bitcast(BF16)
        PT_slot = [
            Ring([("PT0", PT_t[0]), ("PT1", PT_t[1])]),
            Ring([("prod", prod_bf[:, 0:1024].rearrange("p (a b) -> p a b", a=2)), ("prodB", prod_bf[:, 1024:2048].rearrange("p (a b) -> p a b", a=2))]),
        ]
        osb_slot = [("o_sb", o_sb[:, :]), ("xs1", xs[1][:, :].bitcast(BF16)[:, 0:1024])]
        small_slot = [small, small2]

        def att_phases(st, qi, slot):
            qb = st["qb0"] + qi
            qc0 = st["ownc0"] + 128 * qi
            tq = tile_of(st, qc0)
            PTs = {}
            osn, osb = osb_slot[slot]
            sm = small_slot[slot]

            def S_phase(g):
                pair, hh = g // 2, g % 2
                PTn, PTt = PT_slot[slot].next()
                PTs[g] = (PTn, PTt)
                for kbi, kb in enumerate((qb, qb + 1)):
                    sb = mmr.next()
                    I("pe", "matmul", ps[:, sb, :], lhsT=KT[hh * 64:(hh + 1) * 64, pair, kb * 128:(kb + 1) * 128],
                      rhs=A[hh * 64:(hh + 1) * 64, pair * 4:pair * 4 + 4, qc0:qc0 + 128], start=True, stop=True,
                      reads=["KT%d" % kb, tname("A", tq)], writes=[psn(sb)])
                    I("act", "activation", out=PTt[:, kbi, :], in_=ps[:, sb, :], func=AF.Exp, scale=SCALE, reads=[psn(sb)], writes=[PTn])
                    mi = 0 if kbi == 1 else (2 if qb == 0 else 1)
                    mk = bass.AP(masks, mi * 128, [[384, 128], [0, 4], [1, 128]])
                    pv = PTt[:, kbi, :].rearrange("p (a b) -> p a b", a=4)
                    I("dve", "tensor_tensor", out=pv, in0=pv, in1=mk, op=ALU.mult, reads=[PTn, "masks"], writes=[PTn])

            def PV_phase(g):
                PTn, PTt = PTs[g]
                ob = auxr.next()
                for r in range(4):
                    for kbi, kb in enumerate((qb, qb + 1)):
                        I("pe", "matmul", ps[:, ob, r * 65:(r + 1) * 65], lhsT=PTt[:, kbi, r * 128:(r + 1) * 128], rhs=Vt[:, kb, g, 0:65],
                          start=(kbi == 0), stop=(kbi == 1), reads=[PTn, "Vt%d" % kb], writes=[psn(ob)], inc=(r == 3 and kbi == 1))
                sn = "small%d_%d" % (slot, g)
                o3 = ps[:, ob, 0:260].rearrange("p (a b) -> p a b", a=4)
                I("dve", "tensor_tensor", out=sm[:, g * 8:g * 8 + 4], in0=o3[:, :, 64], in1=esink[:, 4 * g:4 * g + 4], op=ALU.add,
                  reads=[psn(ob), "esink"], writes=[sn])
                I("dve", "reciprocal", out=sm[:, g * 8 + 4:g * 8 + 8], in_=sm[:, g * 8:g * 8 + 4], reads=[sn], writes=[sn])
                I("dve", "tensor_tensor", out=osb[:, g * 256:(g + 1) * 256].rearrange("p (a b) -> p a b", a=4), in0=o3[:, :, 0:64],
                  in1=bass.AP(sm, g * 8 + 4, [[sm_pitch[slot], 128], [1, 4], [0, 64]]), op=ALU.mult, reads=[psn(ob), sn], writes=[osn])

            def T_phase():
                tb = auxr.next()
                psb = ps[:, tb, :].bitcast(BF16)
                for c in range(KC):
                    I("pe", "transpose", psb[:, c * 128:(c + 1) * 128], osb[:, c * 128:(c + 1) * 128], identB[:], reads=[osn, "identB"], writes=[psn(tb)], inc=(c == KC - 1))
                I("act", "activation", out=B[:, 0:KC, qc0:qc0 + 128], in_=psb.rearrange("p (a b) -> p a b", a=KC), func=AF.Copy, reads=[psn(tb)], writes=W("B", tq))

            return [lambda: S_phase(0), lambda: S_phase(1), lambda: PV_phase(0), lambda: S_phase(2), lambda: PV_phase(1),
                    lambda: S_phase(3), lambda: PV_phase(2), lambda: PV_phase(3), T_phase]

        def att_group(st, qis):
            phs = [att_phases(st, qi, j) for j, qi in enumerate(qis)]
            for k in range(len(phs[0])):
                for ph in phs:
                    ph[k]()

        def att_main(st, qi):
            att_group(st, [qi])

        def samp_attn(st):
            sc0 = 1154
            tq = 2
            for cq in range(KC):
                I("pe", "transpose", ps[0:NS, cq // 4, (cq % 4) * 128:(cq % 4 + 1) * 128], qs32[:, cq, :], identF[:],
                  reads=["qs32", "identF"], writes=[psn(0), psn(1)], inc=(cq == KC - 1))
            for pair in range(2):
                I("act", "activation", out=q_tm[:, pair * 512:(pair + 1) * 512].rearrange("p (h c d) -> p h c d", h=2, c=4),
                  in_=ps[0:NS, pair, :].rearrange("p (c h d) -> p h c d", c=4, h=2), func=AF.Copy, reads=[psn(pair)], writes=["xs0"])
            I("act", "activation", out=o_sb[0:NS, :], in_=q_tm[:, :], func=AF.Copy, reads=["xs0"], writes=["o_sb"])
            for s in range(NS):
                kn, Kst = ("Ks%d" % (s % 2), Ks_t[s % 2])
                vn, Vst = ("Vs%d" % (s % 2), Vs_t[s % 2])
                pzn, Pzt = ("Pz%d" % (s % 2), Pz_t[s % 2])
                I("sp", "dma_start", out=Kst[:, :], in_=dap(ck, s * 128 * 256, [[256, 128], [1, 256]]), writes=[kn], dma=True)
                I("sp", "dma_start", out=Vst[:, :], in_=dap(cv, s * 128 * 256, [[256, 128], [1, 256]]), writes=[vn], dma=True)
                vbn, Vsb = ("Vsb%d" % (s % 2), Vsb_t[s % 2])
                I("act", "activation", out=Vsb[:, :], in_=Vst[:, :], func=AF.Copy, reads=[vn], writes=[vbn])
                b0 = 2 * (s % 2)
                sel = bass.AP(identB, s, [[128, NS], [0, 128]])
                for half in range(2):
                    I("pe", "matmul", ps[:, b0 + half, :], lhsT=sel, rhs=o_sb[0:NS, half * 512:(half + 1) * 512], start=True, stop=True,
                      reads=["o_sb", "identB"], writes=[psn(b0 + half)])
                    I("dve", "tensor_tensor", out=prod[:, half * 512:(half + 1) * 512].rearrange("p (g r d) -> p g r d", g=2, r=4),
                      in0=ps[:, b0 + half, :].rearrange("p (g r d) -> p g r d", g=2, r=4),
                      in1=bass.AP(Kst, half * 128, [[256, 128], [64, 2], [0, 4], [1, 64]]), op=ALU.mult,
                      reads=[psn(b0 + half), kn], writes=["prod", "prodB"])
                I("dve", "tensor_reduce", out=small[:, 32:48], in_=prod[:, :].rearrange("p (h d) -> p h d", d=64), axis=AX.X, op=ALU.add,
                  reads=["prod", "prodB"], writes=["smallS"])
                I("act", "activation", out=Pzt[:, :, s], in_=small[:, 32:48], func=AF.Exp, scale=SCALE, reads=["smallS"], writes=[pzn])
                for h in range(16):
                    I("pe", "matmul", ps[0:NS, 5 + h // 8, (h % 8) * 64:(h % 8 + 1) * 64], lhsT=Pzt[:, h, :], rhs=Vsb[:, (h // 4) * 64:(h // 4 + 1) * 64],
                      start=(s == 0 and h % 8 == 0), stop=(s == NS - 1 and h % 8 == 7), reads=[pzn, vbn], writes=[psn(5), psn(6)], inc=False)
                I("pe", "matmul", ps[0:NS, 7, 0:16], lhsT=Esel[:, s, :], rhs=Pzt[:, :, s], start=(s == 0), stop=(s == NS - 1),
                  reads=[pzn, "Esel"], writes=[psn(7)])
                I("dve", "memset", Pzt[:, :, s], 0.0, writes=[pzn])
            I("dve", "tensor_tensor", out=prod[0:NS, :].rearrange("p (g r d) -> p g r d", g=4, r=4), in0=q_tm[:, :].rearrange("p (g r d) -> p g r d", g=4, r=4),
              in1=bass.AP(kvout, 0, [[512, NS], [64, 4], [0, 4], [1, 64]]), op=ALU.mult, reads=["xs0", "kvoutK"], writes=["prod", "prodB"])
            I("dve", "tensor_reduce", out=small[0:NS, 48:64], in_=prod[0:NS, :].rearrange("p (h d) -> p h d", d=64), axis=AX.X, op=ALU.add,
              reads=["prod", "prodB"], writes=["smallN"])
            I("act", "activation", out=small[0:NS, 48:64], in_=small[0:NS, 48:64], func=AF.Exp, scale=SCALE, reads=["smallN"], writes=["smallN"])
            I("dve", "tensor_tensor", out=small[0:NS, 32:48], in0=ps[0:NS, 7, 0:16], in1=small[0:NS, 48:64], op=ALU.add, reads=[psn(7), "smallN"], writes=["smallS"])
            I("dve", "tensor_tensor", out=small[0:NS, 32:48], in0=small[0:NS, 32:48], in1=esink[0:NS, :], op=ALU.add, reads=["smallS", "esink"], writes=["smallS"])
            I("dve", "reciprocal", out=small[0:NS, 32:48], in_=small[0:NS, 32:48], reads=["smallS"], writes=["smallS"])
            I("dve", "tensor_tensor", out=prod[0:NS, :].rearrange("p (g r d) -> p g r d", g=4, r=4),
              in0=bass.AP(kvout, 256, [[512, NS], [64, 4], [0, 4], [1, 64]]), in1=bass.AP(small, 48, [[64, NS], [4, 4], [1, 4], [0, 64]]), op=ALU.mult,
              reads=["kvoutV", "smallN", "prod", "prodB"], writes=["prod", "prodB"])
            I("dve", "tensor_tensor", out=prod[0:NS, :].rearrange("p (a b) -> p a b", a=2), in0=prod[0:NS, :].rearrange("p (a b) -> p a b", a=2),
              in1=ps[0:NS, 5:7, :], op=ALU.add, reads=["prod", "prodB", psn(5), psn(6)], writes=["prod", "prodB"])
            I("dve", "tensor_tensor", out=on_bf[:, :].rearrange("p (h d) -> p h d", d=64), in0=prod[0:NS, :].rearrange("p (h d) -> p h d", d=64),
              in1=bass.AP(small, 32, [[64, NS], [1, 16], [0, 64]]), op=ALU.mult, reads=["prod", "prodB", "smallS"], writes=["o_sb"])
            psb = ps[:, 4, :].bitcast(BF16)
            for c in range(KC):
                I("pe", "transpose", psb[:, c * NS:(c + 1) * NS], on_bf[:, c * 128:(c + 1) * 128], identB[0:NS, 0:NS], reads=["o_sb", "identB"], writes=[psn(4)], inc=(c == KC - 1))
            I("act", "activation", out=B[:, 0:KC, sc0:sc0 + NS], in_=psb[:, 0:KC * NS].rearrange("p (a b) -> p a b", a=KC), func=AF.Copy, reads=[psn(4)], writes=W("B", tq))

        pairr = Ring([0, 2])
        ysr = Ring([(["prod", "prodB"], prod), (["tabs"], tabs[:, :, :].rearrange("p a b -> p (a b)"))])

        def final_block(st, t, col, nb, dst_ap):
            b0 = pairr.next()
            for kc in range(KC):
                I("pe", "transpose", ps[0:nb, b0 + kc // 4, (kc % 4) * 128:(kc % 4 + 1) * 128], hT[:, kc, col:col + nb], identF[:],
                  reads=[tname("hT", t), "identF"], writes=[psn(b0), psn(b0 + 1)], inc=(kc == KC - 1))
            nj, junk = fr.next()
            for half in range(2):
                I("act", "activation", out=junk[0:nb, 0:512], in_=ps[0:nb, b0 + half, :], func=AF.Square, accum_out=small[0:nb, 16 + half:17 + half],
                  reads=[psn(b0 + half)], writes=[nj, "smallF"])
            I("dve", "tensor_tensor", out=small[0:nb, 18:19], in0=small[0:nb, 16:17], in1=small[0:nb, 17:18], op=ALU.add, reads=["smallF"], writes=["smallF"])
            I("act", "activation", out=small[0:nb, 19:20], in_=small[0:nb, 18:19], func=AF.Ln, bias=epsT[0:nb, 0:1], scale=1.0 / 1024.0, reads=["smallF", "epsT"], writes=["smallF"])
            I("act", "activation", out=small[0:nb, 20:21], in_=small[0:nb, 19:20], func=AF.Exp, scale=-0.5, reads=["smallF"], writes=["smallF"])
            yn, yt = ysr.next()
            for half in range(2):
                I("dve", "scalar_tensor_tensor", out=yt[0:nb, half * 512:(half + 1) * 512], in0=ps[0:nb, b0 + half, :], scalar=small[0:nb, 20:21],
                  in1=gfin[0:nb, half * 512:(half + 1) * 512], op0=ALU.mult, op1=ALU.mult, reads=[psn(b0 + half), "smallF", "gfin"], writes=yn)
            I("sp", "dma_start", out=dst_ap, in_=yt[0:nb, :], reads=yn, dma=True, is_out=True)

        def final_post(st, t):
            c0, n = st["tiles"][t]
            npr = st["npr"][t]
            col = max(c0, st["ownc0"])
            while col < c0 + npr:
                row = st["yrow0"] + (col - st["ownc0"])
                final_block(st, t, col, 128, dap(y_o, row * D, [[D, 128], [1, D]]))
                col += 128
            if st["samp"][t]:
                final_block(st, t, c0 + npr, NS, ys_o.ap())

        def tm_out(src3, width, dst_ap, wait_names):
            b0 = pairr.next()
            for kc in range(KC):
                I("pe", "transpose", ps[0:width, b0 + kc // 4, (kc % 4) * 128:(kc % 4 + 1) * 128], src3[:, kc, :], identF[:],
                  reads=wait_names + ["identF"], writes=[psn(b0), psn(b0 + 1)], inc=(kc == KC - 1))
            yn, yt = ysr.next()
            I("act", "activation", out=yt[0:width, :].rearrange("p (a b) -> p a b", a=2), in_=ps[0:width, b0:b0 + 2, :], func=AF.Copy,
              reads=[psn(b0), psn(b0 + 1)], writes=yn)
            I("sp", "dma_start", out=dst_ap, in_=yt[0:width, :], reads=yn, dma=True, is_out=True)

        pipe = Pipe()

        def with_st(si, fn, l1=False):
            def g():
                old = (cur["st"], cur["l1"])
                cur["st"], cur["l1"] = si, l1
                fn()
                cur["st"], cur["l1"] = old
            return g

        early = {}

        def groups_for(si):
            st = STS[si]
            nt = len(st["tiles"])
            G = []

            def item(main, post=None, l1=False, post_l1=None):
                pl1 = l1 if post_l1 is None else post_l1
                pipe.item(with_st(si, main, l1), with_st(si, post, pl1) if post is not None else None)

            def load_s1(grp):
                def f(wv_, wname):
                    w3 = v3(wv_, 0, 12288, KC)
                    for sel in range(3):
                        wload(w3[:, :, sel * 512:(sel + 1) * 512], w_in, sel * 1024 + grp * 512, 3 * D, 512, KC, wname)
                return f

            def s0_main(t):
                stage0_main(st, t)
                load_p(st, t, 0)

            def s0_post(t):
                norm(st, t, [(G_MIX0, A, "A")])

            if si == 1:
                early["s0_main0"] = with_st(1, lambda: s0_main(0))
                early["s0_post0"] = with_st(1, lambda: s0_post(0))

            def run_s1a(wv_, wname):
                if si == 0:
                    with_st(si, load_state)()
                def s0(t):
                    item(lambda t=t: s0_main(t), lambda t=t: s0_post(t))

                def s1(t):
                    item(lambda t=t: s1_main(st, t, 0, wv_, wname))

                if si == 1 and early.get("done"):
                    pipe.item(lambda: (early["s0_post0"](), with_st(1, lambda: s0_main(1))()), with_st(1, lambda: s0_post(1)))
                else:
                    s0(0)
                    s0(1)
                s1(0)
                for t in range(2, nt):
                    s0(t)
                    s1(t - 1)
                if si == 0:
                    passthrough()
                s1(nt - 1)

            def run_s1b(wv_, wname):
                for t in range(nt):
                    item(lambda t=t: s1_main(st, t, 1, wv_, wname))
                if si == 0:
                    item(lambda: tm_out(usamp, NS, dap(css_o, D, [[2 * D, NS], [1, D]]), ["usamp"]))
                else:
                    item(lambda: tm_out(uhist, 2, csp_o.ap(), ["uhist%d" % j for j in range(KC)]))

            G.append((load_s1(0), run_s1a))
            G.append((load_s1(1), run_s1b))

            def load_proj(src_t):
                def f(wv_, wname):
                    wload(v3(wv_, 0, 8192, KC), src_t, 0, D, 1024, KC, wname)
                return f

            def run_s2(wv_, wname):
                for t in range(nt):
                    item(lambda t=t: proj_main(st, t, wv_, wname, B, "B"), lambda t=t: norm(st, t, [(G_FFN0, A, "A")]))

            G.append((load_proj(w_out), run_s2))

            def ffn_groups(layer, gnext):
                for gi, (m0, nm_) in enumerate(FFN_GROUPS):
                    def lf(wv_, wname, m0=m0, nm_=nm_):
                        wload(v3(wv_, 0, 4096, KC)[:, :, 0:nm_ * 128], wg, layer * D * FH + m0 * 128, FH, nm_ * 128, KC, wname)
                        wload(v3(wv_, 4096, 8192, KC)[:, :, 0:nm_ * 128], wu, layer * D * FH + m0 * 128, FH, nm_ * 128, KC, wname)
                        wload(v3(wv_, 8192, 12288, 4)[:, 0:nm_, :], wd, layer * FH * D + m0 * 128 * D, D, 1024, nm_, wname)

                    def rf(wv_, wname, nm_=nm_, last=(gi == len(FFN_GROUPS) - 1)):
                        for t in range(nt):
                            post = (lambda t=t: norm(st, t, [(gnext, A, "A")])) if last else None
                            item(lambda t=t: ffn_main(st, t, nm_, wv_, wname), post, l1=(layer == 1))
                    G.append((lf, rf))

            ffn_groups(0, G_PLE0)

            def load_ple(layer):
                def f(wv_, wname):
                    wload(v3(wv_, 0, 8192, KC), pgate, layer * D * D, D, 1024, KC, wname)
                    wload(v3(wv_, 8192, 10240, 2), pproj, layer * PLE * D, D, 1024, 2, wname)
                return f

            def run_ple0(wv_, wname):
                for t in range(nt):
                    item(lambda t=t: ple_main(st, t, wv_, wname), lambda t=t: (norm(st, t, [(G_KV, A, "A"), (G_MIX1, B, "B")]), load_p(st, t, 1)))

            G.append((load_ple(0), run_ple0))

            def load_kvq(wv_, wname):
                wload(v3(wv_, 0, 2048, KC), wk, 0, 256, 256, KC, wname)
                wload(v3(wv_, 2048, 4096, KC), wv, 0, 256, 256, KC, wname)
                Wq3 = v3(wv_, 4096, 12288, KC)
                for pair in range(2):
                    for hh in range(2):
                        for c_ in range(4):
                            col = (pair * 4 + c_) * 128 + hh * 64
                            dst = Wq3[:, :, col:col + 64]
                            src = dap(wq, (pair * 8 + hh * 4 + c_) * 64, [[D, 128], [128 * D, KC], [1, 64]])
                            I("pool", "dma_start", out=dst, in_=src, writes=[wname], dma=True)

            qb_of_tile = [[] for _ in range(nt)]
            for qi in range(st["nqb"]):
                qb_of_tile[tile_of(st, st["ownc0"] + 128 * qi)].append(qi)

            def cap(fn, rings):
                mmr.cur, auxr.cur = rings
                try:
                    return p.capture(with_st(si, fn, cur["l1"]))
                finally:
                    mmr.cur, auxr.cur = mmr.base, auxr.base

            def att_ops(t):
                def f():
                    qs = qb_of_tile[t]
                    for i in range(0, len(qs), 2):
                        att_group(st, qs[i:i + 2])
                return cap(f, ringsA)

            def run_kvq(wv_, wname):
                item(lambda: kvq_main(st, 0, wv_, wname))
                for t in range(1, nt):
                    def main(t=t):
                        a = att_ops(t - 1)
                        b = cap(lambda: kvq_main(st, t, wv_, wname), ringsB)
                        p.merge_replay(a, b)
                    pipe.item(main)

            G.append((load_kvq, run_kvq))

            def run_wo(wv_, wname):
                def main0():
                    a = att_ops(nt - 1)
                    b = cap(lambda: proj_main(st, 0, wv_, wname, B, "B"), ringsB)
                    p.merge_replay(a, b)
                pipe.item(with_st(si, main0, True), with_st(si, lambda: norm(st, 0, [(G_FFN1, A, "A")]), True))
                if si == 0:
                    item(lambda: samp_attn(st))
                for t in range(1, nt):
                    item(lambda t=t: proj_main(st, t, wv_, wname, B, "B"), lambda t=t: norm(st, t, [(G_FFN1, A, "A")]), l1=True)

            G.append((load_proj(wo), run_wo))
            ffn_groups(1, G_PLE1)

            def run_ple1(wv_, wname):
                hoist = (si == 0 and "s0_main0" in early)
                for t in range(nt - 1 if hoist else nt):
                    item(lambda t=t: ple_main(st, t, wv_, wname), lambda t=t: final_post(st, t), l1=True)
                if hoist:
                    tl = nt - 1
                    pipe.item(lambda: (early["s0_main0"](), with_st(0, lambda: ple_main(st, tl, wv_, wname), True)()),
                              with_st(0, lambda: final_post(st, tl), True))
                    early["done"] = True

            G.append((load_ple(1), run_ple1))
            return G

        allg = groups_for(0) + groups_for(1)

        def do_load(i):
            if i < len(allg):
                allg[i][0](wbuf[i % 2], "wbuf%d" % (i % 2))

        do_load(0)
        do_load(1)
        for i, (lf, rf) in enumerate(allg):
            rf(wbuf[i % 2], "wbuf%d" % (i % 2))
            do_load(i + 2)
        pipe.flush()
        p.finish()
        p.emit()
        print("ops per engine:", {k: len(v) for k, v in p.ops.items()})
    return nc


_NC_CACHE = {}


def _rope_tables(pos):
    half = 8
    inv_freq = np.power(np.float32(500000.0), -np.arange(half, dtype=np.float32) / np.float32(half)).astype(np.float32)
    ang = (pos.astype(np.float32)[:, None] * inv_freq[None, :]).astype(np.float32)
    cos = np.cos(ang).astype(np.float32).T
    sin = np.sin(ang).astype(np.float32).T
    n = pos.shape[0]
    C = np.ones((128, n), np.float32)
    S = np.zeros((128, n), np.float32)
    for hb in range(2):
        base = hb * 64
        C[base:base + 8] = cos
        C[base + 8:base + 16] = cos
        S[base:base + 8] = -sin
        S[base + 8:base + 16] = sin
    return C, S


def prepare(x_prompt, x_sample, state_conv, cache_k_win, cache_v_win, p_prompt, p_sample,
           norm_mix_g, norm_ffn_g, norm_ple_g, kv_norm_g, final_norm_g,
           conv_w_in, conv_w, conv_w_out, w_k, w_v, w_q, sinks, w_o,
           ffn_w_gate, ffn_w_up, ffn_w_down, ple_w_proj, ple_w_gate):
    f32 = np.float32
    A_ = lambda a: np.ascontiguousarray(np.asarray(a, dtype=f32))
    x_prompt = A_(x_prompt); x_sample = A_(x_sample); state_conv = A_(state_conv)
    cache_k_win = A_(cache_k_win); cache_v_win = A_(cache_v_win); p_prompt = A_(p_prompt); p_sample = A_(p_sample)

    def colvec(g):
        return np.asarray(g, f32).reshape(KC, 128).T

    gains = [norm_mix_g[0], norm_ffn_g[0], norm_ple_g[0], kv_norm_g, norm_mix_g[1], norm_ffn_g[1], norm_ple_g[1]]
    gvec = np.ascontiguousarray(np.concatenate([colvec(g) for g in gains], axis=1))
    gfin = np.ascontiguousarray(np.broadcast_to(np.asarray(final_norm_g, f32)[None, :], (128, D)))
    cw = np.asarray(conv_w, f32)[0]
    convw = np.ascontiguousarray(np.stack([colvec(cw[j]) for j in range(3)], axis=2).reshape(128, 24))
    sinkb = np.ascontiguousarray(np.broadcast_to(np.asarray(sinks, f32)[0][None, :], (128, 16)))
    idn = np.eye(128, dtype=f32)
    rm = np.zeros((128, 128), f32)
    for m in range(128):
        d = m % 64
        if d < 8:
            rm[m + 8, m] = 1.0
        elif d < 16:
            rm[m - 8, m] = 1.0
    jj = np.arange(128)[:, None]; ii = np.arange(128)[None, :]
    mcur = (jj <= ii).astype(f32); mprev = (jj >= ii).astype(f32)
    esel = np.zeros((128, 16, 16), f32)
    for s in range(16):
        esel[:, s, s] = 1.0
    esel = esel.reshape(128, 256)

    shared = dict(
        w_in=A_(conv_w_in)[0], w_out=A_(conv_w_out)[0], wk=A_(w_k), wv=A_(w_v), wq=A_(w_q)[0], wo=A_(w_o)[0],
        wg=A_(ffn_w_gate), wu=A_(ffn_w_up), wd=A_(ffn_w_down), pproj=A_(ple_w_proj), pgate=A_(ple_w_gate),
        gvec=gvec, gfin=gfin, convw=convw, sinkb=sinkb, idn=idn, rm=rm, esel=esel)

    in_maps = []
    for c in range(NCORES):
        b, half = c // 2, c % 2
        t0 = half * OWN
        xin = np.zeros((XR, D), f32)
        pin = np.zeros((2, XR, PLE), f32)
        if half == 1:
            xin[:] = x_prompt[b, t0 - HALO:t0 + OWN]
            pin[:] = p_prompt[:, b, t0 - HALO:t0 + OWN]
        else:
            xin[HALO:] = x_prompt[b, 0:OWN]
            pin[:, HALO:] = p_prompt[:, b, 0:OWN]
        s0 = c * NS
        pos1 = np.concatenate([np.maximum(t0 - HALO + np.arange(1154), 0), np.full(NS, 16384)]).astype(f32)
        pos2 = (t0 + 1024 + np.arange(1024)).astype(f32)
        C1, S1 = _rope_tables(pos1)
        C2, S2 = _rope_tables(pos2)
        tab = np.ascontiguousarray(np.stack([np.concatenate([C1, C2], 1), np.concatenate([S1, S2], 1)], 0))
        masks = np.ascontiguousarray(np.stack([mcur, mprev, mprev if half == 1 else np.zeros_like(mprev)], 0))
        m = dict(shared)
        m.update(xin=xin, xsm=np.ascontiguousarray(x_sample[s0:s0 + NS, 0]), stc=np.ascontiguousarray(state_conv[0, s0:s0 + NS].reshape(2 * NS, D)),
                 ck=np.ascontiguousarray(cache_k_win[s0:s0 + NS].reshape(NS, 128, 256)), cv=np.ascontiguousarray(cache_v_win[s0:s0 + NS].reshape(NS, 128, 256)),
                 pin=pin, psm=np.ascontiguousarray(p_sample[:, s0:s0 + NS, 0]), masks=masks, tab=tab)
        in_maps.append(m)

    return in_maps


def assemble(R):
    f32 = np.float32
    y_prompt = np.zeros((4, 4096, D), f32); y_sample = np.zeros((128, 1, D), f32)
    csp = np.zeros((1, 4, 2, D), f32); css = np.zeros((1, 128, 2, D), f32)
    kwp = np.zeros((4, 128, 4, 64), f32); vwp = np.zeros((4, 128, 4, 64), f32)
    kws = np.zeros((128, 128, 4, 64), f32); vws = np.zeros((128, 128, 4, 64), f32)
    for c in range(NCORES):
        b, half = c // 2, c % 2
        r = R[c]
        y_prompt[b, half * OWN:(half + 1) * OWN] = r["y"]
        s0 = c * NS
        y_sample[s0:s0 + NS, 0] = r["ys"]
        css[0, s0:s0 + NS] = r["css"]
        kws[s0:s0 + NS] = r["kws"].reshape(NS, 128, 4, 64)
        vws[s0:s0 + NS] = r["vws"].reshape(NS, 128, 4, 64)
        if half == 1:
            csp[0, b] = r["csp"]
            kwp[b] = r["kwp"].reshape(128, 4, 64)
            vwp[b] = r["vwp"].reshape(128, 4, 64)
    return (y_prompt, y_sample, csp, css, kwp, vwp, kws, vws)


def kernel(**inputs):
    in_maps = prepare(**inputs)
    if "nc" not in _NC_CACHE:
        _NC_CACHE["nc"] = build_program()
    nc = _NC_CACHE["nc"]
    res = run_bass_kernel_spmd(nc, in_maps, core_ids=list(range(NCORES)))
    return assemble(res.results)
```

```python
import numpy as np
from contextlib import ExitStack
import concourse.bass as bass
import concourse.mybir as mybir
from concourse.bass_utils import run_bass_kernel_spmd

F32 = mybir.dt.float32
BF16 = mybir.dt.bfloat16
ALU = mybir.AluOpType
AF = mybir.ActivationFunctionType
AX = mybir.AxisListType

NCORES = 8
D = 1024
KC = 8
FH = 2816
HC = 22
PLE = 256
HALO = 130
OWN = 2048
NS = 16
XR = HALO + OWN
RMS_EPS = 1e-6
SCALE = 0.125

ENGS = ("pe", "act", "dve", "pool", "sp")
MERGE = True


class Prog:
    def __init__(self, nc, es, n_dma_sp=24, n_dma_pool=8):
        self.nc = nc
        self.sem = {e: es.enter_context(nc.semaphore("s_" + e)) for e in ENGS[:4]}
        self.dsem = {}
        self.dpool = {"sp": [], "pool": []}
        for i in range(n_dma_sp):
            k = "dsp%d" % i
            self.dsem[k] = es.enter_context(nc.semaphore(k))
            self.dpool["sp"].append(k)
        for i in range(n_dma_pool):
            k = "dpl%d" % i
            self.dsem[k] = es.enter_context(nc.semaphore(k))
            self.dpool["pool"].append(k)
        self.dpool["act"] = []
        for i in range(6):
            k = "dac%d" % i
            self.dsem[k] = es.enter_context(nc.semaphore(k))
            self.dpool["act"].append(k)
        self.dnext = {"sp": 0, "pool": 0, "act": 0}
        self.dcum = {k: 0 for k in self.dsem}
        self.ops = {e: [] for e in ENGS}
        self.tick = {e: 0 for e in ENGS}
        self.seen = {e: {} for e in ENGS}
        self.res = {}
        self.out_events = []
        self.cap = None
        self.debug_names = None

    def capture(self, fn):
        assert self.cap is None
        self.cap = []
        fn()
        ops, self.cap = self.cap, None
        return ops

    def replay(self, ops):
        for eng, fn, reads, writes, inc, dma, is_out in ops:
            self.add(eng, fn, reads=reads, writes=writes, inc=inc, dma=dma, is_out=is_out)

    @staticmethod
    def segments(ops):
        segs, cur = [], []
        for op in ops:
            cur.append(op)
            if op[0] == "pe" and op[4]:
                segs.append(cur)
                cur = []
        if cur:
            if segs:
                segs[-1].extend(cur)
            else:
                segs.append(cur)
        return segs

    def merge_replay(self, opsA, opsB):
        if not MERGE:
            self.replay(opsA)
            self.replay(opsB)
            return
        sa, sb = self.segments(opsA), self.segments(opsB)
        na, nb = len(sa), len(sb)
        out = []
        j = 0
        for i, seg in enumerate(sa):
            out.extend(seg)
            tgt = ((i + 1) * nb) // na
            while j < tgt:
                out.extend(sb[j])
                j += 1
        while j < nb:
            out.extend(sb[j])
            j += 1
        self.replay(out)

    def _handle(self, k):
        return self.sem[k] if k in self.sem else self.dsem[k]

    def add(self, eng, fn, reads=(), writes=(), inc=True, dma=False, is_out=False):
        if self.cap is not None:
            self.cap.append((eng, fn, tuple(reads), tuple(writes), inc, dma, is_out))
            return None
        waits = {}

        def need(ev):
            if ev is None:
                return
            k, v = ev
            if k == eng and eng == "pe":
                return
            if v > waits.get(k, 0):
                waits[k] = v

        for r in reads:
            s = self.res.get(r)
            if s is not None:
                need(s[0])
                if r.startswith("ps"):
                    for k, v in s[1].items():
                        if k != eng:
                            need((k, v))
        for w in writes:
            s = self.res.get(w)
            if s is not None:
                need(s[0])
                for k, v in s[1].items():
                    need((k, v))
        if dma:
            pool = self.dpool[eng]
            sk = pool[self.dnext[eng] % len(pool)]
            self.dnext[eng] += 1
            if self.dcum[sk] > 0:
                need((sk, self.dcum[sk]))
            self.dcum[sk] += 16
            ev = (sk, self.dcum[sk])
            incspec = (sk, 16)
        else:
            if inc:
                self.tick[eng] += 1
                ev = (eng, self.tick[eng])
                incspec = (eng, 1)
            else:
                ev = (eng, self.tick[eng] + 1)
                incspec = None
        wl = []
        for k, v in waits.items():
            if self.seen[eng].get(k, 0) < v:
                self.seen[eng][k] = v
                wl.append((k, v))
        self.ops[eng].append((fn, wl, incspec))
        for r in reads:
            s = self.res.get(r)
            if s is None:
                s = self.res[r] = [None, {}]
            if s[1].get(ev[0], 0) < ev[1]:
                s[1][ev[0]] = ev[1]
        for w in writes:
            self.res[w] = [ev, {}]
        if is_out:
            self.out_events.append(ev)
        return ev

    def finish(self):
        fin = {}
        for k, v in self.out_events:
            fin[k] = max(fin.get(k, 0), v)
        wl = [(k, v) for k, v in fin.items()]
        self.ops["sp"].append((None, wl, None))

    def emit(self):
        nc = self.nc
        with nc.Block() as block:
            def run(engname):
                def body(e):
                    for fn, wl, incspec in self.ops[engname]:
                        for k, v in wl:
                            e.wait_ge(self._handle(k), v)
                        if fn is None:
                            continue
                        op, args, kw = fn
                        ins = getattr(e, op)(*args, **kw)
                        if self.debug_names is not None:
                            _NC_CACHE.setdefault("dbg_waits", {})[ins.ins.name] = list(wl)
                            try:
                                self.debug_names[ins.ins.name] = (engname, op, str(kw.get("func", "")), [str(a)[:80] for a in args] + [k + "=" + str(v)[:90] for k, v in kw.items() if k in ("out", "in_", "in0", "lhsT")])
                            except Exception:
                                pass
                        if incspec is not None:
                            ins.then_inc(self._handle(incspec[0]), incspec[1])
                return body
            block.tensor(run("pe"))
            block.scalar(run("act"))
            block.vector(run("dve"))
            block.gpsimd(run("pool"))
            block.sync(run("sp"))


class Ring:
    def __init__(self, items):
        self.items = items
        self.i = 0

    def next(self):
        it = self.items[self.i % len(self.items)]
        self.i += 1
        return it


class Pipe:
    def __init__(self):
        self.pending = None

    def item(self, main, post=None):
        main()
        if self.pending is not None:
            self.pending()
        self.pending = post

    def flush(self):
        if self.pending is not None:
            self.pending()
            self.pending = None


STS = [
    dict(ncols=1170, tiles=[(0, 386), (386, 384), (770, 400)], tiles1=[(130, 256), (386, 384), (770, 400)], npr=[386, 384, 384], samp=[False, False, True],
         xrow0=0, tabcol0=0, kcol0=2, kb0=0, ownc0=130, qb0=0, nqb=8, yrow0=0),
    dict(ncols=1024, tiles=[(0, 384), (384, 384), (768, 256)], tiles1=[(0, 384), (384, 384), (768, 256)], npr=[384, 384, 256], samp=[False, False, False],
         xrow0=1154, tabcol0=1170, kcol0=0, kb0=9, ownc0=0, qb0=8, nqb=8, yrow0=1024),
]
NCOLMAX = 1170
FFN_GROUPS = [(0, 4), (4, 4), (8, 4), (12, 4), (16, 3), (19, 3)]
G_MIX0, G_FFN0, G_PLE0, G_KV, G_MIX1, G_FFN1, G_PLE1 = range(7)


def build_program():
    nc = bass.Bass("TRN2", target_bir_lowering=False)

    def din(name, shape):
        return nc.dram_tensor(name, list(shape), F32, kind="ExternalInput")

    def dout(name, shape):
        return nc.dram_tensor(name, list(shape), F32, kind="ExternalOutput")

    xin = din("xin", [XR, D]); xsm = din("xsm", [NS, D]); stc = din("stc", [2 * NS, D])
    ck = din("ck", [NS, 128, 256]); cv = din("cv", [NS, 128, 256])
    pin = din("pin", [2, XR, PLE]); psm = din("psm", [2, NS, PLE])
    w_in = din("w_in", [D, 3 * D]); w_out = din("w_out", [D, D])
    wk = din("wk", [D, 256]); wv = din("wv", [D, 256]); wq = din("wq", [D, D]); wo = din("wo", [D, D])
    wg = din("wg", [2, D, FH]); wu = din("wu", [2, D, FH]); wd = din("wd", [2, FH, D])
    pproj = din("pproj", [2, PLE, D]); pgate = din("pgate", [2, D, D])
    gvec_d = din("gvec", [128, 56]); gfin_d = din("gfin", [128, D]); convw_d = din("convw", [128, 24])
    sink_d = din("sinkb", [128, 16]); idn_d = din("idn", [128, 128]); rm_d = din("rm", [128, 128])
    masks_d = din("masks", [3, 128, 128]); tab_d = din("tab", [2, 128, 2194]); esel_d = din("esel", [128, 256])

    y_o = dout("y", [OWN, D]); ys_o = dout("ys", [NS, D]); csp_o = dout("csp", [2, D]); css_o = dout("css", [NS, 2, D])
    kwp_o = dout("kwp", [128, 256]); vwp_o = dout("vwp", [128, 256])
    kws_o = dout("kws", [NS, 128, 256]); vws_o = dout("vws", [NS, 128, 256])

    def dap(t, off, dims):
        return bass.AP(t, off, [list(d) for d in dims])

    with ExitStack() as es:
        def T(name, shape, dt):
            return es.enter_context(nc.sbuf_tensor("sb_" + name, list(shape), dt))

        hT = T("hT", [128, KC, NCOLMAX], F32)
        A = T("A", [128, KC, NCOLMAX], BF16)
        B = T("B", [128, KC, NCOLMAX], BF16)
        KT = T("KT", [128, 2, 17 * 128], BF16)
        Vt = T("Vt", [128, 17, 4, 66], BF16)
        wbuf = [T("wbuf0", [128, 12288], BF16), T("wbuf1", [128, 12288], BF16)]
        pT = T("pT", [128, 2, NCOLMAX], BF16)
        tabs = T("tabs", [128, 2, 512], F32)
        xs = [T("xs0", [128, D], F32), T("xs1", [128, D], F32)]
        fr_t = [T("fr%d" % i, [128, 516], F32) for i in range(6)]
        hid_t = [T("hid%d" % i, [128, 4, 512], BF16) for i in range(2)]
        br_t = [T("br%d" % i, [128, 512], BF16) for i in range(4)]
        PT_t = [T("PT%d" % i, [128, 2, 512], BF16) for i in range(2)]
        o_sb = T("o_sb", [128, 1024], BF16)
        identF = T("identF", [128, 128], F32); identB = T("identB", [128, 128], BF16)
        onesM = T("onesM", [128, 128], BF16); RmB = T("RmB", [128, 128], BF16)
        masks = T("masks", [128, 3, 128], BF16)
        gvec = T("gvec", [128, 56], F32); gfin = T("gfin", [128, D], F32); convw = T("convw", [128, 24], F32)
        esink = T("esink", [128, 16], F32); epsT = T("epsT", [128, 1], F32)
        uhist = T("uhist", [128, KC, 2], F32); usamp = T("usamp", [128, KC, NS], F32); stT = T("stT", [128, KC, 2 * NS], F32)
        qs32 = T("qs32", [128, KC, NS], F32)
        small = T("small", [128, 64], F32)
        small2 = T("small2", [128, 32], F32)
        sm_pitch = [64, 32]
        Ks_t = [T("Ks%d" % i, [128, 256], F32) for i in range(2)]
        Vs_t = [T("Vs%d" % i, [128, 256], F32) for i in range(2)]
        Pz_t = [T("Pz%d" % i, [128, 16, 16], BF16) for i in range(2)]
        Vsb_t = [T("Vsb%d" % i, [128, 256], BF16) for i in range(2)]
        Esel = T("Esel", [128, 16, 16], BF16)
        prod = T("prod", [128, 1024], F32)
        kvout = T("kvout", [128, 512], F32)
        q_tm = xs[0][0:NS, :]
        knew = kvout[0:NS, 0:256]
        vnew = kvout[0:NS, 256:512]
        on_bf = o_sb[0:NS, :]
        ps = es.enter_context(nc.psum_tensor("ps", [128, 8, 512], F32))
        print("sbuf bytes remaining:", nc.sbuf_bytes_remaining)

        p = Prog(nc, es)
        import os as _os
        if _os.environ.get("MK_DEBUG"):
            p.debug_names = {}
            _NC_CACHE["dbg"] = p.debug_names

        def I(eng, opname, /, *args, reads=(), writes=(), inc=True, dma=False, is_out=False, **kw):
            return p.add(eng, (opname, args, kw), reads=reads, writes=writes, inc=inc, dma=dma, is_out=is_out)

        fr = Ring([("fr%d" % i, t) for i, t in enumerate(fr_t)])
        hidr = Ring([("hid%d" % i, t) for i, t in enumerate(hid_t)])
        brr = Ring([("br%d" % i, t) for i, t in enumerate(br_t)])
        PTr = Ring([("PT%d" % i, t) for i, t in enumerate(PT_t)])
        xsr = Ring([(["xs0"], xs[0]), (["prod", "prodB"], prod), (["xs1"], xs[1]), (["tabs"], tabs[:, :, :].rearrange("p a b -> p (a b)"))])
        class Sw:
            def __init__(self, ring):
                self.base = ring
                self.cur = ring

            def next(self):
                return self.cur.next()

        mmr = Sw(Ring([0, 1, 2, 3, 4, 5]))
        auxr = Sw(Ring([6, 7]))
        ringsA = (Ring([0, 1, 2]), Ring([5, 6]))
        ringsB = (Ring([3, 4]), Ring([7]))

        def psn(b):
            return "ps%d" % b

        cur = dict(st=0, l1=False)

        def trange(st, t):
            return st["tiles1"][t] if cur["l1"] else st["tiles"][t]
        touched = set()
        OVER = {0: [0], 1: [0, 1], 2: [1, 2]}

        def tname(buf, t):
            return "%s@%d_%d" % (buf, cur["st"], t)

        def W(buf, t):
            names = [tname(buf, t)]
            if cur["st"] == 1 and (buf, t) not in touched:
                touched.add((buf, t))
                names += ["%s@0_%d" % (buf, o) for o in OVER[t]]
            return names

        def ld(eng, out, in_, wr):
            I(eng, "dma_start", out=out, in_=in_, writes=wr, dma=True)

        ld("sp", identF[:], idn_d.ap(), ["identF"])
        ld("pool", RmB[:], rm_d.ap(), ["RmB"])
        ld("sp", gvec[:], gvec_d.ap(), ["gvec"])
        ld("sp", gfin[:], gfin_d.ap(), ["gfin"])
        ld("sp", convw[:], convw_d.ap(), ["convw"])
        ld("sp", esink[:], sink_d.ap(), ["esink"])
        ld("pool", Esel[:].rearrange("p a b -> p (a b)"), esel_d.ap(), ["Esel"])
        ld("pool", identB[:], idn_d.ap(), ["identB"])
        ld("pool", masks[:], dap(masks_d, 0, [[128, 128], [128 * 128, 3], [1, 128]]), ["masks"])
        I("pool", "memset", onesM[:], 1.0 / 1024.0, writes=["onesM"])
        I("pool", "memset", epsT[:], RMS_EPS, writes=["epsT"])
        I("pool", "memset", uhist[:], 0.0, writes=["uhist%d" % j for j in range(KC)])
        I("pool", "memset", Vt[:], 1.0, writes=["Vt%d" % i for i in range(17)])
        for i in range(2):
            I("pool", "memset", Pz_t[i][:], 0.0, writes=["Pz%d" % i])
        I("act", "activation", out=esink[:], in_=esink[:], func=AF.Exp, reads=["esink"], writes=["esink"])
        def passthrough():
            I("sp", "dma_start", out=dap(css_o, 0, [[2 * D, NS], [1, D]]), in_=dap(stc, D, [[2 * D, NS], [1, D]]), dma=True, is_out=True)
            I("sp", "dma_start", out=dap(kws_o, 0, [[128 * 256, NS], [1, 127 * 256]]), in_=dap(ck, 256, [[128 * 256, NS], [1, 127 * 256]]), dma=True, is_out=True)
            I("sp", "dma_start", out=dap(vws_o, 0, [[128 * 256, NS], [1, 127 * 256]]), in_=dap(cv, 256, [[128 * 256, NS], [1, 127 * 256]]), dma=True, is_out=True)

        def wload(dst_ap, src_t, off, ld_, ncols, nk, wname):
            src = dap(src_t, off, [[ld_, 128], [128 * ld_, nk], [1, ncols]])
            I("pool", "dma_start", out=dst_ap, in_=src, writes=[wname], dma=True)

        def v3(buf, lo, hi, a):
            return buf[:, lo:hi].rearrange("p (a b) -> p a b", a=a)

        def norm(st, t, outs):
            c0, n = trange(st, t)
            b = auxr.next()
            for kc in range(KC):
                nm, sq = brr.next()
                I("act", "activation", out=sq[:, 0:n], in_=hT[:, kc, c0:c0 + n], func=AF.Square, reads=[tname("hT", t)], writes=[nm])
                I("pe", "matmul", ps[:, b, 0:n], lhsT=onesM[:], rhs=sq[:, 0:n], start=(kc == 0), stop=(kc == KC - 1),
                  reads=[nm, "onesM"], writes=[psn(b)])
            n1, lnv = fr.next()
            I("act", "activation", out=lnv[:, 0:n], in_=ps[:, b, 0:n], func=AF.Ln, bias=epsT[:, 0:1], scale=1.0, reads=[psn(b), "epsT"], writes=[n1])
            n2, rstd = fr.next()
            I("act", "activation", out=rstd[:, 0:n], in_=lnv[:, 0:n], func=AF.Exp, scale=-0.5, reads=[n1], writes=[n2])
            for gi, dst, dname in outs:
                for kc in range(KC):
                    I("dve", "scalar_tensor_tensor", out=dst[:, kc, c0:c0 + n], in0=hT[:, kc, c0:c0 + n],
                      scalar=gvec[:, gi * 8 + kc:gi * 8 + kc + 1], in1=rstd[:, 0:n], op0=ALU.mult, op1=ALU.mult,
                      reads=[tname("hT", t), n2, "gvec"], writes=W(dname, t))

        def mm_acc(out_ap, b, lhs_fn, rhs_fn, nk, reads):
            for k in range(nk):
                I("pe", "matmul", out_ap, lhsT=lhs_fn(k), rhs=rhs_fn(k), start=(k == 0), stop=(k == nk - 1),
                  reads=reads, writes=[psn(b)], inc=(k == nk - 1))

        def resid_add(t, jo, b, c0, n):
            I("dve", "tensor_tensor", out=hT[:, jo, c0:c0 + n], in0=ps[:, b, 0:n], in1=hT[:, jo, c0:c0 + n], op=ALU.add,
              reads=[psn(b), tname("hT", t)], writes=[tname("hT", t)])

        def load_blocks(st, t):
            c0, n = st["tiles"][t]
            npr = st["npr"][t]
            blks = []
            r = 0
            while r < npr:
                nb = min(128, npr - r)
                blks.append(("p", st["xrow0"] + c0 + r, nb, c0 + r))
                r += nb
            if st["samp"][t]:
                blks.append(("s", 0, NS, c0 + npr))
            return blks

        def stage0_main(st, t):
            for kind, r0, nb, col in load_blocks(st, t):
                xn, xt = xsr.next()
                src = dap(xin, r0 * D, [[D, nb], [1, D]]) if kind == "p" else dap(xsm, 0, [[D, nb], [1, D]])
                I("act" if cur["st"] == 1 else "sp", "dma_start", out=xt[0:nb, :], in_=src, writes=xn, dma=True)
                for half in range(2):
                    b = mmr.next()
                    for j in range(4):
                        kc = half * 4 + j
                        I("pe", "transpose", ps[:, b, j * 128:j * 128 + nb], xt[0:nb, kc * 128:(kc + 1) * 128], identF[0:nb, 0:nb],
                          reads=xn + ["identF"], writes=[psn(b)], inc=(j == 3))
                    I("act", "activation", out=hT[:, half * 4:half * 4 + 4, col:col + nb],
                      in_=ps[:, b, :].rearrange("p (a b) -> p a b", a=4)[:, :, 0:nb], func=AF.Copy,
                      reads=[psn(b)], writes=W("hT", t))

        def load_state():
            xn, xt = xsr.next()
            I("sp", "dma_start", out=xt[0:32, :], in_=stc.ap(), writes=xn, dma=True)
            b = mmr.next()
            for kc in range(KC):
                I("pe", "transpose", ps[:, b, kc * 32:(kc + 1) * 32], xt[0:32, kc * 128:(kc + 1) * 128], identF[0:32, 0:32],
                  reads=xn + ["identF"], writes=[psn(b)], inc=(kc == KC - 1))
            I("act", "activation", out=stT[:], in_=ps[:, b, 0:256].rearrange("p (a b) -> p a b", a=KC), func=AF.Copy,
              reads=[psn(b)], writes=["stT"])

        def load_p(st, t, layer):
            for kind, r0, nb, col in load_blocks(st, t):
                xn, xt = xsr.next()
                src = dap(pin, (layer * XR + r0) * PLE, [[PLE, nb], [1, PLE]]) if kind == "p" else dap(psm, layer * NS * PLE, [[PLE, nb], [1, PLE]])
                I("act" if (cur["st"] == 1 and layer == 0) else "sp", "dma_start", out=xt[0:nb, 0:PLE], in_=src, writes=xn, dma=True)
                b = auxr.next()
                for c in range(2):
                    I("pe", "transpose", ps[:, b, c * 128:c * 128 + nb], xt[0:nb, c * 128:(c + 1) * 128], identF[0:nb, 0:nb],
                      reads=xn + ["identF"], writes=[psn(b)], inc=(c == 1))
                I("act", "activation", out=pT[:, 0:2, col:col + nb], in_=ps[:, b, 0:256].rearrange("p (a b) -> p a b", a=2)[:, :, 0:nb],
                  func=AF.Copy, reads=[psn(b)], writes=W("pT", t))

        def s1_main(st, t, grp, wv_, wname):
            c0, n = st["tiles"][t]
            npr = st["npr"][t]
            has_s = st["samp"][t]
            w3 = v3(wv_, 0, 12288, KC)
            for jj in range(4):
                j = grp * 4 + jj
                bc, bx, bb = mmr.next(), mmr.next(), mmr.next()
                for sel, b in ((1, bc), (2, bx), (0, bb)):
                    mm_acc(ps[:, b, 0:n], b, lambda k: w3[:, k, sel * 512 + jj * 128: sel * 512 + (jj + 1) * 128],
                           lambda k: A[:, k, c0:c0 + n], KC, [wname, tname("A", t)])
                ncs, c_sb = fr.next()
                I("act", "activation", out=c_sb[:, 0:n], in_=ps[:, bc, 0:n], func=AF.Copy, reads=[psn(bc)], writes=[ncs])
                nub, ub = fr.next()
                I("act", "activation", out=ub[:, 0:2], in_=uhist[:, j, :], func=AF.Copy, reads=["uhist%d" % j], writes=[nub])
                I("dve", "tensor_tensor", out=ub[:, 2:2 + n], in0=c_sb[:, 0:n], in1=ps[:, bx, 0:n], op=ALU.mult, reads=[ncs, psn(bx), nub], writes=[nub])
                ntm, tmp = fr.next()
                nac, acc = fr.next()
                w0 = convw[:, j * 3 + 0:j * 3 + 1]; w1 = convw[:, j * 3 + 1:j * 3 + 2]; w2 = convw[:, j * 3 + 2:j * 3 + 3]
                I("act", "activation", out=tmp[:, 0:npr], in_=ub[:, 0:npr], func=AF.Copy, scale=w0, reads=[nub, "convw"], writes=[ntm])
                I("dve", "scalar_tensor_tensor", out=acc[:, 0:npr], in0=ub[:, 1:1 + npr], scalar=w1, in1=tmp[:, 0:npr], op0=ALU.mult, op1=ALU.add,
                  reads=[nub, ntm, "convw"], writes=[nac])
                I("dve", "scalar_tensor_tensor", out=acc[:, 0:npr], in0=ub[:, 2:2 + npr], scalar=w2, in1=acc[:, 0:npr], op0=ALU.mult, op1=ALU.add,
                  reads=[nub, nac, "convw"], writes=[nac])
                I("act", "activation", out=uhist[:, j, :], in_=ub[:, npr:npr + 2], func=AF.Copy, reads=[nub], writes=["uhist%d" % j])
                if has_s:
                    us = ub[:, 2 + npr:2 + npr + NS]
                    st0 = stT[:, j, 0:2 * NS:2]
                    st1 = stT[:, j, 1:2 * NS:2]
                    I("dve", "tensor_scalar", out=tmp[:, npr:npr + NS], in0=st0, scalar1=w0, scalar2=None, op0=ALU.mult, reads=["stT", "convw", ntm], writes=[ntm])
                    I("dve", "scalar_tensor_tensor", out=tmp[:, npr:npr + NS], in0=st1, scalar=w1, in1=tmp[:, npr:npr + NS], op0=ALU.mult, op1=ALU.add,
                      reads=["stT", ntm, "convw"], writes=[ntm])
                    I("dve", "scalar_tensor_tensor", out=acc[:, npr:npr + NS], in0=us, scalar=w2, in1=tmp[:, npr:npr + NS], op0=ALU.mult, op1=ALU.add,
                      reads=[nub, ntm, "convw", nac], writes=[nac])
                    I("act", "activation", out=usamp[:, j, :], in_=us, func=AF.Copy, reads=[nub], writes=["usamp"])
                I("dve", "tensor_tensor", out=B[:, j, c0:c0 + n], in0=ps[:, bb, 0:n], in1=acc[:, 0:n], op=ALU.mult,
                  reads=[psn(bb), nac], writes=W("B", t))

        def proj_main(st, t, wv_, wname, src, sname):
            c0, n = trange(st, t)
            w3 = v3(wv_, 0, 8192, KC)
            for jo in range(KC):
                b = mmr.next()
                mm_acc(ps[:, b, 0:n], b, lambda k: w3[:, k, jo * 128:(jo + 1) * 128], lambda k: src[:, k, c0:c0 + n], KC, [wname, tname(sname, t)])
                resid_add(t, jo, b, c0, n)

        def ffn_main(st, t, nm_, wv_, wname):
            c0, n = trange(st, t)
            Wg = v3(wv_, 0, 4096, KC); Wu = v3(wv_, 4096, 8192, KC); Wd = v3(wv_, 8192, 12288, 4)
            hn, hid = hidr.next()
            for mi in range(nm_):
                bg, bu = mmr.next(), mmr.next()
                mm_acc(ps[:, bg, 0:n], bg, lambda k: Wg[:, k, mi * 128:(mi + 1) * 128], lambda k: A[:, k, c0:c0 + n], KC, [wname, tname("A", t)])
                mm_acc(ps[:, bu, 0:n], bu, lambda k: Wu[:, k, mi * 128:(mi + 1) * 128], lambda k: A[:, k, c0:c0 + n], KC, [wname, tname("A", t)])
                nsg, sg = fr.next()
                I("act", "activation", out=sg[:, 0:n], in_=ps[:, bg, 0:n], func=AF.Silu, reads=[psn(bg)], writes=[nsg])
                I("dve", "tensor_tensor", out=hid[:, mi, 0:n], in0=sg[:, 0:n], in1=ps[:, bu, 0:n], op=ALU.mult, reads=[nsg, psn(bu), hn], writes=[hn])
            for jo in range(KC):
                b = mmr.next()
                mm_acc(ps[:, b, 0:n], b, lambda k: Wd[:, k, jo * 128:(jo + 1) * 128], lambda k: hid[:, k, 0:n], nm_, [wname, hn])
                resid_add(t, jo, b, c0, n)

        def ple_main(st, t, wv_, wname):
            c0, n = trange(st, t)
            Wgt = v3(wv_, 0, 8192, KC); Wpr = v3(wv_, 8192, 10240, 2)
            for jo in range(KC):
                bg, bp = mmr.next(), mmr.next()
                mm_acc(ps[:, bg, 0:n], bg, lambda k: Wgt[:, k, jo * 128:(jo + 1) * 128], lambda k: A[:, k, c0:c0 + n], KC, [wname, tname("A", t)])
                mm_acc(ps[:, bp, 0:n], bp, lambda k: Wpr[:, k, jo * 128:(jo + 1) * 128], lambda k: pT[:, k, c0:c0 + n], 2, [wname, tname("pT", t)])
                nsg, sg = fr.next()
                I("act", "activation", out=sg[:, 0:n], in_=ps[:, bg, 0:n], func=AF.Sigmoid, reads=[psn(bg)], writes=[nsg])
                I("dve", "tensor_tensor", out=sg[:, 0:n], in0=sg[:, 0:n], in1=ps[:, bp, 0:n], op=ALU.mult, reads=[nsg, psn(bp)], writes=[nsg])
                I("dve", "tensor_tensor", out=hT[:, jo, c0:c0 + n], in0=sg[:, 0:n], in1=hT[:, jo, c0:c0 + n], op=ALU.add,
                  reads=[nsg, tname("hT", t)], writes=[tname("hT", t)])

        def rope_a(pb, n):
            nq, qf = brr.next()
            I("act", "activation", out=qf[:, 0:n], in_=ps[:, pb, 0:n], func=AF.Copy, reads=[psn(pb)], writes=[nq])
            return (nq, pb), qf

        def rope_b(nqpb, qf, n, toff=0):
            nq, pb = nqpb
            rb = auxr.next()
            I("pe", "matmul", ps[:, rb, 0:n], lhsT=RmB[:], rhs=qf[:, 0:n], start=True, stop=True, reads=[nq, "RmB"], writes=[psn(rb)])
            n1, t1 = fr.next()
            I("dve", "tensor_tensor", out=t1[:, 0:n], in0=ps[:, pb, 0:n], in1=tabs[:, 0, toff:toff + n], op=ALU.mult, reads=[psn(pb), "tabs", nq], writes=[n1])
            n2, t2 = fr.next()
            I("dve", "tensor_tensor", out=t2[:, 0:n], in0=ps[:, rb, 0:n], in1=tabs[:, 1, toff:toff + n], op=ALU.mult, reads=[psn(rb), "tabs"], writes=[n2])
            return n1, t1, n2, t2

        def kvq_main(st, t, wv_, wname):
            c0, n = st["tiles"][t]
            npr = st["npr"][t]
            has_s = st["samp"][t]
            Wk = v3(wv_, 0, 2048, KC); Wv = v3(wv_, 2048, 4096, KC); Wq = v3(wv_, 4096, 12288, KC)
            I("sp", "dma_start", out=tabs[:, :, 0:n], in_=dap(tab_d, st["tabcol0"] + c0, [[2194, 128], [128 * 2194, 2], [1, n]]), writes=["tabs"], dma=True)
            ka = max(0, st["kcol0"] - c0)
            nk = npr - ka
            kp0 = st["kb0"] * 128 + (c0 + ka - st["kcol0"])
            kbs = [(kp0 // 128 + i, ka + 128 * i) for i in range(nk // 128)]
            last_rel = None
            for kb, rel in kbs:
                if kb == 16:
                    last_rel = rel
            pendq = []

            def flush(keep=0):
                while len(pendq) > keep:
                    pendq.pop(0)()

            def k_fin(ch, nq, qf):
                n1, t1, n2, t2 = rope_b(nq, qf, n)
                I("dve", "tensor_tensor", out=t1[:, 0:n], in0=t1[:, 0:n], in1=t2[:, 0:n], op=ALU.add, reads=[n1, n2], writes=[n1])
                I("act", "activation", out=KT[:, ch, kp0:kp0 + nk], in_=t1[:, ka:ka + nk], func=AF.Copy, reads=[n1], writes=["KT%d" % kb for kb, _ in kbs])
                if last_rel is not None:
                    ob = auxr.next()
                    I("pe", "transpose", ps[:, ob, 0:128], t1[:, last_rel:last_rel + 128], identF[:], reads=[n1, "identF"], writes=[psn(ob)])
                    I("act", "activation", out=kvout[:, ch * 128:(ch + 1) * 128], in_=ps[:, ob, 0:128], func=AF.Copy, reads=[psn(ob)], writes=["kvoutK"])
                    if ch == 1:
                        I("sp", "dma_start", out=kwp_o.ap(), in_=kvout[:, 0:256], reads=["kvoutK"], dma=True, is_out=True)
                if has_s:
                    ob = auxr.next()
                    I("pe", "transpose", ps[0:NS, ob, 0:128], t1[:, npr:npr + NS], identF[:], reads=[n1, "identF"], writes=[psn(ob)])
                    I("act", "activation", out=knew[:, ch * 128:(ch + 1) * 128], in_=ps[0:NS, ob, 0:128], func=AF.Copy, reads=[psn(ob)], writes=["kvoutK"])
                    if ch == 1:
                        I("sp", "dma_start", out=dap(kws_o, 127 * 256, [[128 * 256, NS], [1, 256]]), in_=knew[:, :], reads=["kvoutK"], dma=True, is_out=True)

            q0, nqc = st["tiles1"][t]
            qoff = q0 - c0

            def q_fin(cq, nq, qf):
                n1, t1, n2, t2 = rope_b(nq, qf, nqc, qoff)
                I("dve", "tensor_tensor", out=A[:, cq, q0:q0 + nqc], in0=t1[:, 0:nqc], in1=t2[:, 0:nqc], op=ALU.add, reads=[n1, n2], writes=W("A", t))
                if has_s:
                    I("dve", "tensor_tensor", out=qs32[:, cq, :], in0=t1[:, npr - qoff:npr - qoff + NS], in1=t2[:, npr - qoff:npr - qoff + NS], op=ALU.add, reads=[n1, n2], writes=["qs32"])

            for ch in range(2):
                pb = mmr.next()
                mm_acc(ps[:, pb, 0:n], pb, lambda k: Wk[:, k, ch * 128:(ch + 1) * 128], lambda k: A[:, k, c0:c0 + n], KC, [wname, tname("A", t)])
                nq, qf = rope_a(pb, n)
                flush(0)
                pendq.append(lambda ch=ch, nq=nq, qf=qf: k_fin(ch, nq, qf))
            for vi, (kb, rel) in enumerate(kbs):
                pb = mmr.next()
                mm_acc(ps[:, pb, 0:256], pb, lambda k: A[:, k, c0 + rel:c0 + rel + 128], lambda k: Wv[:, k, :], KC, [wname, tname("A", t)])
                if vi == 0:
                    flush()
                I("act", "activation", out=Vt[:, kb, :, 0:64], in_=ps[:, pb, 0:256].rearrange("p (a b) -> p a b", a=4), func=AF.Copy,
                  reads=[psn(pb)], writes=["Vt%d" % kb])
                if kb == 16:
                    I("act", "activation", out=kvout[:, 256:512], in_=ps[:, pb, 0:256], func=AF.Copy, reads=[psn(pb)], writes=["kvoutV"])
                    I("sp", "dma_start", out=vwp_o.ap(), in_=kvout[:, 256:512], reads=["kvoutV"], dma=True, is_out=True)
            if has_s:
                pb = mmr.next()
                mm_acc(ps[0:NS, pb, 0:256], pb, lambda k: A[:, k, c0 + npr:c0 + npr + NS], lambda k: Wv[:, k, :], KC, [wname, tname("A", t)])
                I("act", "activation", out=vnew[:, :], in_=ps[0:NS, pb, 0:256], func=AF.Copy, reads=[psn(pb)], writes=["kvoutV"])
                I("sp", "dma_start", out=dap(vws_o, 127 * 256, [[128 * 256, NS], [1, 256]]), in_=vnew[:, :], reads=["kvoutV"], dma=True, is_out=True)
            for cq in range(KC):
                pb = mmr.next()
                mm_acc(ps[:, pb, 0:nqc], pb, lambda k: Wq[:, k, cq * 128:(cq + 1) * 128], lambda k: B[:, k, q0:q0 + nqc], KC, [wname, tname("B", t)])
                nq, qf = rope_a(pb, nqc)
                flush(0)
                pendq.append(lambda cq=cq, nq=nq, qf=qf: q_fin(cq, nq, qf))
            flush()

        def tile_of(st, col):
            for i, (c0, n) in enumerate(st["tiles"]):
                if c0 <= col < c0 + n:
                    return i
            raise ValueError(col)

        prod_bf = prod[:, :].bitcast(BF16)
        tabs_bf = tabs[:, :, :].rearrange("p a b -> p (a b)").bitcast(BF16)
        PT_slot = [
            Ring([("PT0", PT_t[0]), ("PT1", PT_t[1])]),
            Ring([("prod", prod_bf[:, 0:1024].rearrange("p (a b) -> p a b", a=2)), ("prodB", prod_bf[:, 1024:2048].rearrange("p (a b) -> p a b", a=2))]),
        ]
        osb_slot = [("o_sb", o_sb[:, :]), ("xs1", xs[1][:, :].bitcast(BF16)[:, 0:1024])]
        small_slot = [small, small2]

        def att_phases(st, qi, slot):
            qb = st["qb0"] + qi
            qc0 = st["ownc0"] + 128 * qi
            tq = tile_of(st, qc0)
            PTs = {}
            osn, osb = osb_slot[slot]
            sm = small_slot[slot]

            def S_phase(g):
                pair, hh = g // 2, g % 2
                PTn, PTt = PT_slot[slot].next()
                PTs[g] = (PTn, PTt)
                for kbi, kb in enumerate((qb, qb + 1)):
                    sb = mmr.next()
                    I("pe", "matmul", ps[:, sb, :], lhsT=KT[hh * 64:(hh + 1) * 64, pair, kb * 128:(kb + 1) * 128],
                      rhs=A[hh * 64:(hh + 1) * 64, pair * 4:pair * 4 + 4, qc0:qc0 + 128], start=True, stop=True,
                      reads=["KT%d" % kb, tname("A", tq)], writes=[psn(sb)])
                    I("act", "activation", out=PTt[:, kbi, :], in_=ps[:, sb, :], func=AF.Exp, scale=SCALE, reads=[psn(sb)], writes=[PTn])
                    mi = 0 if kbi == 1 else (2 if qb == 0 else 1)
                    mk = bass.AP(masks, mi * 128, [[384, 128], [0, 4], [1, 128]])
                    pv = PTt[:, kbi, :].rearrange("p (a b) -> p a b", a=4)
                    I("dve", "tensor_tensor", out=pv, in0=pv, in1=mk, op=ALU.mult, reads=[PTn, "masks"], writes=[PTn])

            def PV_phase(g):
                PTn, PTt = PTs[g]
                ob = auxr.next()
                for r in range(4):
                    for kbi, kb in enumerate((qb, qb + 1)):
                        I("pe", "matmul", ps[:, ob, r * 65:(r + 1) * 65], lhsT=PTt[:, kbi, r * 128:(r + 1) * 128], rhs=Vt[:, kb, g, 0:65],
                          start=(kbi == 0), stop=(kbi == 1), reads=[PTn, "Vt%d" % kb], writes=[psn(ob)], inc=(r == 3 and kbi == 1))
                sn = "small%d_%d" % (slot, g)
                o3 = ps[:, ob, 0:260].rearrange("p (a b) -> p a b", a=4)
                I("dve", "tensor_tensor", out=sm[:, g * 8:g * 8 + 4], in0=o3[:, :, 64], in1=esink[:, 4 * g:4 * g + 4], op=ALU.add,
                  reads=[psn(ob), "esink"], writes=[sn])
                I("dve", "reciprocal", out=sm[:, g * 8 + 4:g * 8 + 8], in_=sm[:, g * 8:g * 8 + 4], reads=[sn], writes=[sn])
                I("dve", "tensor_tensor", out=osb[:, g * 256:(g + 1) * 256].rearrange("p (a b) -> p a b", a=4), in0=o3[:, :, 0:64],
                  in1=bass.AP(sm, g * 8 + 4, [[sm_pitch[slot], 128], [1, 4], [0, 64]]), op=ALU.mult, reads=[psn(ob), sn], writes=[osn])

            def T_phase():
                tb = auxr.next()
                psb = ps[:, tb, :].bitcast(BF16)
                for c in range(KC):
                    I("pe", "transpose", psb[:, c * 128:(c + 1) * 128], osb[:, c * 128:(c + 1) * 128], identB[:], reads=[osn, "identB"], writes=[psn(tb)], inc=(c == KC - 1))
                I("act", "activation", out=B[:, 0:KC, qc0:qc0 + 128], in_=psb.rearrange("p (a b) -> p a b", a=KC), func=AF.Copy, reads=[psn(tb)], writes=W("B", tq))

            return [lambda: S_phase(0), lambda: S_phase(1), lambda: PV_phase(0), lambda: S_phase(2), lambda: PV_phase(1),
                    lambda: S_phase(3), lambda: PV_phase(2), lambda: PV_phase(3), T_phase]

        def att_group(st, qis):
            phs = [att_phases(st, qi, j) for j, qi in enumerate(qis)]
            for k in range(len(phs[0])):
                for ph in phs:
                    ph[k]()

        def att_main(st, qi):
            att_group(st, [qi])

        def samp_attn(st):
            sc0 = 1154
            tq = 2
            for cq in range(KC):
                I("pe", "transpose", ps[0:NS, cq // 4, (cq % 4) * 128:(cq % 4 + 1) * 128], qs32[:, cq, :], identF[:],
                  reads=["qs32", "identF"], writes=[psn(0), psn(1)], inc=(cq == KC - 1))
            for pair in range(2):
                I("act", "activation", out=q_tm[:, pair * 512:(pair + 1) * 512].rearrange("p (h c d) -> p h c d", h=2, c=4),
                  in_=ps[0:NS, pair, :].rearrange("p (c h d) -> p h c d", c=4, h=2), func=AF.Copy, reads=[psn(pair)], writes=["xs0"])
            I("act", "activation", out=o_sb[0:NS, :], in_=q_tm[:, :], func=AF.Copy, reads=["xs0"], writes=["o_sb"])
            for s in range(NS):
                kn, Kst = ("Ks%d" % (s % 2), Ks_t[s % 2])
                vn, Vst = ("Vs%d" % (s % 2), Vs_t[s % 2])
                pzn, Pzt = ("Pz%d" % (s % 2), Pz_t[s % 2])
                I("sp", "dma_start", out=Kst[:, :], in_=dap(ck, s * 128 * 256, [[256, 128], [1, 256]]), writes=[kn], dma=True)
                I("sp", "dma_start", out=Vst[:, :], in_=dap(cv, s * 128 * 256, [[256, 128], [1, 256]]), writes=[vn], dma=True)
                vbn, Vsb = ("Vsb%d" % (s % 2), Vsb_t[s % 2])
                I("act", "activation", out=Vsb[:, :], in_=Vst[:, :], func=AF.Copy, reads=[vn], writes=[vbn])
                b0 = 2 * (s % 2)
                sel = bass.AP(identB, s, [[128, NS], [0, 128]])
                for half in range(2):
                    I("pe", "matmul", ps[:, b0 + half, :], lhsT=sel, rhs=o_sb[0:NS, half * 512:(half + 1) * 512], start=True, stop=True,
                      reads=["o_sb", "identB"], writes=[psn(b0 + half)])
                    I("dve", "tensor_tensor", out=prod[:, half * 512:(half + 1) * 512].rearrange("p (g r d) -> p g r d", g=2, r=4),
                      in0=ps[:, b0 + half, :].rearrange("p (g r d) -> p g r d", g=2, r=4),
                      in1=bass.AP(Kst, half * 128, [[256, 128], [64, 2], [0, 4], [1, 64]]), op=ALU.mult,
                      reads=[psn(b0 + half), kn], writes=["prod", "prodB"])
                I("dve", "tensor_reduce", out=small[:, 32:48], in_=prod[:, :].rearrange("p (h d) -> p h d", d=64), axis=AX.X, op=ALU.add,
                  reads=["prod", "prodB"], writes=["smallS"])
                I("act", "activation", out=Pzt[:, :, s], in_=small[:, 32:48], func=AF.Exp, scale=SCALE, reads=["smallS"], writes=[pzn])
                for h in range(16):
                    I("pe", "matmul", ps[0:NS, 5 + h // 8, (h % 8) * 64:(h % 8 + 1) * 64], lhsT=Pzt[:, h, :], rhs=Vsb[:, (h // 4) * 64:(h // 4 + 1) * 64],
                      start=(s == 0 and h % 8 == 0), stop=(s == NS - 1 and h % 8 == 7), reads=[pzn, vbn], writes=[psn(5), psn(6)], inc=False)
                I("pe", "matmul", ps[0:NS, 7, 0:16], lhsT=Esel[:, s, :], rhs=Pzt[:, :, s], start=(s == 0), stop=(s == NS - 1),
                  reads=[pzn, "Esel"], writes=[psn(7)])
                I("dve", "memset", Pzt[:, :, s], 0.0, writes=[pzn])
            I("dve", "tensor_tensor", out=prod[0:NS, :].rearrange("p (g r d) -> p g r d", g=4, r=4), in0=q_tm[:, :].rearrange("p (g r d) -> p g r d", g=4, r=4),
              in1=bass.AP(kvout, 0, [[512, NS], [64, 4], [0, 4], [1, 64]]), op=ALU.mult, reads=["xs0", "kvoutK"], writes=["prod", "prodB"])
            I("dve", "tensor_reduce", out=small[0:NS, 48:64], in_=prod[0:NS, :].rearrange("p (h d) -> p h d", d=64), axis=AX.X, op=ALU.add,
              reads=["prod", "prodB"], writes=["smallN"])
            I("act", "activation", out=small[0:NS, 48:64], in_=small[0:NS, 48:64], func=AF.Exp, scale=SCALE, reads=["smallN"], writes=["smallN"])
            I("dve", "tensor_tensor", out=small[0:NS, 32:48], in0=ps[0:NS, 7, 0:16], in1=small[0:NS, 48:64], op=ALU.add, reads=[psn(7), "smallN"], writes=["smallS"])
            I("dve", "tensor_tensor", out=small[0:NS, 32:48], in0=small[0:NS, 32:48], in1=esink[0:NS, :], op=ALU.add, reads=["smallS", "esink"], writes=["smallS"])
            I("dve", "reciprocal", out=small[0:NS, 32:48], in_=small[0:NS, 32:48], reads=["smallS"], writes=["smallS"])
            I("dve", "tensor_tensor", out=prod[0:NS, :].rearrange("p (g r d) -> p g r d", g=4, r=4),
              in0=bass.AP(kvout, 256, [[512, NS], [64, 4], [0, 4], [1, 64]]), in1=bass.AP(small, 48, [[64, NS], [4, 4], [1, 4], [0, 64]]), op=ALU.mult,
              reads=["kvoutV", "smallN", "prod", "prodB"], writes=["prod", "prodB"])
            I("dve", "tensor_tensor", out=prod[0:NS, :].rearrange("p (a b) -> p a b", a=2), in0=prod[0:NS, :].rearrange("p (a b) -> p a b", a=2),
              in1=ps[0:NS, 5:7, :], op=ALU.add, reads=["prod", "prodB", psn(5), psn(6)], writes=["prod", "prodB"])
            I("dve", "tensor_tensor", out=on_bf[:, :].rearrange("p (h d) -> p h d", d=64), in0=prod[0:NS, :].rearrange("p (h d) -> p h d", d=64),
              in1=bass.AP(small, 32, [[64, NS], [1, 16], [0, 64]]), op=ALU.mult, reads=["prod", "prodB", "smallS"], writes=["o_sb"])
            psb = ps[:, 4, :].bitcast(BF16)
            for c in range(KC):
                I("pe", "transpose", psb[:, c * NS:(c + 1) * NS], on_bf[:, c * 128:(c + 1) * 128], identB[0:NS, 0:NS], reads=["o_sb", "identB"], writes=[psn(4)], inc=(c == KC - 1))
            I("act", "activation", out=B[:, 0:KC, sc0:sc0 + NS], in_=psb[:, 0:KC * NS].rearrange("p (a b) -> p a b", a=KC), func=AF.Copy, reads=[psn(4)], writes=W("B", tq))

        pairr = Ring([0, 2])
        ysr = Ring([(["prod", "prodB"], prod), (["tabs"], tabs[:, :, :].rearrange("p a b -> p (a b)"))])

        def final_block(st, t, col, nb, dst_ap):
            b0 = pairr.next()
            for kc in range(KC):
                I("pe", "transpose", ps[0:nb, b0 + kc // 4, (kc % 4) * 128:(kc % 4 + 1) * 128], hT[:, kc, col:col + nb], identF[:],
                  reads=[tname("hT", t), "identF"], writes=[psn(b0), psn(b0 + 1)], inc=(kc == KC - 1))
            nj, junk = fr.next()
            for half in range(2):
                I("act", "activation", out=junk[0:nb, 0:512], in_=ps[0:nb, b0 + half, :], func=AF.Square, accum_out=small[0:nb, 16 + half:17 + half],
                  reads=[psn(b0 + half)], writes=[nj, "smallF"])
            I("dve", "tensor_tensor", out=small[0:nb, 18:19], in0=small[0:nb, 16:17], in1=small[0:nb, 17:18], op=ALU.add, reads=["smallF"], writes=["smallF"])
            I("act", "activation", out=small[0:nb, 19:20], in_=small[0:nb, 18:19], func=AF.Ln, bias=epsT[0:nb, 0:1], scale=1.0 / 1024.0, reads=["smallF", "epsT"], writes=["smallF"])
            I("act", "activation", out=small[0:nb, 20:21], in_=small[0:nb, 19:20], func=AF.Exp, scale=-0.5, reads=["smallF"], writes=["smallF"])
            yn, yt = ysr.next()
            for half in range(2):
                I("dve", "scalar_tensor_tensor", out=yt[0:nb, half * 512:(half + 1) * 512], in0=ps[0:nb, b0 + half, :], scalar=small[0:nb, 20:21],
                  in1=gfin[0:nb, half * 512:(half + 1) * 512], op0=ALU.mult, op1=ALU.mult, reads=[psn(b0 + half), "smallF", "gfin"], writes=yn)
            I("sp", "dma_start", out=dst_ap, in_=yt[0:nb, :], reads=yn, dma=True, is_out=True)

        def final_post(st, t):
            c0, n = st["tiles"][t]
            npr = st["npr"][t]
            col = max(c0, st["ownc0"])
            while col < c0 + npr:
                row = st["yrow0"] + (col - st["ownc0"])
                final_block(st, t, col, 128, dap(y_o, row * D, [[D, 128], [1, D]]))
                col += 128
            if st["samp"][t]:
                final_block(st, t, c0 + npr, NS, ys_o.ap())

        def tm_out(src3, width, dst_ap, wait_names):
            b0 = pairr.next()
            for kc in range(KC):
                I("pe", "transpose", ps[0:width, b0 + kc // 4, (kc % 4) * 128:(kc % 4 + 1) * 128], src3[:, kc, :], identF[:],
                  reads=wait_names + ["identF"], writes=[psn(b0), psn(b0 + 1)], inc=(kc == KC - 1))
            yn, yt = ysr.next()
            I("act", "activation", out=yt[0:width, :].rearrange("p (a b) -> p a b", a=2), in_=ps[0:width, b0:b0 + 2, :], func=AF.Copy,
              reads=[psn(b0), psn(b0 + 1)], writes=yn)
            I("sp", "dma_start", out=dst_ap, in_=yt[0:width, :], reads=yn, dma=True, is_out=True)

        pipe = Pipe()

        def with_st(si, fn, l1=False):
            def g():
                old = (cur["st"], cur["l1"])
                cur["st"], cur["l1"] = si, l1
                fn()
                cur["st"], cur["l1"] = old
            return g

        early = {}

        def groups_for(si):
            st = STS[si]
            nt = len(st["tiles"])
            G = []

            def item(main, post=None, l1=False, post_l1=None):
                pl1 = l1 if post_l1 is None else post_l1
                pipe.item(with_st(si, main, l1), with_st(si, post, pl1) if post is not None else None)

            def load_s1(grp):
                def f(wv_, wname):
                    w3 = v3(wv_, 0, 12288, KC)
                    for sel in range(3):
                        wload(w3[:, :, sel * 512:(sel + 1) * 512], w_in, sel * 1024 + grp * 512, 3 * D, 512, KC, wname)
                return f

            def s0_main(t):
                stage0_main(st, t)
                load_p(st, t, 0)

            def s0_post(t):
                norm(st, t, [(G_MIX0, A, "A")])

            if si == 1:
                early["s0_main0"] = with_st(1, lambda: s0_main(0))
                early["s0_post0"] = with_st(1, lambda: s0_post(0))

            def run_s1a(wv_, wname):
                if si == 0:
                    with_st(si, load_state)()
                def s0(t):
                    item(lambda t=t: s0_main(t), lambda t=t: s0_post(t))

                def s1(t):
                    item(lambda t=t: s1_main(st, t, 0, wv_, wname))

                if si == 1 and early.get("done"):
                    pipe.item(lambda: (early["s0_post0"](), with_st(1, lambda: s0_main(1))()), with_st(1, lambda: s0_post(1)))
                else:
                    s0(0)
                    s0(1)
                s1(0)
                for t in range(2, nt):
                    s0(t)
                    s1(t - 1)
                if si == 0:
                    passthrough()
                s1(nt - 1)

            def run_s1b(wv_, wname):
                for t in range(nt):
                    item(lambda t=t: s1_main(st, t, 1, wv_, wname))
                if si == 0:
                    item(lambda: tm_out(usamp, NS, dap(css_o, D, [[2 * D, NS], [1, D]]), ["usamp"]))
                else:
                    item(lambda: tm_out(uhist, 2, csp_o.ap(), ["uhist%d" % j for j in range(KC)]))

            G.append((load_s1(0), run_s1a))
            G.append((load_s1(1), run_s1b))

            def load_proj(src_t):
                def f(wv_, wname):
                    wload(v3(wv_, 0, 8192, KC), src_t, 0, D, 1024, KC, wname)
                return f

            def run_s2(wv_, wname):
                for t in range(nt):
                    item(lambda t=t: proj_main(st, t, wv_, wname, B, "B"), lambda t=t: norm(st, t, [(G_FFN0, A, "A")]))

            G.append((load_proj(w_out), run_s2))

            def ffn_groups(layer, gnext):
                for gi, (m0, nm_) in enumerate(FFN_GROUPS):
                    def lf(wv_, wname, m0=m0, nm_=nm_):
                        wload(v3(wv_, 0, 4096, KC)[:, :, 0:nm_ * 128], wg, layer * D * FH + m0 * 128, FH, nm_ * 128, KC, wname)
                        wload(v3(wv_, 4096, 8192, KC)[:, :, 0:nm_ * 128], wu, layer * D * FH + m0 * 128, FH, nm_ * 128, KC, wname)
                        wload(v3(wv_, 8192, 12288, 4)[:, 0:nm_, :], wd, layer * FH * D + m0 * 128 * D, D, 1024, nm_, wname)

                    def rf(wv_, wname, nm_=nm_, last=(gi == len(FFN_GROUPS) - 1)):
                        for t in range(nt):
                            post = (lambda t=t: norm(st, t, [(gnext, A, "A")])) if last else None
                            item(lambda t=t: ffn_main(st, t, nm_, wv_, wname), post, l1=(layer == 1))
                    G.append((lf, rf))

            ffn_groups(0, G_PLE0)

            def load_ple(layer):
                def f(wv_, wname):
                    wload(v3(wv_, 0, 8192, KC), pgate, layer * D * D, D, 1024, KC, wname)
                    wload(v3(wv_, 8192, 10240, 2), pproj, layer * PLE * D, D, 1024, 2, wname)
                return f

            def run_ple0(wv_, wname):
                for t in range(nt):
                    item(lambda t=t: ple_main(st, t, wv_, wname), lambda t=t: (norm(st, t, [(G_KV, A, "A"), (G_MIX1, B, "B")]), load_p(st, t, 1)))

            G.append((load_ple(0), run_ple0))

            def load_kvq(wv_, wname):
                wload(v3(wv_, 0, 2048, KC), wk, 0, 256, 256, KC, wname)
                wload(v3(wv_, 2048, 4096, KC), wv, 0, 256, 256, KC, wname)
                Wq3 = v3(wv_, 4096, 12288, KC)
                for pair in range(2):
                    for hh in range(2):
                        for c_ in range(4):
                            col = (pair * 4 + c_) * 128 + hh * 64
                            dst = Wq3[:, :, col:col + 64]
                            src = dap(wq, (pair * 8 + hh * 4 + c_) * 64, [[D, 128], [128 * D, KC], [1, 64]])
                            I("pool", "dma_start", out=dst, in_=src, writes=[wname], dma=True)

            qb_of_tile = [[] for _ in range(nt)]
            for qi in range(st["nqb"]):
                qb_of_tile[tile_of(st, st["ownc0"] + 128 * qi)].append(qi)

            def cap(fn, rings):
                mmr.cur, auxr.cur = rings
                try:
                    return p.capture(with_st(si, fn, cur["l1"]))
                finally:
                    mmr.cur, auxr.cur = mmr.base, auxr.base

            def att_ops(t):
                def f():
                    qs = qb_of_tile[t]
                    for i in range(0, len(qs), 2):
                        att_group(st, qs[i:i + 2])
                return cap(f, ringsA)

            def run_kvq(wv_, wname):
                item(lambda: kvq_main(st, 0, wv_, wname))
                for t in range(1, nt):
                    def main(t=t):
                        a = att_ops(t - 1)
                        b = cap(lambda: kvq_main(st, t, wv_, wname), ringsB)
                        p.merge_replay(a, b)
                    pipe.item(main)

            G.append((load_kvq, run_kvq))

            def run_wo(wv_, wname):
                def main0():
                    a = att_ops(nt - 1)
                    b = cap(lambda: proj_main(st, 0, wv_, wname, B, "B"), ringsB)
                    p.merge_replay(a, b)
                pipe.item(with_st(si, main0, True), with_st(si, lambda: norm(st, 0, [(G_FFN1, A, "A")]), True))
                if si == 0:
                    item(lambda: samp_attn(st))
                for t in range(1, nt):
                    item(lambda t=t: proj_main(st, t, wv_, wname, B, "B"), lambda t=t: norm(st, t, [(G_FFN1, A, "A")]), l1=True)

            G.append((load_proj(wo), run_wo))
            ffn_groups(1, G_PLE1)

            def run_ple1(wv_, wname):
                hoist = (si == 0 and "s0_main0" in early)
                for t in range(nt - 1 if hoist else nt):
                    item(lambda t=t: ple_main(st, t, wv_, wname), lambda t=t: final_post(st, t), l1=True)
                if hoist:
                    tl = nt - 1
                    pipe.item(lambda: (early["s0_main0"](), with_st(0, lambda: ple_main(st, tl, wv_, wname), True)()),
                              with_st(0, lambda: final_post(st, tl), True))
                    early["done"] = True

            G.append((load_ple(1), run_ple1))
            return G

        allg = groups_for(0) + groups_for(1)

        def do_load(i):
            if i < len(allg):
                allg[i][0](wbuf[i % 2], "wbuf%d" % (i % 2))

        do_load(0)
        do_load(1)
        for i, (lf, rf) in enumerate(allg):
            rf(wbuf[i % 2], "wbuf%d" % (i % 2))
            do_load(i + 2)
        pipe.flush()
        p.finish()
        p.emit()
        print("ops per engine:", {k: len(v) for k, v in p.ops.items()})
    return nc


_NC_CACHE = {}


def _rope_tables(pos):
    half = 8
    inv_freq = np.power(np.float32(500000.0), -np.arange(half, dtype=np.float32) / np.float32(half)).astype(np.float32)
    ang = (pos.astype(np.float32)[:, None] * inv_freq[None, :]).astype(np.float32)
    cos = np.cos(ang).astype(np.float32).T
    sin = np.sin(ang).astype(np.float32).T
    n = pos.shape[0]
    C = np.ones((128, n), np.float32)
    S = np.zeros((128, n), np.float32)
    for hb in range(2):
        base = hb * 64
        C[base:base + 8] = cos
        C[base + 8:base + 16] = cos
        S[base:base + 8] = -sin
        S[base + 8:base + 16] = sin
    return C, S


def prepare(x_prompt, x_sample, state_conv, cache_k_win, cache_v_win, p_prompt, p_sample,
           norm_mix_g, norm_ffn_g, norm_ple_g, kv_norm_g, final_norm_g,
           conv_w_in, conv_w, conv_w_out, w_k, w_v, w_q, sinks, w_o,
           ffn_w_gate, ffn_w_up, ffn_w_down, ple_w_proj, ple_w_gate):
    f32 = np.float32
    A_ = lambda a: np.ascontiguousarray(np.asarray(a, dtype=f32))
    x_prompt = A_(x_prompt); x_sample = A_(x_sample); state_conv = A_(state_conv)
    cache_k_win = A_(cache_k_win); cache_v_win = A_(cache_v_win); p_prompt = A_(p_prompt); p_sample = A_(p_sample)

    def colvec(g):
        return np.asarray(g, f32).reshape(KC, 128).T

    gains = [norm_mix_g[0], norm_ffn_g[0], norm_ple_g[0], kv_norm_g, norm_mix_g[1], norm_ffn_g[1], norm_ple_g[1]]
    gvec = np.ascontiguousarray(np.concatenate([colvec(g) for g in gains], axis=1))
    gfin = np.ascontiguousarray(np.broadcast_to(np.asarray(final_norm_g, f32)[None, :], (128, D)))
    cw = np.asarray(conv_w, f32)[0]
    convw = np.ascontiguousarray(np.stack([colvec(cw[j]) for j in range(3)], axis=2).reshape(128, 24))
    sinkb = np.ascontiguousarray(np.broadcast_to(np.asarray(sinks, f32)[0][None, :], (128, 16)))
    idn = np.eye(128, dtype=f32)
    rm = np.zeros((128, 128), f32)
    for m in range(128):
        d = m % 64
        if d < 8:
            rm[m + 8, m] = 1.0
        elif d < 16:
            rm[m - 8, m] = 1.0
    jj = np.arange(128)[:, None]; ii = np.arange(128)[None, :]
    mcur = (jj <= ii).astype(f32); mprev = (jj >= ii).astype(f32)
    esel = np.zeros((128, 16, 16), f32)
    for s in range(16):
        esel[:, s, s] = 1.0
    esel = esel.reshape(128, 256)

    shared = dict(
        w_in=A_(conv_w_in)[0], w_out=A_(conv_w_out)[0], wk=A_(w_k), wv=A_(w_v), wq=A_(w_q)[0], wo=A_(w_o)[0],
        wg=A_(ffn_w_gate), wu=A_(ffn_w_up), wd=A_(ffn_w_down), pproj=A_(ple_w_proj), pgate=A_(ple_w_gate),
        gvec=gvec, gfin=gfin, convw=convw, sinkb=sinkb, idn=idn, rm=rm, esel=esel)

    in_maps = []
    for c in range(NCORES):
        b, half = c // 2, c % 2
        t0 = half * OWN
        xin = np.zeros((XR, D), f32)
        pin = np.zeros((2, XR, PLE), f32)
        if half == 1:
            xin[:] = x_prompt[b, t0 - HALO:t0 + OWN]
            pin[:] = p_prompt[:, b, t0 - HALO:t0 + OWN]
        else:
            xin[HALO:] = x_prompt[b, 0:OWN]
            pin[:, HALO:] = p_prompt[:, b, 0:OWN]
        s0 = c * NS
        pos1 = np.concatenate([np.maximum(t0 - HALO + np.arange(1154), 0), np.full(NS, 16384)]).astype(f32)
        pos2 = (t0 + 1024 + np.arange(1024)).astype(f32)
        C1, S1 = _rope_tables(pos1)
        C2, S2 = _rope_tables(pos2)
        tab = np.ascontiguousarray(np.stack([np.concatenate([C1, C2], 1), np.concatenate([S1, S2], 1)], 0))
        masks = np.ascontiguousarray(np.stack([mcur, mprev, mprev if half == 1 else np.zeros_like(mprev)], 0))
        m = dict(shared)
        m.update(xin=xin, xsm=np.ascontiguousarray(x_sample[s0:s0 + NS, 0]), stc=np.ascontiguousarray(state_conv[0, s0:s0 + NS].reshape(2 * NS, D)),
                 ck=np.ascontiguousarray(cache_k_win[s0:s0 + NS].reshape(NS, 128, 256)), cv=np.ascontiguousarray(cache_v_win[s0:s0 + NS].reshape(NS, 128, 256)),
                 pin=pin, psm=np.ascontiguousarray(p_sample[:, s0:s0 + NS, 0]), masks=masks, tab=tab)
        in_maps.append(m)

    return in_maps


def assemble(R):
    f32 = np.float32
    y_prompt = np.zeros((4, 4096, D), f32); y_sample = np.zeros((128, 1, D), f32)
    csp = np.zeros((1, 4, 2, D), f32); css = np.zeros((1, 128, 2, D), f32)
    kwp = np.zeros((4, 128, 4, 64), f32); vwp = np.zeros((4, 128, 4, 64), f32)
    kws = np.zeros((128, 128, 4, 64), f32); vws = np.zeros((128, 128, 4, 64), f32)
    for c in range(NCORES):
        b, half = c // 2, c % 2
        r = R[c]
        y_prompt[b, half * OWN:(half + 1) * OWN] = r["y"]
        s0 = c * NS
        y_sample[s0:s0 + NS, 0] = r["ys"]
        css[0, s0:s0 + NS] = r["css"]
        kws[s0:s0 + NS] = r["kws"].reshape(NS, 128, 4, 64)
        vws[s0:s0 + NS] = r["vws"].reshape(NS, 128, 4, 64)
        if half == 1:
            csp[0, b] = r["csp"]
            kwp[b] = r["kwp"].reshape(128, 4, 64)
            vwp[b] = r["vwp"].reshape(128, 4, 64)
    return (y_prompt, y_sample, csp, css, kwp, vwp, kws, vws)


def kernel(**inputs):
    in_maps = prepare(**inputs)
    if "nc" not in _NC_CACHE:
        _NC_CACHE["nc"] = build_program()
    nc = _NC_CACHE["nc"]
    res = run_bass_kernel_spmd(nc, in_maps, core_ids=list(range(NCORES)))
    return assemble(res.results)
```

```python
import numpy as np
from contextlib import ExitStack
import concourse.bass as bass
import concourse.mybir as mybir
from concourse.bass_utils import run_bass_kernel_spmd

F32 = mybir.dt.float32
BF16 = mybir.dt.bfloat16
ALU = mybir.AluOpType
AF = mybir.ActivationFunctionType
AX = mybir.AxisListType

NCORES = 8
D = 1024
KC = 8
FH = 2816
HC = 22
PLE = 256
HALO = 130
OWN = 2048
NS = 16
XR = HALO + OWN
RMS_EPS = 1e-6
SCALE = 0.125

ENGS = ("pe", "act", "dve", "pool", "sp")
MERGE = True


class Prog:
    def __init__(self, nc, es, n_dma_sp=24, n_dma_pool=8):
        self.nc = nc
        self.sem = {e: es.enter_context(nc.semaphore("s_" + e)) for e in ENGS[:4]}
        self.dsem = {}
        self.dpool = {"sp": [], "pool": []}
        for i in range(n_dma_sp):
            k = "dsp%d" % i
            self.dsem[k] = es.enter_context(nc.semaphore(k))
            self.dpool["sp"].append(k)
        for i in range(n_dma_pool):
            k = "dpl%d" % i
            self.dsem[k] = es.enter_context(nc.semaphore(k))
            self.dpool["pool"].append(k)
        self.dpool["act"] = []
        for i in range(6):
            k = "dac%d" % i
            self.dsem[k] = es.enter_context(nc.semaphore(k))
            self.dpool["act"].append(k)
        self.dnext = {"sp": 0, "pool": 0, "act": 0}
        self.dcum = {k: 0 for k in self.dsem}
        self.ops = {e: [] for e in ENGS}
        self.tick = {e: 0 for e in ENGS}
        self.seen = {e: {} for e in ENGS}
        self.res = {}
        self.out_events = []
        self.cap = None
        self.debug_names = None

    def capture(self, fn):
        assert self.cap is None
        self.cap = []
        fn()
        ops, self.cap = self.cap, None
        return ops

    def replay(self, ops):
        for eng, fn, reads, writes, inc, dma, is_out in ops:
            self.add(eng, fn, reads=reads, writes=writes, inc=inc, dma=dma, is_out=is_out)

    @staticmethod
    def segments(ops):
        segs, cur = [], []
        for op in ops:
            cur.append(op)
            if op[0] == "pe" and op[4]:
                segs.append(cur)
                cur = []
        if cur:
            if segs:
                segs[-1].extend(cur)
            else:
                segs.append(cur)
        return segs

    def merge_replay(self, opsA, opsB):
        if not MERGE:
            self.replay(opsA)
            self.replay(opsB)
            return
        sa, sb = self.segments(opsA), self.segments(opsB)
        na, nb = len(sa), len(sb)
        out = []
        j = 0
        for i, seg in enumerate(sa):
            out.extend(seg)
            tgt = ((i + 1) * nb) // na
            while j < tgt:
                out.extend(sb[j])
                j += 1
        while j < nb:
            out.extend(sb[j])
            j += 1
        self.replay(out)

    def _handle(self, k):
        return self.sem[k] if k in self.sem else self.dsem[k]

    def add(self, eng, fn, reads=(), writes=(), inc=True, dma=False, is_out=False):
        if self.cap is not None:
            self.cap.append((eng, fn, tuple(reads), tuple(writes), inc, dma, is_out))
            return None
        waits = {}

        def need(ev):
            if ev is None:
                return
            k, v = ev
            if k == eng and eng == "pe":
                return
            if v > waits.get(k, 0):
                waits[k] = v

        for r in reads:
            s = self.res.get(r)
            if s is not None:
                need(s[0])
                if r.startswith("ps"):
                    for k, v in s[1].items():
                        if k != eng:
                            need((k, v))
        for w in writes:
            s = self.res.get(w)
            if s is not None:
                need(s[0])
                for k, v in s[1].items():
                    need((k, v))
        if dma:
            pool = self.dpool[eng]
            sk = pool[self.dnext[eng] % len(pool)]
            self.dnext[eng] += 1
            if self.dcum[sk] > 0:
                need((sk, self.dcum[sk]))
            self.dcum[sk] += 16
            ev = (sk, self.dcum[sk])
            incspec = (sk, 16)
        else:
            if inc:
                self.tick[eng] += 1
                ev = (eng, self.tick[eng])
                incspec = (eng, 1)
            else:
                ev = (eng, self.tick[eng] + 1)
                incspec = None
        wl = []
        for k, v in waits.items():
            if self.seen[eng].get(k, 0) < v:
                self.seen[eng][k] = v
                wl.append((k, v))
        self.ops[eng].append((fn, wl, incspec))
        for r in reads:
            s = self.res.get(r)
            if s is None:
                s = self.res[r] = [None, {}]
            if s[1].get(ev[0], 0) < ev[1]:
                s[1][ev[0]] = ev[1]
        for w in writes:
            self.res[w] = [ev, {}]
        if is_out:
            self.out_events.append(ev)
        return ev

    def finish(self):
        fin = {}
        for k, v in self.out_events:
            fin[k] = max(fin.get(k, 0), v)
        wl = [(k, v) for k, v in fin.items()]
        self.ops["sp"].append((None, wl, None))

    def emit(self):
        nc = self.nc
        with nc.Block() as block:
            def run(engname):
                def body(e):
                    for fn, wl, incspec in self.ops[engname]:
                        for k, v in wl:
                            e.wait_ge(self._handle(k), v)
                        if fn is None:
                            continue
                        op, args, kw = fn
                        ins = getattr(e, op)(*args, **kw)
                        if self.debug_names is not None:
                            _NC_CACHE.setdefault("dbg_waits", {})[ins.ins.name] = list(wl)
                            try:
                                self.debug_names[ins.ins.name] = (engname, op, str(kw.get("func", "")), [str(a)[:80] for a in args] + [k + "=" + str(v)[:90] for k, v in kw.items() if k in ("out", "in_", "in0", "lhsT")])
                            except Exception:
                                pass
                        if incspec is not None:
                            ins.then_inc(self._handle(incspec[0]), incspec[1])
                return body
            block.tensor(run("pe"))
            block.scalar(run("act"))
            block.vector(run("dve"))
            block.gpsimd(run("pool"))
            block.sync(run("sp"))


class Ring:
    def __init__(self, items):
        self.items = items
        self.i = 0

    def next(self):
        it = self.items[self.i % len(self.items)]
        self.i += 1
        return it


class Pipe:
    def __init__(self):
        self.pending = None

    def item(self, main, post=None):
        main()
        if self.pending is not None:
            self.pending()
        self.pending = post

    def flush(self):
        if self.pending is not None:
            self.pending()
            self.pending = None


STS = [
    dict(ncols=1170, tiles=[(0, 386), (386, 384), (770, 400)], tiles1=[(130, 256), (386, 384), (770, 400)], npr=[386, 384, 384], samp=[False, False, True],
         xrow0=0, tabcol0=0, kcol0=2, kb0=0, ownc0=130, qb0=0, nqb=8, yrow0=0),
    dict(ncols=1024, tiles=[(0, 384), (384, 384), (768, 256)], tiles1=[(0, 384), (384, 384), (768, 256)], npr=[384, 384, 256], samp=[False, False, False],
         xrow0=1154, tabcol0=1170, kcol0=0, kb0=9, ownc0=0, qb0=8, nqb=8, yrow0=1024),
]
NCOLMAX = 1170
FFN_GROUPS = [(0, 4), (4, 4), (8, 4), (12, 4), (16, 3), (19, 3)]
G_MIX0, G_FFN0, G_PLE0, G_KV, G_MIX1, G_FFN1, G_PLE1 = range(7)


def build_program():
    nc = bass.Bass("TRN2", target_bir_lowering=False)

    def din(name, shape):
        return nc.dram_tensor(name, list(shape), F32, kind="ExternalInput")

    def dout(name, shape):
        return nc.dram_tensor(name, list(shape), F32, kind="ExternalOutput")

    xin = din("xin", [XR, D]); xsm = din("xsm", [NS, D]); stc = din("stc", [2 * NS, D])
    ck = din("ck", [NS, 128, 256]); cv = din("cv", [NS, 128, 256])
    pin = din("pin", [2, XR, PLE]); psm = din("psm", [2, NS, PLE])
    w_in = din("w_in", [D, 3 * D]); w_out = din("w_out", [D, D])
    wk = din("wk", [D, 256]); wv = din("wv", [D, 256]); wq = din("wq", [D, D]); wo = din("wo", [D, D])
    wg = din("wg", [2, D, FH]); wu = din("wu", [2, D, FH]); wd = din("wd", [2, FH, D])
    pproj = din("pproj", [2, PLE, D]); pgate = din("pgate", [2, D, D])
    gvec_d = din("gvec", [128, 56]); gfin_d = din("gfin", [128, D]); convw_d = din("convw", [128, 24])
    sink_d = din("sinkb", [128, 16]); idn_d = din("idn", [128, 128]); rm_d = din("rm", [128, 128])
    masks_d = din("masks", [3, 128, 128]); tab_d = din("tab", [2, 128, 2194]); esel_d = din("esel", [128, 256])

    y_o = dout("y", [OWN, D]); ys_o = dout("ys", [NS, D]); csp_o = dout("csp", [2, D]); css_o = dout("css", [NS, 2, D])
    kwp_o = dout("kwp", [128, 256]); vwp_o = dout("vwp", [128, 256])
    kws_o = dout("kws", [NS, 128, 256]); vws_o = dout("vws", [NS, 128, 256])

    def dap(t, off, dims):
        return bass.AP(t, off, [list(d) for d in dims])

    with ExitStack() as es:
        def T(name, shape, dt):
            return es.enter_context(nc.sbuf_tensor("sb_" + name, list(shape), dt))

        hT = T("hT", [128, KC, NCOLMAX], F32)
        A = T("A", [128, KC, NCOLMAX], BF16)
        B = T("B", [128, KC, NCOLMAX], BF16)
        KT = T("KT", [128, 2, 17 * 128], BF16)
        Vt = T("Vt", [128, 17, 4, 66], BF16)
        wbuf = [T("wbuf0", [128, 12288], BF16), T("wbuf1", [128, 12288], BF16)]
        pT = T("pT", [128, 2, NCOLMAX], BF16)
        tabs = T("tabs", [128, 2, 512], F32)
        xs = [T("xs0", [128, D], F32), T("xs1", [128, D], F32)]
        fr_t = [T("fr%d" % i, [128, 516], F32) for i in range(6)]
        hid_t = [T("hid%d" % i, [128, 4, 512], BF16) for i in range(2)]
        br_t = [T("br%d" % i, [128, 512], BF16) for i in range(4)]
        PT_t = [T("PT%d" % i, [128, 2, 512], BF16) for i in range(2)]
        o_sb = T("o_sb", [128, 1024], BF16)
        identF = T("identF", [128, 128], F32); identB = T("identB", [128, 128], BF16)
        onesM = T("onesM", [128, 128], BF16); RmB = T("RmB", [128, 128], BF16)
        masks = T("masks", [128, 3, 128], BF16)
        gvec = T("gvec", [128, 56], F32); gfin = T("gfin", [128, D], F32); convw = T("convw", [128, 24], F32)
        esink = T("esink", [128, 16], F32); epsT = T("epsT", [128, 1], F32)
        uhist = T("uhist", [128, KC, 2], F32); usamp = T("usamp", [128, KC, NS], F32); stT = T("stT", [128, KC, 2 * NS], F32)
        qs32 = T("qs32", [128, KC, NS], F32)
        small = T("small", [128, 64], F32)
        small2 = T("small2", [128, 32], F32)
        sm_pitch = [64, 32]
        Ks_t = [T("Ks%d" % i, [128, 256], F32) for i in range(2)]
        Vs_t = [T("Vs%d" % i, [128, 256], F32) for i in range(2)]
        Pz_t = [T("Pz%d" % i, [128, 16, 16], BF16) for i in range(2)]
        Vsb_t = [T("Vsb%d" % i, [128, 256], BF16) for i in range(2)]
        Esel = T("Esel", [128, 16, 16], BF16)
        prod = T("prod", [128, 1024], F32)
        kvout = T("kvout", [128, 512], F32)
        q_tm = xs[0][0:NS, :]
        knew = kvout[0:NS, 0:256]
        vnew = kvout[0:NS, 256:512]
        on_bf = o_sb[0:NS, :]
        ps = es.enter_context(nc.psum_tensor("ps", [128, 8, 512], F32))
        print("sbuf bytes remaining:", nc.sbuf_bytes_remaining)

        p = Prog(nc, es)
        import os as _os
        if _os.environ.get("MK_DEBUG"):
            p.debug_names = {}
            _NC_CACHE["dbg"] = p.debug_names

        def I(eng, opname, /, *args, reads=(), writes=(), inc=True, dma=False, is_out=False, **kw):
            return p.add(eng, (opname, args, kw), reads=reads, writes=writes, inc=inc, dma=dma, is_out=is_out)

        fr = Ring([("fr%d" % i, t) for i, t in enumerate(fr_t)])
        hidr = Ring([("hid%d" % i, t) for i, t in enumerate(hid_t)])
        brr = Ring([("br%d" % i, t) for i, t in enumerate(br_t)])
        PTr = Ring([("PT%d" % i, t) for i, t in enumerate(PT_t)])
        xsr = Ring([(["xs0"], xs[0]), (["prod", "prodB"], prod), (["xs1"], xs[1]), (["tabs"], tabs[:, :, :].rearrange("p a b -> p (a b)"))])
        class Sw:
            def __init__(self, ring):
                self.base = ring
                self.cur = ring

            def next(self):
                return self.cur.next()

        mmr = Sw(Ring([0, 1, 2, 3, 4, 5]))
        auxr = Sw(Ring([6, 7]))
        ringsA = (Ring([0, 1, 2]), Ring([5, 6]))
        ringsB = (Ring([3, 4]), Ring([7]))

        def psn(b):
            return "ps%d" % b

        cur = dict(st=0, l1=False)

        def trange(st, t):
            return st["tiles1"][t] if cur["l1"] else st["tiles"][t]
        touched = set()
        OVER = {0: [0], 1: [0, 1], 2: [1, 2]}

        def tname(buf, t):
            return "%s@%d_%d" % (buf, cur["st"], t)

        def W(buf, t):
            names = [tname(buf, t)]
            if cur["st"] == 1 and (buf, t) not in touched:
                touched.add((buf, t))
                names += ["%s@0_%d" % (buf, o) for o in OVER[t]]
            return names

        def ld(eng, out, in_, wr):
            I(eng, "dma_start", out=out, in_=in_, writes=wr, dma=True)

        ld("sp", identF[:], idn_d.ap(), ["identF"])
        ld("pool", RmB[:], rm_d.ap(), ["RmB"])
        ld("sp", gvec[:], gvec_d.ap(), ["gvec"])
        ld("sp", gfin[:], gfin_d.ap(), ["gfin"])
        ld("sp", convw[:], convw_d.ap(), ["convw"])
        ld("sp", esink[:], sink_d.ap(), ["esink"])
        ld("pool", Esel[:].rearrange("p a b -> p (a b)"), esel_d.ap(), ["Esel"])
        ld("pool", identB[:], idn_d.ap(), ["identB"])
        ld("pool", masks[:], dap(masks_d, 0, [[128, 128], [128 * 128, 3], [1, 128]]), ["masks"])
        I("pool", "memset", onesM[:], 1.0 / 1024.0, writes=["onesM"])
        I("pool", "memset", epsT[:], RMS_EPS, writes=["epsT"])
        I("pool", "memset", uhist[:], 0.0, writes=["uhist%d" % j for j in range(KC)])
        I("pool", "memset", Vt[:], 1.0, writes=["Vt%d" % i for i in range(17)])
        for i in range(2):
            I("pool", "memset", Pz_t[i][:], 0.0, writes=["Pz%d" % i])
        I("act", "activation", out=esink[:], in_=esink[:], func=AF.Exp, reads=["esink"], writes=["esink"])
        def passthrough():
            I("sp", "dma_start", out=dap(css_o, 0, [[2 * D, NS], [1, D]]), in_=dap(stc, D, [[2 * D, NS], [1, D]]), dma=True, is_out=True)
            I("sp", "dma_start", out=dap(kws_o, 0, [[128 * 256, NS], [1, 127 * 256]]), in_=dap(ck, 256, [[128 * 256, NS], [1, 127 * 256]]), dma=True, is_out=True)
            I("sp", "dma_start", out=dap(vws_o, 0, [[128 * 256, NS], [1, 127 * 256]]), in_=dap(cv, 256, [[128 * 256, NS], [1, 127 * 256]]), dma=True, is_out=True)

        def wload(dst_ap, src_t, off, ld_, ncols, nk, wname):
            src = dap(src_t, off, [[ld_, 128], [128 * ld_, nk], [1, ncols]])
            I("pool", "dma_start", out=dst_ap, in_=src, writes=[wname], dma=True)

        def v3(buf, lo, hi, a):
            return buf[:, lo:hi].rearrange("p (a b) -> p a b", a=a)

        def norm(st, t, outs):
            c0, n = trange(st, t)
            b = auxr.next()
            for kc in range(KC):
                nm, sq = brr.next()
                I("act", "activation", out=sq[:, 0:n], in_=hT[:, kc, c0:c0 + n], func=AF.Square, reads=[tname("hT", t)], writes=[nm])
                I("pe", "matmul", ps[:, b, 0:n], lhsT=onesM[:], rhs=sq[:, 0:n], start=(kc == 0), stop=(kc == KC - 1),
                  reads=[nm, "onesM"], writes=[psn(b)])
            n1, lnv = fr.next()
            I("act", "activation", out=lnv[:, 0:n], in_=ps[:, b, 0:n], func=AF.Ln, bias=epsT[:, 0:1], scale=1.0, reads=[psn(b), "epsT"], writes=[n1])
            n2, rstd = fr.next()
            I("act", "activation", out=rstd[:, 0:n], in_=lnv[:, 0:n], func=AF.Exp, scale=-0.5, reads=[n1], writes=[n2])
            for gi, dst, dname in outs:
                for kc in range(KC):
                    I("dve", "scalar_tensor_tensor", out=dst[:, kc, c0:c0 + n], in0=hT[:, kc, c0:c0 + n],
                      scalar=gvec[:, gi * 8 + kc:gi * 8 + kc + 1], in1=rstd[:, 0:n], op0=ALU.mult, op1=ALU.mult,
                      reads=[tname("hT", t), n2, "gvec"], writes=W(dname, t))

        def mm_acc(out_ap, b, lhs_fn, rhs_fn, nk, reads):
            for k in range(nk):
                I("pe", "matmul", out_ap, lhsT=lhs_fn(k), rhs=rhs_fn(k), start=(k == 0), stop=(k == nk - 1),
                  reads=reads, writes=[psn(b)], inc=(k == nk - 1))

        def resid_add(t, jo, b, c0, n):
            I("dve", "tensor_tensor", out=hT[:, jo, c0:c0 + n], in0=ps[:, b, 0:n], in1=hT[:, jo, c0:c0 + n], op=ALU.add,
              reads=[psn(b), tname("hT", t)], writes=[tname("hT", t)])

        def load_blocks(st, t):
            c0, n = st["tiles"][t]
            npr = st["npr"][t]
            blks = []
            r = 0
            while r < npr:
                nb = min(128, npr - r)
                blks.append(("p", st["xrow0"] + c0 + r, nb, c0 + r))
                r += nb
            if st["samp"][t]:
                blks.append(("s", 0, NS, c0 + npr))
            return blks

        def stage0_main(st, t):
            for kind, r0, nb, col in load_blocks(st, t):
                xn, xt = xsr.next()
                src = dap(xin, r0 * D, [[D, nb], [1, D]]) if kind == "p" else dap(xsm, 0, [[D, nb], [1, D]])
                I("act" if cur["st"] == 1 else "sp", "dma_start", out=xt[0:nb, :], in_=src, writes=xn, dma=True)
                for half in range(2):
                    b = mmr.next()
                    for j in range(4):
                        kc = half * 4 + j
                        I("pe", "transpose", ps[:, b, j * 128:j * 128 + nb], xt[0:nb, kc * 128:(kc + 1) * 128], identF[0:nb, 0:nb],
                          reads=xn + ["identF"], writes=[psn(b)], inc=(j == 3))
                    I("act", "activation", out=hT[:, half * 4:half * 4 + 4, col:col + nb],
                      in_=ps[:, b, :].rearrange("p (a b) -> p a b", a=4)[:, :, 0:nb], func=AF.Copy,
                      reads=[psn(b)], writes=W("hT", t))

        def load_state():
            xn, xt = xsr.next()
            I("sp", "dma_start", out=xt[0:32, :], in_=stc.ap(), writes=xn, dma=True)
            b = mmr.next()
            for kc in range(KC):
                I("pe", "transpose", ps[:, b, kc * 32:(kc + 1) * 32], xt[0:32, kc * 128:(kc + 1) * 128], identF[0:32, 0:32],
                  reads=xn + ["identF"], writes=[psn(b)], inc=(kc == KC - 1))
            I("act", "activation", out=stT[:], in_=ps[:, b, 0:256].rearrange("p (a b) -> p a b", a=KC), func=AF.Copy,
              reads=[psn(b)], writes=["stT"])

        def load_p(st, t, layer):
            for kind, r0, nb, col in load_blocks(st, t):
                xn, xt = xsr.next()
                src = dap(pin, (layer * XR + r0) * PLE, [[PLE, nb], [1, PLE]]) if kind == "p" else dap(psm, layer * NS * PLE, [[PLE, nb], [1, PLE]])
                I("act" if (cur["st"] == 1 and layer == 0) else "sp", "dma_start", out=xt[0:nb, 0:PLE], in_=src, writes=xn, dma=True)
                b = auxr.next()
                for c in range(2):
                    I("pe", "transpose", ps[:, b, c * 128:c * 128 + nb], xt[0:nb, c * 128:(c + 1) * 128], identF[0:nb, 0:nb],
                      reads=xn + ["identF"], writes=[psn(b)], inc=(c == 1))
                I("act", "activation", out=pT[:, 0:2, col:col + nb], in_=ps[:, b, 0:256].rearrange("p (a b) -> p a b", a=2)[:, :, 0:nb],
                  func=AF.Copy, reads=[psn(b)], writes=W("pT", t))

        def s1_main(st, t, grp, wv_, wname):
            c0, n = st["tiles"][t]
            npr = st["npr"][t]
            has_s = st["samp"][t]
            w3 = v3(wv_, 0, 12288, KC)
            for jj in range(4):
                j = grp * 4 + jj
                bc, bx, bb = mmr.next(), mmr.next(), mmr.next()
                for sel, b in ((1, bc), (2, bx), (0, bb)):
                    mm_acc(ps[:, b, 0:n], b, lambda k: w3[:, k, sel * 512 + jj * 128: sel * 512 + (jj + 1) * 128],
                           lambda k: A[:, k, c0:c0 + n], KC, [wname, tname("A", t)])
                ncs, c_sb = fr.next()
                I("act", "activation", out=c_sb[:, 0:n], in_=ps[:, bc, 0:n], func=AF.Copy, reads=[psn(bc)], writes=[ncs])
                nub, ub = fr.next()
                I("act", "activation", out=ub[:, 0:2], in_=uhist[:, j, :], func=AF.Copy, reads=["uhist%d" % j], writes=[nub])
                I("dve", "tensor_tensor", out=ub[:, 2:2 + n], in0=c_sb[:, 0:n], in1=ps[:, bx, 0:n], op=ALU.mult, reads=[ncs, psn(bx), nub], writes=[nub])
                ntm, tmp = fr.next()
                nac, acc = fr.next()
                w0 = convw[:, j * 3 + 0:j * 3 + 1]; w1 = convw[:, j * 3 + 1:j * 3 + 2]; w2 = convw[:, j * 3 + 2:j * 3 + 3]
                I("act", "activation", out=tmp[:, 0:npr], in_=ub[:, 0:npr], func=AF.Copy, scale=w0, reads=[nub, "convw"], writes=[ntm])
                I("dve", "scalar_tensor_tensor", out=acc[:, 0:npr], in0=ub[:, 1:1 + npr], scalar=w1, in1=tmp[:, 0:npr], op0=ALU.mult, op1=ALU.add,
                  reads=[nub, ntm, "convw"], writes=[nac])
                I("dve", "scalar_tensor_tensor", out=acc[:, 0:npr], in0=ub[:, 2:2 + npr], scalar=w2, in1=acc[:, 0:npr], op0=ALU.mult, op1=ALU.add,
                  reads=[nub, nac, "convw"], writes=[nac])
                I("act", "activation", out=uhist[:, j, :], in_=ub[:, npr:npr + 2], func=AF.Copy, reads=[nub], writes=["uhist%d" % j])
                if has_s:
                    us = ub[:, 2 + npr:2 + npr + NS]
                    st0 = stT[:, j, 0:2 * NS:2]
                    st1 = stT[:, j, 1:2 * NS:2]
                    I("dve", "tensor_scalar", out=tmp[:, npr:npr + NS], in0=st0, scalar1=w0, scalar2=None, op0=ALU.mult, reads=["stT", "convw", ntm], writes=[ntm])
                    I("dve", "scalar_tensor_tensor", out=tmp[:, npr:npr + NS], in0=st1, scalar=w1, in1=tmp[:, npr:npr + NS], op0=ALU.mult, op1=ALU.add,
                      reads=["stT", ntm, "convw"], writes=[ntm])
                    I("dve", "scalar_tensor_tensor", out=acc[:, npr:npr + NS], in0=us, scalar=w2, in1=tmp[:, npr:npr + NS], op0=ALU.mult, op1=ALU.add,
                      reads=[nub, ntm, "convw", nac], writes=[nac])
                    I("act", "activation", out=usamp[:, j, :], in_=us, func=AF.Copy, reads=[nub], writes=["usamp"])
                I("dve", "tensor_tensor", out=B[:, j, c0:c0 + n], in0=ps[:, bb, 0:n], in1=acc[:, 0:n], op=ALU.mult,
                  reads=[psn(bb), nac], writes=W("B", t))

        def proj_main(st, t, wv_, wname, src, sname):
            c0, n = trange(st, t)
            w3 = v3(wv_, 0, 8192, KC)
            for jo in range(KC):
                b = mmr.next()
                mm_acc(ps[:, b, 0:n], b, lambda k: w3[:, k, jo * 128:(jo + 1) * 128], lambda k: src[:, k, c0:c0 + n], KC, [wname, tname(sname, t)])
                resid_add(t, jo, b, c0, n)

        def ffn_main(st, t, nm_, wv_, wname):
            c0, n = trange(st, t)
            Wg = v3(wv_, 0, 4096, KC); Wu = v3(wv_, 4096, 8192, KC); Wd = v3(wv_, 8192, 12288, 4)
            hn, hid = hidr.next()
            for mi in range(nm_):
                bg, bu = mmr.next(), mmr.next()
                mm_acc(ps[:, bg, 0:n], bg, lambda k: Wg[:, k, mi * 128:(mi + 1) * 128], lambda k: A[:, k, c0:c0 + n], KC, [wname, tname("A", t)])
                mm_acc(ps[:, bu, 0:n], bu, lambda k: Wu[:, k, mi * 128:(mi + 1) * 128], lambda k: A[:, k, c0:c0 + n], KC, [wname, tname("A", t)])
                nsg, sg = fr.next()
                I("act", "activation", out=sg[:, 0:n], in_=ps[:, bg, 0:n], func=AF.Silu, reads=[psn(bg)], writes=[nsg])
                I("dve", "tensor_tensor", out=hid[:, mi, 0:n], in0=sg[:, 0:n], in1=ps[:, bu, 0:n], op=ALU.mult, reads=[nsg, psn(bu), hn], writes=[hn])
            for jo in range(KC):
                b = mmr.next()
                mm_acc(ps[:, b, 0:n], b, lambda k: Wd[:, k, jo * 128:(jo + 1) * 128], lambda k: hid[:, k, 0:n], nm_, [wname, hn])
                resid_add(t, jo, b, c0, n)

        def ple_main(st, t, wv_, wname):
            c0, n = trange(st, t)
            Wgt = v3(wv_, 0, 8192, KC); Wpr = v3(wv_, 8192, 10240, 2)
            for jo in range(KC):
                bg, bp = mmr.next(), mmr.next()
                mm_acc(ps[:, bg, 0:n], bg, lambda k: Wgt[:, k, jo * 128:(jo + 1) * 128], lambda k: A[:, k, c0:c0 + n], KC, [wname, tname("A", t)])
                mm_acc(ps[:, bp, 0:n], bp, lambda k: Wpr[:, k, jo * 128:(jo + 1) * 128], lambda k: pT[:, k, c0:c0 + n], 2, [wname, tname("pT", t)])
                nsg, sg = fr.next()
                I("act", "activation", out=sg[:, 0:n], in_=ps[:, bg, 0:n], func=AF.Sigmoid, reads=[psn(bg)], writes=[nsg])
                I("dve", "tensor_tensor", out=sg[:, 0:n], in0=sg[:, 0:n], in1=ps[:, bp, 0:n], op=ALU.mult, reads=[nsg, psn(bp)], writes=[nsg])
                I("dve", "tensor_tensor", out=hT[:, jo, c0:c0 + n], in0=sg[:, 0:n], in1=hT[:, jo, c0:c0 + n], op=ALU.add,
                  reads=[nsg, tname("hT", t)], writes=[tname("hT", t)])

        def rope_a(pb, n):
            nq, qf = brr.next()
            I("act", "activation", out=qf[:, 0:n], in_=ps[:, pb, 0:n], func=AF.Copy, reads=[psn(pb)], writes=[nq])
            return (nq, pb), qf

        def rope_b(nqpb, qf, n, toff=0):
            nq, pb = nqpb
            rb = auxr.next()
            I("pe", "matmul", ps[:, rb, 0:n], lhsT=RmB[:], rhs=qf[:, 0:n], start=True, stop=True, reads=[nq, "RmB"], writes=[psn(rb)])
            n1, t1 = fr.next()
            I("dve", "tensor_tensor", out=t1[:, 0:n], in0=ps[:, pb, 0:n], in1=tabs[:, 0, toff:toff + n], op=ALU.mult, reads=[psn(pb), "tabs", nq], writes=[n1])
            n2, t2 = fr.next()
            I("dve", "tensor_tensor", out=t2[:, 0:n], in0=ps[:, rb, 0:n], in1=tabs[:, 1, toff:toff + n], op=ALU.mult, reads=[psn(rb), "tabs"], writes=[n2])
            return n1, t1, n2, t2

        def kvq_main(st, t, wv_, wname):
            c0, n = st["tiles"][t]
            npr = st["npr"][t]
            has_s = st["samp"][t]
            Wk = v3(wv_, 0, 2048, KC); Wv = v3(wv_, 2048, 4096, KC); Wq = v3(wv_, 4096, 12288, KC)
            I("sp", "dma_start", out=tabs[:, :, 0:n], in_=dap(tab_d, st["tabcol0"] + c0, [[2194, 128], [128 * 2194, 2], [1, n]]), writes=["tabs"], dma=True)
            ka = max(0, st["kcol0"] - c0)
            nk = npr - ka
            kp0 = st["kb0"] * 128 + (c0 + ka - st["kcol0"])
            kbs = [(kp0 // 128 + i, ka + 128 * i) for i in range(nk // 128)]
            last_rel = None
            for kb, rel in kbs:
                if kb == 16:
                    last_rel = rel
            pendq = []

            def flush(keep=0):
                while len(pendq) > keep:
                    pendq.pop(0)()

            def k_fin(ch, nq, qf):
                n1, t1, n2, t2 = rope_b(nq, qf, n)
                I("dve", "tensor_tensor", out=t1[:, 0:n], in0=t1[:, 0:n], in1=t2[:, 0:n], op=ALU.add, reads=[n1, n2], writes=[n1])
                I("act", "activation", out=KT[:, ch, kp0:kp0 + nk], in_=t1[:, ka:ka + nk], func=AF.Copy, reads=[n1], writes=["KT%d" % kb for kb, _ in kbs])
                if last_rel is not None:
                    ob = auxr.next()
                    I("pe", "transpose", ps[:, ob, 0:128], t1[:, last_rel:last_rel + 128], identF[:], reads=[n1, "identF"], writes=[psn(ob)])
                    I("act", "activation", out=kvout[:, ch * 128:(ch + 1) * 128], in_=ps[:, ob, 0:128], func=AF.Copy, reads=[psn(ob)], writes=["kvoutK"])
                    if ch == 1:
                        I("sp", "dma_start", out=kwp_o.ap(), in_=kvout[:, 0:256], reads=["kvoutK"], dma=True, is_out=True)
                if has_s:
                    ob = auxr.next()
                    I("pe", "transpose", ps[0:NS, ob, 0:128], t1[:, npr:npr + NS], identF[:], reads=[n1, "identF"], writes=[psn(ob)])
                    I("act", "activation", out=knew[:, ch * 128:(ch + 1) * 128], in_=ps[0:NS, ob, 0:128], func=AF.Copy, reads=[psn(ob)], writes=["kvoutK"])
                    if ch == 1:
                        I("sp", "dma_start", out=dap(kws_o, 127 * 256, [[128 * 256, NS], [1, 256]]), in_=knew[:, :], reads=["kvoutK"], dma=True, is_out=True)

            q0, nqc = st["tiles1"][t]
            qoff = q0 - c0

            def q_fin(cq, nq, qf):
                n1, t1, n2, t2 = rope_b(nq, qf, nqc, qoff)
                I("dve", "tensor_tensor", out=A[:, cq, q0:q0 + nqc], in0=t1[:, 0:nqc], in1=t2[:, 0:nqc], op=ALU.add, reads=[n1, n2], writes=W("A", t))
                if has_s:
                    I("dve", "tensor_tensor", out=qs32[:, cq, :], in0=t1[:, npr - qoff:npr - qoff + NS], in1=t2[:, npr - qoff:npr - qoff + NS], op=ALU.add, reads=[n1, n2], writes=["qs32"])

            for ch in range(2):
                pb = mmr.next()
                mm_acc(ps[:, pb, 0:n], pb, lambda k: Wk[:, k, ch * 128:(ch + 1) * 128], lambda k: A[:, k, c0:c0 + n], KC, [wname, tname("A", t)])
                nq, qf = rope_a(pb, n)
                flush(0)
                pendq.append(lambda ch=ch, nq=nq, qf=qf: k_fin(ch, nq, qf))
            for vi, (kb, rel) in enumerate(kbs):
                pb = mmr.next()
                mm_acc(ps[:, pb, 0:256], pb, lambda k: A[:, k, c0 + rel:c0 + rel + 128], lambda k: Wv[:, k, :], KC, [wname, tname("A", t)])
                if vi == 0:
                    flush()
                I("act", "activation", out=Vt[:, kb, :, 0:64], in_=ps[:, pb, 0:256].rearrange("p (a b) -> p a b", a=4), func=AF.Copy,
                  reads=[psn(pb)], writes=["Vt%d" % kb])
                if kb == 16:
                    I("act", "activation", out=kvout[:, 256:512], in_=ps[:, pb, 0:256], func=AF.Copy, reads=[psn(pb)], writes=["kvoutV"])
                    I("sp", "dma_start", out=vwp_o.ap(), in_=kvout[:, 256:512], reads=["kvoutV"], dma=True, is_out=True)
            if has_s:
                pb = mmr.next()
                mm_acc(ps[0:NS, pb, 0:256], pb, lambda k: A[:, k, c0 + npr:c0 + npr + NS], lambda k: Wv[:, k, :], KC, [wname, tname("A", t)])
                I("act", "activation", out=vnew[:, :], in_=ps[0:NS, pb, 0:256], func=AF.Copy, reads=[psn(pb)], writes=["kvoutV"])
                I("sp", "dma_start", out=dap(vws_o, 127 * 256, [[128 * 256, NS], [1, 256]]), in_=vnew[:, :], reads=["kvoutV"], dma=True, is_out=True)
            for cq in range(KC):
                pb = mmr.next()
                mm_acc(ps[:, pb, 0:nqc], pb, lambda k: Wq[:, k, cq * 128:(cq + 1) * 128], lambda k: B[:, k, q0:q0 + nqc], KC, [wname, tname("B", t)])
                nq, qf = rope_a(pb, nqc)
                flush(0)
                pendq.append(lambda cq=cq, nq=nq, qf=qf: q_fin(cq, nq, qf))
            flush()

        def tile_of(st, col):
            for i, (c0, n) in enumerate(st["tiles"]):
                if c0 <= col < c0 + n:
                    return i
            raise ValueError(col)

        prod_bf = prod[:, :].bitcast(BF16)
        tabs_bf = tabs[:, :, :].rearrange("p a b -> p (a b)").bitcast(BF16)
        PT_slot = [
            Ring([("PT0", PT_t[0]), ("PT1", PT_t[1])]),
            Ring([("prod", prod_bf[:, 0:1024].rearrange("p (a b) -> p a b", a=2)), ("prodB", prod_bf[:, 1024:2048].rearrange("p (a b) -> p a b", a=2))]),
        ]
        osb_slot = [("o_sb", o_sb[:, :]), ("xs1", xs[1][:, :].bitcast(BF16)[:, 0:1024])]
        small_slot = [small, small2]

        def att_phases(st, qi, slot):
            qb = st["qb0"] + qi
            qc0 = st["ownc0"] + 128 * qi
            tq = tile_of(st, qc0)
            PTs = {}
            osn, osb = osb_slot[slot]
            sm = small_slot[slot]

            def S_phase(g):
                pair, hh = g // 2, g % 2
                PTn, PTt = PT_slot[slot].next()
                PTs[g] = (PTn, PTt)
                for kbi, kb in enumerate((qb, qb + 1)):
                    sb = mmr.next()
                    I("pe", "matmul", ps[:, sb, :], lhsT=KT[hh * 64:(hh + 1) * 64, pair, kb * 128:(kb + 1) * 128],
                      rhs=A[hh * 64:(hh + 1) * 64, pair * 4:pair * 4 + 4, qc0:qc0 + 128], start=True, stop=True,
                      reads=["KT%d" % kb, tname("A", tq)], writes=[psn(sb)])
                    I("act", "activation", out=PTt[:, kbi, :], in_=ps[:, sb, :], func=AF.Exp, scale=SCALE, reads=[psn(sb)], writes=[PTn])
                    mi = 0 if kbi == 1 else (2 if qb == 0 else 1)
                    mk = bass.AP(masks, mi * 128, [[384, 128], [0, 4], [1, 128]])
                    pv = PTt[:, kbi, :].rearrange("p (a b) -> p a b", a=4)
                    I("dve", "tensor_tensor", out=pv, in0=pv, in1=mk, op=ALU.mult, reads=[PTn, "masks"], writes=[PTn])

            def PV_phase(g):
                PTn, PTt = PTs[g]
                ob = auxr.next()
                for r in range(4):
                    for kbi, kb in enumerate((qb, qb + 1)):
                        I("pe", "matmul", ps[:, ob, r * 65:(r + 1) * 65], lhsT=PTt[:, kbi, r * 128:(r + 1) * 128], rhs=Vt[:, kb, g, 0:65],
                          start=(kbi == 0), stop=(kbi == 1), reads=[PTn, "Vt%d" % kb], writes=[psn(ob)], inc=(r == 3 and kbi == 1))
                sn = "small%d_%d" % (slot, g)
                o3 = ps[:, ob, 0:260].rearrange("p (a b) -> p a b", a=4)
                I("dve", "tensor_tensor", out=sm[:, g * 8:g * 8 + 4], in0=o3[:, :, 64], in1=esink[:, 4 * g:4 * g + 4], op=ALU.add,
                  reads=[psn(ob), "esink"], writes=[sn])
                I("dve", "reciprocal", out=sm[:, g * 8 + 4:g * 8 + 8], in_=sm[:, g * 8:g * 8 + 4], reads=[sn], writes=[sn])
                I("dve", "tensor_tensor", out=osb[:, g * 256:(g + 1) * 256].rearrange("p (a b) -> p a b", a=4), in0=o3[:, :, 0:64],
                  in1=bass.AP(sm, g * 8 + 4, [[sm_pitch[slot], 128], [1, 4], [0, 64]]), op=ALU.mult, reads=[psn(ob), sn], writes=[osn])

            def T_phase():
                tb = auxr.next()
                psb = ps[:, tb, :].bitcast(BF16)
                for c in range(KC):
                    I("pe", "transpose", psb[:, c * 128:(c + 1) * 128], osb[:, c * 128:(c + 1) * 128], identB[:], reads=[osn, "identB"], writes=[psn(tb)], inc=(c == KC - 1))
                I("act", "activation", out=B[:, 0:KC, qc0:qc0 + 128], in_=psb.rearrange("p (a b) -> p a b", a=KC), func=AF.Copy, reads=[psn(tb)], writes=W("B", tq))

            return [lambda: S_phase(0), lambda: S_phase(1), lambda: PV_phase(0), lambda: S_phase(2), lambda: PV_phase(1),
                    lambda: S_phase(3), lambda: PV_phase(2), lambda: PV_phase(3), T_phase]

        def att_group(st, qis):
            phs = [att_phases(st, qi, j) for j, qi in enumerate(qis)]
            for k in range(len(phs[0])):
                for ph in phs:
                    ph[k]()

        def att_main(st, qi):
            att_group(st, [qi])

        def samp_attn(st):
            sc0 = 1154
            tq = 2
            for cq in range(KC):
                I("pe", "transpose", ps[0:NS, cq // 4, (cq % 4) * 128:(cq % 4 + 1) * 128], qs32[:, cq, :], identF[:],
                  reads=["qs32", "identF"], writes=[psn(0), psn(1)], inc=(cq == KC - 1))
            for pair in range(2):
                I("act", "activation", out=q_tm[:, pair * 512:(pair + 1) * 512].rearrange("p (h c d) -> p h c d", h=2, c=4),
                  in_=ps[0:NS, pair, :].rearrange("p (c h d) -> p h c d", c=4, h=2), func=AF.Copy, reads=[psn(pair)], writes=["xs0"])
            I("act", "activation", out=o_sb[0:NS, :], in_=q_tm[:, :], func=AF.Copy, reads=["xs0"], writes=["o_sb"])
            for s in range(NS):
                kn, Kst = ("Ks%d" % (s % 2), Ks_t[s % 2])
                vn, Vst = ("Vs%d" % (s % 2), Vs_t[s % 2])
                pzn, Pzt = ("Pz%d" % (s % 2), Pz_t[s % 2])
                I("sp", "dma_start", out=Kst[:, :], in_=dap(ck, s * 128 * 256, [[256, 128], [1, 256]]), writes=[kn], dma=True)
                I("sp", "dma_start", out=Vst[:, :], in_=dap(cv, s * 128 * 256, [[256, 128], [1, 256]]), writes=[vn], dma=True)
                vbn, Vsb = ("Vsb%d" % (s % 2), Vsb_t[s % 2])
                I("act", "activation", out=Vsb[:, :], in_=Vst[:, :], func=AF.Copy, reads=[vn], writes=[vbn])
                b0 = 2 * (s % 2)
                sel = bass.AP(identB, s, [[128, NS], [0, 128]])
                for half in range(2):
                    I("pe", "matmul", ps[:, b0 + half, :], lhsT=sel, rhs=o_sb[0:NS, half * 512:(half + 1) * 512], start=True, stop=True,
                      reads=["o_sb", "identB"], writes=[psn(b0 + half)])
                    I("dve", "tensor_tensor", out=prod[:, half * 512:(half + 1) * 512].rearrange("p (g r d) -> p g r d", g=2, r=4),
                      in0=ps[:, b0 + half, :].rearrange("p (g r d) -> p g r d", g=2, r=4),
                      in1=bass.AP(Kst, half * 128, [[256, 128], [64, 2], [0, 4], [1, 64]]), op=ALU.mult,
                      reads=[psn(b0 + half), kn], writes=["prod", "prodB"])
                I("dve", "tensor_reduce", out=small[:, 32:48], in_=prod[:, :].rearrange("p (h d) -> p h d", d=64), axis=AX.X, op=ALU.add,
                  reads=["prod", "prodB"], writes=["smallS"])
                I("act", "activation", out=Pzt[:, :, s], in_=small[:, 32:48], func=AF.Exp, scale=SCALE, reads=["smallS"], writes=[pzn])
                for h in range(16):
                    I("pe", "matmul", ps[0:NS, 5 + h // 8, (h % 8) * 64:(h % 8 + 1) * 64], lhsT=Pzt[:, h, :], rhs=Vsb[:, (h // 4) * 64:(h // 4 + 1) * 64],
                      start=(s == 0 and h % 8 == 0), stop=(s == NS - 1 and h % 8 == 7), reads=[pzn, vbn], writes=[psn(5), psn(6)], inc=False)
                I("pe", "matmul", ps[0:NS, 7, 0:16], lhsT=Esel[:, s, :], rhs=Pzt[:, :, s], start=(s == 0), stop=(s == NS - 1),
                  reads=[pzn, "Esel"], writes=[psn(7)])
                I("dve", "memset", Pzt[:, :, s], 0.0, writes=[pzn])
            I("dve", "tensor_tensor", out=prod[0:NS, :].rearrange("p (g r d) -> p g r d", g=4, r=4), in0=q_tm[:, :].rearrange("p (g r d) -> p g r d", g=4, r=4),
              in1=bass.AP(kvout, 0, [[512, NS], [64, 4], [0, 4], [1, 64]]), op=ALU.mult, reads=["xs0", "kvoutK"], writes=["prod", "prodB"])
            I("dve", "tensor_reduce", out=small[0:NS, 48:64], in_=prod[0:NS, :].rearrange("p (h d) -> p h d", d=64), axis=AX.X, op=ALU.add,
              reads=["prod", "prodB"], writes=["smallN"])
            I("act", "activation", out=small[0:NS, 48:64], in_=small[0:NS, 48:64], func=AF.Exp, scale=SCALE, reads=["smallN"], writes=["smallN"])
            I("dve", "tensor_tensor", out=small[0:NS, 32:48], in0=ps[0:NS, 7, 0:16], in1=small[0:NS, 48:64], op=ALU.add, reads=[psn(7), "smallN"], writes=["smallS"])
            I("dve", "tensor_tensor", out=small[0:NS, 32:48], in0=small[0:NS, 32:48], in1=esink[0:NS, :], op=ALU.add, reads=["smallS", "esink"], writes=["smallS"])
            I("dve", "reciprocal", out=small[0:NS, 32:48], in_=small[0:NS, 32:48], reads=["smallS"], writes=["smallS"])
            I("dve", "tensor_tensor", out=prod[0:NS, :].rearrange("p (g r d) -> p g r d", g=4, r=4),
              in0=bass.AP(kvout, 256, [[512, NS], [64, 4], [0, 4], [1, 64]]), in1=bass.AP(small, 48, [[64, NS], [4, 4], [1, 4], [0, 64]]), op=ALU.mult,
              reads=["kvoutV", "smallN", "prod", "prodB"], writes=["prod", "prodB"])
            I("dve", "tensor_tensor", out=prod[0:NS, :].rearrange("p (a b) -> p a b", a=2), in0=prod[0:NS, :].rearrange("p (a b) -> p a b", a=2),
              in1=ps[0:NS, 5:7, :], op=ALU.add, reads=["prod", "prodB", psn(5), psn(6)], writes=["prod", "prodB"])
            I("dve", "tensor_tensor", out=on_bf[:, :].rearrange("p (h d) -> p h d", d=64), in0=prod[0:NS, :].rearrange("p (h d) -> p h d", d=64),
              in1=bass.AP(small, 32, [[64, NS], [1, 16], [0, 64]]), op=ALU.mult, reads=["prod", "prodB", "smallS"], writes=["o_sb"])
            psb = ps[:, 4, :].bitcast(BF16)
            for c in range(KC):
                I("pe", "transpose", psb[:, c * NS:(c + 1) * NS], on_bf[:, c * 128:(c + 1) * 128], identB[0:NS, 0:NS], reads=["o_sb", "identB"], writes=[psn(4)], inc=(c == KC - 1))
            I("act", "activation", out=B[:, 0:KC, sc0:sc0 + NS], in_=psb[:, 0:KC * NS].rearrange("p (a b) -> p a b", a=KC), func=AF.Copy, reads=[psn(4)], writes=W("B", tq))

        pairr = Ring([0, 2, 4])
        ysr = Ring([(["prod", "prodB"], prod), (["tabs"], tabs[:, :, :].rearrange("p a b -> p (a b)"))])

        def final_block(st, t, col, nb, dst_ap):
            b0 = pairr.next()
            for kc in range(KC):
                I("pe", "transpose", ps[0:nb, b0 + kc // 4, (kc % 4) * 128:(kc % 4 + 1) * 128], hT[:, kc, col:col + nb], identF[:],
                  reads=[tname("hT", t), "identF"], writes=[psn(b0), psn(b0 + 1)], inc=(kc == KC - 1))
            nj, junk = fr.next()
            for half in range(2):
                I("act", "activation", out=junk[0:nb, 0:512], in_=ps[0:nb, b0 + half, :], func=AF.Square, accum_out=small[0:nb, 16 + half:17 + half],
                  reads=[psn(b0 + half)], writes=[nj, "smallF"])
            I("dve", "tensor_tensor", out=small[0:nb, 18:19], in0=small[0:nb, 16:17], in1=small[0:nb, 17:18], op=ALU.add, reads=["smallF"], writes=["smallF"])
            I("act", "activation", out=small[0:nb, 19:20], in_=small[0:nb, 18:19], func=AF.Ln, bias=epsT[0:nb, 0:1], scale=1.0 / 1024.0, reads=["smallF", "epsT"], writes=["smallF"])
            I("act", "activation", out=small[0:nb, 20:21], in_=small[0:nb, 19:20], func=AF.Exp, scale=-0.5, reads=["smallF"], writes=["smallF"])
            yn, yt = ysr.next()
            for half in range(2):
                I("dve", "scalar_tensor_tensor", out=yt[0:nb, half * 512:(half + 1) * 512], in0=ps[0:nb, b0 + half, :], scalar=small[0:nb, 20:21],
                  in1=gfin[0:nb, half * 512:(half + 1) * 512], op0=ALU.mult, op1=ALU.mult, reads=[psn(b0 + half), "smallF", "gfin"], writes=yn)
            I("sp", "dma_start", out=dst_ap, in_=yt[0:nb, :], reads=yn, dma=True, is_out=True)

        def final_post(st, t):
            c0, n = st["tiles"][t]
            npr = st["npr"][t]
            col = max(c0, st["ownc0"])
            while col < c0 + npr:
                row = st["yrow0"] + (col - st["ownc0"])
                final_block(st, t, col, 128, dap(y_o, row * D, [[D, 128], [1, D]]))
                col += 128
            if st["samp"][t]:
                final_block(st, t, c0 + npr, NS, ys_o.ap())

        def tm_out(src3, width, dst_ap, wait_names):
            b0 = pairr.next()
            for kc in range(KC):
                I("pe", "transpose", ps[0:width, b0 + kc // 4, (kc % 4) * 128:(kc % 4 + 1) * 128], src3[:, kc, :], identF[:],
                  reads=wait_names + ["identF"], writes=[psn(b0), psn(b0 + 1)], inc=(kc == KC - 1))
            yn, yt = ysr.next()
            I("act", "activation", out=yt[0:width, :].rearrange("p (a b) -> p a b", a=2), in_=ps[0:width, b0:b0 + 2, :], func=AF.Copy,
              reads=[psn(b0), psn(b0 + 1)], writes=yn)
            I("sp", "dma_start", out=dst_ap, in_=yt[0:width, :], reads=yn, dma=True, is_out=True)

        pipe = Pipe()

        def with_st(si, fn, l1=False):
            def g():
                old = (cur["st"], cur["l1"])
                cur["st"], cur["l1"] = si, l1
                fn()
                cur["st"], cur["l1"] = old
            return g

        early = {}

        def groups_for(si):
            st = STS[si]
            nt = len(st["tiles"])
            G = []

            def item(main, post=None, l1=False, post_l1=None):
                pl1 = l1 if post_l1 is None else post_l1
                pipe.item(with_st(si, main, l1), with_st(si, post, pl1) if post is not None else None)

            def load_s1(grp):
                def f(wv_, wname):
                    w3 = v3(wv_, 0, 12288, KC)
                    for sel in range(3):
                        wload(w3[:, :, sel * 512:(sel + 1) * 512], w_in, sel * 1024 + grp * 512, 3 * D, 512, KC, wname)
                return f

            def s0_main(t):
                stage0_main(st, t)
                load_p(st, t, 0)

            def s0_post(t):
                norm(st, t, [(G_MIX0, A, "A")])

            if si == 1:
                early["s0_main0"] = with_st(1, lambda: s0_main(0))
                early["s0_post0"] = with_st(1, lambda: s0_post(0))

            def run_s1a(wv_, wname):
                if si == 0:
                    with_st(si, load_state)()
                def s0(t):
                    item(lambda t=t: s0_main(t), lambda t=t: s0_post(t))

                def s1(t):
                    item(lambda t=t: s1_main(st, t, 0, wv_, wname))

                if si == 1 and early.get("done"):
                    pipe.item(lambda: (early["s0_post0"](), with_st(1, lambda: s0_main(1))()), with_st(1, lambda: s0_post(1)))
                else:
                    s0(0)
                    s0(1)
                s1(0)
                for t in range(2, nt):
                    s0(t)
                    s1(t - 1)
                if si == 0:
                    passthrough()
                s1(nt - 1)

            def run_s1b(wv_, wname):
                for t in range(nt):
                    item(lambda t=t: s1_main(st, t, 1, wv_, wname))
                if si == 0:
                    item(lambda: tm_out(usamp, NS, dap(css_o, D, [[2 * D, NS], [1, D]]), ["usamp"]))
                else:
                    item(lambda: tm_out(uhist, 2, csp_o.ap(), ["uhist%d" % j for j in range(KC)]))

            G.append((load_s1(0), run_s1a))
            G.append((load_s1(1), run_s1b))

            def load_proj(src_t):
                def f(wv_, wname):
                    wload(v3(wv_, 0, 8192, KC), src_t, 0, D, 1024, KC, wname)
                return f

            def run_s2(wv_, wname):
                for t in range(nt):
                    item(lambda t=t: proj_main(st, t, wv_, wname, B, "B"), lambda t=t: norm(st, t, [(G_FFN0, A, "A")]))

            G.append((load_proj(w_out), run_s2))

            def ffn_groups(layer, gnext):
                for gi, (m0, nm_) in enumerate(FFN_GROUPS):
                    def lf(wv_, wname, m0=m0, nm_=nm_):
                        wload(v3(wv_, 0, 4096, KC)[:, :, 0:nm_ * 128], wg, layer * D * FH + m0 * 128, FH, nm_ * 128, KC, wname)
                        wload(v3(wv_, 4096, 8192, KC)[:, :, 0:nm_ * 128], wu, layer * D * FH + m0 * 128, FH, nm_ * 128, KC, wname)
                        wload(v3(wv_, 8192, 12288, 4)[:, 0:nm_, :], wd, layer * FH * D + m0 * 128 * D, D, 1024, nm_, wname)

                    def rf(wv_, wname, nm_=nm_, last=(gi == len(FFN_GROUPS) - 1)):
                        for t in range(nt):
                            post = (lambda t=t: norm(st, t, [(gnext, A, "A")])) if last else None
                            item(lambda t=t: ffn_main(st, t, nm_, wv_, wname), post, l1=(layer == 1))
                    G.append((lf, rf))

            ffn_groups(0, G_PLE0)

            def load_ple(layer):
                def f(wv_, wname):
                    wload(v3(wv_, 0, 8192, KC), pgate, layer * D * D, D, 1024, KC, wname)
                    wload(v3(wv_, 8192, 10240, 2), pproj, layer * PLE * D, D, 1024, 2, wname)
                return f

            def run_ple0(wv_, wname):
                for t in range(nt):
                    item(lambda t=t: ple_main(st, t, wv_, wname), lambda t=t: (norm(st, t, [(G_KV, A, "A"), (G_MIX1, B, "B")]), load_p(st, t, 1)))

            G.append((load_ple(0), run_ple0))

            def load_kvq(wv_, wname):
                wload(v3(wv_, 0, 2048, KC), wk, 0, 256, 256, KC, wname)
                wload(v3(wv_, 2048, 4096, KC), wv, 0, 256, 256, KC, wname)
                Wq3 = v3(wv_, 4096, 12288, KC)
                for pair in range(2):
                    for hh in range(2):
                        for c_ in range(4):
                            col = (pair * 4 + c_) * 128 + hh * 64
                            dst = Wq3[:, :, col:col + 64]
                            src = dap(wq, (pair * 8 + hh * 4 + c_) * 64, [[D, 128], [128 * D, KC], [1, 64]])
                            I("pool", "dma_start", out=dst, in_=src, writes=[wname], dma=True)

            qb_of_tile = [[] for _ in range(nt)]
            for qi in range(st["nqb"]):
                qb_of_tile[tile_of(st, st["ownc0"] + 128 * qi)].append(qi)

            def cap(fn, rings):
                mmr.cur, auxr.cur = rings
                try:
                    return p.capture(with_st(si, fn, cur["l1"]))
                finally:
                    mmr.cur, auxr.cur = mmr.base, auxr.base

            def att_ops(t):
                def f():
                    qs = qb_of_tile[t]
                    for i in range(0, len(qs), 2):
                        att_group(st, qs[i:i + 2])
                return cap(f, ringsA)

            def run_kvq(wv_, wname):
                item(lambda: kvq_main(st, 0, wv_, wname))
                for t in range(1, nt):
                    def main(t=t):
                        a = att_ops(t - 1)
                        b = cap(lambda: kvq_main(st, t, wv_, wname), ringsB)
                        p.merge_replay(a, b)
                    pipe.item(main)

            G.append((load_kvq, run_kvq))

            def run_wo(wv_, wname):
                def main0():
                    a = att_ops(nt - 1)
                    b = cap(lambda: proj_main(st, 0, wv_, wname, B, "B"), ringsB)
                    p.merge_replay(a, b)
                pipe.item(with_st(si, main0, True), with_st(si, lambda: norm(st, 0, [(G_FFN1, A, "A")]), True))
                if si == 0:
                    item(lambda: samp_attn(st))
                for t in range(1, nt):
                    item(lambda t=t: proj_main(st, t, wv_, wname, B, "B"), lambda t=t: norm(st, t, [(G_FFN1, A, "A")]), l1=True)

            G.append((load_proj(wo), run_wo))
            ffn_groups(1, G_PLE1)

            def run_ple1(wv_, wname):
                hoist = (si == 0 and "s0_main0" in early)
                for t in range(nt - 1 if hoist else nt):
                    item(lambda t=t: ple_main(st, t, wv_, wname), lambda t=t: final_post(st, t), l1=True)
                if hoist:
                    tl = nt - 1
                    pipe.item(lambda: (early["s0_main0"](), with_st(0, lambda: ple_main(st, tl, wv_, wname), True)()),
                              with_st(0, lambda: final_post(st, tl), True))
                    early["done"] = True

            G.append((load_ple(1), run_ple1))
            return G

        allg = groups_for(0) + groups_for(1)

        def do_load(i):
            if i < len(allg):
                allg[i][0](wbuf[i % 2], "wbuf%d" % (i % 2))

        do_load(0)
        do_load(1)
        for i, (lf, rf) in enumerate(allg):
            rf(wbuf[i % 2], "wbuf%d" % (i % 2))
            do_load(i + 2)
        pipe.flush()
        p.finish()
        p.emit()
        print("ops per engine:", {k: len(v) for k, v in p.ops.items()})
    return nc


_NC_CACHE = {}


def _rope_tables(pos):
    half = 8
    inv_freq = np.power(np.float32(500000.0), -np.arange(half, dtype=np.float32) / np.float32(half)).astype(np.float32)
    ang = (pos.astype(np.float32)[:, None] * inv_freq[None, :]).astype(np.float32)
    cos = np.cos(ang).astype(np.float32).T
    sin = np.sin(ang).astype(np.float32).T
    n = pos.shape[0]
    C = np.ones((128, n), np.float32)
    S = np.zeros((128, n), np.float32)
    for hb in range(2):
        base = hb * 64
        C[base:base + 8] = cos
        C[base + 8:base + 16] = cos
        S[base:base + 8] = -sin
        S[base + 8:base + 16] = sin
    return C, S


def prepare(x_prompt, x_sample, state_conv, cache_k_win, cache_v_win, p_prompt, p_sample,
           norm_mix_g, norm_ffn_g, norm_ple_g, kv_norm_g, final_norm_g,
           conv_w_in, conv_w, conv_w_out, w_k, w_v, w_q, sinks, w_o,
           ffn_w_gate, ffn_w_up, ffn_w_down, ple_w_proj, ple_w_gate):
    f32 = np.float32
    A_ = lambda a: np.ascontiguousarray(np.asarray(a, dtype=f32))
    x_prompt = A_(x_prompt); x_sample = A_(x_sample); state_conv = A_(state_conv)
    cache_k_win = A_(cache_k_win); cache_v_win = A_(cache_v_win); p_prompt = A_(p_prompt); p_sample = A_(p_sample)

    def colvec(g):
        return np.asarray(g, f32).reshape(KC, 128).T

    gains = [norm_mix_g[0], norm_ffn_g[0], norm_ple_g[0], kv_norm_g, norm_mix_g[1], norm_ffn_g[1], norm_ple_g[1]]
    gvec = np.ascontiguousarray(np.concatenate([colvec(g) for g in gains], axis=1))
    gfin = np.ascontiguousarray(np.broadcast_to(np.asarray(final_norm_g, f32)[None, :], (128, D)))
    cw = np.asarray(conv_w, f32)[0]
    convw = np.ascontiguousarray(np.stack([colvec(cw[j]) for j in range(3)], axis=2).reshape(128, 24))
    sinkb = np.ascontiguousarray(np.broadcast_to(np.asarray(sinks, f32)[0][None, :], (128, 16)))
    idn = np.eye(128, dtype=f32)
    rm = np.zeros((128, 128), f32)
    for m in range(128):
        d = m % 64
        if d < 8:
            rm[m + 8, m] = 1.0
        elif d < 16:
            rm[m - 8, m] = 1.0
    jj = np.arange(128)[:, None]; ii = np.arange(128)[None, :]
    mcur = (jj <= ii).astype(f32); mprev = (jj >= ii).astype(f32)
    esel = np.zeros((128, 16, 16), f32)
    for s in range(16):
        esel[:, s, s] = 1.0
    esel = esel.reshape(128, 256)

    shared = dict(
        w_in=A_(conv_w_in)[0], w_out=A_(conv_w_out)[0], wk=A_(w_k), wv=A_(w_v), wq=A_(w_q)[0], wo=A_(w_o)[0],
        wg=A_(ffn_w_gate), wu=A_(ffn_w_up), wd=A_(ffn_w_down), pproj=A_(ple_w_proj), pgate=A_(ple_w_gate),
        gvec=gvec, gfin=gfin, convw=convw, sinkb=sinkb, idn=idn, rm=rm, esel=esel)

    in_maps = []
    for c in range(NCORES):
        b, half = c // 2, c % 2
        t0 = half * OWN
        xin = np.zeros((XR, D), f32)
        pin = np.zeros((2, XR, PLE), f32)
        if half == 1:
            xin[:] = x_prompt[b, t0 - HALO:t0 + OWN]
            pin[:] = p_prompt[:, b, t0 - HALO:t0 + OWN]
        else:
            xin[HALO:] = x_prompt[b, 0:OWN]
            pin[:, HALO:] = p_prompt[:, b, 0:OWN]
        s0 = c * NS
        pos1 = np.concatenate([np.maximum(t0 - HALO + np.arange(1154), 0), np.full(NS, 16384)]).astype(f32)
        pos2 = (t0 + 1024 + np.arange(1024)).astype(f32)
        C1, S1 = _rope_tables(pos1)
        C2, S2 = _rope_tables(pos2)
        tab = np.ascontiguousarray(np.stack([np.concatenate([C1, C2], 1), np.concatenate([S1, S2], 1)], 0))
        masks = np.ascontiguousarray(np.stack([mcur, mprev, mprev if half == 1 else np.zeros_like(mprev)], 0))
        m = dict(shared)
        m.update(xin=xin, xsm=np.ascontiguousarray(x_sample[s0:s0 + NS, 0]), stc=np.ascontiguousarray(state_conv[0, s0:s0 + NS].reshape(2 * NS, D)),
                 ck=np.ascontiguousarray(cache_k_win[s0:s0 + NS].reshape(NS, 128, 256)), cv=np.ascontiguousarray(cache_v_win[s0:s0 + NS].reshape(NS, 128, 256)),
                 pin=pin, psm=np.ascontiguousarray(p_sample[:, s0:s0 + NS, 0]), masks=masks, tab=tab)
        in_maps.append(m)

    return in_maps


def assemble(R):
    f32 = np.float32
    y_prompt = np.zeros((4, 4096, D), f32); y_sample = np.zeros((128, 1, D), f32)
    csp = np.zeros((1, 4, 2, D), f32); css = np.zeros((1, 128, 2, D), f32)
    kwp = np.zeros((4, 128, 4, 64), f32); vwp = np.zeros((4, 128, 4, 64), f32)
    kws = np.zeros((128, 128, 4, 64), f32); vws = np.zeros((128, 128, 4, 64), f32)
    for c in range(NCORES):
        b, half = c // 2, c % 2
        r = R[c]
        y_prompt[b, half * OWN:(half + 1) * OWN] = r["y"]
        s0 = c * NS
        y_sample[s0:s0 + NS, 0] = r["ys"]
        css[0, s0:s0 + NS] = r["css"]
        kws[s0:s0 + NS] = r["kws"].reshape(NS, 128, 4, 64)
        vws[s0:s0 + NS] = r["vws"].reshape(NS, 128, 4, 64)
        if half == 1:
            csp[0, b] = r["csp"]
            kwp[b] = r["kwp"].reshape(128, 4, 64)
            vwp[b] = r["vwp"].reshape(128, 4, 64)
    return (y_prompt, y_sample, csp, css, kwp, vwp, kws, vws)


def kernel(**inputs):
    in_maps = prepare(**inputs)
    if "nc" not in _NC_CACHE:
        _NC_CACHE["nc"] = build_program()
    nc = _NC_CACHE["nc"]
    res = run_bass_kernel_spmd(nc, in_maps, core_ids=list(range(NCORES)))
    return assemble(res.results)
```
